# Optimizing a Trainium2 kernel written in Bass

```python
import math
import jax, jax.numpy as jnp
from jax import lax
import numpy as np

D_MODEL = 1024
BATCH = 16
SEQ = 256
DEPTH = 4
DEC_BATCH = 4
DEC_SEQ = 1024
PAST_LEN = 512

GRID_W = 64
N_MIXERS = 3
N_ATTN_LAYERS = (DEPTH + 2) // N_MIXERS
N_HGRN_LAYERS = (DEPTH + 1) // N_MIXERS
N_SSM_LAYERS = DEPTH // N_MIXERS

ATTN_HEADS = 16
ATTN_KV_HEADS = 4
ATTN_GROUP = ATTN_HEADS // ATTN_KV_HEADS
HEAD_DIM = D_MODEL // ATTN_HEADS
WINDOW = 128
ATTN_BLOCK = 128
ROPE_BASE = 10000.0

HGRN_EXPAND = 128
HGRN_HEADS = D_MODEL // HGRN_EXPAND
HGRN_DK = HGRN_EXPAND
HGRN_DV = D_MODEL // HGRN_HEADS
HGRN_FDIM = HGRN_HEADS * HGRN_DK
HGRN_VDIM = HGRN_HEADS * HGRN_DV
HGRN_CHUNK = 32

SSM_GROUP = 16
SSM_GROUPS = D_MODEL // SSM_GROUP
SSM_STATE = 64

D_FF = 2816
CONV_WIDTH = 3
NORM_EPS = 1e-6

F32 = jnp.float32

kernel_name = 'hybrid_diffusion_swa_hgrn2_s5_step'


def rmsnorm(x, g):
    xf = x.astype(F32)
    y = xf * lax.rsqrt(jnp.mean(xf * xf, axis=-1, keepdims=True) + NORM_EPS)
    return (y * g.astype(F32)).astype(x.dtype)


def ada_modulation(cond, w, b):
    m = jax.nn.silu(cond) @ w + b
    m = m.reshape(m.shape[:-1] + (1, m.shape[-1]))
    return jnp.split(m, 6, axis=-1)


def modulate(x, shift, scale):
    return x * (1 + scale) + shift


def conv_ffn(xn, w_up, conv_w, conv_b, w_down):
    s = xn.shape[1]
    h = xn @ w_up
    half = CONV_WIDTH // 2
    hp = jnp.pad(h, ((0, 0), (half, half), (0, 0)))
    h = sum(hp[:, j:j + s] * conv_w[j] for j in range(CONV_WIDTH)) + conv_b
    gate, val = jnp.split(h, 2, axis=-1)
    return (jax.nn.silu(gate) * val) @ w_down


def axial_rope(s):
    n_rows = s // GRID_W
    row = jnp.repeat(jnp.arange(n_rows, dtype=F32), GRID_W)
    col = jnp.tile(jnp.arange(GRID_W, dtype=F32), n_rows)
    half = HEAD_DIM // 2
    inv_freq = 1.0 / (ROPE_BASE ** (jnp.arange(0, half, 2, dtype=F32) / half))
    ar = row[:, None] * inv_freq
    ac = col[:, None] * inv_freq
    ang = jnp.concatenate([ar, ar, ac, ac], axis=-1)
    return jnp.cos(ang), jnp.sin(ang)


def _rotate_half(x):
    a, b = jnp.split(x, 2, axis=-1)
    return jnp.concatenate([-b, a], axis=-1)


def apply_rope(x, cos, sin):
    shape = (x.shape[1],) + (1,) * (x.ndim - 3) + (HEAD_DIM,)
    cos = cos.reshape(shape)
    sin = sin.reshape(shape)
    xf = x.astype(F32)
    xr, xc = jnp.split(xf, 2, axis=-1)
    rot = jnp.concatenate([_rotate_half(xr), _rotate_half(xc)], axis=-1)
    return (xf * cos + rot * sin).astype(x.dtype)


def attn_qkv(xn, w):
    b, s, _ = xn.shape
    nq = ATTN_HEADS * HEAD_DIM
    nk = ATTN_KV_HEADS * HEAD_DIM
    qkv = xn @ w
    q = qkv[..., :nq].reshape(b, s, ATTN_KV_HEADS, ATTN_GROUP, HEAD_DIM)
    k = qkv[..., nq:nq + nk].reshape(b, s, ATTN_KV_HEADS, HEAD_DIM)
    v = qkv[..., nq + nk:].reshape(b, s, ATTN_KV_HEADS, HEAD_DIM)
    return q, k, v


def context_attention(q, k, v, sink):
    b, l = q.shape[:2]
    s = jnp.einsum('blhgd,bmhd->bhglm', q, k).astype(F32) * HEAD_DIM ** -0.5
    s_sink = jnp.broadcast_to(sink.astype(F32).reshape(1, ATTN_KV_HEADS, ATTN_GROUP, 1, 1), s.shape[:-1] + (1,))
    p = jax.nn.softmax(jnp.concatenate([s, s_sink], axis=-1), axis=-1)[..., :l]
    o = jnp.einsum('bhglm,bmhd->blhgd', p.astype(v.dtype), v)
    return o.reshape(b, l, ATTN_HEADS * HEAD_DIM)


def latent_attention(q, k, v, ck, cv, sink):
    b, s = q.shape[:2]
    blk = ATTN_BLOCK
    nb = s // blk
    scale = HEAD_DIM ** -0.5
    qb = q.reshape(b, nb, blk, ATTN_KV_HEADS, ATTN_GROUP, HEAD_DIM)
    pad = ((0, 0), (blk, blk), (0, 0), (0, 0))
    kp = jnp.pad(k, pad).reshape(b, nb + 2, blk, ATTN_KV_HEADS, HEAD_DIM)
    vp = jnp.pad(v, pad).reshape(b, nb + 2, blk, ATTN_KV_HEADS, HEAD_DIM)
    kb = jnp.concatenate([kp[:, 0:nb], kp[:, 1:nb + 1], kp[:, 2:nb + 2]], axis=2)
    vb = jnp.concatenate([vp[:, 0:nb], vp[:, 1:nb + 1], vp[:, 2:nb + 2]], axis=2)
    rel = jnp.arange(3 * blk)[None, :] - blk - jnp.arange(blk)[:, None]
    band = jnp.abs(rel) <= WINDOW
    kpos = jnp.arange(nb)[:, None] * blk - blk + jnp.arange(3 * blk)[None, :]
    valid = (kpos >= 0) & (kpos < s)
    mask = band[None, :, :] & valid[:, None, :]
    s_lat = jnp.einsum('bnqhgd,bnkhd->bnhgqk', qb, kb).astype(F32) * scale
    s_lat = jnp.where(mask[None, :, None, None], s_lat, -jnp.inf)
    s_ctx = jnp.einsum('bnqhgd,blhd->bnhgql', qb, ck).astype(F32) * scale
    s_sink = jnp.broadcast_to(sink.astype(F32).reshape(1, 1, ATTN_KV_HEADS, ATTN_GROUP, 1, 1), s_lat.shape[:-1] + (1,))
    p = jax.nn.softmax(jnp.concatenate([s_lat, s_ctx, s_sink], axis=-1), axis=-1)
    n_lat = 3 * blk
    n_ctx = ck.shape[1]
    p_lat = p[..., :n_lat].astype(v.dtype)
    p_ctx = p[..., n_lat:n_lat + n_ctx].astype(cv.dtype)
    o = jnp.einsum('bnhgqk,bnkhd->bnqhgd', p_lat, vb) + jnp.einsum('bnhgql,blhd->bnqhgd', p_ctx, cv)
    return o.reshape(b, s, ATTN_HEADS * HEAD_DIM)


def hgrn_chunk_scan(q, k, v, log_f, s0):
    b, s, h, dk = q.shape
    dv = v.shape[-1]
    c = HGRN_CHUNK
    nc = s // c

    def chunks(t):
        return t.reshape(b, nc, c, h, t.shape[-1]).transpose(1, 0, 3, 2, 4)

    causal = jnp.tril(jnp.ones((c, c), dtype=bool))[:, :, None]

    def step(state, xs):
        qc, kc, vc, gc = xs
        cum = jnp.cumsum(gc, axis=2)
        inter = jnp.einsum('bhtd,bhdv->bhtv', qc * jnp.exp(cum), state)
        diff = cum[:, :, :, None, :] - cum[:, :, None, :, :]
        decay = jnp.exp(jnp.where(causal, diff, -jnp.inf))
        att = jnp.einsum('bhtsd,bhsd->bhts', qc[:, :, :, None, :] * decay, kc)
        out = inter + jnp.einsum('bhts,bhsv->bhtv', att, vc)
        last = cum[:, :, -1, :]
        k_dec = kc * jnp.exp(last[:, :, None, :] - cum)
        new_state = jnp.exp(last)[..., None] * state + jnp.einsum('bhsd,bhsv->bhdv', k_dec, vc)
        return new_state, out

    s_fin, o = lax.scan(step, s0, (chunks(q), chunks(k), chunks(v), chunks(log_f)))
    o = o.transpose(1, 0, 3, 2, 4).reshape(b, s, h, dv)
    return o, s_fin


def hgrn_mix(xn, w_in, lb, g_norm, w_o, s0_f, s0_b):
    b, s, _ = xn.shape
    proj = (xn @ w_in).astype(F32)
    cuts = (HGRN_FDIM, HGRN_FDIM + HGRN_VDIM, 2 * HGRN_FDIM + HGRN_VDIM, 3 * HGRN_FDIM + HGRN_VDIM)
    q, i, zf, zb, g = jnp.split(proj, cuts, axis=-1)

    def heads(t):
        return t.reshape(b, s, HGRN_HEADS, t.shape[-1] // HGRN_HEADS)

    q, i, zf, zb, g = heads(q), heads(i), heads(zf), heads(zb), heads(g)

    def log_forget(z, lbd):
        lbd = lbd.astype(F32).reshape(HGRN_HEADS, HGRN_DK)
        return jnp.logaddexp(jnp.log(lbd), jnp.log1p(-lbd) + jax.nn.log_sigmoid(z))

    lf_f = log_forget(zf, lb[0])
    lf_b = log_forget(zb, lb[1])
    o_f, s_f = hgrn_chunk_scan(q, -jnp.expm1(lf_f), i, lf_f, s0_f)
    o_b, s_b = hgrn_chunk_scan(jnp.flip(q, 1), jnp.flip(-jnp.expm1(lf_b), 1), jnp.flip(i, 1), jnp.flip(lf_b, 1), s0_b)
    o = o_f + jnp.flip(o_b, 1)
    o = o * lax.rsqrt(jnp.mean(o * o, axis=-1, keepdims=True) + NORM_EPS) * g_norm.astype(F32) * jax.nn.silu(g)
    return o.reshape(b, s, HGRN_VDIM).astype(xn.dtype) @ w_o, s_f, s_b


def ssm_discretize(a_re, a_im, log_dt, b_re, b_im):
    lam = lax.complex(jnp.minimum(a_re.astype(F32), -1e-4), a_im.astype(F32))
    dt = jnp.exp(log_dt.astype(F32))[:, None]
    lam_bar = jnp.exp(lam * dt)
    b_bar = ((lam_bar - 1.0) / lam)[..., None] * lax.complex(b_re.astype(F32), b_im.astype(F32))
    return lam_bar, b_bar


def ssm_scan(u, lam_bar, b_bar, h0):
    bu = jnp.einsum('bsgi,gpi->bsgp', u.astype(jnp.complex64), b_bar)
    bu = bu.at[:, 0].add(lam_bar * h0)
    a = jnp.broadcast_to(lam_bar, bu.shape)

    def combine(e1, e2):
        a1, b1 = e1
        a2, b2 = e2
        return a1 * a2, a2 * b1 + b2

    _, h = lax.associative_scan(combine, (a, bu), axis=1)
    return h


def ssm_mix(xn, a_re, a_im, log_dt, b_re, b_im, c_re, c_im, d, w_glu, h0_f, h0_b):
    b, s, dm = xn.shape
    u = xn.astype(F32).reshape(b, s, SSM_GROUPS, SSM_GROUP)
    y = d.astype(F32) * xn.astype(F32)
    finals = []
    for direction, h0 in enumerate((h0_f, h0_b)):
        lam_bar, b_bar = ssm_discretize(a_re[direction], a_im[direction], log_dt[direction], b_re[direction], b_im[direction])
        ud = u if direction == 0 else jnp.flip(u, axis=1)
        h = ssm_scan(ud, lam_bar, b_bar, h0)
        c_mat = lax.complex(c_re[direction].astype(F32), c_im[direction].astype(F32))
        yd = jnp.einsum('gip,bsgp->bsgi', c_mat, h).real.reshape(b, s, dm)
        y = y + (yd if direction == 0 else jnp.flip(yd, axis=1))
        finals.append(h[:, -1])
    g = jax.nn.gelu(y).astype(xn.dtype)
    val, gate = jnp.split(g @ w_glu, 2, axis=-1)
    return val * jax.nn.sigmoid(gate), finals[0], finals[1]


def to_complex(st):
    st = st.astype(F32)
    return lax.complex(st[..., 0], st[..., 1])


def from_complex(h):
    return jnp.stack([h.real, h.imag], axis=-1)


def setup_inputs(seed: int = 0) -> dict:
    key = jax.random.key(seed)
    ks = iter(jax.random.split(key, 40))

    def nrm(shape, scale):
        return scale * jax.random.normal(next(ks), shape, F32)

    D = D_MODEL
    qkv_w = (ATTN_HEADS + 2 * ATTN_KV_HEADS) * HEAD_DIM
    hg_in = 3 * HGRN_FDIM + 2 * HGRN_VDIM
    ssm_gp = (N_SSM_LAYERS, 2, SSM_GROUPS, SSM_STATE)
    n_idx = jnp.arange(SSM_STATE, dtype=F32)
    return {
        'x_prompt': nrm((BATCH, SEQ, D), 1.0),
        'x_sample': nrm((DEC_BATCH, DEC_SEQ, D), 1.0),
        'cache_k': nrm((DEC_BATCH, N_ATTN_LAYERS, PAST_LEN, ATTN_KV_HEADS, HEAD_DIM), 1.0),
        'cache_v': nrm((DEC_BATCH, N_ATTN_LAYERS, PAST_LEN, ATTN_KV_HEADS, HEAD_DIM), 1.0),
        'state_hgrn': nrm((DEC_BATCH, N_HGRN_LAYERS, 2, HGRN_HEADS, HGRN_DK, HGRN_DV), 0.5),
        'state_ssm': nrm((DEC_BATCH, N_SSM_LAYERS, 2, SSM_GROUPS, SSM_STATE, 2), 0.1),
        'c': nrm((DEC_BATCH, D), 1.0),
        'c_ctx': nrm((D,), 1.0),
        'ada_w': nrm((DEPTH, D, 6 * D), 0.5 * D ** -0.5),
        'ada_b': nrm((DEPTH, 6 * D), 0.01),
        'norm1_g': 1.0 + nrm((DEPTH, D), 0.02),
        'norm2_g': 1.0 + nrm((DEPTH, D), 0.02),
        'attn_wqkv': nrm((N_ATTN_LAYERS, D, qkv_w), D ** -0.5),
        'attn_wo': nrm((N_ATTN_LAYERS, ATTN_HEADS * HEAD_DIM, D), (ATTN_HEADS * HEAD_DIM) ** -0.5),
        'attn_sink': nrm((N_ATTN_LAYERS, ATTN_HEADS), 0.5),
        'hgrn_w_in': nrm((N_HGRN_LAYERS, D, hg_in), D ** -0.5),
        'hgrn_lb': nrm((DEPTH, 2, HGRN_FDIM), 0.5),
        'hgrn_g_norm': 1.0 + nrm((N_HGRN_LAYERS, HGRN_DV), 0.02),
        'hgrn_wo': nrm((N_HGRN_LAYERS, HGRN_VDIM, D), HGRN_VDIM ** -0.5),
        'ssm_a_re': -0.5 + nrm(ssm_gp, 0.01),
        'ssm_a_im': math.pi * n_idx + nrm(ssm_gp, 0.01),
        'ssm_log_dt': jax.random.uniform(next(ks), (N_SSM_LAYERS, 2, SSM_GROUPS), F32, math.log(1e-3), math.log(1e-1)),
        'ssm_b_re': nrm(ssm_gp + (SSM_GROUP,), (2 * SSM_GROUP) ** -0.5),
        'ssm_b_im': nrm(ssm_gp + (SSM_GROUP,), (2 * SSM_GROUP) ** -0.5),
        'ssm_c_re': nrm((N_SSM_LAYERS, 2, SSM_GROUPS, SSM_GROUP, SSM_STATE), SSM_STATE ** -0.5),
        'ssm_c_im': nrm((N_SSM_LAYERS, 2, SSM_GROUPS, SSM_GROUP, SSM_STATE), SSM_STATE ** -0.5),
        'ssm_d': nrm((N_SSM_LAYERS, D), 0.5),
        'ssm_w_glu': nrm((N_SSM_LAYERS, D, 2 * D), D ** -0.5),
        'ffn_w_up': nrm((DEPTH, D, 2 * D_FF), D ** -0.5),
        'ffn_conv_w': nrm((DEPTH, CONV_WIDTH, 2 * D_FF), 0.6),
        'ffn_conv_b': nrm((DEPTH, 2 * D_FF), 0.01),
        'ffn_w_down': nrm((DEPTH, D_FF, D), D_FF ** -0.5),
        'final_g': 1.0 + nrm((D,), 0.02),
    }


def reference(x_prompt, x_sample, cache_k, cache_v, state_hgrn, state_ssm, c, c_ctx,
              ada_w, ada_b, norm1_g, norm2_g, attn_wqkv, attn_wo, attn_sink,
              hgrn_w_in, hgrn_lb, hgrn_g_norm, hgrn_wo,
              ssm_a_re, ssm_a_im, ssm_log_dt, ssm_b_re, ssm_b_im, ssm_c_re, ssm_c_im, ssm_d, ssm_w_glu,
              ffn_w_up, ffn_conv_w, ffn_conv_b, ffn_w_down, final_g):
    bp = x_prompt.shape[0]
    s_lat = x_sample.shape[1]
    rope_cos, rope_sin = axial_rope(s_lat)
    lb_p = jax.nn.softmax(hgrn_lb.astype(F32), axis=0)
    lb_all = jnp.cumsum(lb_p, axis=0) - lb_p[:1]

    xp, xs = x_prompt, x_sample
    new_k, new_v, new_hgrn, new_ssm = [], [], [], []
    for l in range(DEPTH):
        kind, j = l % N_MIXERS, l // N_MIXERS
        mod_p = ada_modulation(c_ctx, ada_w[l], ada_b[l])
        mod_s = ada_modulation(c, ada_w[l], ada_b[l])
        hp = modulate(rmsnorm(xp, norm1_g[l]), mod_p[0], mod_p[1])
        hs = modulate(rmsnorm(xs, norm1_g[l]), mod_s[0], mod_s[1])
        if kind == 0:
            q, k, v = attn_qkv(hp, attn_wqkv[j])
            op = context_attention(q, k, v, attn_sink[j]) @ attn_wo[j]
            new_k.append(k)
            new_v.append(v)
            q, k, v = attn_qkv(hs, attn_wqkv[j])
            q = apply_rope(q, rope_cos, rope_sin)
            k = apply_rope(k, rope_cos, rope_sin)
            os_ = latent_attention(q, k, v, cache_k[:, j], cache_v[:, j], attn_sink[j]) @ attn_wo[j]
        elif kind == 1:
            z = jnp.zeros((bp, HGRN_HEADS, HGRN_DK, HGRN_DV), F32)
            op, sf, sb = hgrn_mix(hp, hgrn_w_in[j], lb_all[l], hgrn_g_norm[j], hgrn_wo[j], z, z)
            new_hgrn.append(jnp.stack([sf, sb], axis=1))
            os_, _, _ = hgrn_mix(hs, hgrn_w_in[j], lb_all[l], hgrn_g_norm[j], hgrn_wo[j],
                                 state_hgrn[:, j, 0].astype(F32), state_hgrn[:, j, 1].astype(F32))
        else:
            ssm_args = (ssm_a_re[j], ssm_a_im[j], ssm_log_dt[j], ssm_b_re[j], ssm_b_im[j],
                        ssm_c_re[j], ssm_c_im[j], ssm_d[j], ssm_w_glu[j])
            z = jnp.zeros((bp, SSM_GROUPS, SSM_STATE), jnp.complex64)
            op, hf, hb = ssm_mix(hp, *ssm_args, z, z)
            new_ssm.append(jnp.stack([from_complex(hf), from_complex(hb)], axis=1))
            os_, _, _ = ssm_mix(hs, *ssm_args, to_complex(state_ssm[:, j, 0]), to_complex(state_ssm[:, j, 1]))
        xp = xp + mod_p[2] * op
        xs = xs + mod_s[2] * os_
        hp = modulate(rmsnorm(xp, norm2_g[l]), mod_p[3], mod_p[4])
        hs = modulate(rmsnorm(xs, norm2_g[l]), mod_s[3], mod_s[4])
        xp = xp + mod_p[5] * conv_ffn(hp, ffn_w_up[l], ffn_conv_w[l], ffn_conv_b[l], ffn_w_down[l])
        xs = xs + mod_s[5] * conv_ffn(hs, ffn_w_up[l], ffn_conv_w[l], ffn_conv_b[l], ffn_w_down[l])

    y_prompt = rmsnorm(xp, final_g)
    y_sample = rmsnorm(xs, final_g)
    new_cache_k = jnp.stack(new_k, axis=1)
    new_cache_v = jnp.stack(new_v, axis=1)
    new_state_hgrn = jnp.stack(new_hgrn, axis=1)
    new_state_ssm = jnp.stack(new_ssm, axis=1)
    return (y_prompt, y_sample, new_cache_k, new_cache_v, new_state_hgrn, new_state_ssm)
```

```python
import numpy as np
from concourse.bass_utils import run_bass_kernel_spmd

from contextlib import ExitStack
import concourse.bass as bass
import concourse.mybir as mybir

F32 = mybir.dt.float32
F32R = mybir.dt.float32r
BF16 = mybir.dt.bfloat16
AF = mybir.ActivationFunctionType
ALU = mybir.AluOpType
AX = mybir.AxisListType


class Prog:
    ENGS = ("pe", "act", "dve", "pool", "sp")

    def __init__(self, nc, es: ExitStack):
        self.nc = nc
        self.es = es
        self.recs = {e: [] for e in self.ENGS}
        self.cnt = {e: 0 for e in self.ENGS}
        self.known = {e: {} for e in self.ENGS}
        self.state = {}
        self.sems = {}
        self.dcnt = {}
        for e in self.ENGS:
            self.sems[("e", e)] = es.enter_context(nc.semaphore("sem_" + e))
        self.psn = 0

    def sb(self, name, shape, dt=F32):
        return self.es.enter_context(self.nc.sbuf_tensor("sb_" + name, list(shape), dt))

    def ps(self, name, shape, dt=F32):
        return self.es.enter_context(self.nc.psum_tensor(name, list(shape), dt))

    def dsem(self, name):
        k = ("d", name)
        if k not in self.sems:
            self.sems[k] = self.es.enter_context(self.nc.semaphore("dsem_" + name))
            self.dcnt[k] = 0
        return k

    def _st(self, k):
        s = self.state.get(k)
        if s is None:
            s = {"w": {}, "r": {}}
            self.state[k] = s
        return s

    def _deps(self, eng, r, w):
        deps = {}
        def add(d):
            if d is None:
                return
            sk, v = d
            if deps.get(sk, 0) < v:
                deps[sk] = v
        for k in r:
            st = self._st(k)
            for sk, v in st["w"].items():
                add((sk, v))
            if isinstance(k, tuple) and k[0] == "pq":
                for sk, v in st["r"].items():
                    if sk != ("e", eng):
                        add((sk, v))
        for k in w:
            s = self._st(k)
            for sk, v in s["w"].items():
                add((sk, v))
            for sk, v in s["r"].items():
                add((sk, v))
        waits = []
        kn = self.known[eng]
        for sk, v in deps.items():
            if eng == "pe" and sk == ("e", "pe"):
                continue
            if kn.get(sk, 0) < v:
                waits.append((sk, v))
                kn[sk] = v
        return waits

    def _commit(self, comp, r, w):
        sk, v = comp
        for k in w:
            self.state[k] = {"w": {sk: v}, "r": {}}
        for k in r:
            s = self._st(k)
            if s["r"].get(sk, 0) < v:
                s["r"][sk] = v

    def alias(self, dst, src):
        mw, mr = {}, {}
        for k in src:
            st = self._st(k)
            for sk, v in st["w"].items():
                mw[sk] = max(mw.get(sk, 0), v)
            for sk, v in st["r"].items():
                mr[sk] = max(mr.get(sk, 0), v)
        for k in dst:
            self.state[k] = {"w": dict(mw), "r": dict(mr)}

    def op(self, eng, fn, r=(), w=(), inc=True):
        inc = True
        waits = self._deps(eng, r, w)
        sk = ("e", eng)
        comp = (sk, self.cnt[eng] + 1)
        if inc:
            self.cnt[eng] += 1
        self.recs[eng].append((waits, fn, (sk, 1) if inc else None))
        self._commit(comp, r, w)

    def dma(self, out, in_, r=(), w=(), sem=None, q="sp", **kw):
        if sem is None:
            sem = "_".join(str(x) for x in (w[0] if isinstance(w[0], tuple) else (w[0],)))
        waits = self._deps(q, r, w)
        sk = self.dsem(sem)
        self.dcnt[sk] += 16
        comp = (sk, self.dcnt[sk])
        self.recs[q].append((waits, (lambda e, o=out, i=in_, kw=kw: e.dma_start(out=o, in_=i, **kw)), (sk, 16)))
        self._commit(comp, r, w)

    def wait_all_dma(self, q="sp"):
        waits = []
        for sk, v in self.dcnt.items():
            if v > 0 and self.known[q].get(sk, 0) < v:
                waits.append((sk, v))
                self.known[q][sk] = v
        self.recs[q].append((waits, None, None))

    def barrier_all(self):
        tgt = {("e", e): self.cnt[e] for e in self.ENGS if self.cnt[e] > 0}
        for sk, v in self.dcnt.items():
            if v > 0:
                tgt[sk] = v
        for e in self.ENGS:
            waits = []
            for sk, v in tgt.items():
                if sk == ("e", e):
                    continue
                if self.known[e].get(sk, 0) < v:
                    waits.append((sk, v))
                    self.known[e][sk] = v
            if waits:
                self.recs[e].append((waits, None, None))

    def emit(self):
        nc = self.nc
        sems = self.sems
        recs = self.recs

        def replay(name):
            def f(e):
                for waits, fn, inc in recs[name]:
                    for sk, v in waits:
                        e.wait_ge(sems[sk], v)
                    if fn is not None:
                        ins = fn(e)
                        if inc is not None:
                            ins.then_inc(sems[inc[0]], inc[1])
            return f

        with nc.Block() as block:
            block.tensor(replay("pe"))
            block.scalar(replay("act"))
            block.vector(replay("dve"))
            block.gpsimd(replay("pool"))
            block.sync(replay("sp"))

    def stats(self):
        return {e: len(self.recs[e]) for e in self.ENGS}

D = 1024
NT = 1024
DFF = 2816
NFC = 22
DEPTH = 4
EPS = 1e-6

STAGE = {"mixers": True, "layers": 4, "kv_only": False}


def build_program():
    nc = bass.Bass("TRN2", target_bir_lowering=False)

    def din(name, shape, dt=F32):
        return nc.dram_tensor(name, list(shape), dt, kind="ExternalInput").ap()

    def dout(name, shape, dt=F32):
        return nc.dram_tensor(name, list(shape), dt, kind="ExternalOutput").ap()

    I = {}
    I["x"] = din("x", [D + 256, NT])
    I["cond"] = din("cond", [128, 8])
    I["keep"] = din("keep", [128, 2])
    I["ident"] = din("ident", [128, 128])
    I["ada_w"] = din("ada_w", [DEPTH, 12, 128, 8, 512])
    I["ada_b"] = din("ada_b", [DEPTH, 128, 48])
    I["ng"] = din("ng", [128, 2, DEPTH, 8])
    I["ffn_w_up"] = din("ffn_w_up", [DEPTH, NFC, 128, 2, 8, 128])
    I["ffn_conv_w"] = din("ffn_conv_w", [DEPTH, 128, 3, 2 * NFC])
    I["ffn_conv_b"] = din("ffn_conv_b", [DEPTH, 128, 2 * NFC])
    I["ffn_w_down"] = din("ffn_w_down", [DEPTH, 8, 128, NFC, 128])
    I["final_g"] = din("final_g", [128, 8])
    I["wkv"] = din("wkv", [2, 128, 8, 512])
    I["hw_in"] = din("hw_in", [8, 5, 128, 8, 128])
    I["hwo"] = din("hwo", [8, 128, 8, 128])
    I["hlb"] = din("hlb", [128, 4, 2, 8])
    I["hgn"] = din("hgn", [128, 1])
    I["hs0"] = din("hs0", [2, 8, 128, 128])
    I["hmask"] = din("hmask", [128, 2, 128])
    I["sBreR"] = din("sBreR", [2, 8, 128, 64]); I["sBimR"] = din("sBimR", [2, 8, 128, 64])
    I["sAreR"] = din("sAreR", [2, 8, 128, 64]); I["sAimR"] = din("sAimR", [2, 8, 128, 64])
    I["sDtR"] = din("sDtR", [2, 8, 128, 1])
    I["sAreQ"] = din("sAreQ", [2, 128, 32]); I["sAimQ"] = din("sAimQ", [2, 128, 32]); I["sDtQ"] = din("sDtQ", [2, 128, 32])
    I["sCreQ"] = din("sCreQ", [2, 128, 32, 16]); I["sCimQ"] = din("sCimQ", [2, 128, 32, 16])
    I["sH0"] = din("sH0", [2, 128, 32, 2])
    I["sD"] = din("sD", [128, 8])
    I["smask"] = din("smask", [128, 12])
    I["wglu"] = din("wglu", [16, 128, 8, 128])
    if not STAGE.get("kv_only"):
        I["wq"] = din("wq", [2, 8, 128, 8, 128])
        I["wk"] = din("wk", [2, 4, 128, 8, 128])
        I["wo"] = din("wo", [2, 8, 128, 8, 128])
        I["sink"] = din("sink", [2, 128, 16])
        I["rmat"] = din("rmat", [128, 128])
        I["amask"] = din("amask", [128, 8, 2, 128])
        I["ckd"] = din("ckd", [2, 512, 512])
        I["cvp"] = din("cvp", [2, 512, 1024])
    O = {}
    O["y"] = dout("y", [D, NT])
    O["nk"] = dout("nk", [2, NT, 256])
    O["nv"] = dout("nv", [2, NT, 256])
    O["hg"] = dout("hg", [64, 128, 128])
    O["ssm"] = dout("ssm", [8, 64, 64, 2])
    if STAGE.get("ssm_dbg"):
        O["dbg"] = dout("dbg", [128, 4096 + 1024 + 512])
        O["dbg2"] = dout("dbg2", [128, 6 * 1024])
        O["dbg3"] = dout("dbg3", [128, 8 * 1024], BF16)

    with ExitStack() as es:
        P = Prog(nc, es)
        xT = P.sb("xT", [128, 10, NT])
        hT = P.sb("hT", [128, 8, NT], BF16)
        aT = P.sb("aT", [128, 12, NT], BF16)
        rstd = P.sb("rstd", [128, NT])
        tmpA = P.sb("tmpA", [128, NT])
        tmpB = P.sb("tmpB", [128, NT])
        cg = [P.sb("cg%d" % i, [128, NT]) for i in range(2)]
        cv = [P.sb("cv%d" % i, [128, NT]) for i in range(2)]
        sq = [P.sb("sq%d" % i, [128, NT], BF16) for i in range(2)]
        ident = P.sb("ident", [128, 128])
        ones_bf = P.sb("ones_bf", [128, 128], BF16)
        one_f = P.sb("one_f", [128, 1])
        keep = P.sb("keep", [128, 2])
        cond = P.sb("cond", [128, 8])
        s_bf = P.sb("s_bf", [128, 8], BF16)
        mod = P.sb("mod", [128, 48])
        adab = P.sb("adab", [128, 48])
        ng = P.sb("ng", [128, 2, DEPTH, 8])
        fg = P.sb("fg", [128, 8])
        AB = P.sb("AB", [128, 4, 8])
        cw = P.sb("cw", [128, 3, 2 * NFC])
        cb = P.sb("cb", [128, 2 * NFC])
        cwk = P.sb("cwk", [128, 2, 2 * NFC])
        wada = [P.sb("wada%d" % i, [128, 8, 512], BF16) for i in range(2)]
        wup = [P.sb("wup%d" % i, [128, 2, 8, 128], BF16) for i in range(2)]
        wdn = [P.sb("wdn%d" % i, [128, 11, 128], BF16) for i in range(2)]
        vlat = P.sb("vlat", [128, 8, 4, 2, 128], BF16)
        vctx = P.sb("vctx", [128, 4, 4, 2, 128], BF16)
        kctxT = P.sb("kctxT", [128, 4, 512], BF16)
        ckd = P.sb("ckd", [128, 4, 512], BF16)
        amask = P.sb("amask", [128, 8, 2, 128], BF16)
        rmat = P.sb("rmat", [128, 128], BF16)
        ident_bf = P.sb("ident_bf", [128, 128], BF16)
        qb = [P.sb("qb%d" % i, [128, 512], BF16) for i in range(2)]
        esink = P.sb("esink", [128, 16])
        ones_f = P.sb("ones_f", [128, 128])
        m32 = P.sb("m32", [128, NT])
        hlb = P.sb("hlb", [128, 4, 2, 8])
        lbp = P.sb("lbp", [128, 2, 2, 8])
        hgn = P.sb("hgn", [128, 1])
        hmask = P.sb("hmask", [128, 2, 128], BF16)
        Sf = [P.sb("Sf%d" % i, [128, 128]) for i in range(2)]
        Sb = [P.sb("Sb%d" % i, [128, 128], BF16) for i in range(2)]
        adec = [P.sb("adec%d" % i, [128, 32]) for i in range(2)]
        rowm = P.sb("rowm", [128, 4])
        ctmp = P.sb("ctmp", [128, 32])
        vtok = P.sb("vtok", [128, 8, 128], BF16)
        qhat = [P.sb("qhat%d" % i, [128, NT], BF16) for i in range(2)]
        ktil = [P.sb("ktil%d" % i, [128, NT], BF16) for i in range(2)]
        kdT = [P.sb("kdT%d" % i, [128, NT], BF16) for i in range(2)]
        attm = [P.sb("attm%d" % i, [128, 128], BF16) for i in range(2)]
        smask = P.sb("smask", [128, 12])
        sD = P.sb("sD", [128, 8])
        sH = P.sb("sH", [128, 32, 2])
        sHent = P.sb("sHent", [128, 4, 2])
        sHx = P.sb("sHx", [128, 4, 2])
        pq = [P.ps("pq%d" % i, [128, 1024]) for i in range(4)]

        def bank(i):
            return pq[i // 2][:, (i % 2) * 512:(i % 2) * 512 + 512], ("pq", i)

        P.dma(ident[:], I["ident"], w=["ident"])
        P.dma(keep[:], I["keep"], w=["keep"])
        P.dma(cond[:], I["cond"], w=["cond"])
        P.dma(ng[:], I["ng"], w=["ng"])
        P.dma(fg[:], I["final_g"], w=["fg"])
        P.op("dve", lambda e: e.memset(ones_bf[:], 1.0), w=["ones_bf"])
        P.op("dve", lambda e: e.memset(one_f[:], 1.0), w=["one_f"])
        P.op("dve", lambda e: e.memset(ones_f[:], 1.0), w=["ones_f"])
        P.op("dve", lambda e: e.tensor_copy(out=ident_bf[:], in_=ident[:]), r=["ident"], w=["ident_bf"])
        if not STAGE.get("kv_only"):
            P.dma(rmat[:], I["rmat"], w=["rmat"], q="pool")
            P.dma(amask[:], I["amask"], w=["amask"], q="pool")
        P.op("pool", lambda e: e.memset(m32[:], 1.0), w=["m32"])
        P.op("pool", lambda e: e.memset(m32[:, 0:NT:32], 0.0), w=["m32"])
        P.op("pool", lambda e: e.memset(rowm[:], 0.0), w=["rowm"])
        for c4 in range(3):
            P.op("pool", lambda e, c4=c4: e.memset(rowm[c4 * 32:(c4 + 1) * 32, c4:c4 + 1], 1.0), w=["rowm"])
        P.op("pool", lambda e: e.memset(rowm[96:128, 3:4], 1.0), w=["rowm"])
        P.dma(hlb[:], I["hlb"], w=["hlb"])
        P.dma(hgn[:], I["hgn"], w=["hgn"])
        P.dma(hmask[:], I["hmask"], w=["hmask"], q="pool")
        P.op("dve", lambda e: e.memset(vlat[:], 0.0), w=["vlat"])
        P.op("dve", lambda e: e.memset(vlat[:, :, :, 0, 64:65], 1.0), w=["vlat"])
        P.op("dve", lambda e: e.memset(vlat[:, :, :, 1, 0:1], 1.0), w=["vlat"])
        P.op("act", lambda e: e.activation(out=s_bf[:], in_=cond[:], func=AF.Silu), r=["cond"], w=["s_bf"])

        P.dma(xT[:], I["x"].rearrange("(k p) t -> p k t", p=128), w=[("xT", k) for k in range(10)], sem="xin")

        def ada_layer(l):
            P.dma(adab[:], I["ada_b"][l], w=["adab"])
            ps, pk = bank(4)
            for n in range(12):
                wb = wada[n % 2]
                P.dma(wb[:], I["ada_w"][l, n],
                      w=[("wada", n % 2)], q="pool")
                for c4 in range(4):
                    c = n * 4 + c4
                    for k in range(8):
                        P.op("pe", lambda e, ps=ps, k=k, wb=wb, c=c, c4=c4: e.matmul(ps[:, c:c + 1], lhsT=wb[:, k, c4 * 128:(c4 + 1) * 128], rhs=s_bf[:, k:k + 1], start=(k == 0), stop=(k == 7)),
                             r=[("wada", n % 2), "s_bf"], w=[pk])
            P.op("dve", lambda e, ps=ps: e.tensor_tensor(out=mod[:], in0=ps[:, 0:48], in1=adab[:], op=ALU.add), r=[pk, "adab"], w=["mod"])
            for j in range(2):
                P.op("dve", lambda e, j=j: e.scalar_tensor_tensor(out=AB[:, 2 * j, :], in0=mod[:, (3 * j + 1) * 8:(3 * j + 2) * 8], scalar=1.0,
                                                                 in1=ng[:, j, l, :], op0=ALU.add, op1=ALU.mult),
                     r=["mod", "ng"], w=["AB"])
                P.op("dve", lambda e, j=j: e.tensor_copy(out=AB[:, 2 * j + 1, :], in_=mod[:, (3 * j) * 8:(3 * j + 1) * 8]), r=["mod"], w=["AB"])

        def rms_stats():
            b0, k0 = bank(6)
            b1, k1 = bank(7)
            for k in range(8):
                s = sq[k % 2]
                P.op("act", lambda e, s=s, k=k: e.activation(out=s[:], in_=xT[:, k, :], func=AF.Square), r=[("xT", k)], w=[("sq", k % 2)])
                for th, (b, bk) in enumerate(((b0, k0), (b1, k1))):
                    P.op("pe", lambda e, b=b, s=s, th=th, k=k: e.matmul(b, lhsT=ones_bf[:], rhs=s[:, th * 512:(th + 1) * 512], start=(k == 0), stop=(k == 7)),
                         r=[("sq", k % 2), "ones_bf"], w=[bk], inc=True)
            for th, (b, bk) in enumerate(((b0, k0), (b1, k1))):
                P.op("act", lambda e, b=b, th=th: e.activation(out=tmpA[:, th * 512:(th + 1) * 512], in_=b, func=AF.Sqrt, scale=1.0 / D, bias=eps_t[:, 0:1]),
                     r=[bk, "eps"], w=["tmpA"])
            P.op("dve", lambda e: e.reciprocal(out=rstd[:], in_=tmpA[:]), r=["tmpA"], w=["rstd"])

        def norm_mod(j):
            rms_stats()
            for k in range(8):
                t = tmpA if k % 2 == 0 else tmpB
                tk = "tmpA" if k % 2 == 0 else "tmpB"
                P.op("dve", lambda e, t=t, k=k: e.scalar_tensor_tensor(out=t[:], in0=xT[:, k, :], scalar=AB[:, 2 * j, k:k + 1], in1=rstd[:], op0=ALU.mult, op1=ALU.mult),
                     r=[("xT", k), "AB", "rstd"], w=[tk])
                P.op("act", lambda e, t=t, k=k: e.activation(out=hT[:, k, :], in_=t[:], func=AF.Identity, bias=AB[:, 2 * j + 1, k:k + 1], scale=1.0),
                     r=[tk, "AB"], w=[("hT", k)])

        def ffn(l):
            P.dma(cw[:], I["ffn_conv_w"][l], w=["cw"])
            P.dma(cb[:], I["ffn_conv_b"][l], w=["cb"])
            for jj, j in enumerate((0, 2)):
                P.op("dve", lambda e, jj=jj, j=j: e.tensor_scalar(out=cwk[:, jj, :], in0=cw[:, j, :], scalar1=keep[:, 1:2], scalar2=None, op0=ALU.mult),
                     r=["cw", "keep"], w=["cwk"])
            for grp in range(2):
                for fc in range(grp * 11, grp * 11 + 11):
                    wb = wup[fc % 2]
                    P.dma(wb[:], I["ffn_w_up"][l, fc], w=[("wup", fc % 2, 0), ("wup", fc % 2, 1)], q="pool")
                    outs = []
                    for gv in range(2):
                        pt = pq[(fc % 2) * 2 + gv]
                        pks = [("pq", ((fc % 2) * 2 + gv) * 2 + th) for th in range(2)]
                        for th in range(2):
                            for k in range(8):
                                P.op("pe", lambda e, pt=pt, th=th, k=k, gv=gv, wb=wb: e.matmul(pt[:, th * 512:(th + 1) * 512], lhsT=wb[:, gv, k, :], rhs=hT[:, k, th * 512:(th + 1) * 512],
                                                                                    start=(k == 0), stop=(k == 7)),
                                     r=[("wup", fc % 2, gv), ("hT", k)], w=[pks[th]], inc=(k == 7))
                        c = (cg if gv == 0 else cv)[fc % 2]
                        ck = ("cg" if gv == 0 else "cv", fc % 2)
                        col = gv * NFC + fc
                        P.op("act", lambda e, c=c, pt=pt, col=col: e.activation(out=c[:], in_=pt[:], func=AF.Identity, scale=cw[:, 1, col:col + 1], bias=cb[:, col:col + 1]),
                             r=pks + ["cw", "cb"], w=[ck])
                        P.op("dve", lambda e, c=c, pt=pt, col=col: e.scalar_tensor_tensor(out=c[:, 1:NT], in0=pt[:, 0:NT - 1], scalar=cw[:, 0, col:col + 1], in1=c[:, 1:NT], op0=ALU.mult, op1=ALU.add),
                             r=pks + ["cw", ck], w=[ck])
                        P.op("dve", lambda e, c=c, pt=pt, col=col: e.scalar_tensor_tensor(out=c[:, 0:NT - 1], in0=pt[:, 1:NT], scalar=cw[:, 2, col:col + 1], in1=c[:, 0:NT - 1], op0=ALU.mult, op1=ALU.add),
                             r=pks + ["cw", ck], w=[ck])
                        P.op("dve", lambda e, c=c, pt=pt, col=col: e.scalar_tensor_tensor(out=c[:, 256:NT:256], in0=pt[:, 255:NT - 1:256], scalar=cwk[:, 0, col:col + 1], in1=c[:, 256:NT:256], op0=ALU.mult, op1=ALU.add),
                             r=pks + ["cwk", ck], w=[ck])
                        P.op("dve", lambda e, c=c, pt=pt, col=col: e.scalar_tensor_tensor(out=c[:, 255:NT - 1:256], in0=pt[:, 256:NT:256], scalar=cwk[:, 1, col:col + 1], in1=c[:, 255:NT - 1:256], op0=ALU.mult, op1=ALU.add),
                             r=pks + ["cwk", ck], w=[ck])
                        outs.append((c, ck))
                    (cgt, cgk), (cvt, cvk) = outs
                    P.op("act", lambda e, cgt=cgt: e.activation(out=cgt[:], in_=cgt[:], func=AF.Silu), r=[cgk], w=[cgk])
                    P.op("pool", lambda e, cgt=cgt, cvt=cvt, fc=fc: e.tensor_tensor(out=aT[:, fc % 11, :], in0=cgt[:], in1=cvt[:], op=ALU.mult), r=[cgk, cvk], w=[("aT", fc % 11)])
                for dc in range(8):
                    wb = wdn[dc % 2]
                    P.dma(wb[:], I["ffn_w_down"][l, dc][:, grp * 11:grp * 11 + 11, :],
                          w=[("wdn", dc % 2)], q="pool")
                    for th in range(2):
                        ps, pk = bank((dc * 2 + th) % 8)
                        for fc in range(11):
                            P.op("pe", lambda e, ps=ps, fc=fc, th=th, wb=wb: e.matmul(ps, lhsT=wb[:, fc, :], rhs=aT[:, fc, th * 512:(th + 1) * 512], start=(fc == 0), stop=(fc == 10)),
                                 r=[("wdn", dc % 2), ("aT", fc)], w=[pk])
                        P.op("dve", lambda e, ps=ps, dc=dc, th=th: e.scalar_tensor_tensor(out=xT[:, dc, th * 512:(th + 1) * 512], in0=ps, scalar=mod[:, 40 + dc:41 + dc],
                                                                                         in1=xT[:, dc, th * 512:(th + 1) * 512], op0=ALU.mult, op1=ALU.add),
                             r=[pk, "mod", ("xT", dc)], w=[("xT", dc)])

        def attention(l):
            j = l // 3
            cosT, sinT = xT[:, 8, :], xT[:, 9, :]
            t1, t2 = cv[0], cv[1]
            if STAGE.get("kv_only"):
                wkv = wada[0]
                P.dma(wkv[:], I["wkv"][j], w=[("wada", 0)], q="pool")
                for tb in range(8):
                    ps, pk = bank(tb % 4)
                    for k in range(8):
                        P.op("pe", lambda e, ps=ps, k=k, tb=tb: e.matmul(ps, lhsT=hT[:, k, tb * 128:(tb + 1) * 128], rhs=wkv[:, k, :], start=(k == 0), stop=(k == 7)),
                             r=[("wada", 0), ("hT", k)], w=[pk])
                    kvt = tmpA if tb % 2 == 0 else tmpB
                    kvk = "tmpA" if tb % 2 == 0 else "tmpB"
                    P.op("act", lambda e, ps=ps, kvt=kvt: e.copy(out=kvt[:, 0:512], in_=ps), r=[pk], w=[kvk])
                    P.dma(O["nk"][j, tb * 128:(tb + 1) * 128, :], kvt[:, 0:256], r=[kvk], sem="kvout%d" % (tb % 2))
                    P.dma(O["nv"][j, tb * 128:(tb + 1) * 128, :], kvt[:, 256:512], r=[kvk], sem="kvout%d" % (tb % 2))
                return
            P.dma(esink[:], I["sink"][j], w=["esink"])
            P.op("act", lambda e: e.activation(out=esink[:], in_=esink[:], func=AF.Exp), r=["esink"], w=["esink"])
            if STAGE.get("attn_upto", 9) < 1:
                return
            P.dma(ckd[:], I["ckd"][j].rearrange("(kb p) n -> p kb n", p=128), w=["ckd"], q="pool")
            P.dma(vctx[:].rearrange("p kb g v n -> p kb (g v n)"), I["cvp"][j].rearrange("(kb p) n -> p kb n", p=128), w=["vctx"], q="pool")
            for g in range(4):
                ps, pk = bank(g)
                for kb in range(4):
                    P.op("pe", lambda e, ps=ps, kb=kb, g=g: e.matmul(ps[:, kb * 128:(kb + 1) * 128], lhsT=ckd[:, kb, g * 128:(g + 1) * 128], rhs=ident_bf[:], start=True, stop=True),
                         r=["ckd", "ident_bf"], w=[pk])
                P.op("act", lambda e, ps=ps, g=g: e.copy(out=kctxT[:, g, :], in_=ps), r=[pk], w=["kctxT"])
            if STAGE.get("attn_upto", 9) < 2:
                return
            def proj_rope(wsrc, dst, dkey, idx):
                wb = wup[idx % 2]
                P.dma(wb[:, 0], wsrc, w=[("wup", idx % 2, 0)], q="pool")
                for th in range(2):
                    ps, pk = bank((idx * 2 + th) % 4)
                    rps, rpk = bank(4 + (idx * 2 + th) % 2)
                    for k in range(8):
                        P.op("pe", lambda e, ps=ps, k=k, th=th, wb=wb: e.matmul(ps, lhsT=wb[:, 0, k, :], rhs=hT[:, k, th * 512:(th + 1) * 512], start=(k == 0), stop=(k == 7)),
                             r=[("wup", idx % 2, 0), ("hT", k)], w=[pk])
                    q_ = qb[th]
                    if STAGE.get("pr", 9) < 1:
                        continue
                    P.op("act", lambda e, ps=ps, q_=q_: e.copy(out=q_[:], in_=ps), r=[pk], w=[("qb", th)])
                    if STAGE.get("pr", 9) < 2:
                        continue
                    P.op("pe", lambda e, rps=rps, q_=q_: e.matmul(rps, lhsT=rmat[:], rhs=q_[:], start=True, stop=True), r=[("qb", th), "rmat"], w=[rpk])
                    sl = slice(th * 512, (th + 1) * 512)
                    if STAGE.get("pr", 9) < 3 or idx >= STAGE.get("pridx", 99):
                        continue
                    P.op("dve", lambda e, ps=ps, sl=sl: e.scalar_tensor_tensor(out=t1[:, sl], in0=ps, scalar=1.0, in1=cosT[:, sl], op0=ALU.mult, op1=ALU.mult), r=[pk, ("xT", 8), ("qb", th)], w=[("cv", 0)])
                    if STAGE.get("pr", 9) < 4:
                        continue
                    P.op("dve", lambda e, rps=rps, sl=sl: e.scalar_tensor_tensor(out=t2[:, sl], in0=rps, scalar=1.0, in1=sinT[:, sl], op0=ALU.mult, op1=ALU.mult), r=[rpk, ("xT", 9)], w=[("cv", 1)])
                    if STAGE.get("pr", 9) < 5:
                        continue
                    P.op("dve", lambda e, sl=sl, dst=dst: e.tensor_tensor(out=dst[:, sl], in0=t1[:, sl], in1=t2[:, sl], op=ALU.add), r=[("cv", 0), ("cv", 1)], w=[dkey])
            for qc in range(8):
                proj_rope(I["wq"][j, qc], aT[:, qc, :], ("aT", qc), qc)
            for g in range(4):
                proj_rope(I["wk"][j, g], aT[:, 8 + g, :], ("aT", 8 + g), 8 + g)
            if STAGE.get("attn_upto", 9) < 3:
                return
            wkv = wada[0]
            P.dma(wkv[:], I["wkv"][j], w=[("wada", 0)], q="pool")
            for tb in range(8):
                ps, pk = bank(tb % 4)
                for k in range(8):
                    P.op("pe", lambda e, ps=ps, k=k, tb=tb: e.matmul(ps, lhsT=hT[:, k, tb * 128:(tb + 1) * 128], rhs=wkv[:, k, :], start=(k == 0), stop=(k == 7)),
                         r=[("wada", 0), ("hT", k)], w=[pk])
                kvt = tmpA if tb % 2 == 0 else tmpB
                kvk = "tmpA" if tb % 2 == 0 else "tmpB"
                P.op("act", lambda e, ps=ps, kvt=kvt: e.copy(out=kvt[:, 0:512], in_=ps), r=[pk], w=[kvk])
                P.dma(O["nk"][j, tb * 128:(tb + 1) * 128, :], kvt[:, 0:256], r=[kvk], sem="kvout%d" % (tb % 2))
                P.dma(O["nv"][j, tb * 128:(tb + 1) * 128, :], kvt[:, 256:512], r=[kvk], sem="kvout%d" % (tb % 2))
                P.op("dve", lambda e, ps=ps, tb=tb: e.tensor_copy(out=vlat[:, tb, :, 0, 0:64], in_=ps[:, 256:512].rearrange("p (g d) -> p g d", g=4)), r=[pk], w=["vlat"])
                P.op("dve", lambda e, ps=ps, tb=tb: e.tensor_copy(out=vlat[:, tb, :, 1, 64:128], in_=ps[:, 256:512].rearrange("p (g d) -> p g d", g=4)), r=[pk], w=["vlat"])
            if STAGE.get("attn_upto", 9) < 4:
                return
            dsb = tmpA[:].rearrange("p (a n) -> p a n", a=2)
            osb = [cv[0][:, 0:512], cv[1][:, 0:512]]
            tb_bf = tmpB[:].bitcast(BF16)
            ebuf = [tb_bf[:, i * 512:(i + 1) * 512] for i in range(3)]
            P.alias(["dsb"], ["tmpA"])
            P.alias([("osb", 0)], [("cv", 0)])
            P.alias([("osb", 1)], [("cv", 1)])
            P.alias([("ebuf", i) for i in range(3)], ["tmpB"])
            sc = 0
            for h in range(STAGE.get("nheads", 16)):
                g, qc, pb, var = h // 4, h // 2, (h % 2) * 64, h % 2
                dr = 64 if var == 0 else 0
                qh = aT[pb:pb + 64, qc, :]
                kh = aT[pb:pb + 64, 8 + g, :]
                kch = kctxT[pb:pb + 64, g, :]
                for th in range(2):
                    it = h * 2 + th
                    OP, opk = bank(4 + it % 2)
                    first = True
                    for kb in range(4):
                        ps, pk = bank(sc % 4); eb = ebuf[sc % 3]; ek = ("ebuf", sc % 3); sc += 1
                        P.op("pe", lambda e, ps=ps, kb=kb, th=th, kch=kch, qh=qh: e.matmul(ps, lhsT=kch[:, kb * 128:(kb + 1) * 128], rhs=qh[:, th * 512:(th + 1) * 512], start=True, stop=True),
                             r=["kctxT", ("aT", qc)], w=[pk])
                        P.op("act", lambda e, ps=ps, eb=eb: e.activation(out=eb[:], in_=ps, func=AF.Exp, scale=0.125), r=[pk], w=[ek])
                        P.op("pe", lambda e, OP=OP, eb=eb, kb=kb, g=g, var=var, first=first: e.matmul(OP, lhsT=vctx[:, kb, g, var, :], rhs=eb[:], start=first, stop=False),
                             r=["vctx", ek], w=[opk])
                        first = False
                    jbs = [jb for jb in range(8) if max(jb - 1, 4 * th) <= min(jb + 1, 4 * th + 3)]
                    for jb in jbs:
                        i0 = max(jb - 1, 4 * th); i1 = min(jb + 1, 4 * th + 3)
                        n = (i1 - i0 + 1) * 128
                        ps, pk = bank(sc % 4); eb = ebuf[sc % 3]; ek = ("ebuf", sc % 3); sc += 1
                        P.op("pe", lambda e, ps=ps, jb=jb, i0=i0, n=n, kh=kh, qh=qh: e.matmul(ps[:, 0:n], lhsT=kh[:, jb * 128:(jb + 1) * 128], rhs=qh[:, i0 * 128:i0 * 128 + n], start=True, stop=True),
                             r=[("aT", 8 + g), ("aT", qc)], w=[pk])
                        P.op("act", lambda e, ps=ps, eb=eb, n=n: e.activation(out=eb[:, 0:n], in_=ps[:, 0:n], func=AF.Exp, scale=0.125), r=[pk], w=[ek])
                        for i in range(i0, i1 + 1):
                            if i == jb:
                                continue
                            off = 0 if i == jb + 1 else 1
                            c0 = (i - i0) * 128
                            P.op("dve", lambda e, eb=eb, c0=c0, i=i, off=off: e.tensor_tensor(out=eb[:, c0:c0 + 128], in0=eb[:, c0:c0 + 128], in1=amask[:, i, off, :], op=ALU.mult),
                                 r=[ek, "amask"], w=[ek])
                        o0 = (i0 - 4 * th) * 128
                        P.op("pe", lambda e, OP=OP, eb=eb, jb=jb, g=g, var=var, o0=o0, n=n, jbs=jbs: e.matmul(OP[:, o0:o0 + n], lhsT=vlat[:, jb, g, var, :], rhs=eb[:, 0:n], start=False, stop=(jb == jbs[-1])),
                             r=["vlat", ek], w=[opk])
                    P.op("dve", lambda e, OP=OP, dr=dr, h=h: e.tensor_scalar(out=dsb[dr:dr + 1, 0, :], in0=OP[dr:dr + 1, :], scalar1=esink[dr:dr + 1, h:h + 1], scalar2=None, op0=ALU.add),
                         r=[opk, "esink"], w=["dsb"])
                    P.op("dve", lambda e, dr=dr: e.reciprocal(out=dsb[dr:dr + 1, 1, :], in_=dsb[dr:dr + 1, 0, :]), r=["dsb"], w=["dsb"])
                    BC, bck = bank(6 + it % 2)
                    P.op("pe", lambda e, BC=BC, dr=dr: e.matmul(BC, lhsT=ones_f[dr:dr + 1, :], rhs=dsb[dr:dr + 1, 1, :], start=True, stop=True), r=["dsb", "ones_f"], w=[bck])
                    ob = osb[it % 2]; obk = ("osb", it % 2)
                    P.op("act", lambda e, OP=OP, ob=ob, pb=pb: e.copy(out=ob[pb:pb + 64, :], in_=OP[pb:pb + 64, :]), r=[opk], w=[obk])
                    P.op("dve", lambda e, BC=BC, ob=ob, pb=pb, qc=qc, th=th: e.tensor_tensor(out=aT[pb:pb + 64, qc, th * 512:(th + 1) * 512], in0=ob[pb:pb + 64, :], in1=BC[pb:pb + 64, :], op=ALU.mult),
                         r=[obk, bck], w=[("aT", qc)])
            P.alias(["tmpA"], ["dsb"])
            P.alias([("cv", 0)], [("osb", 0)])
            P.alias([("cv", 1)], [("osb", 1)])
            P.alias(["tmpB"], [("ebuf", i) for i in range(3)])
            for dc in range(8):
                wb = wup[dc % 2]
                P.dma(wb[:, 0], I["wo"][j, dc], w=[("wup", dc % 2, 0)], q="pool")
                for th in range(2):
                    ps, pk = bank((dc * 2 + th) % 4)
                    for k in range(8):
                        P.op("pe", lambda e, ps=ps, k=k, th=th, wb=wb: e.matmul(ps, lhsT=wb[:, 0, k, :], rhs=aT[:, k, th * 512:(th + 1) * 512], start=(k == 0), stop=(k == 7)),
                             r=[("wup", dc % 2, 0), ("aT", k)], w=[pk])
                    P.op("dve", lambda e, ps=ps, dc=dc, th=th: e.scalar_tensor_tensor(out=xT[:, dc, th * 512:(th + 1) * 512], in0=ps, scalar=mod[:, 16 + dc:17 + dc],
                                                                                     in1=xT[:, dc, th * 512:(th + 1) * 512], op0=ALU.mult, op1=ALU.add),
                         r=[pk, "mod", ("xT", dc)], w=[("xT", dc)])

        def hgrn(l):
            CH = 32
            NCH = NT // CH
            P.op("act", lambda e: e.activation(out=hlb[:], in_=hlb[:], func=AF.Exp), r=["hlb"], w=["hlb"])
            P.op("dve", lambda e: e.tensor_tensor(out=lbp[:, 1], in0=hlb[:, 0], in1=hlb[:, 1], op=ALU.add), r=["hlb"], w=["lbp"])
            P.op("dve", lambda e: e.tensor_tensor(out=lbp[:, 1], in0=lbp[:, 1], in1=hlb[:, 2], op=ALU.add), r=["hlb", "lbp"], w=["lbp"])
            P.op("dve", lambda e: e.tensor_tensor(out=lbp[:, 1], in0=lbp[:, 1], in1=hlb[:, 3], op=ALU.add), r=["hlb", "lbp"], w=["lbp"])
            P.op("dve", lambda e: e.reciprocal(out=lbp[:, 1], in_=lbp[:, 1]), r=["lbp"], w=["lbp"])
            P.op("dve", lambda e: e.tensor_tensor(out=lbp[:, 0], in0=lbp[:, 1], in1=hlb[:, 1], op=ALU.mult), r=["hlb", "lbp"], w=["lbp"])
            P.op("dve", lambda e: e.tensor_scalar(out=lbp[:, 1], in0=lbp[:, 0], scalar1=-1.0, scalar2=1.0, op0=ALU.mult, op1=ALU.add), r=["lbp"], w=["lbp"])
            bufQ, bufF, bufK, bufC = cg[0], cg[1], cv[0], cv[1]
            kQ, kF, kK, kC = ("cg", 0), ("cg", 1), ("cv", 0), ("cv", 1)
            oacc = rstd
            kdtok = [vlat[:].rearrange("p a g v n -> p (a g v n)")[:, d * 4096:(d + 1) * 4096].rearrange("p (b c n) -> p b c n", b=8, c=4) for d in range(2)]
            P.alias([("kdtok", 0), ("kdtok", 1)], ["vlat"])
            for h in range(8):
                P.dma(wup[0][:], I["hw_in"][h, 0:2].rearrange("a p k n -> p a k n"), w=[("wup", 0, 0), ("wup", 0, 1)], q="pool")
                P.dma(wup[1][:], I["hw_in"][h, 2:4].rearrange("a p k n -> p a k n"), w=[("wup", 1, 0), ("wup", 1, 1)], q="pool")
                P.dma(wdn[0][:, 0:8, :], I["hw_in"][h, 4], w=[("wdn", 0)], q="pool")

                def proj(wap, wkeys, bi):
                    outs = []
                    for th in range(2):
                        ps, pk = bank(bi * 2 + th)
                        for kk in range(8):
                            P.op("pe", lambda e, ps=ps, kk=kk, th=th, wap=wap: e.matmul(ps, lhsT=wap[:, kk, :], rhs=hT[:, kk, th * 512:(th + 1) * 512], start=(kk == 0), stop=(kk == 7)),
                                 r=list(wkeys) + [("hT", kk)], w=[pk])
                        outs.append((ps, pk))
                    return outs
                for th, (ps, pk) in enumerate(proj(wup[0][:, 0], [("wup", 0, 0)], 0)):
                    P.op("act", lambda e, ps=ps, th=th: e.copy(out=bufQ[:, th * 512:(th + 1) * 512], in_=ps), r=[pk], w=[kQ])
                for blk in range(8):
                    ps, pk = bank(2 + blk % 2)
                    for kk in range(8):
                        P.op("pe", lambda e, ps=ps, kk=kk, blk=blk: e.matmul(ps[:, 0:128], lhsT=hT[:, kk, blk * 128:(blk + 1) * 128], rhs=wup[0][:, 1, kk, :], start=(kk == 0), stop=(kk == 7)),
                             r=[("wup", 0, 1), ("hT", kk)], w=[pk])
                    P.op("act", lambda e, ps=ps, blk=blk: e.copy(out=vtok[:, blk, :], in_=ps[:, 0:128]), r=[pk], w=["vtok"])
                for d in range(2):
                    for th, (ps, pk) in enumerate(proj(wup[1][:, d], [("wup", 1, d)], 2 + d)):
                        P.op("act", lambda e, ps=ps, th=th: e.activation(out=bufF[:, th * 512:(th + 1) * 512], in_=ps, func=AF.Sigmoid), r=[pk], w=[kF])
                    P.op("dve", lambda e, d=d, h=h: e.tensor_scalar(out=bufF[:], in0=bufF[:], scalar1=lbp[:, 1, d, h:h + 1], scalar2=lbp[:, 0, d, h:h + 1], op0=ALU.mult, op1=ALU.add),
                         r=[kF, "lbp"], w=[kF])
                    P.op("dve", lambda e: e.tensor_scalar(out=bufK[:], in0=bufF[:], scalar1=-1.0, scalar2=1.0, op0=ALU.mult, op1=ALU.add), r=[kF], w=[kK])
                    P.op("act", lambda e: e.activation(out=bufF[:], in_=bufF[:], func=AF.Ln), r=[kF], w=[kF])
                    P.op("dve", lambda e: e.tensor_tensor_scan(out=bufC[:], data0=m32[:], data1=bufF[:], initial=0.0, op0=ALU.mult, op1=ALU.add), r=["m32", kF], w=[kC])
                    if d == 1:
                        P.op("dve", lambda e: e.scalar_tensor_tensor(out=tmpA[:], in0=bufC[:], scalar=-1.0, in1=bufF[:], op0=ALU.mult, op1=ALU.add), r=[kC, kF], w=["tmpA"])
                        P.op("act", lambda e: e.copy(out=ctmp[:], in_=bufC[:, CH - 1:NT:CH]), r=[kC], w=["ctmp"])
                        P.op("dve", lambda e: e.tensor_tensor(out=bufC[:].rearrange("p (c t) -> p c t", t=CH), in0=tmpA[:].rearrange("p (c t) -> p c t", t=CH),
                                                              in1=ctmp[:].unsqueeze(2).to_broadcast([128, NCH, CH]), op=ALU.add),
                             r=["tmpA", "ctmp"], w=[kC])
                        ctot = bufC[:, 0:NT:CH]
                    else:
                        ctot = bufC[:, CH - 1:NT:CH]
                    P.op("act", lambda e, d=d, ctot=ctot: e.activation(out=adec[d][:], in_=ctot, func=AF.Exp), r=[kC], w=[("adec", d)])
                    P.op("act", lambda e: e.activation(out=tmpA[:], in_=bufC[:], func=AF.Exp), r=[kC], w=["tmpA"])
                    P.op("dve", lambda e, d=d: e.tensor_tensor(out=qhat[d][:], in0=bufQ[:], in1=tmpA[:], op=ALU.mult), r=[kQ, "tmpA"], w=[("qhat", d)])
                    P.op("dve", lambda e: e.tensor_scalar(out=tmpB[:], in0=bufC[:], scalar1=-1.0, scalar2=85.0, op0=ALU.mult, op1=ALU.min), r=[kC], w=["tmpB"])
                    P.op("act", lambda e: e.activation(out=tmpB[:], in_=tmpB[:], func=AF.Exp), r=["tmpB"], w=["tmpB"])
                    P.op("dve", lambda e: e.tensor_tensor(out=tmpB[:], in0=tmpB[:], in1=bufK[:], op=ALU.mult), r=["tmpB", kK], w=["tmpB"])
                    P.op("act", lambda e, d=d: e.copy(out=ktil[d][:], in_=tmpB[:]), r=["tmpB"], w=[("ktil", d)])
                    P.op("dve", lambda e, d=d: e.tensor_tensor(out=kdT[d][:].rearrange("p (c t) -> p c t", t=CH), in0=tmpB[:].rearrange("p (c t) -> p c t", t=CH),
                                                                in1=adec[d][:].unsqueeze(2).to_broadcast([128, NCH, CH]), op=ALU.mult),
                         r=["tmpB", ("adec", d)], w=[("kdT", d)])
                    for blk in range(8):
                        ps, pk = bank(6 + blk % 2)
                        pst = ps.bitcast(BF16)
                        P.op("pe", lambda e, pst=pst, blk=blk, d=d: e.transpose(pst[:, 0:128], kdT[d][:, blk * 128:(blk + 1) * 128], ident_bf[:]), r=[("kdT", d), "ident_bf"], w=[pk])
                        for c4 in range(4):
                            P.op("act", lambda e, pst=pst, blk=blk, c4=c4, d=d: e.activation(out=kdtok[d][:, blk, c4, :], in_=pst[:, 0:128], func=AF.Identity, scale=rowm[:, c4:c4 + 1]),
                                 r=[pk, "rowm"], w=[("kdtok", d)])
                    P.dma(Sf[d][:], I["hs0"][d, h], w=[("Sf", d)])
                    P.op("act", lambda e, d=d: e.copy(out=Sb[d][:], in_=Sf[d][:]), r=[("Sf", d)], w=[("Sb", d)])
                for d in range(2):
                    blks = range(8) if d == 0 else range(7, -1, -1)
                    for bi_, blk in enumerate(blks):
                        bsl = slice(blk * 128, (blk + 1) * 128)
                        aps, apk = bank(0 + d * 2)
                        ops_, opk = bank(1 + d * 2)
                        dps, dpk = bank(4 + d)
                        P.op("pe", lambda e, aps=aps, d=d, bsl=bsl: e.matmul(aps[:, 0:128], lhsT=ktil[d][:, bsl], rhs=qhat[d][:, bsl], start=True, stop=True),
                             r=[("ktil", d), ("qhat", d)], w=[apk])
                        am = attm[d]
                        P.op("dve", lambda e, aps=aps, am=am, d=d: e.tensor_tensor(out=am[:], in0=aps[:, 0:128], in1=hmask[:, d, :], op=ALU.mult), r=[apk, "hmask"], w=[("attm", d)])
                        P.op("pe", lambda e, ops_=ops_, am=am, blk=blk: e.matmul(ops_[:, 0:128], lhsT=vtok[:, blk, :], rhs=am[:], start=True, stop=False),
                             r=["vtok", ("attm", d)], w=[opk])
                        for c4 in range(4):
                            P.op("pe", lambda e, dps=dps, c4=c4, blk=blk, d=d: e.matmul(dps[:, c4 * 128:(c4 + 1) * 128], lhsT=kdtok[d][:, blk, c4, :], rhs=vtok[:, blk, :], start=True, stop=True),
                                 r=[("kdtok", d), "vtok"], w=[dpk])
                        cs = range(4) if d == 0 else range(3, -1, -1)
                        for c4 in cs:
                            ch = blk * 4 + c4
                            csl = slice(ch * CH, (ch + 1) * CH)
                            P.op("pe", lambda e, ops_=ops_, c4=c4, csl=csl, d=d, cs=cs: e.matmul(ops_[:, c4 * CH:(c4 + 1) * CH], lhsT=Sb[d][:], rhs=qhat[d][:, csl], start=False, stop=(c4 == list(cs)[-1])),
                                 r=[("Sb", d), ("qhat", d)], w=[opk])
                            P.op("dve", lambda e, dps=dps, c4=c4, ch=ch, d=d: e.scalar_tensor_tensor(out=Sf[d][:], in0=Sf[d][:], scalar=adec[d][:, ch:ch + 1], in1=dps[:, c4 * 128:(c4 + 1) * 128], op0=ALU.mult, op1=ALU.add),
                                 r=[("Sf", d), ("adec", d), dpk], w=[("Sf", d)])
                            seg_end = (ch % 8 == 7) if d == 0 else (ch % 8 == 0)
                            if seg_end:
                                seg = ch // 8
                                P.dma(O["hg"][(seg * 2 + d) * 8 + h], Sf[d][:], r=[("Sf", d)], sem="hgout%d" % d)
                                P.op("dve", lambda e, d=d: e.tensor_scalar(out=Sf[d][:], in0=Sf[d][:], scalar1=keep[:, 0:1], scalar2=None, op0=ALU.mult), r=[("Sf", d), "keep"], w=[("Sf", d)])
                            P.op("act", lambda e, d=d: e.copy(out=Sb[d][:], in_=Sf[d][:]), r=[("Sf", d)], w=[("Sb", d)])
                        if d == 0:
                            P.op("act", lambda e, ops_=ops_, bsl=bsl: e.copy(out=oacc[:, bsl], in_=ops_[:, 0:128]), r=[opk], w=["rstd"])
                        else:
                            P.op("dve", lambda e, ops_=ops_, bsl=bsl: e.tensor_tensor(out=oacc[:, bsl], in0=ops_[:, 0:128], in1=oacc[:, bsl], op=ALU.add), r=[opk, "rstd"], w=["rstd"])
                P.op("act", lambda e: e.activation(out=sq[0][:], in_=oacc[:], func=AF.Square), r=["rstd"], w=[("sq", 0)])
                for th in range(2):
                    ps, pk = bank(6 + th)
                    P.op("pe", lambda e, ps=ps, th=th: e.matmul(ps, lhsT=ones_bf[:], rhs=sq[0][:, th * 512:(th + 1) * 512], start=True, stop=True), r=[("sq", 0), "ones_bf"], w=[pk])
                    P.op("act", lambda e, ps=ps, th=th: e.activation(out=tmpA[:, th * 512:(th + 1) * 512], in_=ps, func=AF.Sqrt, scale=1.0 / 128, bias=eps_t[:, 0:1]), r=[pk, "eps"], w=["tmpA"])
                P.op("dve", lambda e: e.reciprocal(out=tmpA[:], in_=tmpA[:]), r=["tmpA"], w=["tmpA"])
                P.op("dve", lambda e: e.scalar_tensor_tensor(out=tmpA[:], in0=oacc[:], scalar=hgn[:, 0:1], in1=tmpA[:], op0=ALU.mult, op1=ALU.mult), r=["rstd", "hgn", "tmpA"], w=["tmpA"])
                for th, (ps, pk) in enumerate(proj(wdn[0][:, 0:8, :], [("wdn", 0)], 2)):
                    P.op("act", lambda e, ps=ps, th=th: e.activation(out=tmpB[:, th * 512:(th + 1) * 512], in_=ps, func=AF.Silu), r=[pk], w=["tmpB"])
                P.op("dve", lambda e, h=h: e.tensor_tensor(out=aT[:, h, :], in0=tmpA[:], in1=tmpB[:], op=ALU.mult), r=["tmpA", "tmpB"], w=[("aT", h)])
            P.alias(["vlat"], [("kdtok", 0), ("kdtok", 1)])
            P.op("dve", lambda e: e.memset(vlat[:], 0.0), w=["vlat"])
            P.op("dve", lambda e: e.memset(vlat[:, :, :, 0, 64:65], 1.0), w=["vlat"])
            P.op("dve", lambda e: e.memset(vlat[:, :, :, 1, 0:1], 1.0), w=["vlat"])
            for dc in range(8):
                wb = wup[dc % 2]
                P.dma(wb[:, 0], I["hwo"][dc], w=[("wup", dc % 2, 0)], q="pool")
                for th in range(2):
                    ps, pk = bank((dc * 2 + th) % 4)
                    for kk in range(8):
                        P.op("pe", lambda e, ps=ps, kk=kk, th=th, wb=wb: e.matmul(ps, lhsT=wb[:, 0, kk, :], rhs=aT[:, kk, th * 512:(th + 1) * 512], start=(kk == 0), stop=(kk == 7)),
                             r=[("wup", dc % 2, 0), ("aT", kk)], w=[pk])
                    P.op("dve", lambda e, ps=ps, dc=dc, th=th: e.scalar_tensor_tensor(out=xT[:, dc, th * 512:(th + 1) * 512], in0=ps, scalar=mod[:, 16 + dc:17 + dc],
                                                                                     in1=xT[:, dc, th * 512:(th + 1) * 512], op0=ALU.mult, op1=ALU.add),
                         r=[pk, "mod", ("xT", dc)], w=[("xT", dc)])

        def ssm(l):
            SEG = 256
            rs32 = rstd[:]
            sr_ = [rs32[:, i * 64:(i + 1) * 64] for i in range(16)]
            sq32 = sq[0][:].bitcast(F32)
            sq_ = [sq32[:, i * 32:(i + 1) * 32] for i in range(14)]
            sCq = [sq[1][:, i * 512:(i + 1) * 512].rearrange("p (g i) -> p g i", g=32) for i in range(2)]
            P.alias(["srr"], ["rstd"])
            P.alias(["sqq"], [("sq", 0)])
            P.alias(["sCq"], [("sq", 1)])
            m256 = m32
            P.op("pool", lambda e: e.memset(m256[:], 1.0), r=["m32"], w=["m32"])
            P.op("pool", lambda e: e.memset(m256[:, 0:NT:256], 0.0), r=["m32"], w=["m32"])
            P.dma(smask[:], I["smask"], w=["smask"])
            P.dma(sD[:], I["sD"], w=["sD"])
            TT = lambda e, o, a, b, op: e.tensor_tensor(out=o, in0=a, in1=b, op=op)

            def vop(eng, o, a, b, op, r, w):
                P.op(eng, lambda e, o=o, a=a, b=b, op=op: e.tensor_tensor(out=o, in0=a, in1=b, op=op), r=r, w=w)

            def cmul(eng, ore, oim, are_, aim_, bre, bim, t1, t2, r, w, tk):
                vop(eng, t1, are_, bre, ALU.mult, r, [tk[0]])
                vop(eng, t2, aim_, bim, ALU.mult, r, [tk[1]])
                vop(eng, ore, t1, t2, ALU.subtract, [tk[0], tk[1]], w)
                vop(eng, t1, are_, bim, ALU.mult, r, [tk[0]])
                vop(eng, t2, aim_, bre, ALU.mult, r, [tk[1]])
                vop(eng, oim, t1, t2, ALU.add, [tk[0], tk[1]], w)

            def lam_params(are_ap, aim_ap, dt_scalar_or_ap, S, key, n, per_part_dt, need_inv=True):
                arec, th, mag, c, s_, t1, t2, imag = S[0], S[1], S[2], S[3], S[4], S[5], S[6], S[7]
                K = [key]
                P.op("dve", lambda e: e.tensor_scalar(out=arec[:], in0=are_ap, scalar1=-1e-4, scalar2=None, op0=ALU.min), r=K, w=K)
                if per_part_dt:
                    dtx = S[8]
                    P.op("act", lambda e: e.activation(out=dtx[:, 0:1], in_=dt_scalar_or_ap, func=AF.Exp), r=K, w=K)
                    P.op("dve", lambda e: e.tensor_scalar(out=th[:], in0=aim_ap, scalar1=dtx[:, 0:1], scalar2=None, op0=ALU.mult), r=K, w=K)
                    P.op("dve", lambda e: e.tensor_scalar(out=mag[:], in0=arec[:], scalar1=dtx[:, 0:1], scalar2=None, op0=ALU.mult), r=K, w=K)
                else:
                    dtx = S[8]
                    P.op("act", lambda e: e.activation(out=dtx[:], in_=dt_scalar_or_ap, func=AF.Exp), r=K, w=K)
                    vop("dve", th[:], aim_ap, dtx[:], ALU.mult, K, K)
                    vop("dve", mag[:], arec[:], dtx[:], ALU.mult, K, K)
                P.op("act", lambda e: e.activation(out=imag[:], in_=mag[:], func=AF.Exp, scale=-1.0), r=K, w=K)
                P.op("act", lambda e: e.activation(out=mag[:], in_=mag[:], func=AF.Exp), r=K, w=K)
                P.op("act", lambda e: e.activation(out=s_[:], in_=th[:], func=AF.Sin, scale=1.0 / 64), r=K, w=K)
                P.op("act", lambda e: e.activation(out=c[:], in_=th[:], func=AF.Sin, scale=1.0 / 64, bias=halfpi[:, 0:1]), r=K + ["halfpi"], w=K)
                for _ in range(6):
                    vop("dve", t1[:], c[:], s_[:], ALU.mult, K, K)
                    vop("dve", c[:], c[:], c[:], ALU.mult, K, K)
                    vop("dve", s_[:], s_[:], s_[:], ALU.mult, K, K)
                    vop("dve", c[:], c[:], s_[:], ALU.subtract, K, K)
                    P.op("dve", lambda e: e.tensor_scalar(out=s_[:], in0=t1[:], scalar1=2.0, scalar2=None, op0=ALU.mult), r=K, w=K)
                L1re, L1im, Lm1re, Lm1im = S[9], S[10], S[11], S[12]
                vop("dve", L1re[:], mag[:], c[:], ALU.mult, K, K)
                vop("dve", L1im[:], mag[:], s_[:], ALU.mult, K, K)
                if need_inv:
                    vop("dve", Lm1re[:], imag[:], c[:], ALU.mult, K, K)
                    vop("dve", Lm1im[:], imag[:], s_[:], ALU.mult, K, K)
                    P.op("dve", lambda e: e.tensor_scalar(out=Lm1im[:], in0=Lm1im[:], scalar1=-1.0, scalar2=None, op0=ALU.mult), r=K, w=K)
                return dict(L1re=L1re, L1im=L1im, Lm1re=Lm1re, Lm1im=Lm1im, are=arec)

            A_, B_, C_, D_, Gr, Gi = cg[0], cg[1], cv[0], cv[1], tmpA, tmpB
            kA, kB, kC2, kD, kGr, kGi = ("cg", 0), ("cg", 1), ("cv", 0), ("cv", 1), "tmpA", "tmpB"
            vl = vlat[:].rearrange("p a g v n -> p (a g v n)").bitcast(F32)
            Tp_re, Tp_im, Tm_re, Tm_im = [vl[:, i * 1024:(i + 1) * 1024].rearrange("p (g t) -> p g t", g=4) for i in range(4)]
            P.alias(["stab"], ["vlat"])
            vc = vctx[:].rearrange("p a g v n -> p (a g v n)")
            W1pad = vc[:, 0:2048].rearrange("p (q r n) -> p q r n", q=8, r=2)
            Ewpad = vc[:, 2048:4096].rearrange("p (q r n) -> p q r n", q=8, r=2)
            P.alias(["W1pad", "Ewpad"], ["vctx"])
            Hre_b = qhat[0][:].rearrange("p (g t) -> p g t", g=4)
            Him_b = qhat[1][:].rearrange("p (g t) -> p g t", g=4)
            ysb = aT

            def _body():
              for d in range(2):
                  P.dma(sq_[0], I["sAreQ"][d], w=["sqq"])
                  P.dma(sq_[1], I["sAimQ"][d], w=["sqq"])
                  P.dma(sq_[13], I["sDtQ"][d], w=["sqq"])
                  P.dma(sCq[0], I["sCreQ"][d], w=["sCq"], q="pool")
                  P.dma(sCq[1], I["sCimQ"][d], w=["sCq"], q="pool")
                  P.dma(sH[:], I["sH0"][d], w=["sH"])
                  Q = lam_params(sq_[0], sq_[1], sq_[13], sq_[2:13] + [sq_[0], sq_[1]], "sqq", 32, False)
                  for s in range(8):
                      k0, k1 = s // 2, 4 + s // 2
                      g8b = (4 * s) % 8
                      P.op("pool", lambda e: e.memset(vc[:], 0.0), r=["W1pad", "Ewpad"], w=["W1pad", "Ewpad"])
                      for half, kk_ in enumerate((k0, k1)):
                          Rk = ["srr"]
                          P.dma(sr_[0], I["sAreR"][d, kk_], w=Rk)
                          P.dma(sr_[1], I["sAimR"][d, kk_], w=Rk)
                          P.dma(sr_[13][:, 0:1], I["sDtR"][d, kk_], w=Rk)
                          P.dma(sr_[14], I["sBreR"][d, kk_], w=Rk)
                          P.dma(sr_[15], I["sBimR"][d, kk_], w=Rk)
                          R_ = lam_params(sr_[0], sr_[1], sr_[13][:, 0:1], sr_[2:13] + [sr_[0], sr_[0]], "srr", 64, True, need_inv=False)
                          nre, den, cre, cim, t1, t2 = sr_[3], sr_[4], sr_[5], sr_[6], sr_[7], sr_[8]
                          aim_ = sr_[1]
                          P.op("dve", lambda e, R_=R_: e.tensor_scalar(out=nre[:], in0=R_["L1re"][:], scalar1=-1.0, scalar2=None, op0=ALU.add), r=Rk, w=Rk)
                          vop("dve", den[:], R_["are"][:], R_["are"][:], ALU.mult, Rk, Rk)
                          vop("dve", t1[:], aim_[:], aim_[:], ALU.mult, Rk, Rk)
                          vop("dve", den[:], den[:], t1[:], ALU.add, Rk, Rk)
                          P.op("dve", lambda e: e.reciprocal(out=den[:], in_=den[:]), r=Rk, w=Rk)
                          vop("dve", t1[:], nre[:], R_["are"][:], ALU.mult, Rk, Rk)
                          vop("dve", t2[:], R_["L1im"][:], aim_[:], ALU.mult, Rk, Rk)
                          vop("dve", cre[:], t1[:], t2[:], ALU.add, Rk, Rk)
                          vop("dve", cre[:], cre[:], den[:], ALU.mult, Rk, Rk)
                          vop("dve", t1[:], R_["L1im"][:], R_["are"][:], ALU.mult, Rk, Rk)
                          vop("dve", t2[:], nre[:], aim_[:], ALU.mult, Rk, Rk)
                          vop("dve", cim[:], t1[:], t2[:], ALU.subtract, Rk, Rk)
                          vop("dve", cim[:], cim[:], den[:], ALU.mult, Rk, Rk)
                          wre, wim = sr_[9], sr_[10]
                          cmul("dve", wre[:], wim[:], cre[:], cim[:], sr_[14][:], sr_[15][:], t1[:], t2[:], Rk, Rk, ["srr", "srr"])
                          for q in range(4):
                              g8 = g8b + q
                              for ri, wsrc in enumerate((wre, wim)):
                                  P.op("act", lambda e, q=q, half=half, ri=ri, wsrc=wsrc, g8=g8: e.activation(out=W1pad[:, half * 4 + q, ri, half * 64:(half + 1) * 64], in_=wsrc[:], func=AF.Identity, scale=smask[:, g8:g8 + 1]),
                                       r=Rk + ["smask"], w=["W1pad"])
                              gq = 4 * s + q
                              P.op("act", lambda e, q=q, half=half, g8=g8, gq=gq: e.activation(out=Ewpad[:, half * 4 + q, 0, g8 * 16:(g8 + 1) * 16], in_=sCq[0][:, gq, :], func=AF.Identity, scale=smask[:, 8 + half:9 + half]),
                                   r=["sCq", "smask"], w=["Ewpad"])
                              P.op("act", lambda e, q=q, half=half, g8=g8, gq=gq: e.activation(out=Ewpad[:, half * 4 + q, 1, g8 * 16:(g8 + 1) * 16], in_=sCq[1][:, gq, :], func=AF.Identity, scale=smask[:, 10 + half:11 + half]),
                                   r=["sCq", "smask"], w=["Ewpad"])
                      gsl = slice(4 * s, 4 * s + 4)
                      for (Tre, Tim, bre_, bim_) in ((Tp_re, Tp_im, Q["L1re"], Q["L1im"]), (Tm_re, Tm_im, Q["Lm1re"], Q["Lm1im"])):
                          i0 = 0 if d == 0 else SEG - 1
                          P.op("dve", lambda e, Tre=Tre, bre_=bre_, i0=i0, gsl=gsl: e.tensor_copy(out=Tre[:, :, i0:i0 + 1], in_=bre_[:, gsl].unsqueeze(2)), r=["sqq"], w=["stab"])
                          P.op("dve", lambda e, Tim=Tim, bim_=bim_, i0=i0, gsl=gsl: e.tensor_copy(out=Tim[:, :, i0:i0 + 1], in_=bim_[:, gsl].unsqueeze(2)), r=["sqq"], w=["stab"])
                          L = 1
                          while L < SEG:
                              if d == 0:
                                  src = slice(0, L); dst = slice(L, 2 * L); piv = L - 1
                              else:
                                  src = slice(SEG - L, SEG); dst = slice(SEG - 2 * L, SEG - L); piv = SEG - L
                              zr = Tre[:, :, piv:piv + 1].to_broadcast([128, 4, L])
                              zi = Tim[:, :, piv:piv + 1].to_broadcast([128, 4, L])
                              cmul("dve", Tre[:, :, dst], Tim[:, :, dst], Tre[:, :, src], Tim[:, :, src], zr, zi,
                                   A_[:, 0:4 * L].rearrange("p (g t) -> p g t", g=4), B_[:, 0:4 * L].rearrange("p (g t) -> p g t", g=4), ["stab"], ["stab"], [kA, kB])
                              L *= 2
                      P.op("dve", lambda e, gsl=gsl: e.tensor_copy(out=sHent[:], in_=sH[:, gsl, :]), r=["sH"], w=["sHent"])
                      segs = range(4) if d == 0 else range(3, -1, -1)
                      for seg in segs:
                          tsl = slice(seg * SEG, (seg + 1) * SEG)
                          Sre, Sim = pq[0], pq[1]
                          skr = [("pq", 0), ("pq", 1)]; ski = [("pq", 2), ("pq", 3)]
                          for ri, (St, sk) in enumerate(((Sre, skr), (Sim, ski))):
                              for q in range(4):
                                  for half, kk_ in enumerate((k0, k1)):
                                      P.op("pe", lambda e, St=St, q=q, half=half, kk_=kk_, ri=ri, tsl=tsl: e.matmul(St[:, q * SEG:(q + 1) * SEG], lhsT=W1pad[:, half * 4 + q, ri, :], rhs=hT[:, kk_, tsl], start=(half == 0), stop=(half == 1)),
                                           r=["W1pad", ("hT", kk_)], w=[sk[q // 2]])
                          S3r = Sre[:].rearrange("p (g t) -> p g t", g=4); S3i = Sim[:].rearrange("p (g t) -> p g t", g=4)
                          A3, B3, C3, D3 = [x[:].rearrange("p (g t) -> p g t", g=4) for x in (A_, B_, C_, D_)]
                          G3r, G3i = Gr[:].rearrange("p (g t) -> p g t", g=4), Gi[:].rearrange("p (g t) -> p g t", g=4)
                          vop("dve", A3, S3r, Tm_re, ALU.mult, skr + ["stab"], [kA])
                          vop("dve", B3, S3i, Tm_im, ALU.mult, ski + ["stab"], [kB])
                          vop("dve", A3, A3, B3, ALU.subtract, [kA, kB], [kA])
                          vop("dve", C3, S3i, Tm_re, ALU.mult, ski + ["stab"], [kC2])
                          vop("dve", D3, S3r, Tm_im, ALU.mult, skr + ["stab"], [kD])
                          vop("dve", C3, C3, D3, ALU.add, [kC2, kD], [kC2])
                          for (src_, dstt, ks, kd_) in ((A_, Gr, kA, kGr), (C_, Gi, kC2, kGi)):
                              P.op("dve", lambda e, src_=src_, dstt=dstt: e.tensor_tensor_scan(out=dstt[:], data0=m256[:], data1=src_[:], initial=0.0, op0=ALU.mult, op1=ALU.add), r=["m32", ks], w=[kd_])
                              if d == 1:
                                  s3 = src_[:].rearrange("p (g t) -> p g t", g=4); d3 = dstt[:].rearrange("p (g t) -> p g t", g=4)
                                  P.op("act", lambda e, dstt=dstt: e.copy(out=ctmp[:, 0:4], in_=dstt[:, SEG - 1:NT:SEG]), r=[kd_], w=["ctmp"])
                                  P.op("dve", lambda e, s3=s3, d3=d3: e.tensor_tensor(out=d3, in0=s3, in1=d3, op=ALU.subtract), r=[ks, kd_], w=[kd_])
                                  P.op("dve", lambda e, d3=d3: e.tensor_tensor(out=d3, in0=d3, in1=ctmp[:, 0:4].unsqueeze(2).to_broadcast([128, 4, SEG]), op=ALU.add), r=[kd_, "ctmp"], w=[kd_])
                          vop("dve", G3r, G3r, sHent[:, :, 0:1].to_broadcast([128, 4, SEG]), ALU.add, [kGr, "sHent"], [kGr])
                          vop("dve", G3i, G3i, sHent[:, :, 1:2].to_broadcast([128, 4, SEG]), ALU.add, [kGi, "sHent"], [kGi])
                          vop("dve", A3, G3r, Tp_re, ALU.mult, [kGr, "stab"], [kA])
                          vop("dve", B3, G3i, Tp_im, ALU.mult, [kGi, "stab"], [kB])
                          vop("dve", Hre_b, A3, B3, ALU.subtract, [kA, kB], [("qhat", 0)])
                          vop("dve", C3, G3r, Tp_im, ALU.mult, [kGr, "stab"], [kC2])
                          vop("dve", D3, G3i, Tp_re, ALU.mult, [kGi, "stab"], [kD])
                          vop("dve", Him_b, C3, D3, ALU.add, [kC2, kD], [("qhat", 1)])
                          xi = SEG - 1 if d == 0 else 0
                          vop("dve", sHx[:, :, 0:1], A3[:, :, xi:xi + 1], B3[:, :, xi:xi + 1], ALU.subtract, [kA, kB], ["sHx"])
                          vop("dve", sHx[:, :, 1:2], C3[:, :, xi:xi + 1], D3[:, :, xi:xi + 1], ALU.add, [kC2, kD], ["sHx"])
                          for half in range(2):
                              P.dma(O["ssm"][seg * 2 + d, half * 32 + 4 * s: half * 32 + 4 * s + 4].rearrange("g p r -> p g r"), sHx[half * 64:(half + 1) * 64, :, :], r=["sHx"], sem="ssmout")
                          P.op("dve", lambda e: e.tensor_scalar(out=sHent[:], in0=sHx[:], scalar1=keep[:, 0:1], scalar2=None, op0=ALU.mult), r=["sHx", "keep"], w=["sHent"])
                          if STAGE.get("ssm_dbg"):
                              P.dma(O["dbg2"][:, 0:1024], A_[:], r=[kA], sem="dbg")
                              P.dma(O["dbg2"][:, 1024:2048], B_[:], r=[kB], sem="dbg")
                              P.dma(O["dbg2"][:, 2048:3072], C_[:], r=[kC2], sem="dbg")
                              P.dma(O["dbg2"][:, 3072:4096], D_[:], r=[kD], sem="dbg")
                              P.dma(O["dbg2"][:, 4096:5120], Gr[:], r=[kGr], sem="dbg")
                              P.dma(O["dbg2"][:, 5120:6144], Gi[:], r=[kGi], sem="dbg")
                              P.dma(O["dbg3"], hT[:].rearrange("p k t -> p (k t)"), r=[("hT", kk) for kk in range(8)], sem="dbg")
                              P.dma(O["dbg"][:, 0:4096], vl, r=["stab"], sem="dbg")
                              P.dma(O["dbg"][:, 4096:5120], rs32, r=["srr"], sem="dbg")
                              P.dma(O["dbg"][:, 5120:5632], sq32, r=["sqq"], sem="dbg")
                              raise StopIteration
                          for half, kk_ in enumerate((k0, k1)):
                              yp_, ypk = bank(4 + half)
                              n = 0
                              for q in range(4):
                                  for ri, Hb in enumerate((Hre_b, Him_b)):
                                      P.op("pe", lambda e, yp_=yp_, q=q, half=half, ri=ri, Hb=Hb, n=n: e.matmul(yp_[:, 0:SEG], lhsT=Ewpad[:, half * 4 + q, ri, :], rhs=Hb[:, q, :], start=(n == 0), stop=(n == 7)),
                                           r=["Ewpad", ("qhat", ri)], w=[ypk])
                                      n += 1
                              first = (d == 0 and s % 2 == 0)
                              if first:
                                  P.op("dve", lambda e, yp_=yp_, kk_=kk_, tsl=tsl: e.scalar_tensor_tensor(out=ysb[:, kk_, tsl], in0=hT[:, kk_, tsl], scalar=sD[:, kk_:kk_ + 1], in1=yp_[:, 0:SEG], op0=ALU.mult, op1=ALU.add),
                                       r=[ypk, ("hT", kk_), "sD"], w=[("aT", kk_)])
                              else:
                                  P.op("dve", lambda e, yp_=yp_, kk_=kk_, tsl=tsl: e.tensor_tensor(out=ysb[:, kk_, tsl], in0=yp_[:, 0:SEG], in1=ysb[:, kk_, tsl], op=ALU.add),
                                       r=[ypk, ("aT", kk_)], w=[("aT", kk_)])

            try:
                _body()
            except StopIteration:
                pass
            P.alias(["vlat"], ["stab"])
            P.alias(["vctx"], ["W1pad", "Ewpad"])
            P.alias(["rstd"], ["srr"])
            P.alias([("sq", 0)], ["sqq"])
            P.alias([("sq", 1)], ["sCq"])
            P.op("dve", lambda e: e.memset(vlat[:], 0.0), w=["vlat"])
            P.op("dve", lambda e: e.memset(vlat[:, :, :, 0, 64:65], 1.0), w=["vlat"])
            P.op("dve", lambda e: e.memset(vlat[:, :, :, 1, 0:1], 1.0), w=["vlat"])
            for kk_ in range(8):
                yk = ysb[:, kk_, :]
                P.op("dve", lambda e, yk=yk: e.tensor_tensor(out=A_[:], in0=yk, in1=yk, op=ALU.mult), r=[("aT", kk_)], w=[kA])
                P.op("dve", lambda e: e.tensor_scalar(out=A_[:], in0=A_[:], scalar1=0.044715, scalar2=1.0, op0=ALU.mult, op1=ALU.add), r=[kA], w=[kA])
                P.op("pool", lambda e, yk=yk: e.tensor_tensor(out=A_[:], in0=A_[:], in1=yk, op=ALU.mult), r=[kA, ("aT", kk_)], w=[kA])
                P.op("act", lambda e: e.activation(out=A_[:], in_=A_[:], func=AF.Sigmoid, scale=1.5957691216057308), r=[kA], w=[kA])
                P.op("pool", lambda e, yk=yk: e.tensor_tensor(out=yk, in0=A_[:], in1=yk, op=ALU.mult), r=[kA, ("aT", kk_)], w=[("aT", kk_)])
            for dc in range(8):
                wb = wup[dc % 2]
                P.dma(wb[:, 0], I["wglu"][dc], w=[("wup", dc % 2, 0)], q="pool")
                P.dma(wb[:, 1], I["wglu"][8 + dc], w=[("wup", dc % 2, 1)], q="pool")
                for th in range(2):
                    vps, vpk = bank((dc * 2 + th) % 4)
                    gps, gpk = bank(4 + (dc * 2 + th) % 4)
                    for gv, (ps, pk) in enumerate(((vps, vpk), (gps, gpk))):
                        for kk_ in range(8):
                            P.op("pe", lambda e, ps=ps, kk_=kk_, th=th, wb=wb, gv=gv: e.matmul(ps, lhsT=wb[:, gv, kk_, :], rhs=ysb[:, kk_, th * 512:(th + 1) * 512], start=(kk_ == 0), stop=(kk_ == 7)),
                                 r=[("wup", dc % 2, gv), ("aT", kk_)], w=[pk])
                    sl = slice(th * 512, (th + 1) * 512)
                    P.op("act", lambda e, gps=gps, sl=sl: e.activation(out=B_[:, sl], in_=gps, func=AF.Sigmoid), r=[gpk], w=[kB])
                    P.op("dve", lambda e, vps=vps, sl=sl: e.tensor_tensor(out=B_[:, sl], in0=vps, in1=B_[:, sl], op=ALU.mult), r=[vpk, kB], w=[kB])
                    P.op("dve", lambda e, dc=dc, sl=sl: e.scalar_tensor_tensor(out=xT[:, dc, sl], in0=B_[:, sl], scalar=mod[:, 16 + dc:17 + dc], in1=xT[:, dc, sl], op0=ALU.mult, op1=ALU.add),
                         r=[kB, "mod", ("xT", dc)], w=[("xT", dc)])

        eps_t = P.sb("eps_t", [128, 1])
        P.op("dve", lambda e: e.memset(eps_t[:], EPS), w=["eps"])
        halfpi = P.sb("halfpi", [128, 1])
        P.op("dve", lambda e: e.memset(halfpi[:], 1.5707963267948966), w=["halfpi"])


        for l in range(STAGE["layers"]):
            ada_layer(l)
            norm_mod(0)
            if STAGE["mixers"]:
                if l % 3 == 0:
                    attention(l)
                elif l % 3 == 1 and STAGE.get("hgrn", True):
                    hgrn(l)
                elif l % 3 == 2 and STAGE.get("ssm", True):
                    ssm(l)
            norm_mod(1)
            ffn(l)

        rms_stats()
        for k in range(8):
            P.op("dve", lambda e, k=k: e.scalar_tensor_tensor(out=xT[:, k, :], in0=xT[:, k, :], scalar=fg[:, k:k + 1], in1=rstd[:], op0=ALU.mult, op1=ALU.mult),
                 r=[("xT", k), "fg", "rstd"], w=[("xT", k)])
        for k in range(8):
            P.dma(O["y"][k * 128:(k + 1) * 128, :], xT[:, k, :], r=[("xT", k)], sem="yout")
        P.op("pool", lambda e: e.memset(rstd[:], 0.0), r=["rstd"], w=["rstd"])
        if not (STAGE["mixers"] and STAGE.get("hgrn", True) and STAGE["layers"] > 1):
            for i in range(8):
                P.dma(O["hg"][i * 8:(i + 1) * 8].rearrange("a p n -> p a n"), rstd[:].rearrange("p (a n) -> p a n", a=8), r=["rstd"], sem="sout")
        if not (STAGE["mixers"] and STAGE.get("ssm", True) and STAGE["layers"] > 2):
            P.dma(O["ssm"].rearrange("a g p r -> p a (g r)")[0:64], rstd[0:64, :].rearrange("p (a n) -> p a n", a=8), r=["rstd"], sem="sout")
        P.wait_all_dma()
        P.emit()
    return nc

def _c(a):
    return np.ascontiguousarray(a, dtype=np.float32)


def prep_inputs(inp):
    g = {k: np.asarray(v) for k, v in inp.items()}
    sh = {}
    sh["ident"] = np.eye(128, dtype=np.float32)
    sh["ada_w"] = _c(g["ada_w"].reshape(DEPTH, 8, 128, 12, 512).transpose(0, 3, 2, 1, 4))
    sh["ada_b"] = _c(g["ada_b"].reshape(DEPTH, 48, 128).transpose(0, 2, 1))
    sh["ng"] = _c(np.stack([g["norm1_g"], g["norm2_g"]]).reshape(2, DEPTH, 8, 128).transpose(3, 0, 1, 2))
    sh["final_g"] = _c(g["final_g"].reshape(8, 128).T)
    sh["ffn_w_up"] = _c(g["ffn_w_up"].reshape(DEPTH, 8, 128, 2, NFC, 128).transpose(0, 4, 2, 3, 1, 5))
    sh["ffn_w_down"] = _c(g["ffn_w_down"].reshape(DEPTH, NFC, 128, 8, 128).transpose(0, 3, 2, 1, 4))
    sh["ffn_conv_w"] = _c(g["ffn_conv_w"].reshape(DEPTH, 3, 2 * NFC, 128).transpose(0, 3, 1, 2))
    sh["ffn_conv_b"] = _c(g["ffn_conv_b"].reshape(DEPTH, 2 * NFC, 128).transpose(0, 2, 1))
    wqkv = g["attn_wqkv"]
    sh["wq"] = _c(wqkv[:, :, 0:1024].reshape(2, 8, 128, 8, 128).transpose(0, 3, 2, 1, 4))
    wk = wqkv[:, :, 1024:1280].reshape(2, 8, 128, 4, 1, 64)
    sh["wk"] = _c(np.broadcast_to(wk, (2, 8, 128, 4, 2, 64)).reshape(2, 8, 128, 4, 128).transpose(0, 3, 2, 1, 4))
    sh["wkv"] = _c(wqkv[:, :, 1024:1536].reshape(2, 8, 128, 512).transpose(0, 2, 1, 3))
    sh["wo"] = _c(g["attn_wo"].reshape(2, 8, 128, 8, 128).transpose(0, 3, 2, 1, 4))
    sh["sink"] = _c(np.broadcast_to(g["attn_sink"][:, None, :], (2, 128, 16)))
    hw = g["hgrn_w_in"][0]
    sh["hw_in"] = _c(hw.reshape(8, 128, 5, 8, 128).transpose(3, 2, 1, 0, 4))
    sh["hwo"] = _c(g["hgrn_wo"][0].reshape(8, 128, 8, 128).transpose(2, 1, 0, 3))
    sh["hlb"] = _c(g["hgrn_lb"].reshape(4, 2, 8, 128).transpose(3, 0, 1, 2))
    sh["hgn"] = _c(g["hgrn_g_norm"][0].reshape(128, 1))
    ii = np.arange(128)
    same = (ii[:, None] // 32) == (ii[None, :] // 32)
    sh["hmask"] = _c(np.stack([same & (ii[:, None] <= ii[None, :]), same & (ii[:, None] >= ii[None, :])], axis=1))
    are, aim, ldt = g["ssm_a_re"][0], g["ssm_a_im"][0], g["ssm_log_dt"][0]
    bre, bim, cre, cim = g["ssm_b_re"][0], g["ssm_b_im"][0], g["ssm_c_re"][0], g["ssm_c_im"][0]
    sh["sBreR"] = _c(bre.reshape(2, 8, 8, 64, 16).transpose(0, 1, 2, 4, 3).reshape(2, 8, 128, 64))
    sh["sBimR"] = _c(bim.reshape(2, 8, 8, 64, 16).transpose(0, 1, 2, 4, 3).reshape(2, 8, 128, 64))
    sh["sAreR"] = _c(np.broadcast_to(are.reshape(2, 8, 8, 1, 64), (2, 8, 8, 16, 64)).reshape(2, 8, 128, 64))
    sh["sAimR"] = _c(np.broadcast_to(aim.reshape(2, 8, 8, 1, 64), (2, 8, 8, 16, 64)).reshape(2, 8, 128, 64))
    sh["sDtR"] = _c(np.broadcast_to(ldt.reshape(2, 8, 8, 1, 1), (2, 8, 8, 16, 1)).reshape(2, 8, 128, 1))
    sh["sAreQ"] = _c(are.reshape(2, 2, 32, 64).transpose(0, 1, 3, 2).reshape(2, 128, 32))
    sh["sAimQ"] = _c(aim.reshape(2, 2, 32, 64).transpose(0, 1, 3, 2).reshape(2, 128, 32))
    sh["sDtQ"] = _c(np.broadcast_to(ldt.reshape(2, 2, 1, 32), (2, 2, 64, 32)).reshape(2, 128, 32))
    sh["sCreQ"] = _c(cre.reshape(2, 2, 32, 16, 64).transpose(0, 1, 4, 2, 3).reshape(2, 128, 32, 16))
    sh["sCimQ"] = _c(cim.reshape(2, 2, 32, 16, 64).transpose(0, 1, 4, 2, 3).reshape(2, 128, 32, 16))
    sh["sD"] = _c(g["ssm_d"][0].reshape(8, 128).T)
    sm = np.zeros((128, 12), np.float32)
    for q in range(8):
        sm[q * 16:(q + 1) * 16, q] = 1.0
    sm[0:64, 8] = 1.0; sm[64:128, 9] = 1.0; sm[0:64, 10] = -1.0; sm[64:128, 11] = -1.0
    sh["smask"] = sm
    sh["wglu"] = _c(g["ssm_w_glu"][0].reshape(8, 128, 16, 128).transpose(2, 1, 0, 3))
    tt = np.arange(NT)
    inv = 1.0 / (10000.0 ** (np.arange(0, 32, 2, dtype=np.float32) / np.float32(32)))
    ar = (tt // 64).astype(np.float32)[:, None] * inv.astype(np.float32)
    ac = (tt % 64).astype(np.float32)[:, None] * inv.astype(np.float32)
    ang = np.concatenate([ar, ar, ac, ac], axis=-1).astype(np.float32)
    cosS = _c(np.concatenate([np.cos(ang).T] * 2, axis=0)); sinS = _c(np.concatenate([np.sin(ang).T] * 2, axis=0))
    rm = np.zeros((128, 128), np.float32)
    for m in range(128):
        d = m % 64
        if (d % 32) < 16:
            rm[m + 16, m] = -1.0
        else:
            rm[m - 16, m] = 1.0
    sh["rmat"] = rm
    kk = np.arange(128)[:, None]; qq = np.arange(128)[None, :]
    mS = np.zeros((128, 8, 2, 128), np.float32); mP = np.zeros((128, 8, 2, 128), np.float32)
    for i in range(8):
        if i >= 1:
            mS[:, i, 0, :] = (kk >= qq)
        if i <= 6:
            mS[:, i, 1, :] = (kk <= qq)
        mP[:, i, 0, :] = 1.0 if i % 2 == 1 else 0.0
        mP[:, i, 1, :] = 1.0 if i % 2 == 0 else 0.0
    if STAGE.get("kv_only"):
        for kk_ in ("wq", "wk", "wo", "sink", "rmat"):
            sh.pop(kk_, None)
    maps = []
    for c in range(8):
        m = dict(sh)
        if c < 4:
            xc = g["x_sample"][c]
            cond = g["c"][c]
            kp = 1.0
        else:
            xc = g["x_prompt"][4 * (c - 4):4 * (c - 4) + 4].reshape(NT, D)
            cond = g["c_ctx"]
            kp = 0.0
        if c < 4:
            ropec = cosS; ropes = sinS; m["amask"] = mS
            ck = g["cache_k"][c]
            m["ckd"] = _c(np.broadcast_to(ck[:, :, :, None, :], (2, 512, 4, 2, 64)).reshape(2, 512, 512))
            cv = g["cache_v"][c]
            vp = np.zeros((2, 512, 4, 2, 128), np.float32)
            vp[:, :, :, 0, 0:64] = cv; vp[:, :, :, 0, 64] = 1.0
            vp[:, :, :, 1, 64:128] = cv; vp[:, :, :, 1, 0] = 1.0
            m["cvp"] = vp.reshape(2, 512, 1024)
        else:
            ropec = np.ones((128, NT), np.float32); ropes = np.zeros((128, NT), np.float32); m["amask"] = mP
            m["ckd"] = np.zeros((2, 512, 512), np.float32); m["cvp"] = np.zeros((2, 512, 1024), np.float32)
        if STAGE.get("kv_only"):
            for kk_ in ("amask", "ckd", "cvp"):
                m.pop(kk_, None)
        if c < 4:
            st = g["state_ssm"][c, 0]
            m["sH0"] = _c(st.reshape(2, 2, 32, 64, 2).transpose(0, 1, 3, 2, 4).reshape(2, 128, 32, 2))
        else:
            m["sH0"] = np.zeros((2, 128, 32, 2), np.float32)
        m["hs0"] = _c(g["state_hgrn"][c, 0]) if c < 4 else np.zeros((2, 8, 128, 128), np.float32)
        m["x"] = _c(np.concatenate([xc.T, ropec, ropes], axis=0))
        m["cond"] = _c(cond.reshape(8, 128).T)
        m["keep"] = _c(np.stack([np.full(128, kp), np.full(128, kp - 1.0)], axis=1))
        maps.append(m)
    return maps


def assemble(results):
    f = lambda a: np.asarray(a, dtype=np.float32)
    ys = np.stack([f(results[c]["y"]).T for c in range(4)])
    yp = np.concatenate([f(results[c]["y"]).T.reshape(4, 256, D) for c in range(4, 8)])
    nk = np.concatenate([f(results[c]["nk"]).reshape(2, 4, 256, 4, 64).transpose(1, 0, 2, 3, 4) for c in range(4, 8)])
    nv = np.concatenate([f(results[c]["nv"]).reshape(2, 4, 256, 4, 64).transpose(1, 0, 2, 3, 4) for c in range(4, 8)])
    hg = np.concatenate([f(results[c]["hg"]).reshape(4, 1, 2, 8, 128, 128) for c in range(4, 8)])
    ssm = np.concatenate([f(results[c]["ssm"]).reshape(4, 1, 2, 64, 64, 2) for c in range(4, 8)])
    return (np.ascontiguousarray(yp), np.ascontiguousarray(ys), np.ascontiguousarray(nk), np.ascontiguousarray(nv),
            np.ascontiguousarray(hg), np.ascontiguousarray(ssm))


def kernel(**inputs):
    nc = build_program()
    maps = prep_inputs(inputs)
    res = run_bass_kernel_spmd(nc, maps, core_ids=list(range(8)))
    return assemble(res.results)
```

```python
import numpy as np
from concourse.bass_utils import run_bass_kernel_spmd

from contextlib import ExitStack
import concourse.bass as bass
import concourse.mybir as mybir

F32 = mybir.dt.float32
F32R = mybir.dt.float32r
BF16 = mybir.dt.bfloat16
AF = mybir.ActivationFunctionType
ALU = mybir.AluOpType
AX = mybir.AxisListType


class Prog:
    ENGS = ("pe", "act", "dve", "pool", "sp")

    def __init__(self, nc, es: ExitStack):
        self.nc = nc
        self.es = es
        self.recs = {e: [] for e in self.ENGS}
        self.cnt = {e: 0 for e in self.ENGS}
        self.known = {e: {} for e in self.ENGS}
        self.state = {}
        self.sems = {}
        self.dcnt = {}
        for e in self.ENGS:
            self.sems[("e", e)] = es.enter_context(nc.semaphore("sem_" + e))
        self.psn = 0

    def sb(self, name, shape, dt=F32):
        return self.es.enter_context(self.nc.sbuf_tensor("sb_" + name, list(shape), dt))

    def ps(self, name, shape, dt=F32):
        return self.es.enter_context(self.nc.psum_tensor(name, list(shape), dt))

    def dsem(self, name):
        k = ("d", name)
        if k not in self.sems:
            self.sems[k] = self.es.enter_context(self.nc.semaphore("dsem_" + name))
            self.dcnt[k] = 0
        return k

    def _st(self, k):
        s = self.state.get(k)
        if s is None:
            s = {"w": {}, "r": {}}
            self.state[k] = s
        return s

    def _deps(self, eng, r, w):
        deps = {}
        def add(d):
            if d is None:
                return
            sk, v = d
            if deps.get(sk, 0) < v:
                deps[sk] = v
        for k in r:
            st = self._st(k)
            for sk, v in st["w"].items():
                add((sk, v))
            if isinstance(k, tuple) and k[0] == "pq":
                for sk, v in st["r"].items():
                    if sk != ("e", eng):
                        add((sk, v))
        for k in w:
            s = self._st(k)
            for sk, v in s["w"].items():
                add((sk, v))
            for sk, v in s["r"].items():
                add((sk, v))
        waits = []
        kn = self.known[eng]
        for sk, v in deps.items():
            if eng == "pe" and sk == ("e", "pe"):
                continue
            if kn.get(sk, 0) < v:
                waits.append((sk, v))
                kn[sk] = v
        return waits

    def _commit(self, comp, r, w):
        sk, v = comp
        for k in w:
            self.state[k] = {"w": {sk: v}, "r": {}}
        for k in r:
            s = self._st(k)
            if s["r"].get(sk, 0) < v:
                s["r"][sk] = v

    def alias(self, dst, src):
        mw, mr = {}, {}
        for k in src:
            st = self._st(k)
            for sk, v in st["w"].items():
                mw[sk] = max(mw.get(sk, 0), v)
            for sk, v in st["r"].items():
                mr[sk] = max(mr.get(sk, 0), v)
        for k in dst:
            self.state[k] = {"w": dict(mw), "r": dict(mr)}

    def op(self, eng, fn, r=(), w=(), inc=True):
        inc = True
        waits = self._deps(eng, r, w)
        sk = ("e", eng)
        comp = (sk, self.cnt[eng] + 1)
        if inc:
            self.cnt[eng] += 1
        self.recs[eng].append((waits, fn, (sk, 1) if inc else None))
        self._commit(comp, r, w)

    def dma(self, out, in_, r=(), w=(), sem=None, q="sp", **kw):
        if sem is None:
            sem = "_".join(str(x) for x in (w[0] if isinstance(w[0], tuple) else (w[0],)))
        waits = self._deps(q, r, w)
        sk = self.dsem(sem)
        self.dcnt[sk] += 16
        comp = (sk, self.dcnt[sk])
        self.recs[q].append((waits, (lambda e, o=out, i=in_, kw=kw: e.dma_start(out=o, in_=i, **kw)), (sk, 16)))
        self._commit(comp, r, w)

    def wait_all_dma(self, q="sp"):
        waits = []
        for sk, v in self.dcnt.items():
            if v > 0 and self.known[q].get(sk, 0) < v:
                waits.append((sk, v))
                self.known[q][sk] = v
        self.recs[q].append((waits, None, None))

    def barrier_all(self):
        tgt = {("e", e): self.cnt[e] for e in self.ENGS if self.cnt[e] > 0}
        for sk, v in self.dcnt.items():
            if v > 0:
                tgt[sk] = v
        for e in self.ENGS:
            waits = []
            for sk, v in tgt.items():
                if sk == ("e", e):
                    continue
                if self.known[e].get(sk, 0) < v:
                    waits.append((sk, v))
                    self.known[e][sk] = v
            if waits:
                self.recs[e].append((waits, None, None))

    def emit(self):
        nc = self.nc
        sems = self.sems
        recs = self.recs

        def replay(name):
            def f(e):
                for waits, fn, inc in recs[name]:
                    for sk, v in waits:
                        e.wait_ge(sems[sk], v)
                    if fn is not None:
                        ins = fn(e)
                        if inc is not None:
                            ins.then_inc(sems[inc[0]], inc[1])
            return f

        with nc.Block() as block:
            block.tensor(replay("pe"))
            block.scalar(replay("act"))
            block.vector(replay("dve"))
            block.gpsimd(replay("pool"))
            block.sync(replay("sp"))

    def stats(self):
        return {e: len(self.recs[e]) for e in self.ENGS}

D = 1024
NT = 1024
DFF = 2816
NFC = 22
DEPTH = 4
EPS = 1e-6

STAGE = {"mixers": True, "layers": 4, "kv_only": False}


def build_program():
    nc = bass.Bass("TRN2", target_bir_lowering=False)

    def din(name, shape, dt=F32):
        return nc.dram_tensor(name, list(shape), dt, kind="ExternalInput").ap()

    def dout(name, shape, dt=F32):
        return nc.dram_tensor(name, list(shape), dt, kind="ExternalOutput").ap()

    I = {}
    I["x"] = din("x", [D + 256, NT])
    I["cond"] = din("cond", [128, 8])
    I["keep"] = din("keep", [128, 2])
    I["ident"] = din("ident", [128, 128])
    I["ada_w"] = din("ada_w", [DEPTH, 12, 128, 8, 512])
    I["ada_b"] = din("ada_b", [DEPTH, 128, 48])
    I["ng"] = din("ng", [128, 2, DEPTH, 8])
    I["ffn_w_up"] = din("ffn_w_up", [DEPTH, NFC, 128, 2, 8, 128])
    I["ffn_conv_w"] = din("ffn_conv_w", [DEPTH, 128, 3, 2 * NFC])
    I["ffn_conv_b"] = din("ffn_conv_b", [DEPTH, 128, 2 * NFC])
    I["ffn_w_down"] = din("ffn_w_down", [DEPTH, 8, 128, NFC, 128])
    I["final_g"] = din("final_g", [128, 8])
    I["wkv"] = din("wkv", [2, 128, 8, 512])
    I["hw_in"] = din("hw_in", [8, 5, 128, 8, 128])
    I["hwo"] = din("hwo", [8, 128, 8, 128])
    I["hlb"] = din("hlb", [128, 4, 2, 8])
    I["hgn"] = din("hgn", [128, 1])
    I["hs0"] = din("hs0", [2, 8, 128, 128])
    I["hmask"] = din("hmask", [128, 2, 128])
    I["sBreR"] = din("sBreR", [2, 8, 128, 64]); I["sBimR"] = din("sBimR", [2, 8, 128, 64])
    I["sAreR"] = din("sAreR", [2, 8, 128, 64]); I["sAimR"] = din("sAimR", [2, 8, 128, 64])
    I["sDtR"] = din("sDtR", [2, 8, 128, 1])
    I["sAreQ"] = din("sAreQ", [2, 128, 32]); I["sAimQ"] = din("sAimQ", [2, 128, 32]); I["sDtQ"] = din("sDtQ", [2, 128, 32])
    I["sCreQ"] = din("sCreQ", [2, 128, 32, 16]); I["sCimQ"] = din("sCimQ", [2, 128, 32, 16])
    I["sH0"] = din("sH0", [2, 128, 32, 2])
    I["sD"] = din("sD", [128, 8])
    I["smask"] = din("smask", [128, 12])
    I["wglu"] = din("wglu", [16, 128, 8, 128])
    if not STAGE.get("kv_only"):
        I["wq"] = din("wq", [2, 8, 128, 8, 128])
        I["wk"] = din("wk", [2, 4, 128, 8, 128])
        I["wo"] = din("wo", [2, 8, 128, 8, 128])
        I["sink"] = din("sink", [2, 128, 16])
        I["rmat"] = din("rmat", [128, 128])
        I["amask"] = din("amask", [128, 8, 2, 128])
        I["ckd"] = din("ckd", [2, 512, 512])
        I["cvp"] = din("cvp", [2, 512, 1024])
    O = {}
    O["y"] = dout("y", [D, NT])
    O["nk"] = dout("nk", [2, NT, 256])
    O["nv"] = dout("nv", [2, NT, 256])
    O["hg"] = dout("hg", [64, 128, 128])
    O["ssm"] = dout("ssm", [8, 64, 64, 2])
    if STAGE.get("ssm_dbg"):
        O["dbg"] = dout("dbg", [128, 4096 + 1024 + 512])
        O["dbg2"] = dout("dbg2", [128, 6 * 1024])
        O["dbg3"] = dout("dbg3", [128, 8 * 1024], BF16)

    with ExitStack() as es:
        P = Prog(nc, es)
        xT = P.sb("xT", [128, 10, NT])
        hT = P.sb("hT", [128, 8, NT], BF16)
        aT = P.sb("aT", [128, 12, NT], BF16)
        rstd = P.sb("rstd", [128, NT])
        tmpA = P.sb("tmpA", [128, NT])
        tmpB = P.sb("tmpB", [128, NT])
        cg = [P.sb("cg%d" % i, [128, NT]) for i in range(2)]
        cv = [P.sb("cv%d" % i, [128, NT]) for i in range(2)]
        sq = [P.sb("sq%d" % i, [128, NT], BF16) for i in range(2)]
        ident = P.sb("ident", [128, 128])
        ones_bf = P.sb("ones_bf", [128, 128], BF16)
        one_f = P.sb("one_f", [128, 1])
        keep = P.sb("keep", [128, 2])
        cond = P.sb("cond", [128, 8])
        s_bf = P.sb("s_bf", [128, 8], BF16)
        mod = P.sb("mod", [128, 48])
        adab = P.sb("adab", [128, 48])
        ng = P.sb("ng", [128, 2, DEPTH, 8])
        fg = P.sb("fg", [128, 8])
        AB = P.sb("AB", [128, 4, 8])
        cw = P.sb("cw", [128, 3, 2 * NFC])
        cb = P.sb("cb", [128, 2 * NFC])
        cwk = P.sb("cwk", [128, 2, 2 * NFC])
        wada = [P.sb("wada%d" % i, [128, 8, 512], BF16) for i in range(2)]
        wup = [P.sb("wup%d" % i, [128, 2, 8, 128], BF16) for i in range(2)]
        wdn = [P.sb("wdn%d" % i, [128, 11, 128], BF16) for i in range(2)]
        vlat = P.sb("vlat", [128, 8, 4, 2, 128], BF16)
        vctx = P.sb("vctx", [128, 4, 4, 2, 128], BF16)
        kctxT = P.sb("kctxT", [128, 4, 512], BF16)
        ckd = P.sb("ckd", [128, 4, 512], BF16)
        amask = P.sb("amask", [128, 8, 2, 128], BF16)
        rmat = P.sb("rmat", [128, 128], BF16)
        ident_bf = P.sb("ident_bf", [128, 128], BF16)
        qb = [P.sb("qb%d" % i, [128, 512], BF16) for i in range(2)]
        esink = P.sb("esink", [128, 16])
        ones_f = P.sb("ones_f", [128, 128])
        m32 = P.sb("m32", [128, NT])
        hlb = P.sb("hlb", [128, 4, 2, 8])
        lbp = P.sb("lbp", [128, 2, 2, 8])
        hgn = P.sb("hgn", [128, 1])
        hmask = P.sb("hmask", [128, 2, 128], BF16)
        Sf = [P.sb("Sf%d" % i, [128, 128]) for i in range(2)]
        Sb = [P.sb("Sb%d" % i, [128, 128], BF16) for i in range(2)]
        adec = [P.sb("adec%d" % i, [128, 32]) for i in range(2)]
        rowm = P.sb("rowm", [128, 4])
        ctmp = P.sb("ctmp", [128, 32])
        vtok = P.sb("vtok", [128, 8, 128], BF16)
        qhat = [P.sb("qhat%d" % i, [128, NT], BF16) for i in range(2)]
        ktil = [P.sb("ktil%d" % i, [128, NT], BF16) for i in range(2)]
        kdT = [P.sb("kdT%d" % i, [128, NT], BF16) for i in range(2)]
        attm = [P.sb("attm%d" % i, [128, 128], BF16) for i in range(2)]
        smask = P.sb("smask", [128, 12])
        sD = P.sb("sD", [128, 8])
        sH = P.sb("sH", [128, 32, 2])
        sHent = P.sb("sHent", [128, 4, 2])
        sHx = P.sb("sHx", [128, 4, 2])
        pq = [P.ps("pq%d" % i, [128, 1024]) for i in range(4)]

        def bank(i):
            return pq[i // 2][:, (i % 2) * 512:(i % 2) * 512 + 512], ("pq", i)

        P.dma(ident[:], I["ident"], w=["ident"])
        P.dma(keep[:], I["keep"], w=["keep"])
        P.dma(cond[:], I["cond"], w=["cond"])
        P.dma(ng[:], I["ng"], w=["ng"])
        P.dma(fg[:], I["final_g"], w=["fg"])
        P.op("dve", lambda e: e.memset(ones_bf[:], 1.0), w=["ones_bf"])
        P.op("dve", lambda e: e.memset(one_f[:], 1.0), w=["one_f"])
        P.op("dve", lambda e: e.memset(ones_f[:], 1.0), w=["ones_f"])
        P.op("dve", lambda e: e.tensor_copy(out=ident_bf[:], in_=ident[:]), r=["ident"], w=["ident_bf"])
        if not STAGE.get("kv_only"):
            P.dma(rmat[:], I["rmat"], w=["rmat"], q="pool")
            P.dma(amask[:], I["amask"], w=["amask"], q="pool")
        P.op("pool", lambda e: e.memset(m32[:], 1.0), w=["m32"])
        P.op("pool", lambda e: e.memset(m32[:, 0:NT:32], 0.0), w=["m32"])
        P.op("pool", lambda e: e.memset(rowm[:], 0.0), w=["rowm"])
        for c4 in range(3):
            P.op("pool", lambda e, c4=c4: e.memset(rowm[c4 * 32:(c4 + 1) * 32, c4:c4 + 1], 1.0), w=["rowm"])
        P.op("pool", lambda e: e.memset(rowm[96:128, 3:4], 1.0), w=["rowm"])
        P.dma(hlb[:], I["hlb"], w=["hlb"])
        P.dma(hgn[:], I["hgn"], w=["hgn"])
        P.dma(hmask[:], I["hmask"], w=["hmask"], q="pool")
        P.op("dve", lambda e: e.memset(vlat[:], 0.0), w=["vlat"])
        P.op("dve", lambda e: e.memset(vlat[:, :, :, 0, 64:65], 1.0), w=["vlat"])
        P.op("dve", lambda e: e.memset(vlat[:, :, :, 1, 0:1], 1.0), w=["vlat"])
        P.op("act", lambda e: e.activation(out=s_bf[:], in_=cond[:], func=AF.Silu), r=["cond"], w=["s_bf"])

        P.dma(xT[:], I["x"].rearrange("(k p) t -> p k t", p=128), w=[("xT", k) for k in range(10)], sem="xin")

        def ada_layer(l):
            P.dma(adab[:], I["ada_b"][l], w=["adab"])
            ps, pk = bank(4)
            for n in range(12):
                wb = wada[n % 2]
                P.dma(wb[:], I["ada_w"][l, n],
                      w=[("wada", n % 2)], q="pool")
                for c4 in range(4):
                    c = n * 4 + c4
                    for k in range(8):
                        P.op("pe", lambda e, ps=ps, k=k, wb=wb, c=c, c4=c4: e.matmul(ps[:, c:c + 1], lhsT=wb[:, k, c4 * 128:(c4 + 1) * 128], rhs=s_bf[:, k:k + 1], start=(k == 0), stop=(k == 7)),
                             r=[("wada", n % 2), "s_bf"], w=[pk])
            P.op("dve", lambda e, ps=ps: e.tensor_tensor(out=mod[:], in0=ps[:, 0:48], in1=adab[:], op=ALU.add), r=[pk, "adab"], w=["mod"])
            for j in range(2):
                P.op("dve", lambda e, j=j: e.scalar_tensor_tensor(out=AB[:, 2 * j, :], in0=mod[:, (3 * j + 1) * 8:(3 * j + 2) * 8], scalar=1.0,
                                                                 in1=ng[:, j, l, :], op0=ALU.add, op1=ALU.mult),
                     r=["mod", "ng"], w=["AB"])
                P.op("dve", lambda e, j=j: e.tensor_copy(out=AB[:, 2 * j + 1, :], in_=mod[:, (3 * j) * 8:(3 * j + 1) * 8]), r=["mod"], w=["AB"])

        def rms_stats():
            b0, k0 = bank(6)
            b1, k1 = bank(7)
            for k in range(8):
                s = sq[k % 2]
                P.op("act", lambda e, s=s, k=k: e.activation(out=s[:], in_=xT[:, k, :], func=AF.Square), r=[("xT", k)], w=[("sq", k % 2)])
                for th, (b, bk) in enumerate(((b0, k0), (b1, k1))):
                    P.op("pe", lambda e, b=b, s=s, th=th, k=k: e.matmul(b, lhsT=ones_bf[:], rhs=s[:, th * 512:(th + 1) * 512], start=(k == 0), stop=(k == 7)),
                         r=[("sq", k % 2), "ones_bf"], w=[bk], inc=True)
            for th, (b, bk) in enumerate(((b0, k0), (b1, k1))):
                P.op("act", lambda e, b=b, th=th: e.activation(out=tmpA[:, th * 512:(th + 1) * 512], in_=b, func=AF.Sqrt, scale=1.0 / D, bias=eps_t[:, 0:1]),
                     r=[bk, "eps"], w=["tmpA"])
            P.op("dve", lambda e: e.reciprocal(out=rstd[:], in_=tmpA[:]), r=["tmpA"], w=["rstd"])

        def norm_mod(j):
            rms_stats()
            for k in range(8):
                t = tmpA if k % 2 == 0 else tmpB
                tk = "tmpA" if k % 2 == 0 else "tmpB"
                P.op("dve", lambda e, t=t, k=k: e.scalar_tensor_tensor(out=t[:], in0=xT[:, k, :], scalar=AB[:, 2 * j, k:k + 1], in1=rstd[:], op0=ALU.mult, op1=ALU.mult),
                     r=[("xT", k), "AB", "rstd"], w=[tk])
                P.op("act", lambda e, t=t, k=k: e.activation(out=hT[:, k, :], in_=t[:], func=AF.Identity, bias=AB[:, 2 * j + 1, k:k + 1], scale=1.0),
                     r=[tk, "AB"], w=[("hT", k)])

        def ffn(l):
            P.dma(cw[:], I["ffn_conv_w"][l], w=["cw"])
            P.dma(cb[:], I["ffn_conv_b"][l], w=["cb"])
            for jj, j in enumerate((0, 2)):
                P.op("dve", lambda e, jj=jj, j=j: e.tensor_scalar(out=cwk[:, jj, :], in0=cw[:, j, :], scalar1=keep[:, 1:2], scalar2=None, op0=ALU.mult),
                     r=["cw", "keep"], w=["cwk"])
            for grp in range(2):
                for fc in range(grp * 11, grp * 11 + 11):
                    wb = wup[fc % 2]
                    P.dma(wb[:], I["ffn_w_up"][l, fc], w=[("wup", fc % 2, 0), ("wup", fc % 2, 1)], q="pool")
                    outs = []
                    for gv in range(2):
                        pt = pq[(fc % 2) * 2 + gv]
                        pks = [("pq", ((fc % 2) * 2 + gv) * 2 + th) for th in range(2)]
                        for th in range(2):
                            for k in range(8):
                                P.op("pe", lambda e, pt=pt, th=th, k=k, gv=gv, wb=wb: e.matmul(pt[:, th * 512:(th + 1) * 512], lhsT=wb[:, gv, k, :], rhs=hT[:, k, th * 512:(th + 1) * 512],
                                                                                    start=(k == 0), stop=(k == 7)),
                                     r=[("wup", fc % 2, gv), ("hT", k)], w=[pks[th]], inc=(k == 7))
                        c = (cg if gv == 0 else cv)[fc % 2]
                        ck = ("cg" if gv == 0 else "cv", fc % 2)
                        col = gv * NFC + fc
                        P.op("act", lambda e, c=c, pt=pt, col=col: e.activation(out=c[:], in_=pt[:], func=AF.Identity, scale=cw[:, 1, col:col + 1], bias=cb[:, col:col + 1]),
                             r=pks + ["cw", "cb"], w=[ck])
                        P.op("dve", lambda e, c=c, pt=pt, col=col: e.scalar_tensor_tensor(out=c[:, 1:NT], in0=pt[:, 0:NT - 1], scalar=cw[:, 0, col:col + 1], in1=c[:, 1:NT], op0=ALU.mult, op1=ALU.add),
                             r=pks + ["cw", ck], w=[ck])
                        P.op("dve", lambda e, c=c, pt=pt, col=col: e.scalar_tensor_tensor(out=c[:, 0:NT - 1], in0=pt[:, 1:NT], scalar=cw[:, 2, col:col + 1], in1=c[:, 0:NT - 1], op0=ALU.mult, op1=ALU.add),
                             r=pks + ["cw", ck], w=[ck])
                        P.op("dve", lambda e, c=c, pt=pt, col=col: e.scalar_tensor_tensor(out=c[:, 256:NT:256], in0=pt[:, 255:NT - 1:256], scalar=cwk[:, 0, col:col + 1], in1=c[:, 256:NT:256], op0=ALU.mult, op1=ALU.add),
                             r=pks + ["cwk", ck], w=[ck])
                        P.op("dve", lambda e, c=c, pt=pt, col=col: e.scalar_tensor_tensor(out=c[:, 255:NT - 1:256], in0=pt[:, 256:NT:256], scalar=cwk[:, 1, col:col + 1], in1=c[:, 255:NT - 1:256], op0=ALU.mult, op1=ALU.add),
                             r=pks + ["cwk", ck], w=[ck])
                        outs.append((c, ck))
                    (cgt, cgk), (cvt, cvk) = outs
                    P.op("act", lambda e, cgt=cgt: e.activation(out=cgt[:], in_=cgt[:], func=AF.Silu), r=[cgk], w=[cgk])
                    P.op("dve", lambda e, cgt=cgt, cvt=cvt, fc=fc: e.tensor_tensor(out=aT[:, fc % 11, :], in0=cgt[:], in1=cvt[:], op=ALU.mult), r=[cgk, cvk], w=[("aT", fc % 11)])
                for dc in range(8):
                    wb = wdn[dc % 2]
                    P.dma(wb[:], I["ffn_w_down"][l, dc][:, grp * 11:grp * 11 + 11, :],
                          w=[("wdn", dc % 2)], q="pool")
                    for th in range(2):
                        ps, pk = bank((dc * 2 + th) % 8)
                        for fc in range(11):
                            P.op("pe", lambda e, ps=ps, fc=fc, th=th, wb=wb: e.matmul(ps, lhsT=wb[:, fc, :], rhs=aT[:, fc, th * 512:(th + 1) * 512], start=(fc == 0), stop=(fc == 10)),
                                 r=[("wdn", dc % 2), ("aT", fc)], w=[pk])
                        P.op("dve", lambda e, ps=ps, dc=dc, th=th: e.scalar_tensor_tensor(out=xT[:, dc, th * 512:(th + 1) * 512], in0=ps, scalar=mod[:, 40 + dc:41 + dc],
                                                                                         in1=xT[:, dc, th * 512:(th + 1) * 512], op0=ALU.mult, op1=ALU.add),
                             r=[pk, "mod", ("xT", dc)], w=[("xT", dc)])

        def attention(l):
            j = l // 3
            cosT, sinT = xT[:, 8, :], xT[:, 9, :]
            t1, t2 = cv[0], cv[1]
            if STAGE.get("kv_only"):
                wkv = wada[0]
                P.dma(wkv[:], I["wkv"][j], w=[("wada", 0)], q="pool")
                for tb in range(8):
                    ps, pk = bank(tb % 4)
                    for k in range(8):
                        P.op("pe", lambda e, ps=ps, k=k, tb=tb: e.matmul(ps, lhsT=hT[:, k, tb * 128:(tb + 1) * 128], rhs=wkv[:, k, :], start=(k == 0), stop=(k == 7)),
                             r=[("wada", 0), ("hT", k)], w=[pk])
                    kvt = tmpA if tb % 2 == 0 else tmpB
                    kvk = "tmpA" if tb % 2 == 0 else "tmpB"
                    P.op("act", lambda e, ps=ps, kvt=kvt: e.copy(out=kvt[:, 0:512], in_=ps), r=[pk], w=[kvk])
                    P.dma(O["nk"][j, tb * 128:(tb + 1) * 128, :], kvt[:, 0:256], r=[kvk], sem="kvout%d" % (tb % 2))
                    P.dma(O["nv"][j, tb * 128:(tb + 1) * 128, :], kvt[:, 256:512], r=[kvk], sem="kvout%d" % (tb % 2))
                return
            P.dma(esink[:], I["sink"][j], w=["esink"])
            P.op("act", lambda e: e.activation(out=esink[:], in_=esink[:], func=AF.Exp), r=["esink"], w=["esink"])
            if STAGE.get("attn_upto", 9) < 1:
                return
            P.dma(ckd[:], I["ckd"][j].rearrange("(kb p) n -> p kb n", p=128), w=["ckd"], q="pool")
            P.dma(vctx[:].rearrange("p kb g v n -> p kb (g v n)"), I["cvp"][j].rearrange("(kb p) n -> p kb n", p=128), w=["vctx"], q="pool")
            for g in range(4):
                ps, pk = bank(g)
                for kb in range(4):
                    P.op("pe", lambda e, ps=ps, kb=kb, g=g: e.matmul(ps[:, kb * 128:(kb + 1) * 128], lhsT=ckd[:, kb, g * 128:(g + 1) * 128], rhs=ident_bf[:], start=True, stop=True),
                         r=["ckd", "ident_bf"], w=[pk])
                P.op("act", lambda e, ps=ps, g=g: e.copy(out=kctxT[:, g, :], in_=ps), r=[pk], w=["kctxT"])
            if STAGE.get("attn_upto", 9) < 2:
                return
            def proj_rope(wsrc, dst, dkey, idx):
                wb = wup[idx % 2]
                P.dma(wb[:, 0], wsrc, w=[("wup", idx % 2, 0)], q="pool")
                for th in range(2):
                    ps, pk = bank((idx * 2 + th) % 4)
                    rps, rpk = bank(4 + (idx * 2 + th) % 2)
                    for k in range(8):
                        P.op("pe", lambda e, ps=ps, k=k, th=th, wb=wb: e.matmul(ps, lhsT=wb[:, 0, k, :], rhs=hT[:, k, th * 512:(th + 1) * 512], start=(k == 0), stop=(k == 7)),
                             r=[("wup", idx % 2, 0), ("hT", k)], w=[pk])
                    q_ = qb[th]
                    if STAGE.get("pr", 9) < 1:
                        continue
                    P.op("act", lambda e, ps=ps, q_=q_: e.copy(out=q_[:], in_=ps), r=[pk], w=[("qb", th)])
                    if STAGE.get("pr", 9) < 2:
                        continue
                    P.op("pe", lambda e, rps=rps, q_=q_: e.matmul(rps, lhsT=rmat[:], rhs=q_[:], start=True, stop=True), r=[("qb", th), "rmat"], w=[rpk])
                    sl = slice(th * 512, (th + 1) * 512)
                    if STAGE.get("pr", 9) < 3 or idx >= STAGE.get("pridx", 99):
                        continue
                    P.op("dve", lambda e, ps=ps, sl=sl: e.scalar_tensor_tensor(out=t1[:, sl], in0=ps, scalar=1.0, in1=cosT[:, sl], op0=ALU.mult, op1=ALU.mult), r=[pk, ("xT", 8), ("qb", th)], w=[("cv", 0)])
                    if STAGE.get("pr", 9) < 4:
                        continue
                    P.op("dve", lambda e, rps=rps, sl=sl: e.scalar_tensor_tensor(out=t2[:, sl], in0=rps, scalar=1.0, in1=sinT[:, sl], op0=ALU.mult, op1=ALU.mult), r=[rpk, ("xT", 9)], w=[("cv", 1)])
                    if STAGE.get("pr", 9) < 5:
                        continue
                    P.op("dve", lambda e, sl=sl, dst=dst: e.tensor_tensor(out=dst[:, sl], in0=t1[:, sl], in1=t2[:, sl], op=ALU.add), r=[("cv", 0), ("cv", 1)], w=[dkey])
            for qc in range(8):
                proj_rope(I["wq"][j, qc], aT[:, qc, :], ("aT", qc), qc)
            for g in range(4):
                proj_rope(I["wk"][j, g], aT[:, 8 + g, :], ("aT", 8 + g), 8 + g)
            if STAGE.get("attn_upto", 9) < 3:
                return
            wkv = wada[0]
            P.dma(wkv[:], I["wkv"][j], w=[("wada", 0)], q="pool")
            for tb in range(8):
                ps, pk = bank(tb % 4)
                for k in range(8):
                    P.op("pe", lambda e, ps=ps, k=k, tb=tb: e.matmul(ps, lhsT=hT[:, k, tb * 128:(tb + 1) * 128], rhs=wkv[:, k, :], start=(k == 0), stop=(k == 7)),
                         r=[("wada", 0), ("hT", k)], w=[pk])
                kvt = tmpA if tb % 2 == 0 else tmpB
                kvk = "tmpA" if tb % 2 == 0 else "tmpB"
                P.op("act", lambda e, ps=ps, kvt=kvt: e.copy(out=kvt[:, 0:512], in_=ps), r=[pk], w=[kvk])
                P.dma(O["nk"][j, tb * 128:(tb + 1) * 128, :], kvt[:, 0:256], r=[kvk], sem="kvout%d" % (tb % 2))
                P.dma(O["nv"][j, tb * 128:(tb + 1) * 128, :], kvt[:, 256:512], r=[kvk], sem="kvout%d" % (tb % 2))
                P.op("dve", lambda e, ps=ps, tb=tb: e.tensor_copy(out=vlat[:, tb, :, 0, 0:64], in_=ps[:, 256:512].rearrange("p (g d) -> p g d", g=4)), r=[pk], w=["vlat"])
                P.op("dve", lambda e, ps=ps, tb=tb: e.tensor_copy(out=vlat[:, tb, :, 1, 64:128], in_=ps[:, 256:512].rearrange("p (g d) -> p g d", g=4)), r=[pk], w=["vlat"])
            if STAGE.get("attn_upto", 9) < 4:
                return
            dsb = tmpA[:].rearrange("p (a n) -> p a n", a=2)
            osb = [cv[0][:, 0:512], cv[1][:, 0:512]]
            tb_bf = tmpB[:].bitcast(BF16)
            ebuf = [tb_bf[:, i * 512:(i + 1) * 512] for i in range(3)]
            P.alias(["dsb"], ["tmpA"])
            P.alias([("osb", 0)], [("cv", 0)])
            P.alias([("osb", 1)], [("cv", 1)])
            P.alias([("ebuf", i) for i in range(3)], ["tmpB"])
            sc = 0
            for h in range(STAGE.get("nheads", 16)):
                g, qc, pb, var = h // 4, h // 2, (h % 2) * 64, h % 2
                dr = 64 if var == 0 else 0
                qh = aT[pb:pb + 64, qc, :]
                kh = aT[pb:pb + 64, 8 + g, :]
                kch = kctxT[pb:pb + 64, g, :]
                for th in range(2):
                    it = h * 2 + th
                    OP, opk = bank(4 + it % 2)
                    first = True
                    for kb in range(4):
                        ps, pk = bank(sc % 4); eb = ebuf[sc % 3]; ek = ("ebuf", sc % 3); sc += 1
                        P.op("pe", lambda e, ps=ps, kb=kb, th=th, kch=kch, qh=qh: e.matmul(ps, lhsT=kch[:, kb * 128:(kb + 1) * 128], rhs=qh[:, th * 512:(th + 1) * 512], start=True, stop=True),
                             r=["kctxT", ("aT", qc)], w=[pk])
                        P.op("act", lambda e, ps=ps, eb=eb: e.activation(out=eb[:], in_=ps, func=AF.Exp, scale=0.125), r=[pk], w=[ek])
                        P.op("pe", lambda e, OP=OP, eb=eb, kb=kb, g=g, var=var, first=first: e.matmul(OP, lhsT=vctx[:, kb, g, var, :], rhs=eb[:], start=first, stop=False),
                             r=["vctx", ek], w=[opk])
                        first = False
                    jbs = [jb for jb in range(8) if max(jb - 1, 4 * th) <= min(jb + 1, 4 * th + 3)]
                    for jb in jbs:
                        i0 = max(jb - 1, 4 * th); i1 = min(jb + 1, 4 * th + 3)
                        n = (i1 - i0 + 1) * 128
                        ps, pk = bank(sc % 4); eb = ebuf[sc % 3]; ek = ("ebuf", sc % 3); sc += 1
                        P.op("pe", lambda e, ps=ps, jb=jb, i0=i0, n=n, kh=kh, qh=qh: e.matmul(ps[:, 0:n], lhsT=kh[:, jb * 128:(jb + 1) * 128], rhs=qh[:, i0 * 128:i0 * 128 + n], start=True, stop=True),
                             r=[("aT", 8 + g), ("aT", qc)], w=[pk])
                        P.op("act", lambda e, ps=ps, eb=eb, n=n: e.activation(out=eb[:, 0:n], in_=ps[:, 0:n], func=AF.Exp, scale=0.125), r=[pk], w=[ek])
                        for i in range(i0, i1 + 1):
                            if i == jb:
                                continue
                            off = 0 if i == jb + 1 else 1
                            c0 = (i - i0) * 128
                            P.op("dve", lambda e, eb=eb, c0=c0, i=i, off=off: e.tensor_tensor(out=eb[:, c0:c0 + 128], in0=eb[:, c0:c0 + 128], in1=amask[:, i, off, :], op=ALU.mult),
                                 r=[ek, "amask"], w=[ek])
                        o0 = (i0 - 4 * th) * 128
                        P.op("pe", lambda e, OP=OP, eb=eb, jb=jb, g=g, var=var, o0=o0, n=n, jbs=jbs: e.matmul(OP[:, o0:o0 + n], lhsT=vlat[:, jb, g, var, :], rhs=eb[:, 0:n], start=False, stop=(jb == jbs[-1])),
                             r=["vlat", ek], w=[opk])
                    P.op("dve", lambda e, OP=OP, dr=dr, h=h: e.tensor_scalar(out=dsb[dr:dr + 1, 0, :], in0=OP[dr:dr + 1, :], scalar1=esink[dr:dr + 1, h:h + 1], scalar2=None, op0=ALU.add),
                         r=[opk, "esink"], w=["dsb"])
                    P.op("dve", lambda e, dr=dr: e.reciprocal(out=dsb[dr:dr + 1, 1, :], in_=dsb[dr:dr + 1, 0, :]), r=["dsb"], w=["dsb"])
                    BC, bck = bank(6 + it % 2)
                    P.op("pe", lambda e, BC=BC, dr=dr: e.matmul(BC, lhsT=ones_f[dr:dr + 1, :], rhs=dsb[dr:dr + 1, 1, :], start=True, stop=True), r=["dsb", "ones_f"], w=[bck])
                    ob = osb[it % 2]; obk = ("osb", it % 2)
                    P.op("act", lambda e, OP=OP, ob=ob, pb=pb: e.copy(out=ob[pb:pb + 64, :], in_=OP[pb:pb + 64, :]), r=[opk], w=[obk])
                    P.op("dve", lambda e, BC=BC, ob=ob, pb=pb, qc=qc, th=th: e.tensor_tensor(out=aT[pb:pb + 64, qc, th * 512:(th + 1) * 512], in0=ob[pb:pb + 64, :], in1=BC[pb:pb + 64, :], op=ALU.mult),
                         r=[obk, bck], w=[("aT", qc)])
            P.alias(["tmpA"], ["dsb"])
            P.alias([("cv", 0)], [("osb", 0)])
            P.alias([("cv", 1)], [("osb", 1)])
            P.alias(["tmpB"], [("ebuf", i) for i in range(3)])
            for dc in range(8):
                wb = wup[dc % 2]
                P.dma(wb[:, 0], I["wo"][j, dc], w=[("wup", dc % 2, 0)], q="pool")
                for th in range(2):
                    ps, pk = bank((dc * 2 + th) % 4)
                    for k in range(8):
                        P.op("pe", lambda e, ps=ps, k=k, th=th, wb=wb: e.matmul(ps, lhsT=wb[:, 0, k, :], rhs=aT[:, k, th * 512:(th + 1) * 512], start=(k == 0), stop=(k == 7)),
                             r=[("wup", dc % 2, 0), ("aT", k)], w=[pk])
                    P.op("dve", lambda e, ps=ps, dc=dc, th=th: e.scalar_tensor_tensor(out=xT[:, dc, th * 512:(th + 1) * 512], in0=ps, scalar=mod[:, 16 + dc:17 + dc],
                                                                                     in1=xT[:, dc, th * 512:(th + 1) * 512], op0=ALU.mult, op1=ALU.add),
                         r=[pk, "mod", ("xT", dc)], w=[("xT", dc)])

        def hgrn(l):
            CH = 32
            NCH = NT // CH
            P.op("act", lambda e: e.activation(out=hlb[:], in_=hlb[:], func=AF.Exp), r=["hlb"], w=["hlb"])
            P.op("dve", lambda e: e.tensor_tensor(out=lbp[:, 1], in0=hlb[:, 0], in1=hlb[:, 1], op=ALU.add), r=["hlb"], w=["lbp"])
            P.op("dve", lambda e: e.tensor_tensor(out=lbp[:, 1], in0=lbp[:, 1], in1=hlb[:, 2], op=ALU.add), r=["hlb", "lbp"], w=["lbp"])
            P.op("dve", lambda e: e.tensor_tensor(out=lbp[:, 1], in0=lbp[:, 1], in1=hlb[:, 3], op=ALU.add), r=["hlb", "lbp"], w=["lbp"])
            P.op("dve", lambda e: e.reciprocal(out=lbp[:, 1], in_=lbp[:, 1]), r=["lbp"], w=["lbp"])
            P.op("dve", lambda e: e.tensor_tensor(out=lbp[:, 0], in0=lbp[:, 1], in1=hlb[:, 1], op=ALU.mult), r=["hlb", "lbp"], w=["lbp"])
            P.op("dve", lambda e: e.tensor_scalar(out=lbp[:, 1], in0=lbp[:, 0], scalar1=-1.0, scalar2=1.0, op0=ALU.mult, op1=ALU.add), r=["lbp"], w=["lbp"])
            bufQ, bufF, bufK, bufC = cg[0], cg[1], cv[0], cv[1]
            kQ, kF, kK, kC = ("cg", 0), ("cg", 1), ("cv", 0), ("cv", 1)
            oacc = rstd
            kdtok = [vlat[:].rearrange("p a g v n -> p (a g v n)")[:, d * 4096:(d + 1) * 4096].rearrange("p (b c n) -> p b c n", b=8, c=4) for d in range(2)]
            P.alias([("kdtok", 0), ("kdtok", 1)], ["vlat"])
            for h in range(8):
                P.dma(wup[0][:], I["hw_in"][h, 0:2].rearrange("a p k n -> p a k n"), w=[("wup", 0, 0), ("wup", 0, 1)], q="pool")
                P.dma(wup[1][:], I["hw_in"][h, 2:4].rearrange("a p k n -> p a k n"), w=[("wup", 1, 0), ("wup", 1, 1)], q="pool")
                P.dma(wdn[0][:, 0:8, :], I["hw_in"][h, 4], w=[("wdn", 0)], q="pool")

                def proj(wap, wkeys, bi):
                    outs = []
                    for th in range(2):
                        ps, pk = bank(bi * 2 + th)
                        for kk in range(8):
                            P.op("pe", lambda e, ps=ps, kk=kk, th=th, wap=wap: e.matmul(ps, lhsT=wap[:, kk, :], rhs=hT[:, kk, th * 512:(th + 1) * 512], start=(kk == 0), stop=(kk == 7)),
                                 r=list(wkeys) + [("hT", kk)], w=[pk])
                        outs.append((ps, pk))
                    return outs
                for th, (ps, pk) in enumerate(proj(wup[0][:, 0], [("wup", 0, 0)], 0)):
                    P.op("act", lambda e, ps=ps, th=th: e.copy(out=bufQ[:, th * 512:(th + 1) * 512], in_=ps), r=[pk], w=[kQ])
                for blk in range(8):
                    ps, pk = bank(2 + blk % 2)
                    for kk in range(8):
                        P.op("pe", lambda e, ps=ps, kk=kk, blk=blk: e.matmul(ps[:, 0:128], lhsT=hT[:, kk, blk * 128:(blk + 1) * 128], rhs=wup[0][:, 1, kk, :], start=(kk == 0), stop=(kk == 7)),
                             r=[("wup", 0, 1), ("hT", kk)], w=[pk])
                    P.op("act", lambda e, ps=ps, blk=blk: e.copy(out=vtok[:, blk, :], in_=ps[:, 0:128]), r=[pk], w=["vtok"])
                for d in range(2):
                    for th, (ps, pk) in enumerate(proj(wup[1][:, d], [("wup", 1, d)], 2 + d)):
                        P.op("act", lambda e, ps=ps, th=th: e.activation(out=bufF[:, th * 512:(th + 1) * 512], in_=ps, func=AF.Sigmoid), r=[pk], w=[kF])
                    P.op("dve", lambda e, d=d, h=h: e.tensor_scalar(out=bufF[:], in0=bufF[:], scalar1=lbp[:, 1, d, h:h + 1], scalar2=lbp[:, 0, d, h:h + 1], op0=ALU.mult, op1=ALU.add),
                         r=[kF, "lbp"], w=[kF])
                    P.op("dve", lambda e: e.tensor_scalar(out=bufK[:], in0=bufF[:], scalar1=-1.0, scalar2=1.0, op0=ALU.mult, op1=ALU.add), r=[kF], w=[kK])
                    P.op("act", lambda e: e.activation(out=bufF[:], in_=bufF[:], func=AF.Ln), r=[kF], w=[kF])
                    P.op("dve", lambda e: e.tensor_tensor_scan(out=bufC[:], data0=m32[:], data1=bufF[:], initial=0.0, op0=ALU.mult, op1=ALU.add), r=["m32", kF], w=[kC])
                    if d == 1:
                        P.op("dve", lambda e: e.scalar_tensor_tensor(out=tmpA[:], in0=bufC[:], scalar=-1.0, in1=bufF[:], op0=ALU.mult, op1=ALU.add), r=[kC, kF], w=["tmpA"])
                        P.op("act", lambda e: e.copy(out=ctmp[:], in_=bufC[:, CH - 1:NT:CH]), r=[kC], w=["ctmp"])
                        P.op("dve", lambda e: e.tensor_tensor(out=bufC[:].rearrange("p (c t) -> p c t", t=CH), in0=tmpA[:].rearrange("p (c t) -> p c t", t=CH),
                                                              in1=ctmp[:].unsqueeze(2).to_broadcast([128, NCH, CH]), op=ALU.add),
                             r=["tmpA", "ctmp"], w=[kC])
                        ctot = bufC[:, 0:NT:CH]
                    else:
                        ctot = bufC[:, CH - 1:NT:CH]
                    P.op("act", lambda e, d=d, ctot=ctot: e.activation(out=adec[d][:], in_=ctot, func=AF.Exp), r=[kC], w=[("adec", d)])
                    P.op("act", lambda e: e.activation(out=tmpA[:], in_=bufC[:], func=AF.Exp), r=[kC], w=["tmpA"])
                    P.op("dve", lambda e, d=d: e.tensor_tensor(out=qhat[d][:], in0=bufQ[:], in1=tmpA[:], op=ALU.mult), r=[kQ, "tmpA"], w=[("qhat", d)])
                    P.op("dve", lambda e: e.tensor_scalar(out=tmpB[:], in0=bufC[:], scalar1=-1.0, scalar2=85.0, op0=ALU.mult, op1=ALU.min), r=[kC], w=["tmpB"])
                    P.op("act", lambda e: e.activation(out=tmpB[:], in_=tmpB[:], func=AF.Exp), r=["tmpB"], w=["tmpB"])
                    P.op("dve", lambda e: e.tensor_tensor(out=tmpB[:], in0=tmpB[:], in1=bufK[:], op=ALU.mult), r=["tmpB", kK], w=["tmpB"])
                    P.op("act", lambda e, d=d: e.copy(out=ktil[d][:], in_=tmpB[:]), r=["tmpB"], w=[("ktil", d)])
                    P.op("dve", lambda e, d=d: e.tensor_tensor(out=kdT[d][:].rearrange("p (c t) -> p c t", t=CH), in0=tmpB[:].rearrange("p (c t) -> p c t", t=CH),
                                                                in1=adec[d][:].unsqueeze(2).to_broadcast([128, NCH, CH]), op=ALU.mult),
                         r=["tmpB", ("adec", d)], w=[("kdT", d)])
                    for blk in range(8):
                        ps, pk = bank(6 + blk % 2)
                        pst = ps.bitcast(BF16)
                        P.op("pe", lambda e, pst=pst, blk=blk, d=d: e.transpose(pst[:, 0:128], kdT[d][:, blk * 128:(blk + 1) * 128], ident_bf[:]), r=[("kdT", d), "ident_bf"], w=[pk])
                        for c4 in range(4):
                            P.op("act", lambda e, pst=pst, blk=blk, c4=c4, d=d: e.activation(out=kdtok[d][:, blk, c4, :], in_=pst[:, 0:128], func=AF.Identity, scale=rowm[:, c4:c4 + 1]),
                                 r=[pk, "rowm"], w=[("kdtok", d)])
                    P.dma(Sf[d][:], I["hs0"][d, h], w=[("Sf", d)])
                    P.op("act", lambda e, d=d: e.copy(out=Sb[d][:], in_=Sf[d][:]), r=[("Sf", d)], w=[("Sb", d)])
                P.op("pool", lambda e: e.memset(oacc[:], 0.0), r=["rstd"], w=["rstd"])
                for bi_ in range(8):
                    for d in range(2):
                        blk = bi_ if d == 0 else 7 - bi_
                        bsl = slice(blk * 128, (blk + 1) * 128)
                        aps, apk = bank(0 + d * 2)
                        ops_, opk = bank(1 + d * 2)
                        dps, dpk = bank(4 + d)
                        P.op("pe", lambda e, aps=aps, d=d, bsl=bsl: e.matmul(aps[:, 0:128], lhsT=ktil[d][:, bsl], rhs=qhat[d][:, bsl], start=True, stop=True),
                             r=[("ktil", d), ("qhat", d)], w=[apk])
                        am = attm[d]
                        P.op("dve", lambda e, aps=aps, am=am, d=d: e.tensor_tensor(out=am[:], in0=aps[:, 0:128], in1=hmask[:, d, :], op=ALU.mult), r=[apk, "hmask"], w=[("attm", d)])
                        P.op("pe", lambda e, ops_=ops_, am=am, blk=blk: e.matmul(ops_[:, 0:128], lhsT=vtok[:, blk, :], rhs=am[:], start=True, stop=False),
                             r=["vtok", ("attm", d)], w=[opk])
                        for c4 in range(4):
                            P.op("pe", lambda e, dps=dps, c4=c4, blk=blk, d=d: e.matmul(dps[:, c4 * 128:(c4 + 1) * 128], lhsT=kdtok[d][:, blk, c4, :], rhs=vtok[:, blk, :], start=True, stop=True),
                                 r=[("kdtok", d), "vtok"], w=[dpk])
                        cs = range(4) if d == 0 else range(3, -1, -1)
                        for c4 in cs:
                            ch = blk * 4 + c4
                            csl = slice(ch * CH, (ch + 1) * CH)
                            P.op("pe", lambda e, ops_=ops_, c4=c4, csl=csl, d=d, cs=cs: e.matmul(ops_[:, c4 * CH:(c4 + 1) * CH], lhsT=Sb[d][:], rhs=qhat[d][:, csl], start=False, stop=(c4 == list(cs)[-1])),
                                 r=[("Sb", d), ("qhat", d)], w=[opk])
                            P.op("dve", lambda e, dps=dps, c4=c4, ch=ch, d=d: e.scalar_tensor_tensor(out=Sf[d][:], in0=Sf[d][:], scalar=adec[d][:, ch:ch + 1], in1=dps[:, c4 * 128:(c4 + 1) * 128], op0=ALU.mult, op1=ALU.add),
                                 r=[("Sf", d), ("adec", d), dpk], w=[("Sf", d)])
                            seg_end = (ch % 8 == 7) if d == 0 else (ch % 8 == 0)
                            if seg_end:
                                seg = ch // 8
                                P.dma(O["hg"][(seg * 2 + d) * 8 + h], Sf[d][:], r=[("Sf", d)], sem="hgout%d" % d)
                                P.op("dve", lambda e, d=d: e.tensor_scalar(out=Sf[d][:], in0=Sf[d][:], scalar1=keep[:, 0:1], scalar2=None, op0=ALU.mult), r=[("Sf", d), "keep"], w=[("Sf", d)])
                            P.op("act", lambda e, d=d: e.copy(out=Sb[d][:], in_=Sf[d][:]), r=[("Sf", d)], w=[("Sb", d)])
                        P.op("dve", lambda e, ops_=ops_, bsl=bsl: e.tensor_tensor(out=oacc[:, bsl], in0=ops_[:, 0:128], in1=oacc[:, bsl], op=ALU.add), r=[opk, "rstd"], w=["rstd"])
                P.op("act", lambda e: e.activation(out=sq[0][:], in_=oacc[:], func=AF.Square), r=["rstd"], w=[("sq", 0)])
                for th in range(2):
                    ps, pk = bank(6 + th)
                    P.op("pe", lambda e, ps=ps, th=th: e.matmul(ps, lhsT=ones_bf[:], rhs=sq[0][:, th * 512:(th + 1) * 512], start=True, stop=True), r=[("sq", 0), "ones_bf"], w=[pk])
                    P.op("act", lambda e, ps=ps, th=th: e.activation(out=tmpA[:, th * 512:(th + 1) * 512], in_=ps, func=AF.Sqrt, scale=1.0 / 128, bias=eps_t[:, 0:1]), r=[pk, "eps"], w=["tmpA"])
                P.op("dve", lambda e: e.reciprocal(out=tmpA[:], in_=tmpA[:]), r=["tmpA"], w=["tmpA"])
                P.op("dve", lambda e: e.scalar_tensor_tensor(out=tmpA[:], in0=oacc[:], scalar=hgn[:, 0:1], in1=tmpA[:], op0=ALU.mult, op1=ALU.mult), r=["rstd", "hgn", "tmpA"], w=["tmpA"])
                for th, (ps, pk) in enumerate(proj(wdn[0][:, 0:8, :], [("wdn", 0)], 2)):
                    P.op("act", lambda e, ps=ps, th=th: e.activation(out=tmpB[:, th * 512:(th + 1) * 512], in_=ps, func=AF.Silu), r=[pk], w=["tmpB"])
                P.op("dve", lambda e, h=h: e.tensor_tensor(out=aT[:, h, :], in0=tmpA[:], in1=tmpB[:], op=ALU.mult), r=["tmpA", "tmpB"], w=[("aT", h)])
            P.alias(["vlat"], [("kdtok", 0), ("kdtok", 1)])
            P.op("dve", lambda e: e.memset(vlat[:], 0.0), w=["vlat"])
            P.op("dve", lambda e: e.memset(vlat[:, :, :, 0, 64:65], 1.0), w=["vlat"])
            P.op("dve", lambda e: e.memset(vlat[:, :, :, 1, 0:1], 1.0), w=["vlat"])
            for dc in range(8):
                wb = wup[dc % 2]
                P.dma(wb[:, 0], I["hwo"][dc], w=[("wup", dc % 2, 0)], q="pool")
                for th in range(2):
                    ps, pk = bank((dc * 2 + th) % 4)
                    for kk in range(8):
                        P.op("pe", lambda e, ps=ps, kk=kk, th=th, wb=wb: e.matmul(ps, lhsT=wb[:, 0, kk, :], rhs=aT[:, kk, th * 512:(th + 1) * 512], start=(kk == 0), stop=(kk == 7)),
                             r=[("wup", dc % 2, 0), ("aT", kk)], w=[pk])
                    P.op("dve", lambda e, ps=ps, dc=dc, th=th: e.scalar_tensor_tensor(out=xT[:, dc, th * 512:(th + 1) * 512], in0=ps, scalar=mod[:, 16 + dc:17 + dc],
                                                                                     in1=xT[:, dc, th * 512:(th + 1) * 512], op0=ALU.mult, op1=ALU.add),
                         r=[pk, "mod", ("xT", dc)], w=[("xT", dc)])

        def ssm(l):
            SEG = 256
            rs32 = rstd[:]
            sr_ = [rs32[:, i * 64:(i + 1) * 64] for i in range(16)]
            sq32 = sq[0][:].bitcast(F32)
            sq_ = [sq32[:, i * 32:(i + 1) * 32] for i in range(14)]
            sCq = [sq[1][:, i * 512:(i + 1) * 512].rearrange("p (g i) -> p g i", g=32) for i in range(2)]
            P.alias(["srr"], ["rstd"])
            P.alias(["sqq"], [("sq", 0)])
            P.alias(["sCq"], [("sq", 1)])
            m256 = m32
            P.op("pool", lambda e: e.memset(m256[:], 1.0), r=["m32"], w=["m32"])
            P.op("pool", lambda e: e.memset(m256[:, 0:NT:256], 0.0), r=["m32"], w=["m32"])
            P.dma(smask[:], I["smask"], w=["smask"])
            P.dma(sD[:], I["sD"], w=["sD"])
            TT = lambda e, o, a, b, op: e.tensor_tensor(out=o, in0=a, in1=b, op=op)

            def vop(eng, o, a, b, op, r, w):
                P.op(eng, lambda e, o=o, a=a, b=b, op=op: e.tensor_tensor(out=o, in0=a, in1=b, op=op), r=r, w=w)

            def cmul(eng, ore, oim, are_, aim_, bre, bim, t1, t2, r, w, tk):
                vop(eng, t1, are_, bre, ALU.mult, r, [tk[0]])
                vop(eng, t2, aim_, bim, ALU.mult, r, [tk[1]])
                vop(eng, ore, t1, t2, ALU.subtract, [tk[0], tk[1]], w)
                vop(eng, t1, are_, bim, ALU.mult, r, [tk[0]])
                vop(eng, t2, aim_, bre, ALU.mult, r, [tk[1]])
                vop(eng, oim, t1, t2, ALU.add, [tk[0], tk[1]], w)

            def lam_params(are_ap, aim_ap, dt_scalar_or_ap, S, key, n, per_part_dt, need_inv=True):
                arec, th, mag, c, s_, t1, t2, imag = S[0], S[1], S[2], S[3], S[4], S[5], S[6], S[7]
                K = [key]
                P.op("dve", lambda e: e.tensor_scalar(out=arec[:], in0=are_ap, scalar1=-1e-4, scalar2=None, op0=ALU.min), r=K, w=K)
                if per_part_dt:
                    dtx = S[8]
                    P.op("act", lambda e: e.activation(out=dtx[:, 0:1], in_=dt_scalar_or_ap, func=AF.Exp), r=K, w=K)
                    P.op("dve", lambda e: e.tensor_scalar(out=th[:], in0=aim_ap, scalar1=dtx[:, 0:1], scalar2=None, op0=ALU.mult), r=K, w=K)
                    P.op("dve", lambda e: e.tensor_scalar(out=mag[:], in0=arec[:], scalar1=dtx[:, 0:1], scalar2=None, op0=ALU.mult), r=K, w=K)
                else:
                    dtx = S[8]
                    P.op("act", lambda e: e.activation(out=dtx[:], in_=dt_scalar_or_ap, func=AF.Exp), r=K, w=K)
                    vop("dve", th[:], aim_ap, dtx[:], ALU.mult, K, K)
                    vop("dve", mag[:], arec[:], dtx[:], ALU.mult, K, K)
                P.op("act", lambda e: e.activation(out=imag[:], in_=mag[:], func=AF.Exp, scale=-1.0), r=K, w=K)
                P.op("act", lambda e: e.activation(out=mag[:], in_=mag[:], func=AF.Exp), r=K, w=K)
                P.op("act", lambda e: e.activation(out=s_[:], in_=th[:], func=AF.Sin, scale=1.0 / 64), r=K, w=K)
                P.op("act", lambda e: e.activation(out=c[:], in_=th[:], func=AF.Sin, scale=1.0 / 64, bias=halfpi[:, 0:1]), r=K + ["halfpi"], w=K)
                for _ in range(6):
                    vop("dve", t1[:], c[:], s_[:], ALU.mult, K, K)
                    vop("dve", c[:], c[:], c[:], ALU.mult, K, K)
                    vop("dve", s_[:], s_[:], s_[:], ALU.mult, K, K)
                    vop("dve", c[:], c[:], s_[:], ALU.subtract, K, K)
                    P.op("dve", lambda e: e.tensor_scalar(out=s_[:], in0=t1[:], scalar1=2.0, scalar2=None, op0=ALU.mult), r=K, w=K)
                L1re, L1im, Lm1re, Lm1im = S[9], S[10], S[11], S[12]
                vop("dve", L1re[:], mag[:], c[:], ALU.mult, K, K)
                vop("dve", L1im[:], mag[:], s_[:], ALU.mult, K, K)
                if need_inv:
                    vop("dve", Lm1re[:], imag[:], c[:], ALU.mult, K, K)
                    vop("dve", Lm1im[:], imag[:], s_[:], ALU.mult, K, K)
                    P.op("dve", lambda e: e.tensor_scalar(out=Lm1im[:], in0=Lm1im[:], scalar1=-1.0, scalar2=None, op0=ALU.mult), r=K, w=K)
                return dict(L1re=L1re, L1im=L1im, Lm1re=Lm1re, Lm1im=Lm1im, are=arec)

            A_, B_, C_, D_, Gr, Gi = cg[0], cg[1], cv[0], cv[1], tmpA, tmpB
            kA, kB, kC2, kD, kGr, kGi = ("cg", 0), ("cg", 1), ("cv", 0), ("cv", 1), "tmpA", "tmpB"
            vl = vlat[:].rearrange("p a g v n -> p (a g v n)").bitcast(F32)
            Tre_all = vl[:, 0:2048].rearrange("p (g t) -> p g t", g=8)
            Tim_all = vl[:, 2048:4096].rearrange("p (g t) -> p g t", g=8)
            Tp_re, Tm_re, Tp_im, Tm_im = Tre_all[:, 0:4], Tre_all[:, 4:8], Tim_all[:, 0:4], Tim_all[:, 4:8]
            P.alias(["stab"], ["vlat"])
            vc = vctx[:].rearrange("p a g v n -> p (a g v n)")
            W1pad = vc[:, 0:2048].rearrange("p (q r n) -> p q r n", q=8, r=2)
            Ewpad = vc[:, 2048:4096].rearrange("p (q r n) -> p q r n", q=8, r=2)
            P.alias(["W1pad", "Ewpad"], ["vctx"])
            Hre_b = qhat[0][:].rearrange("p (g t) -> p g t", g=4)
            Him_b = qhat[1][:].rearrange("p (g t) -> p g t", g=4)
            ysb = aT

            def _body():
              for d in range(2):
                  P.dma(sq_[0], I["sAreQ"][d], w=["sqq"])
                  P.dma(sq_[1], I["sAimQ"][d], w=["sqq"])
                  P.dma(sq_[13], I["sDtQ"][d], w=["sqq"])
                  P.dma(sCq[0], I["sCreQ"][d], w=["sCq"], q="pool")
                  P.dma(sCq[1], I["sCimQ"][d], w=["sCq"], q="pool")
                  P.dma(sH[:], I["sH0"][d], w=["sH"])
                  Q = lam_params(sq_[0], sq_[1], sq_[13], sq_[2:13] + [sq_[0], sq_[1]], "sqq", 32, False)
                  for s in range(8):
                      k0, k1 = s // 2, 4 + s // 2
                      g8b = (4 * s) % 8
                      P.op("pool", lambda e: e.memset(vc[:], 0.0), r=["W1pad", "Ewpad"], w=["W1pad", "Ewpad"])
                      for half, kk_ in enumerate((k0, k1)):
                          Rk = ["srr"]
                          P.dma(sr_[0], I["sAreR"][d, kk_], w=Rk)
                          P.dma(sr_[1], I["sAimR"][d, kk_], w=Rk)
                          P.dma(sr_[13][:, 0:1], I["sDtR"][d, kk_], w=Rk)
                          P.dma(sr_[14], I["sBreR"][d, kk_], w=Rk)
                          P.dma(sr_[15], I["sBimR"][d, kk_], w=Rk)
                          R_ = lam_params(sr_[0], sr_[1], sr_[13][:, 0:1], sr_[2:13] + [sr_[0], sr_[0]], "srr", 64, True, need_inv=False)
                          nre, den, cre, cim, t1, t2 = sr_[3], sr_[4], sr_[5], sr_[6], sr_[7], sr_[8]
                          aim_ = sr_[1]
                          P.op("dve", lambda e, R_=R_: e.tensor_scalar(out=nre[:], in0=R_["L1re"][:], scalar1=-1.0, scalar2=None, op0=ALU.add), r=Rk, w=Rk)
                          vop("dve", den[:], R_["are"][:], R_["are"][:], ALU.mult, Rk, Rk)
                          vop("dve", t1[:], aim_[:], aim_[:], ALU.mult, Rk, Rk)
                          vop("dve", den[:], den[:], t1[:], ALU.add, Rk, Rk)
                          P.op("dve", lambda e: e.reciprocal(out=den[:], in_=den[:]), r=Rk, w=Rk)
                          vop("dve", t1[:], nre[:], R_["are"][:], ALU.mult, Rk, Rk)
                          vop("dve", t2[:], R_["L1im"][:], aim_[:], ALU.mult, Rk, Rk)
                          vop("dve", cre[:], t1[:], t2[:], ALU.add, Rk, Rk)
                          vop("dve", cre[:], cre[:], den[:], ALU.mult, Rk, Rk)
                          vop("dve", t1[:], R_["L1im"][:], R_["are"][:], ALU.mult, Rk, Rk)
                          vop("dve", t2[:], nre[:], aim_[:], ALU.mult, Rk, Rk)
                          vop("dve", cim[:], t1[:], t2[:], ALU.subtract, Rk, Rk)
                          vop("dve", cim[:], cim[:], den[:], ALU.mult, Rk, Rk)
                          wre, wim = sr_[9], sr_[10]
                          cmul("dve", wre[:], wim[:], cre[:], cim[:], sr_[14][:], sr_[15][:], t1[:], t2[:], Rk, Rk, ["srr", "srr"])
                          for q in range(4):
                              g8 = g8b + q
                              for ri, wsrc in enumerate((wre, wim)):
                                  P.op("act", lambda e, q=q, half=half, ri=ri, wsrc=wsrc, g8=g8: e.activation(out=W1pad[:, half * 4 + q, ri, half * 64:(half + 1) * 64], in_=wsrc[:], func=AF.Identity, scale=smask[:, g8:g8 + 1]),
                                       r=Rk + ["smask"], w=["W1pad"])
                              gq = 4 * s + q
                              P.op("act", lambda e, q=q, half=half, g8=g8, gq=gq: e.activation(out=Ewpad[:, half * 4 + q, 0, g8 * 16:(g8 + 1) * 16], in_=sCq[0][:, gq, :], func=AF.Identity, scale=smask[:, 8 + half:9 + half]),
                                   r=["sCq", "smask"], w=["Ewpad"])
                              P.op("act", lambda e, q=q, half=half, g8=g8, gq=gq: e.activation(out=Ewpad[:, half * 4 + q, 1, g8 * 16:(g8 + 1) * 16], in_=sCq[1][:, gq, :], func=AF.Identity, scale=smask[:, 10 + half:11 + half]),
                                   r=["sCq", "smask"], w=["Ewpad"])
                      gsl = slice(4 * s, 4 * s + 4)
                      i0 = 0 if d == 0 else SEG - 1
                      for (Tre, Tim, bre_, bim_) in ((Tp_re, Tp_im, Q["L1re"], Q["L1im"]), (Tm_re, Tm_im, Q["Lm1re"], Q["Lm1im"])):
                          P.op("dve", lambda e, Tre=Tre, bre_=bre_, i0=i0, gsl=gsl: e.tensor_copy(out=Tre[:, :, i0:i0 + 1], in_=bre_[:, gsl].unsqueeze(2)), r=["sqq"], w=["stab"])
                          P.op("dve", lambda e, Tim=Tim, bim_=bim_, i0=i0, gsl=gsl: e.tensor_copy(out=Tim[:, :, i0:i0 + 1], in_=bim_[:, gsl].unsqueeze(2)), r=["sqq"], w=["stab"])
                      L = 1
                      while L < SEG:
                          if d == 0:
                              src = slice(0, L); dst = slice(L, 2 * L); piv = L - 1
                          else:
                              src = slice(SEG - L, SEG); dst = slice(SEG - 2 * L, SEG - L); piv = SEG - L
                          zr = Tre_all[:, :, piv:piv + 1].to_broadcast([128, 8, L])
                          zi = Tim_all[:, :, piv:piv + 1].to_broadcast([128, 8, L])
                          cmul("dve", Tre_all[:, :, dst], Tim_all[:, :, dst], Tre_all[:, :, src], Tim_all[:, :, src], zr, zi,
                               A_[:, 0:8 * L].rearrange("p (g t) -> p g t", g=8), B_[:, 0:8 * L].rearrange("p (g t) -> p g t", g=8), ["stab"], ["stab"], [kA, kB])
                          L *= 2
                      P.op("dve", lambda e, gsl=gsl: e.tensor_copy(out=sHent[:], in_=sH[:, gsl, :]), r=["sH"], w=["sHent"])
                      segs = range(4) if d == 0 else range(3, -1, -1)
                      for seg in segs:
                          tsl = slice(seg * SEG, (seg + 1) * SEG)
                          Sre, Sim = pq[0], pq[1]
                          skr = [("pq", 0), ("pq", 1)]; ski = [("pq", 2), ("pq", 3)]
                          for ri, (St, sk) in enumerate(((Sre, skr), (Sim, ski))):
                              for q in range(4):
                                  for half, kk_ in enumerate((k0, k1)):
                                      P.op("pe", lambda e, St=St, q=q, half=half, kk_=kk_, ri=ri, tsl=tsl: e.matmul(St[:, q * SEG:(q + 1) * SEG], lhsT=W1pad[:, half * 4 + q, ri, :], rhs=hT[:, kk_, tsl], start=(half == 0), stop=(half == 1)),
                                           r=["W1pad", ("hT", kk_)], w=[sk[q // 2]])
                          S3r = Sre[:].rearrange("p (g t) -> p g t", g=4); S3i = Sim[:].rearrange("p (g t) -> p g t", g=4)
                          A3, B3, C3, D3 = [x[:].rearrange("p (g t) -> p g t", g=4) for x in (A_, B_, C_, D_)]
                          G3r, G3i = Gr[:].rearrange("p (g t) -> p g t", g=4), Gi[:].rearrange("p (g t) -> p g t", g=4)
                          vop("dve", A3, S3r, Tm_re, ALU.mult, skr + ["stab"], [kA])
                          vop("dve", B3, S3i, Tm_im, ALU.mult, ski + ["stab"], [kB])
                          vop("dve", A3, A3, B3, ALU.subtract, [kA, kB], [kA])
                          vop("dve", C3, S3i, Tm_re, ALU.mult, ski + ["stab"], [kC2])
                          vop("dve", D3, S3r, Tm_im, ALU.mult, skr + ["stab"], [kD])
                          vop("dve", C3, C3, D3, ALU.add, [kC2, kD], [kC2])
                          for (src_, dstt, ks, kd_) in ((A_, Gr, kA, kGr), (C_, Gi, kC2, kGi)):
                              P.op("dve", lambda e, src_=src_, dstt=dstt: e.tensor_tensor_scan(out=dstt[:], data0=m256[:], data1=src_[:], initial=0.0, op0=ALU.mult, op1=ALU.add), r=["m32", ks], w=[kd_])
                              if d == 1:
                                  s3 = src_[:].rearrange("p (g t) -> p g t", g=4); d3 = dstt[:].rearrange("p (g t) -> p g t", g=4)
                                  P.op("act", lambda e, dstt=dstt: e.copy(out=ctmp[:, 0:4], in_=dstt[:, SEG - 1:NT:SEG]), r=[kd_], w=["ctmp"])
                                  P.op("dve", lambda e, s3=s3, d3=d3: e.tensor_tensor(out=d3, in0=s3, in1=d3, op=ALU.subtract), r=[ks, kd_], w=[kd_])
                                  P.op("dve", lambda e, d3=d3: e.tensor_tensor(out=d3, in0=d3, in1=ctmp[:, 0:4].unsqueeze(2).to_broadcast([128, 4, SEG]), op=ALU.add), r=[kd_, "ctmp"], w=[kd_])
                          vop("dve", G3r, G3r, sHent[:, :, 0:1].to_broadcast([128, 4, SEG]), ALU.add, [kGr, "sHent"], [kGr])
                          vop("dve", G3i, G3i, sHent[:, :, 1:2].to_broadcast([128, 4, SEG]), ALU.add, [kGi, "sHent"], [kGi])
                          vop("dve", A3, G3r, Tp_re, ALU.mult, [kGr, "stab"], [kA])
                          vop("dve", B3, G3i, Tp_im, ALU.mult, [kGi, "stab"], [kB])
                          vop("dve", Hre_b, A3, B3, ALU.subtract, [kA, kB], [("qhat", 0)])
                          vop("dve", C3, G3r, Tp_im, ALU.mult, [kGr, "stab"], [kC2])
                          vop("dve", D3, G3i, Tp_re, ALU.mult, [kGi, "stab"], [kD])
                          vop("dve", Him_b, C3, D3, ALU.add, [kC2, kD], [("qhat", 1)])
                          xi = SEG - 1 if d == 0 else 0
                          vop("dve", sHx[:, :, 0:1], A3[:, :, xi:xi + 1], B3[:, :, xi:xi + 1], ALU.subtract, [kA, kB], ["sHx"])
                          vop("dve", sHx[:, :, 1:2], C3[:, :, xi:xi + 1], D3[:, :, xi:xi + 1], ALU.add, [kC2, kD], ["sHx"])
                          for half in range(2):
                              P.dma(O["ssm"][seg * 2 + d, half * 32 + 4 * s: half * 32 + 4 * s + 4].rearrange("g p r -> p g r"), sHx[half * 64:(half + 1) * 64, :, :], r=["sHx"], sem="ssmout")
                          P.op("dve", lambda e: e.tensor_scalar(out=sHent[:], in0=sHx[:], scalar1=keep[:, 0:1], scalar2=None, op0=ALU.mult), r=["sHx", "keep"], w=["sHent"])
                          if STAGE.get("ssm_dbg"):
                              P.dma(O["dbg2"][:, 0:1024], A_[:], r=[kA], sem="dbg")
                              P.dma(O["dbg2"][:, 1024:2048], B_[:], r=[kB], sem="dbg")
                              P.dma(O["dbg2"][:, 2048:3072], C_[:], r=[kC2], sem="dbg")
                              P.dma(O["dbg2"][:, 3072:4096], D_[:], r=[kD], sem="dbg")
                              P.dma(O["dbg2"][:, 4096:5120], Gr[:], r=[kGr], sem="dbg")
                              P.dma(O["dbg2"][:, 5120:6144], Gi[:], r=[kGi], sem="dbg")
                              P.dma(O["dbg3"], hT[:].rearrange("p k t -> p (k t)"), r=[("hT", kk) for kk in range(8)], sem="dbg")
                              P.dma(O["dbg"][:, 0:4096], vl, r=["stab"], sem="dbg")
                              P.dma(O["dbg"][:, 4096:5120], rs32, r=["srr"], sem="dbg")
                              P.dma(O["dbg"][:, 5120:5632], sq32, r=["sqq"], sem="dbg")
                              raise StopIteration
                          for half, kk_ in enumerate((k0, k1)):
                              yp_, ypk = bank(4 + half)
                              n = 0
                              for q in range(4):
                                  for ri, Hb in enumerate((Hre_b, Him_b)):
                                      P.op("pe", lambda e, yp_=yp_, q=q, half=half, ri=ri, Hb=Hb, n=n: e.matmul(yp_[:, 0:SEG], lhsT=Ewpad[:, half * 4 + q, ri, :], rhs=Hb[:, q, :], start=(n == 0), stop=(n == 7)),
                                           r=["Ewpad", ("qhat", ri)], w=[ypk])
                                      n += 1
                              first = (d == 0 and s % 2 == 0)
                              if first:
                                  P.op("dve", lambda e, yp_=yp_, kk_=kk_, tsl=tsl: e.scalar_tensor_tensor(out=ysb[:, kk_, tsl], in0=hT[:, kk_, tsl], scalar=sD[:, kk_:kk_ + 1], in1=yp_[:, 0:SEG], op0=ALU.mult, op1=ALU.add),
                                       r=[ypk, ("hT", kk_), "sD"], w=[("aT", kk_)])
                              else:
                                  P.op("dve", lambda e, yp_=yp_, kk_=kk_, tsl=tsl: e.tensor_tensor(out=ysb[:, kk_, tsl], in0=yp_[:, 0:SEG], in1=ysb[:, kk_, tsl], op=ALU.add),
                                       r=[ypk, ("aT", kk_)], w=[("aT", kk_)])

            try:
                _body()
            except StopIteration:
                pass
            P.alias(["vlat"], ["stab"])
            P.alias(["vctx"], ["W1pad", "Ewpad"])
            P.alias(["rstd"], ["srr"])
            P.alias([("sq", 0)], ["sqq"])
            P.alias([("sq", 1)], ["sCq"])
            P.op("dve", lambda e: e.memset(vlat[:], 0.0), w=["vlat"])
            P.op("dve", lambda e: e.memset(vlat[:, :, :, 0, 64:65], 1.0), w=["vlat"])
            P.op("dve", lambda e: e.memset(vlat[:, :, :, 1, 0:1], 1.0), w=["vlat"])
            for kk_ in range(8):
                yk = ysb[:, kk_, :]
                P.op("dve", lambda e, yk=yk: e.tensor_tensor(out=A_[:], in0=yk, in1=yk, op=ALU.mult), r=[("aT", kk_)], w=[kA])
                P.op("dve", lambda e: e.tensor_scalar(out=A_[:], in0=A_[:], scalar1=0.044715, scalar2=1.0, op0=ALU.mult, op1=ALU.add), r=[kA], w=[kA])
                P.op("pool", lambda e, yk=yk: e.tensor_tensor(out=A_[:], in0=A_[:], in1=yk, op=ALU.mult), r=[kA, ("aT", kk_)], w=[kA])
                P.op("act", lambda e: e.activation(out=A_[:], in_=A_[:], func=AF.Sigmoid, scale=1.5957691216057308), r=[kA], w=[kA])
                P.op("pool", lambda e, yk=yk: e.tensor_tensor(out=yk, in0=A_[:], in1=yk, op=ALU.mult), r=[kA, ("aT", kk_)], w=[("aT", kk_)])
            for dc in range(8):
                wb = wup[dc % 2]
                P.dma(wb[:, 0], I["wglu"][dc], w=[("wup", dc % 2, 0)], q="pool")
                P.dma(wb[:, 1], I["wglu"][8 + dc], w=[("wup", dc % 2, 1)], q="pool")
                for th in range(2):
                    vps, vpk = bank((dc * 2 + th) % 4)
                    gps, gpk = bank(4 + (dc * 2 + th) % 4)
                    for gv, (ps, pk) in enumerate(((vps, vpk), (gps, gpk))):
                        for kk_ in range(8):
                            P.op("pe", lambda e, ps=ps, kk_=kk_, th=th, wb=wb, gv=gv: e.matmul(ps, lhsT=wb[:, gv, kk_, :], rhs=ysb[:, kk_, th * 512:(th + 1) * 512], start=(kk_ == 0), stop=(kk_ == 7)),
                                 r=[("wup", dc % 2, gv), ("aT", kk_)], w=[pk])
                    sl = slice(th * 512, (th + 1) * 512)
                    P.op("act", lambda e, gps=gps, sl=sl: e.activation(out=B_[:, sl], in_=gps, func=AF.Sigmoid), r=[gpk], w=[kB])
                    P.op("dve", lambda e, vps=vps, sl=sl: e.tensor_tensor(out=B_[:, sl], in0=vps, in1=B_[:, sl], op=ALU.mult), r=[vpk, kB], w=[kB])
                    P.op("dve", lambda e, dc=dc, sl=sl: e.scalar_tensor_tensor(out=xT[:, dc, sl], in0=B_[:, sl], scalar=mod[:, 16 + dc:17 + dc], in1=xT[:, dc, sl], op0=ALU.mult, op1=ALU.add),
                         r=[kB, "mod", ("xT", dc)], w=[("xT", dc)])

        eps_t = P.sb("eps_t", [128, 1])
        P.op("dve", lambda e: e.memset(eps_t[:], EPS), w=["eps"])
        halfpi = P.sb("halfpi", [128, 1])
        P.op("dve", lambda e: e.memset(halfpi[:], 1.5707963267948966), w=["halfpi"])


        for l in range(STAGE["layers"]):
            ada_layer(l)
            norm_mod(0)
            if STAGE["mixers"]:
                if l % 3 == 0:
                    attention(l)
                elif l % 3 == 1 and STAGE.get("hgrn", True):
                    hgrn(l)
                elif l % 3 == 2 and STAGE.get("ssm", True):
                    ssm(l)
            norm_mod(1)
            ffn(l)

        rms_stats()
        for k in range(8):
            P.op("dve", lambda e, k=k: e.scalar_tensor_tensor(out=xT[:, k, :], in0=xT[:, k, :], scalar=fg[:, k:k + 1], in1=rstd[:], op0=ALU.mult, op1=ALU.mult),
                 r=[("xT", k), "fg", "rstd"], w=[("xT", k)])
        for k in range(8):
            P.dma(O["y"][k * 128:(k + 1) * 128, :], xT[:, k, :], r=[("xT", k)], sem="yout")
        P.op("pool", lambda e: e.memset(rstd[:], 0.0), r=["rstd"], w=["rstd"])
        if not (STAGE["mixers"] and STAGE.get("hgrn", True) and STAGE["layers"] > 1):
            for i in range(8):
                P.dma(O["hg"][i * 8:(i + 1) * 8].rearrange("a p n -> p a n"), rstd[:].rearrange("p (a n) -> p a n", a=8), r=["rstd"], sem="sout")
        if not (STAGE["mixers"] and STAGE.get("ssm", True) and STAGE["layers"] > 2):
            for a_ in range(8):
                P.dma(O["ssm"][a_].rearrange("g p r -> g (p r)"), rstd[0:64, 0:128], r=["rstd"], sem="sout")
        P.wait_all_dma()
        P.emit()
    return nc

def _c(a):
    return np.ascontiguousarray(a, dtype=np.float32)


def prep_inputs(inp):
    g = {k: np.asarray(v) for k, v in inp.items()}
    sh = {}
    sh["ident"] = np.eye(128, dtype=np.float32)
    sh["ada_w"] = _c(g["ada_w"].reshape(DEPTH, 8, 128, 12, 512).transpose(0, 3, 2, 1, 4))
    sh["ada_b"] = _c(g["ada_b"].reshape(DEPTH, 48, 128).transpose(0, 2, 1))
    sh["ng"] = _c(np.stack([g["norm1_g"], g["norm2_g"]]).reshape(2, DEPTH, 8, 128).transpose(3, 0, 1, 2))
    sh["final_g"] = _c(g["final_g"].reshape(8, 128).T)
    sh["ffn_w_up"] = _c(g["ffn_w_up"].reshape(DEPTH, 8, 128, 2, NFC, 128).transpose(0, 4, 2, 3, 1, 5))
    sh["ffn_w_down"] = _c(g["ffn_w_down"].reshape(DEPTH, NFC, 128, 8, 128).transpose(0, 3, 2, 1, 4))
    sh["ffn_conv_w"] = _c(g["ffn_conv_w"].reshape(DEPTH, 3, 2 * NFC, 128).transpose(0, 3, 1, 2))
    sh["ffn_conv_b"] = _c(g["ffn_conv_b"].reshape(DEPTH, 2 * NFC, 128).transpose(0, 2, 1))
    wqkv = g["attn_wqkv"]
    sh["wq"] = _c(wqkv[:, :, 0:1024].reshape(2, 8, 128, 8, 128).transpose(0, 3, 2, 1, 4))
    wk = wqkv[:, :, 1024:1280].reshape(2, 8, 128, 4, 1, 64)
    sh["wk"] = _c(np.broadcast_to(wk, (2, 8, 128, 4, 2, 64)).reshape(2, 8, 128, 4, 128).transpose(0, 3, 2, 1, 4))
    sh["wkv"] = _c(wqkv[:, :, 1024:1536].reshape(2, 8, 128, 512).transpose(0, 2, 1, 3))
    sh["wo"] = _c(g["attn_wo"].reshape(2, 8, 128, 8, 128).transpose(0, 3, 2, 1, 4))
    sh["sink"] = _c(np.broadcast_to(g["attn_sink"][:, None, :], (2, 128, 16)))
    hw = g["hgrn_w_in"][0]
    sh["hw_in"] = _c(hw.reshape(8, 128, 5, 8, 128).transpose(3, 2, 1, 0, 4))
    sh["hwo"] = _c(g["hgrn_wo"][0].reshape(8, 128, 8, 128).transpose(2, 1, 0, 3))
    sh["hlb"] = _c(g["hgrn_lb"].reshape(4, 2, 8, 128).transpose(3, 0, 1, 2))
    sh["hgn"] = _c(g["hgrn_g_norm"][0].reshape(128, 1))
    ii = np.arange(128)
    same = (ii[:, None] // 32) == (ii[None, :] // 32)
    sh["hmask"] = _c(np.stack([same & (ii[:, None] <= ii[None, :]), same & (ii[:, None] >= ii[None, :])], axis=1))
    are, aim, ldt = g["ssm_a_re"][0], g["ssm_a_im"][0], g["ssm_log_dt"][0]
    bre, bim, cre, cim = g["ssm_b_re"][0], g["ssm_b_im"][0], g["ssm_c_re"][0], g["ssm_c_im"][0]
    sh["sBreR"] = _c(bre.reshape(2, 8, 8, 64, 16).transpose(0, 1, 2, 4, 3).reshape(2, 8, 128, 64))
    sh["sBimR"] = _c(bim.reshape(2, 8, 8, 64, 16).transpose(0, 1, 2, 4, 3).reshape(2, 8, 128, 64))
    sh["sAreR"] = _c(np.broadcast_to(are.reshape(2, 8, 8, 1, 64), (2, 8, 8, 16, 64)).reshape(2, 8, 128, 64))
    sh["sAimR"] = _c(np.broadcast_to(aim.reshape(2, 8, 8, 1, 64), (2, 8, 8, 16, 64)).reshape(2, 8, 128, 64))
    sh["sDtR"] = _c(np.broadcast_to(ldt.reshape(2, 8, 8, 1, 1), (2, 8, 8, 16, 1)).reshape(2, 8, 128, 1))
    sh["sAreQ"] = _c(are.reshape(2, 2, 32, 64).transpose(0, 1, 3, 2).reshape(2, 128, 32))
    sh["sAimQ"] = _c(aim.reshape(2, 2, 32, 64).transpose(0, 1, 3, 2).reshape(2, 128, 32))
    sh["sDtQ"] = _c(np.broadcast_to(ldt.reshape(2, 2, 1, 32), (2, 2, 64, 32)).reshape(2, 128, 32))
    sh["sCreQ"] = _c(cre.reshape(2, 2, 32, 16, 64).transpose(0, 1, 4, 2, 3).reshape(2, 128, 32, 16))
    sh["sCimQ"] = _c(cim.reshape(2, 2, 32, 16, 64).transpose(0, 1, 4, 2, 3).reshape(2, 128, 32, 16))
    sh["sD"] = _c(g["ssm_d"][0].reshape(8, 128).T)
    sm = np.zeros((128, 12), np.float32)
    for q in range(8):
        sm[q * 16:(q + 1) * 16, q] = 1.0
    sm[0:64, 8] = 1.0; sm[64:128, 9] = 1.0; sm[0:64, 10] = -1.0; sm[64:128, 11] = -1.0
    sh["smask"] = sm
    sh["wglu"] = _c(g["ssm_w_glu"][0].reshape(8, 128, 16, 128).transpose(2, 1, 0, 3))
    tt = np.arange(NT)
    inv = 1.0 / (10000.0 ** (np.arange(0, 32, 2, dtype=np.float32) / np.float32(32)))
    ar = (tt // 64).astype(np.float32)[:, None] * inv.astype(np.float32)
    ac = (tt % 64).astype(np.float32)[:, None] * inv.astype(np.float32)
    ang = np.concatenate([ar, ar, ac, ac], axis=-1).astype(np.float32)
    cosS = _c(np.concatenate([np.cos(ang).T] * 2, axis=0)); sinS = _c(np.concatenate([np.sin(ang).T] * 2, axis=0))
    rm = np.zeros((128, 128), np.float32)
    for m in range(128):
        d = m % 64
        if (d % 32) < 16:
            rm[m + 16, m] = -1.0
        else:
            rm[m - 16, m] = 1.0
    sh["rmat"] = rm
    kk = np.arange(128)[:, None]; qq = np.arange(128)[None, :]
    mS = np.zeros((128, 8, 2, 128), np.float32); mP = np.zeros((128, 8, 2, 128), np.float32)
    for i in range(8):
        if i >= 1:
            mS[:, i, 0, :] = (kk >= qq)
        if i <= 6:
            mS[:, i, 1, :] = (kk <= qq)
        mP[:, i, 0, :] = 1.0 if i % 2 == 1 else 0.0
        mP[:, i, 1, :] = 1.0 if i % 2 == 0 else 0.0
    if STAGE.get("kv_only"):
        for kk_ in ("wq", "wk", "wo", "sink", "rmat"):
            sh.pop(kk_, None)
    maps = []
    for c in range(8):
        m = dict(sh)
        if c < 4:
            xc = g["x_sample"][c]
            cond = g["c"][c]
            kp = 1.0
        else:
            xc = g["x_prompt"][4 * (c - 4):4 * (c - 4) + 4].reshape(NT, D)
            cond = g["c_ctx"]
            kp = 0.0
        if c < 4:
            ropec = cosS; ropes = sinS; m["amask"] = mS
            ck = g["cache_k"][c]
            m["ckd"] = _c(np.broadcast_to(ck[:, :, :, None, :], (2, 512, 4, 2, 64)).reshape(2, 512, 512))
            cv = g["cache_v"][c]
            vp = np.zeros((2, 512, 4, 2, 128), np.float32)
            vp[:, :, :, 0, 0:64] = cv; vp[:, :, :, 0, 64] = 1.0
            vp[:, :, :, 1, 64:128] = cv; vp[:, :, :, 1, 0] = 1.0
            m["cvp"] = vp.reshape(2, 512, 1024)
        else:
            ropec = np.ones((128, NT), np.float32); ropes = np.zeros((128, NT), np.float32); m["amask"] = mP
            m["ckd"] = np.zeros((2, 512, 512), np.float32); m["cvp"] = np.zeros((2, 512, 1024), np.float32)
        if STAGE.get("kv_only"):
            for kk_ in ("amask", "ckd", "cvp"):
                m.pop(kk_, None)
        if c < 4:
            st = g["state_ssm"][c, 0]
            m["sH0"] = _c(st.reshape(2, 2, 32, 64, 2).transpose(0, 1, 3, 2, 4).reshape(2, 128, 32, 2))
        else:
            m["sH0"] = np.zeros((2, 128, 32, 2), np.float32)
        m["hs0"] = _c(g["state_hgrn"][c, 0]) if c < 4 else np.zeros((2, 8, 128, 128), np.float32)
        m["x"] = _c(np.concatenate([xc.T, ropec, ropes], axis=0))
        m["cond"] = _c(cond.reshape(8, 128).T)
        m["keep"] = _c(np.stack([np.full(128, kp), np.full(128, kp - 1.0)], axis=1))
        maps.append(m)
    return maps


def assemble(results):
    f = lambda a: np.asarray(a, dtype=np.float32)
    ys = np.stack([f(results[c]["y"]).T for c in range(4)])
    yp = np.concatenate([f(results[c]["y"]).T.reshape(4, 256, D) for c in range(4, 8)])
    nk = np.concatenate([f(results[c]["nk"]).reshape(2, 4, 256, 4, 64).transpose(1, 0, 2, 3, 4) for c in range(4, 8)])
    nv = np.concatenate([f(results[c]["nv"]).reshape(2, 4, 256, 4, 64).transpose(1, 0, 2, 3, 4) for c in range(4, 8)])
    hg = np.concatenate([f(results[c]["hg"]).reshape(4, 1, 2, 8, 128, 128) for c in range(4, 8)])
    ssm = np.concatenate([f(results[c]["ssm"]).reshape(4, 1, 2, 64, 64, 2) for c in range(4, 8)])
    return (np.ascontiguousarray(yp), np.ascontiguousarray(ys), np.ascontiguousarray(nk), np.ascontiguousarray(nv),
            np.ascontiguousarray(hg), np.ascontiguousarray(ssm))


def kernel(**inputs):
    nc = build_program()
    maps = prep_inputs(inputs)
    res = run_bass_kernel_spmd(nc, maps, core_ids=list(range(8)))
    return assemble(res.results)
```

```python
import numpy as np
from concourse.bass_utils import run_bass_kernel_spmd

from contextlib import ExitStack
import concourse.bass as bass
import concourse.mybir as mybir

F32 = mybir.dt.float32
F32R = mybir.dt.float32r
BF16 = mybir.dt.bfloat16
AF = mybir.ActivationFunctionType
ALU = mybir.AluOpType
AX = mybir.AxisListType


class Prog:
    ENGS = ("pe", "act", "dve", "pool", "sp")

    def __init__(self, nc, es: ExitStack):
        self.nc = nc
        self.es = es
        self.recs = {e: [] for e in self.ENGS}
        self.cnt = {e: 0 for e in self.ENGS}
        self.known = {e: {} for e in self.ENGS}
        self.state = {}
        self.sems = {}
        self.dcnt = {}
        for e in self.ENGS:
            self.sems[("e", e)] = es.enter_context(nc.semaphore("sem_" + e))
        self.psn = 0

    def sb(self, name, shape, dt=F32):
        return self.es.enter_context(self.nc.sbuf_tensor("sb_" + name, list(shape), dt))

    def ps(self, name, shape, dt=F32):
        return self.es.enter_context(self.nc.psum_tensor(name, list(shape), dt))

    def dsem(self, name):
        k = ("d", name)
        if k not in self.sems:
            self.sems[k] = self.es.enter_context(self.nc.semaphore("dsem_" + name))
            self.dcnt[k] = 0
        return k

    def _st(self, k):
        s = self.state.get(k)
        if s is None:
            s = {"w": {}, "r": {}}
            self.state[k] = s
        return s

    def _deps(self, eng, r, w):
        deps = {}
        def add(d):
            if d is None:
                return
            sk, v = d
            if deps.get(sk, 0) < v:
                deps[sk] = v
        for k in r:
            st = self._st(k)
            for sk, v in st["w"].items():
                add((sk, v))
            if isinstance(k, tuple) and k[0] == "pq":
                for sk, v in st["r"].items():
                    if sk != ("e", eng):
                        add((sk, v))
        for k in w:
            s = self._st(k)
            for sk, v in s["w"].items():
                add((sk, v))
            for sk, v in s["r"].items():
                add((sk, v))
        waits = []
        kn = self.known[eng]
        for sk, v in deps.items():
            if eng == "pe" and sk == ("e", "pe"):
                continue
            if kn.get(sk, 0) < v:
                waits.append((sk, v))
                kn[sk] = v
        return waits

    def _commit(self, comp, r, w):
        sk, v = comp
        for k in w:
            self.state[k] = {"w": {sk: v}, "r": {}}
        for k in r:
            s = self._st(k)
            if s["r"].get(sk, 0) < v:
                s["r"][sk] = v

    def alias(self, dst, src):
        mw, mr = {}, {}
        for k in src:
            st = self._st(k)
            for sk, v in st["w"].items():
                mw[sk] = max(mw.get(sk, 0), v)
            for sk, v in st["r"].items():
                mr[sk] = max(mr.get(sk, 0), v)
        for k in dst:
            self.state[k] = {"w": dict(mw), "r": dict(mr)}

    def op(self, eng, fn, r=(), w=(), inc=True):
        inc = True
        waits = self._deps(eng, r, w)
        sk = ("e", eng)
        comp = (sk, self.cnt[eng] + 1)
        if inc:
            self.cnt[eng] += 1
        self.recs[eng].append((waits, fn, (sk, 1) if inc else None))
        self._commit(comp, r, w)

    def dma(self, out, in_, r=(), w=(), sem=None, q="sp", **kw):
        if sem is None:
            sem = "_".join(str(x) for x in (w[0] if isinstance(w[0], tuple) else (w[0],)))
        waits = self._deps(q, r, w)
        sk = self.dsem(sem)
        self.dcnt[sk] += 16
        comp = (sk, self.dcnt[sk])
        self.recs[q].append((waits, (lambda e, o=out, i=in_, kw=kw: e.dma_start(out=o, in_=i, **kw)), (sk, 16)))
        self._commit(comp, r, w)

    def wait_all_dma(self, q="sp"):
        waits = []
        for sk, v in self.dcnt.items():
            if v > 0 and self.known[q].get(sk, 0) < v:
                waits.append((sk, v))
                self.known[q][sk] = v
        self.recs[q].append((waits, None, None))

    def barrier_all(self):
        tgt = {("e", e): self.cnt[e] for e in self.ENGS if self.cnt[e] > 0}
        for sk, v in self.dcnt.items():
            if v > 0:
                tgt[sk] = v
        for e in self.ENGS:
            waits = []
            for sk, v in tgt.items():
                if sk == ("e", e):
                    continue
                if self.known[e].get(sk, 0) < v:
                    waits.append((sk, v))
                    self.known[e][sk] = v
            if waits:
                self.recs[e].append((waits, None, None))

    def emit(self):
        nc = self.nc
        sems = self.sems
        recs = self.recs

        def replay(name):
            def f(e):
                for waits, fn, inc in recs[name]:
                    for sk, v in waits:
                        e.wait_ge(sems[sk], v)
                    if fn is not None:
                        ins = fn(e)
                        if inc is not None:
                            ins.then_inc(sems[inc[0]], inc[1])
            return f

        with nc.Block() as block:
            block.tensor(replay("pe"))
            block.scalar(replay("act"))
            block.vector(replay("dve"))
            block.gpsimd(replay("pool"))
            block.sync(replay("sp"))

    def stats(self):
        return {e: len(self.recs[e]) for e in self.ENGS}

D = 1024
NT = 1024
DFF = 2816
NFC = 22
DEPTH = 4
EPS = 1e-6

STAGE = {"mixers": True, "layers": 4, "kv_only": False}


def build_program():
    nc = bass.Bass("TRN2", target_bir_lowering=False)

    def din(name, shape, dt=F32):
        return nc.dram_tensor(name, list(shape), dt, kind="ExternalInput").ap()

    def dout(name, shape, dt=F32):
        return nc.dram_tensor(name, list(shape), dt, kind="ExternalOutput").ap()

    I = {}
    I["x"] = din("x", [D + 256, NT])
    I["cond"] = din("cond", [128, 8])
    I["keep"] = din("keep", [128, 2])
    I["ident"] = din("ident", [128, 128])
    I["ada_w"] = din("ada_w", [DEPTH, 12, 128, 8, 512])
    I["ada_b"] = din("ada_b", [DEPTH, 128, 48])
    I["ng"] = din("ng", [128, 2, DEPTH, 8])
    I["ffn_w_up"] = din("ffn_w_up", [DEPTH, NFC, 128, 2, 8, 128])
    I["ffn_conv_w"] = din("ffn_conv_w", [DEPTH, 128, 3, 2 * NFC])
    I["ffn_conv_b"] = din("ffn_conv_b", [DEPTH, 128, 2 * NFC])
    I["ffn_w_down"] = din("ffn_w_down", [DEPTH, 8, 128, NFC, 128])
    I["final_g"] = din("final_g", [128, 8])
    I["wkv"] = din("wkv", [2, 128, 8, 512])
    I["hw_in"] = din("hw_in", [8, 5, 128, 8, 128])
    I["hwo"] = din("hwo", [8, 128, 8, 128])
    I["hlb"] = din("hlb", [128, 4, 2, 8])
    I["hgn"] = din("hgn", [128, 1])
    I["hs0"] = din("hs0", [2, 8, 128, 128])
    I["hmask"] = din("hmask", [128, 2, 128])
    I["sBreR"] = din("sBreR", [2, 8, 128, 64]); I["sBimR"] = din("sBimR", [2, 8, 128, 64])
    I["sAreR"] = din("sAreR", [2, 8, 128, 64]); I["sAimR"] = din("sAimR", [2, 8, 128, 64])
    I["sDtR"] = din("sDtR", [2, 8, 128, 1])
    I["sAreQ"] = din("sAreQ", [2, 128, 32]); I["sAimQ"] = din("sAimQ", [2, 128, 32]); I["sDtQ"] = din("sDtQ", [2, 128, 32])
    I["sCreQ"] = din("sCreQ", [2, 128, 32, 16]); I["sCimQ"] = din("sCimQ", [2, 128, 32, 16])
    I["sH0"] = din("sH0", [2, 128, 32, 2])
    I["sD"] = din("sD", [128, 8])
    I["smask"] = din("smask", [128, 12])
    I["wglu"] = din("wglu", [16, 128, 8, 128])
    if not STAGE.get("kv_only"):
        I["wq"] = din("wq", [2, 8, 128, 8, 128])
        I["wk"] = din("wk", [2, 4, 128, 8, 128])
        I["wo"] = din("wo", [2, 8, 128, 8, 128])
        I["sink"] = din("sink", [2, 128, 16])
        I["rmat"] = din("rmat", [128, 128])
        I["amask"] = din("amask", [128, 8, 2, 128])
        I["ckd"] = din("ckd", [2, 512, 512])
        I["cvp"] = din("cvp", [2, 512, 1024])
    O = {}
    O["y"] = dout("y", [D, NT])
    O["nk"] = dout("nk", [2, NT, 256])
    O["nv"] = dout("nv", [2, NT, 256])
    O["hg"] = dout("hg", [64, 128, 128])
    O["ssm"] = dout("ssm", [8, 64, 64, 2])
    if STAGE.get("ssm_dbg"):
        O["dbg"] = dout("dbg", [128, 4096 + 1024 + 512])
        O["dbg2"] = dout("dbg2", [128, 6 * 1024])
        O["dbg3"] = dout("dbg3", [128, 8 * 1024], BF16)

    with ExitStack() as es:
        P = Prog(nc, es)
        xT = P.sb("xT", [128, 10, NT])
        hT = P.sb("hT", [128, 8, NT], BF16)
        aT = P.sb("aT", [128, 12, NT], BF16)
        rstd = P.sb("rstd", [128, NT])
        tmpA = P.sb("tmpA", [128, NT])
        tmpB = P.sb("tmpB", [128, NT])
        cg = [P.sb("cg%d" % i, [128, NT]) for i in range(2)]
        cv = [P.sb("cv%d" % i, [128, NT]) for i in range(2)]
        sq = [P.sb("sq%d" % i, [128, NT], BF16) for i in range(2)]
        ident = P.sb("ident", [128, 128])
        ones_bf = P.sb("ones_bf", [128, 128], BF16)
        one_f = P.sb("one_f", [128, 1])
        keep = P.sb("keep", [128, 2])
        cond = P.sb("cond", [128, 8])
        s_bf = P.sb("s_bf", [128, 8], BF16)
        mod = P.sb("mod", [128, 48])
        adab = P.sb("adab", [128, 48])
        ng = P.sb("ng", [128, 2, DEPTH, 8])
        fg = P.sb("fg", [128, 8])
        AB = P.sb("AB", [128, 4, 8])
        cw = P.sb("cw", [128, 3, 2 * NFC])
        cb = P.sb("cb", [128, 2 * NFC])
        cwk = P.sb("cwk", [128, 2, 2 * NFC])
        wada = [P.sb("wada%d" % i, [128, 8, 512], BF16) for i in range(2)]
        wup = [P.sb("wup%d" % i, [128, 2, 8, 128], BF16) for i in range(2)]
        wdn = [P.sb("wdn%d" % i, [128, 11, 128], BF16) for i in range(2)]
        vlat = P.sb("vlat", [128, 8, 4, 2, 128], BF16)
        vctx = P.sb("vctx", [128, 4, 4, 2, 128], BF16)
        kctxT = P.sb("kctxT", [128, 4, 512], BF16)
        ckd = P.sb("ckd", [128, 4, 512], BF16)
        amask = P.sb("amask", [128, 8, 2, 128], BF16)
        rmat = P.sb("rmat", [128, 128], BF16)
        ident_bf = P.sb("ident_bf", [128, 128], BF16)
        qb = [P.sb("qb%d" % i, [128, 512], BF16) for i in range(2)]
        esink = P.sb("esink", [128, 16])
        ones_f = P.sb("ones_f", [128, 128])
        m32 = P.sb("m32", [128, NT])
        hlb = P.sb("hlb", [128, 4, 2, 8])
        lbp = P.sb("lbp", [128, 2, 2, 8])
        hgn = P.sb("hgn", [128, 1])
        hmask = P.sb("hmask", [128, 2, 128], BF16)
        Sf = [P.sb("Sf%d" % i, [128, 128]) for i in range(2)]
        Sb = [P.sb("Sb%d" % i, [128, 128], BF16) for i in range(2)]
        adec = [P.sb("adec%d" % i, [128, 32]) for i in range(2)]
        rowm = P.sb("rowm", [128, 4])
        ctmp = P.sb("ctmp", [128, 32])
        vtok = P.sb("vtok", [128, 8, 128], BF16)
        qhat = [P.sb("qhat%d" % i, [128, NT], BF16) for i in range(2)]
        ktil = [P.sb("ktil%d" % i, [128, NT], BF16) for i in range(2)]
        kdT = [P.sb("kdT%d" % i, [128, NT], BF16) for i in range(2)]
        attm = [P.sb("attm%d" % i, [128, 128], BF16) for i in range(2)]
        smask = P.sb("smask", [128, 12])
        sD = P.sb("sD", [128, 8])
        sH = P.sb("sH", [128, 32, 2])
        sHent = P.sb("sHent", [128, 4, 2])
        sHx = P.sb("sHx", [128, 4, 2])
        pq = [P.ps("pq%d" % i, [128, 1024]) for i in range(4)]

        def bank(i):
            return pq[i // 2][:, (i % 2) * 512:(i % 2) * 512 + 512], ("pq", i)

        P.dma(ident[:], I["ident"], w=["ident"])
        P.dma(keep[:], I["keep"], w=["keep"])
        P.dma(cond[:], I["cond"], w=["cond"])
        P.dma(ng[:], I["ng"], w=["ng"])
        P.dma(fg[:], I["final_g"], w=["fg"])
        P.op("dve", lambda e: e.memset(ones_bf[:], 1.0), w=["ones_bf"])
        P.op("dve", lambda e: e.memset(one_f[:], 1.0), w=["one_f"])
        P.op("dve", lambda e: e.memset(ones_f[:], 1.0), w=["ones_f"])
        P.op("dve", lambda e: e.tensor_copy(out=ident_bf[:], in_=ident[:]), r=["ident"], w=["ident_bf"])
        if not STAGE.get("kv_only"):
            P.dma(rmat[:], I["rmat"], w=["rmat"], q="pool")
            P.dma(amask[:], I["amask"], w=["amask"], q="pool")
        P.op("pool", lambda e: e.memset(m32[:], 1.0), w=["m32"])
        P.op("pool", lambda e: e.memset(m32[:, 0:NT:32], 0.0), w=["m32"])
        P.op("pool", lambda e: e.memset(rowm[:], 0.0), w=["rowm"])
        for c4 in range(3):
            P.op("pool", lambda e, c4=c4: e.memset(rowm[c4 * 32:(c4 + 1) * 32, c4:c4 + 1], 1.0), w=["rowm"])
        P.op("pool", lambda e: e.memset(rowm[96:128, 3:4], 1.0), w=["rowm"])
        P.dma(hlb[:], I["hlb"], w=["hlb"])
        P.dma(hgn[:], I["hgn"], w=["hgn"])
        P.dma(hmask[:], I["hmask"], w=["hmask"], q="pool")
        P.op("dve", lambda e: e.memset(vlat[:], 0.0), w=["vlat"])
        P.op("dve", lambda e: e.memset(vlat[:, :, :, 0, 64:65], 1.0), w=["vlat"])
        P.op("dve", lambda e: e.memset(vlat[:, :, :, 1, 0:1], 1.0), w=["vlat"])
        P.op("act", lambda e: e.activation(out=s_bf[:], in_=cond[:], func=AF.Silu), r=["cond"], w=["s_bf"])

        P.dma(xT[:], I["x"].rearrange("(k p) t -> p k t", p=128), w=[("xT", k) for k in range(10)], sem="xin")

        def ada_layer(l):
            P.dma(adab[:], I["ada_b"][l], w=["adab"])
            ps, pk = bank(4)
            for n in range(12):
                wb = wada[n % 2]
                P.dma(wb[:], I["ada_w"][l, n],
                      w=[("wada", n % 2)], q="pool")
                for c4 in range(4):
                    c = n * 4 + c4
                    for k in range(8):
                        P.op("pe", lambda e, ps=ps, k=k, wb=wb, c=c, c4=c4: e.matmul(ps[:, c:c + 1], lhsT=wb[:, k, c4 * 128:(c4 + 1) * 128], rhs=s_bf[:, k:k + 1], start=(k == 0), stop=(k == 7)),
                             r=[("wada", n % 2), "s_bf"], w=[pk])
            P.op("dve", lambda e, ps=ps: e.tensor_tensor(out=mod[:], in0=ps[:, 0:48], in1=adab[:], op=ALU.add), r=[pk, "adab"], w=["mod"])
            for j in range(2):
                P.op("dve", lambda e, j=j: e.scalar_tensor_tensor(out=AB[:, 2 * j, :], in0=mod[:, (3 * j + 1) * 8:(3 * j + 2) * 8], scalar=1.0,
                                                                 in1=ng[:, j, l, :], op0=ALU.add, op1=ALU.mult),
                     r=["mod", "ng"], w=["AB"])
                P.op("dve", lambda e, j=j: e.tensor_copy(out=AB[:, 2 * j + 1, :], in_=mod[:, (3 * j) * 8:(3 * j + 1) * 8]), r=["mod"], w=["AB"])

        def rms_stats():
            b0, k0 = bank(6)
            b1, k1 = bank(7)
            for k in range(8):
                s = sq[k % 2]
                P.op("act", lambda e, s=s, k=k: e.activation(out=s[:], in_=xT[:, k, :], func=AF.Square), r=[("xT", k)], w=[("sq", k % 2)])
                for th, (b, bk) in enumerate(((b0, k0), (b1, k1))):
                    P.op("pe", lambda e, b=b, s=s, th=th, k=k: e.matmul(b, lhsT=ones_bf[:], rhs=s[:, th * 512:(th + 1) * 512], start=(k == 0), stop=(k == 7)),
                         r=[("sq", k % 2), "ones_bf"], w=[bk], inc=True)
            for th, (b, bk) in enumerate(((b0, k0), (b1, k1))):
                P.op("act", lambda e, b=b, th=th: e.activation(out=tmpA[:, th * 512:(th + 1) * 512], in_=b, func=AF.Sqrt, scale=1.0 / D, bias=eps_t[:, 0:1]),
                     r=[bk, "eps"], w=["tmpA"])
            P.op("dve", lambda e: e.reciprocal(out=rstd[:], in_=tmpA[:]), r=["tmpA"], w=["rstd"])

        def norm_mod(j):
            rms_stats()
            for k in range(8):
                t = tmpA if k % 2 == 0 else tmpB
                tk = "tmpA" if k % 2 == 0 else "tmpB"
                P.op("dve", lambda e, t=t, k=k: e.scalar_tensor_tensor(out=t[:], in0=xT[:, k, :], scalar=AB[:, 2 * j, k:k + 1], in1=rstd[:], op0=ALU.mult, op1=ALU.mult),
                     r=[("xT", k), "AB", "rstd"], w=[tk])
                P.op("act", lambda e, t=t, k=k: e.activation(out=hT[:, k, :], in_=t[:], func=AF.Identity, bias=AB[:, 2 * j + 1, k:k + 1], scale=1.0),
                     r=[tk, "AB"], w=[("hT", k)])

        def ffn(l):
            P.dma(cw[:], I["ffn_conv_w"][l], w=["cw"])
            P.dma(cb[:], I["ffn_conv_b"][l], w=["cb"])
            for jj, j in enumerate((0, 2)):
                P.op("dve", lambda e, jj=jj, j=j: e.tensor_scalar(out=cwk[:, jj, :], in0=cw[:, j, :], scalar1=keep[:, 1:2], scalar2=None, op0=ALU.mult),
                     r=["cw", "keep"], w=["cwk"])
            for grp in range(2):
                for fc in range(grp * 11, grp * 11 + 11):
                    wb = wup[fc % 2]
                    P.dma(wb[:], I["ffn_w_up"][l, fc], w=[("wup", fc % 2, 0), ("wup", fc % 2, 1)], q="pool")
                    outs = []
                    for gv in range(2):
                        pt = pq[(fc % 2) * 2 + gv]
                        pks = [("pq", ((fc % 2) * 2 + gv) * 2 + th) for th in range(2)]
                        for th in range(2):
                            for k in range(8):
                                P.op("pe", lambda e, pt=pt, th=th, k=k, gv=gv, wb=wb: e.matmul(pt[:, th * 512:(th + 1) * 512], lhsT=wb[:, gv, k, :], rhs=hT[:, k, th * 512:(th + 1) * 512],
                                                                                    start=(k == 0), stop=(k == 7)),
                                     r=[("wup", fc % 2, gv), ("hT", k)], w=[pks[th]], inc=(k == 7))
                        c = (cg if gv == 0 else cv)[fc % 2]
                        ck = ("cg" if gv == 0 else "cv", fc % 2)
                        col = gv * NFC + fc
                        P.op("act", lambda e, c=c, pt=pt, col=col: e.activation(out=c[:], in_=pt[:], func=AF.Identity, scale=cw[:, 1, col:col + 1], bias=cb[:, col:col + 1]),
                             r=pks + ["cw", "cb"], w=[ck])
                        P.op("dve", lambda e, c=c, pt=pt, col=col: e.scalar_tensor_tensor(out=c[:, 1:NT], in0=pt[:, 0:NT - 1], scalar=cw[:, 0, col:col + 1], in1=c[:, 1:NT], op0=ALU.mult, op1=ALU.add),
                             r=pks + ["cw", ck], w=[ck])
                        P.op("dve", lambda e, c=c, pt=pt, col=col: e.scalar_tensor_tensor(out=c[:, 0:NT - 1], in0=pt[:, 1:NT], scalar=cw[:, 2, col:col + 1], in1=c[:, 0:NT - 1], op0=ALU.mult, op1=ALU.add),
                             r=pks + ["cw", ck], w=[ck])
                        P.op("dve", lambda e, c=c, pt=pt, col=col: e.scalar_tensor_tensor(out=c[:, 256:NT:256], in0=pt[:, 255:NT - 1:256], scalar=cwk[:, 0, col:col + 1], in1=c[:, 256:NT:256], op0=ALU.mult, op1=ALU.add),
                             r=pks + ["cwk", ck], w=[ck])
                        P.op("dve", lambda e, c=c, pt=pt, col=col: e.scalar_tensor_tensor(out=c[:, 255:NT - 1:256], in0=pt[:, 256:NT:256], scalar=cwk[:, 1, col:col + 1], in1=c[:, 255:NT - 1:256], op0=ALU.mult, op1=ALU.add),
                             r=pks + ["cwk", ck], w=[ck])
                        outs.append((c, ck))
                    (cgt, cgk), (cvt, cvk) = outs
                    P.op("act", lambda e, cgt=cgt: e.activation(out=cgt[:], in_=cgt[:], func=AF.Silu), r=[cgk], w=[cgk])
                    P.op("dve", lambda e, cgt=cgt, cvt=cvt, fc=fc: e.tensor_tensor(out=aT[:, fc % 11, :], in0=cgt[:], in1=cvt[:], op=ALU.mult), r=[cgk, cvk], w=[("aT", fc % 11)])
                for dc in range(8):
                    wb = wdn[dc % 2]
                    P.dma(wb[:], I["ffn_w_down"][l, dc][:, grp * 11:grp * 11 + 11, :],
                          w=[("wdn", dc % 2)], q="pool")
                    for th in range(2):
                        ps, pk = bank((dc * 2 + th) % 8)
                        for fc in range(11):
                            P.op("pe", lambda e, ps=ps, fc=fc, th=th, wb=wb: e.matmul(ps, lhsT=wb[:, fc, :], rhs=aT[:, fc, th * 512:(th + 1) * 512], start=(fc == 0), stop=(fc == 10)),
                                 r=[("wdn", dc % 2), ("aT", fc)], w=[pk])
                        P.op("dve", lambda e, ps=ps, dc=dc, th=th: e.scalar_tensor_tensor(out=xT[:, dc, th * 512:(th + 1) * 512], in0=ps, scalar=mod[:, 40 + dc:41 + dc],
                                                                                         in1=xT[:, dc, th * 512:(th + 1) * 512], op0=ALU.mult, op1=ALU.add),
                             r=[pk, "mod", ("xT", dc)], w=[("xT", dc)])

        def attention(l):
            j = l // 3
            cosT, sinT = xT[:, 8, :], xT[:, 9, :]
            t1, t2 = cv[0], cv[1]
            if STAGE.get("kv_only"):
                wkv = wada[0]
                P.dma(wkv[:], I["wkv"][j], w=[("wada", 0)], q="pool")
                for tb in range(8):
                    ps, pk = bank(tb % 4)
                    for k in range(8):
                        P.op("pe", lambda e, ps=ps, k=k, tb=tb: e.matmul(ps, lhsT=hT[:, k, tb * 128:(tb + 1) * 128], rhs=wkv[:, k, :], start=(k == 0), stop=(k == 7)),
                             r=[("wada", 0), ("hT", k)], w=[pk])
                    kvt = tmpA if tb % 2 == 0 else tmpB
                    kvk = "tmpA" if tb % 2 == 0 else "tmpB"
                    P.op("act", lambda e, ps=ps, kvt=kvt: e.copy(out=kvt[:, 0:512], in_=ps), r=[pk], w=[kvk])
                    P.dma(O["nk"][j, tb * 128:(tb + 1) * 128, :], kvt[:, 0:256], r=[kvk], sem="kvout%d" % (tb % 2))
                    P.dma(O["nv"][j, tb * 128:(tb + 1) * 128, :], kvt[:, 256:512], r=[kvk], sem="kvout%d" % (tb % 2))
                return
            P.dma(esink[:], I["sink"][j], w=["esink"])
            P.op("act", lambda e: e.activation(out=esink[:], in_=esink[:], func=AF.Exp), r=["esink"], w=["esink"])
            if STAGE.get("attn_upto", 9) < 1:
                return
            P.dma(ckd[:], I["ckd"][j].rearrange("(kb p) n -> p kb n", p=128), w=["ckd"], q="pool")
            P.dma(vctx[:].rearrange("p kb g v n -> p kb (g v n)"), I["cvp"][j].rearrange("(kb p) n -> p kb n", p=128), w=["vctx"], q="pool")
            for g in range(4):
                ps, pk = bank(g)
                for kb in range(4):
                    P.op("pe", lambda e, ps=ps, kb=kb, g=g: e.matmul(ps[:, kb * 128:(kb + 1) * 128], lhsT=ckd[:, kb, g * 128:(g + 1) * 128], rhs=ident_bf[:], start=True, stop=True),
                         r=["ckd", "ident_bf"], w=[pk])
                P.op("act", lambda e, ps=ps, g=g: e.copy(out=kctxT[:, g, :], in_=ps), r=[pk], w=["kctxT"])
            if STAGE.get("attn_upto", 9) < 2:
                return
            def proj_rope(wsrc, dst, dkey, idx):
                wb = wup[idx % 2]
                P.dma(wb[:, 0], wsrc, w=[("wup", idx % 2, 0)], q="pool")
                for th in range(2):
                    ps, pk = bank((idx * 2 + th) % 4)
                    rps, rpk = bank(4 + (idx * 2 + th) % 2)
                    for k in range(8):
                        P.op("pe", lambda e, ps=ps, k=k, th=th, wb=wb: e.matmul(ps, lhsT=wb[:, 0, k, :], rhs=hT[:, k, th * 512:(th + 1) * 512], start=(k == 0), stop=(k == 7)),
                             r=[("wup", idx % 2, 0), ("hT", k)], w=[pk])
                    q_ = qb[th]
                    if STAGE.get("pr", 9) < 1:
                        continue
                    P.op("act", lambda e, ps=ps, q_=q_: e.copy(out=q_[:], in_=ps), r=[pk], w=[("qb", th)])
                    if STAGE.get("pr", 9) < 2:
                        continue
                    P.op("pe", lambda e, rps=rps, q_=q_: e.matmul(rps, lhsT=rmat[:], rhs=q_[:], start=True, stop=True), r=[("qb", th), "rmat"], w=[rpk])
                    sl = slice(th * 512, (th + 1) * 512)
                    if STAGE.get("pr", 9) < 3 or idx >= STAGE.get("pridx", 99):
                        continue
                    P.op("dve", lambda e, ps=ps, sl=sl: e.scalar_tensor_tensor(out=t1[:, sl], in0=ps, scalar=1.0, in1=cosT[:, sl], op0=ALU.mult, op1=ALU.mult), r=[pk, ("xT", 8), ("qb", th)], w=[("cv", 0)])
                    if STAGE.get("pr", 9) < 4:
                        continue
                    P.op("dve", lambda e, rps=rps, sl=sl: e.scalar_tensor_tensor(out=t2[:, sl], in0=rps, scalar=1.0, in1=sinT[:, sl], op0=ALU.mult, op1=ALU.mult), r=[rpk, ("xT", 9)], w=[("cv", 1)])
                    if STAGE.get("pr", 9) < 5:
                        continue
                    P.op("dve", lambda e, sl=sl, dst=dst: e.tensor_tensor(out=dst[:, sl], in0=t1[:, sl], in1=t2[:, sl], op=ALU.add), r=[("cv", 0), ("cv", 1)], w=[dkey])
            for qc in range(8):
                proj_rope(I["wq"][j, qc], aT[:, qc, :], ("aT", qc), qc)
            for g in range(4):
                proj_rope(I["wk"][j, g], aT[:, 8 + g, :], ("aT", 8 + g), 8 + g)
            if STAGE.get("attn_upto", 9) < 3:
                return
            wkv = wada[0]
            P.dma(wkv[:], I["wkv"][j], w=[("wada", 0)], q="pool")
            for tb in range(8):
                ps, pk = bank(tb % 4)
                for k in range(8):
                    P.op("pe", lambda e, ps=ps, k=k, tb=tb: e.matmul(ps, lhsT=hT[:, k, tb * 128:(tb + 1) * 128], rhs=wkv[:, k, :], start=(k == 0), stop=(k == 7)),
                         r=[("wada", 0), ("hT", k)], w=[pk])
                kvt = tmpA if tb % 2 == 0 else tmpB
                kvk = "tmpA" if tb % 2 == 0 else "tmpB"
                P.op("act", lambda e, ps=ps, kvt=kvt: e.copy(out=kvt[:, 0:512], in_=ps), r=[pk], w=[kvk])
                P.dma(O["nk"][j, tb * 128:(tb + 1) * 128, :], kvt[:, 0:256], r=[kvk], sem="kvout%d" % (tb % 2))
                P.dma(O["nv"][j, tb * 128:(tb + 1) * 128, :], kvt[:, 256:512], r=[kvk], sem="kvout%d" % (tb % 2))
                P.op("dve", lambda e, ps=ps, tb=tb: e.tensor_copy(out=vlat[:, tb, :, 0, 0:64], in_=ps[:, 256:512].rearrange("p (g d) -> p g d", g=4)), r=[pk], w=["vlat"])
                P.op("dve", lambda e, ps=ps, tb=tb: e.tensor_copy(out=vlat[:, tb, :, 1, 64:128], in_=ps[:, 256:512].rearrange("p (g d) -> p g d", g=4)), r=[pk], w=["vlat"])
            if STAGE.get("attn_upto", 9) < 4:
                return
            dsb = tmpA[:].rearrange("p (a n) -> p a n", a=2)
            osb = [cv[0][:, 0:512], cv[1][:, 0:512]]
            tb_bf = tmpB[:].bitcast(BF16)
            ebuf = [tb_bf[:, i * 512:(i + 1) * 512] for i in range(3)]
            P.alias(["dsb"], ["tmpA"])
            P.alias([("osb", 0)], [("cv", 0)])
            P.alias([("osb", 1)], [("cv", 1)])
            P.alias([("ebuf", i) for i in range(3)], ["tmpB"])
            sc = 0
            for h in range(STAGE.get("nheads", 16)):
                g, qc, pb, var = h // 4, h // 2, (h % 2) * 64, h % 2
                dr = 64 if var == 0 else 0
                qh = aT[pb:pb + 64, qc, :]
                kh = aT[pb:pb + 64, 8 + g, :]
                kch = kctxT[pb:pb + 64, g, :]
                for th in range(2):
                    it = h * 2 + th
                    OP, opk = bank(4 + it % 2)
                    first = True
                    for kb in range(4):
                        ps, pk = bank(sc % 4); eb = ebuf[sc % 3]; ek = ("ebuf", sc % 3); sc += 1
                        P.op("pe", lambda e, ps=ps, kb=kb, th=th, kch=kch, qh=qh: e.matmul(ps, lhsT=kch[:, kb * 128:(kb + 1) * 128], rhs=qh[:, th * 512:(th + 1) * 512], start=True, stop=True),
                             r=["kctxT", ("aT", qc)], w=[pk])
                        P.op("act", lambda e, ps=ps, eb=eb: e.activation(out=eb[:], in_=ps, func=AF.Exp, scale=0.125), r=[pk], w=[ek])
                        P.op("pe", lambda e, OP=OP, eb=eb, kb=kb, g=g, var=var, first=first: e.matmul(OP, lhsT=vctx[:, kb, g, var, :], rhs=eb[:], start=first, stop=False),
                             r=["vctx", ek], w=[opk])
                        first = False
                    jbs = [jb for jb in range(8) if max(jb - 1, 4 * th) <= min(jb + 1, 4 * th + 3)]
                    for jb in jbs:
                        i0 = max(jb - 1, 4 * th); i1 = min(jb + 1, 4 * th + 3)
                        n = (i1 - i0 + 1) * 128
                        ps, pk = bank(sc % 4); eb = ebuf[sc % 3]; ek = ("ebuf", sc % 3); sc += 1
                        P.op("pe", lambda e, ps=ps, jb=jb, i0=i0, n=n, kh=kh, qh=qh: e.matmul(ps[:, 0:n], lhsT=kh[:, jb * 128:(jb + 1) * 128], rhs=qh[:, i0 * 128:i0 * 128 + n], start=True, stop=True),
                             r=[("aT", 8 + g), ("aT", qc)], w=[pk])
                        P.op("act", lambda e, ps=ps, eb=eb, n=n: e.activation(out=eb[:, 0:n], in_=ps[:, 0:n], func=AF.Exp, scale=0.125), r=[pk], w=[ek])
                        for i in range(i0, i1 + 1):
                            if i == jb:
                                continue
                            off = 0 if i == jb + 1 else 1
                            c0 = (i - i0) * 128
                            P.op("dve", lambda e, eb=eb, c0=c0, i=i, off=off: e.tensor_tensor(out=eb[:, c0:c0 + 128], in0=eb[:, c0:c0 + 128], in1=amask[:, i, off, :], op=ALU.mult),
                                 r=[ek, "amask"], w=[ek])
                        o0 = (i0 - 4 * th) * 128
                        P.op("pe", lambda e, OP=OP, eb=eb, jb=jb, g=g, var=var, o0=o0, n=n, jbs=jbs: e.matmul(OP[:, o0:o0 + n], lhsT=vlat[:, jb, g, var, :], rhs=eb[:, 0:n], start=False, stop=(jb == jbs[-1])),
                             r=["vlat", ek], w=[opk])
                    P.op("dve", lambda e, OP=OP, dr=dr, h=h: e.tensor_scalar(out=dsb[dr:dr + 1, 0, :], in0=OP[dr:dr + 1, :], scalar1=esink[dr:dr + 1, h:h + 1], scalar2=None, op0=ALU.add),
                         r=[opk, "esink"], w=["dsb"])
                    P.op("dve", lambda e, dr=dr: e.reciprocal(out=dsb[dr:dr + 1, 1, :], in_=dsb[dr:dr + 1, 0, :]), r=["dsb"], w=["dsb"])
                    BC, bck = bank(6 + it % 2)
                    P.op("pe", lambda e, BC=BC, dr=dr: e.matmul(BC, lhsT=ones_f[dr:dr + 1, :], rhs=dsb[dr:dr + 1, 1, :], start=True, stop=True), r=["dsb", "ones_f"], w=[bck])
                    ob = osb[it % 2]; obk = ("osb", it % 2)
                    P.op("act", lambda e, OP=OP, ob=ob, pb=pb: e.copy(out=ob[pb:pb + 64, :], in_=OP[pb:pb + 64, :]), r=[opk], w=[obk])
                    P.op("dve", lambda e, BC=BC, ob=ob, pb=pb, qc=qc, th=th: e.tensor_tensor(out=aT[pb:pb + 64, qc, th * 512:(th + 1) * 512], in0=ob[pb:pb + 64, :], in1=BC[pb:pb + 64, :], op=ALU.mult),
                         r=[obk, bck], w=[("aT", qc)])
            P.alias(["tmpA"], ["dsb"])
            P.alias([("cv", 0)], [("osb", 0)])
            P.alias([("cv", 1)], [("osb", 1)])
            P.alias(["tmpB"], [("ebuf", i) for i in range(3)])
            for dc in range(8):
                wb = wup[dc % 2]
                P.dma(wb[:, 0], I["wo"][j, dc], w=[("wup", dc % 2, 0)], q="pool")
                for th in range(2):
                    ps, pk = bank((dc * 2 + th) % 4)
                    for k in range(8):
                        P.op("pe", lambda e, ps=ps, k=k, th=th, wb=wb: e.matmul(ps, lhsT=wb[:, 0, k, :], rhs=aT[:, k, th * 512:(th + 1) * 512], start=(k == 0), stop=(k == 7)),
                             r=[("wup", dc % 2, 0), ("aT", k)], w=[pk])
                    P.op("dve", lambda e, ps=ps, dc=dc, th=th: e.scalar_tensor_tensor(out=xT[:, dc, th * 512:(th + 1) * 512], in0=ps, scalar=mod[:, 16 + dc:17 + dc],
                                                                                     in1=xT[:, dc, th * 512:(th + 1) * 512], op0=ALU.mult, op1=ALU.add),
                         r=[pk, "mod", ("xT", dc)], w=[("xT", dc)])

        def hgrn(l):
            CH = 32
            NCH = NT // CH
            P.op("act", lambda e: e.activation(out=hlb[:], in_=hlb[:], func=AF.Exp), r=["hlb"], w=["hlb"])
            P.op("dve", lambda e: e.tensor_tensor(out=lbp[:, 1], in0=hlb[:, 0], in1=hlb[:, 1], op=ALU.add), r=["hlb"], w=["lbp"])
            P.op("dve", lambda e: e.tensor_tensor(out=lbp[:, 1], in0=lbp[:, 1], in1=hlb[:, 2], op=ALU.add), r=["hlb", "lbp"], w=["lbp"])
            P.op("dve", lambda e: e.tensor_tensor(out=lbp[:, 1], in0=lbp[:, 1], in1=hlb[:, 3], op=ALU.add), r=["hlb", "lbp"], w=["lbp"])
            P.op("dve", lambda e: e.reciprocal(out=lbp[:, 1], in_=lbp[:, 1]), r=["lbp"], w=["lbp"])
            P.op("dve", lambda e: e.tensor_tensor(out=lbp[:, 0], in0=lbp[:, 1], in1=hlb[:, 1], op=ALU.mult), r=["hlb", "lbp"], w=["lbp"])
            P.op("dve", lambda e: e.tensor_scalar(out=lbp[:, 1], in0=lbp[:, 0], scalar1=-1.0, scalar2=1.0, op0=ALU.mult, op1=ALU.add), r=["lbp"], w=["lbp"])
            bufQ, bufF, bufK, bufC = cg[0], cg[1], cv[0], cv[1]
            kQ, kF, kK, kC = ("cg", 0), ("cg", 1), ("cv", 0), ("cv", 1)
            oacc = rstd
            kdtok = [vlat[:].rearrange("p a g v n -> p (a g v n)")[:, d * 4096:(d + 1) * 4096].rearrange("p (b c n) -> p b c n", b=8, c=4) for d in range(2)]
            P.alias([("kdtok", 0), ("kdtok", 1)], ["vlat"])
            for h in range(8):
                P.dma(wup[0][:], I["hw_in"][h, 0:2].rearrange("a p k n -> p a k n"), w=[("wup", 0, 0), ("wup", 0, 1)], q="pool")
                P.dma(wup[1][:], I["hw_in"][h, 2:4].rearrange("a p k n -> p a k n"), w=[("wup", 1, 0), ("wup", 1, 1)], q="pool")
                P.dma(wdn[0][:, 0:8, :], I["hw_in"][h, 4], w=[("wdn", 0)], q="pool")

                def proj(wap, wkeys, bi):
                    outs = []
                    for th in range(2):
                        ps, pk = bank(bi * 2 + th)
                        for kk in range(8):
                            P.op("pe", lambda e, ps=ps, kk=kk, th=th, wap=wap: e.matmul(ps, lhsT=wap[:, kk, :], rhs=hT[:, kk, th * 512:(th + 1) * 512], start=(kk == 0), stop=(kk == 7)),
                                 r=list(wkeys) + [("hT", kk)], w=[pk])
                        outs.append((ps, pk))
                    return outs
                for th, (ps, pk) in enumerate(proj(wup[0][:, 0], [("wup", 0, 0)], 0)):
                    P.op("act", lambda e, ps=ps, th=th: e.copy(out=bufQ[:, th * 512:(th + 1) * 512], in_=ps), r=[pk], w=[kQ])
                for blk in range(8):
                    ps, pk = bank(2 + blk % 2)
                    for kk in range(8):
                        P.op("pe", lambda e, ps=ps, kk=kk, blk=blk: e.matmul(ps[:, 0:128], lhsT=hT[:, kk, blk * 128:(blk + 1) * 128], rhs=wup[0][:, 1, kk, :], start=(kk == 0), stop=(kk == 7)),
                             r=[("wup", 0, 1), ("hT", kk)], w=[pk])
                    P.op("act", lambda e, ps=ps, blk=blk: e.copy(out=vtok[:, blk, :], in_=ps[:, 0:128]), r=[pk], w=["vtok"])
                for d in range(2):
                    for th, (ps, pk) in enumerate(proj(wup[1][:, d], [("wup", 1, d)], 2 + d)):
                        P.op("act", lambda e, ps=ps, th=th: e.activation(out=bufF[:, th * 512:(th + 1) * 512], in_=ps, func=AF.Sigmoid), r=[pk], w=[kF])
                    P.op("dve", lambda e, d=d, h=h: e.tensor_scalar(out=bufF[:], in0=bufF[:], scalar1=lbp[:, 1, d, h:h + 1], scalar2=lbp[:, 0, d, h:h + 1], op0=ALU.mult, op1=ALU.add),
                         r=[kF, "lbp"], w=[kF])
                    P.op("dve", lambda e: e.tensor_scalar(out=bufK[:], in0=bufF[:], scalar1=-1.0, scalar2=1.0, op0=ALU.mult, op1=ALU.add), r=[kF], w=[kK])
                    P.op("act", lambda e: e.activation(out=bufF[:], in_=bufF[:], func=AF.Ln), r=[kF], w=[kF])
                    P.op("dve", lambda e: e.tensor_tensor_scan(out=bufC[:], data0=m32[:], data1=bufF[:], initial=0.0, op0=ALU.mult, op1=ALU.add), r=["m32", kF], w=[kC])
                    if d == 1:
                        P.op("dve", lambda e: e.scalar_tensor_tensor(out=tmpA[:], in0=bufC[:], scalar=-1.0, in1=bufF[:], op0=ALU.mult, op1=ALU.add), r=[kC, kF], w=["tmpA"])
                        P.op("act", lambda e: e.copy(out=ctmp[:], in_=bufC[:, CH - 1:NT:CH]), r=[kC], w=["ctmp"])
                        P.op("dve", lambda e: e.tensor_tensor(out=bufC[:].rearrange("p (c t) -> p c t", t=CH), in0=tmpA[:].rearrange("p (c t) -> p c t", t=CH),
                                                              in1=ctmp[:].unsqueeze(2).to_broadcast([128, NCH, CH]), op=ALU.add),
                             r=["tmpA", "ctmp"], w=[kC])
                        ctot = bufC[:, 0:NT:CH]
                    else:
                        ctot = bufC[:, CH - 1:NT:CH]
                    P.op("act", lambda e, d=d, ctot=ctot: e.activation(out=adec[d][:], in_=ctot, func=AF.Exp), r=[kC], w=[("adec", d)])
                    P.op("act", lambda e: e.activation(out=tmpA[:], in_=bufC[:], func=AF.Exp), r=[kC], w=["tmpA"])
                    P.op("dve", lambda e, d=d: e.tensor_tensor(out=qhat[d][:], in0=bufQ[:], in1=tmpA[:], op=ALU.mult), r=[kQ, "tmpA"], w=[("qhat", d)])
                    P.op("dve", lambda e: e.tensor_scalar(out=tmpB[:], in0=bufC[:], scalar1=-1.0, scalar2=85.0, op0=ALU.mult, op1=ALU.min), r=[kC], w=["tmpB"])
                    P.op("act", lambda e: e.activation(out=tmpB[:], in_=tmpB[:], func=AF.Exp), r=["tmpB"], w=["tmpB"])
                    P.op("dve", lambda e: e.tensor_tensor(out=tmpB[:], in0=tmpB[:], in1=bufK[:], op=ALU.mult), r=["tmpB", kK], w=["tmpB"])
                    P.op("act", lambda e, d=d: e.copy(out=ktil[d][:], in_=tmpB[:]), r=["tmpB"], w=[("ktil", d)])
                    P.op("dve", lambda e, d=d: e.tensor_tensor(out=kdT[d][:].rearrange("p (c t) -> p c t", t=CH), in0=tmpB[:].rearrange("p (c t) -> p c t", t=CH),
                                                                in1=adec[d][:].unsqueeze(2).to_broadcast([128, NCH, CH]), op=ALU.mult),
                         r=["tmpB", ("adec", d)], w=[("kdT", d)])
                    for blk in range(8):
                        ps, pk = bank(6 + blk % 2)
                        pst = ps.bitcast(BF16)
                        P.op("pe", lambda e, pst=pst, blk=blk, d=d: e.transpose(pst[:, 0:128], kdT[d][:, blk * 128:(blk + 1) * 128], ident_bf[:]), r=[("kdT", d), "ident_bf"], w=[pk])
                        for c4 in range(4):
                            P.op("act", lambda e, pst=pst, blk=blk, c4=c4, d=d: e.activation(out=kdtok[d][:, blk, c4, :], in_=pst[:, 0:128], func=AF.Identity, scale=rowm[:, c4:c4 + 1]),
                                 r=[pk, "rowm"], w=[("kdtok", d)])
                    P.dma(Sf[d][:], I["hs0"][d, h], w=[("Sf", d)])
                    P.op("act", lambda e, d=d: e.copy(out=Sb[d][:], in_=Sf[d][:]), r=[("Sf", d)], w=[("Sb", d)])
                P.op("pool", lambda e: e.memset(oacc[:], 0.0), r=["rstd"], w=["rstd"])
                for bi_ in range(8):
                    for d in range(2):
                        blk = bi_ if d == 0 else 7 - bi_
                        bsl = slice(blk * 128, (blk + 1) * 128)
                        aps, apk = bank(0 + d * 2)
                        ops_, opk = bank(1 + d * 2)
                        dps, dpk = bank(4 + d)
                        P.op("pe", lambda e, aps=aps, d=d, bsl=bsl: e.matmul(aps[:, 0:128], lhsT=ktil[d][:, bsl], rhs=qhat[d][:, bsl], start=True, stop=True),
                             r=[("ktil", d), ("qhat", d)], w=[apk])
                        am = attm[d]
                        P.op("dve", lambda e, aps=aps, am=am, d=d: e.tensor_tensor(out=am[:], in0=aps[:, 0:128], in1=hmask[:, d, :], op=ALU.mult), r=[apk, "hmask"], w=[("attm", d)])
                        P.op("pe", lambda e, ops_=ops_, am=am, blk=blk: e.matmul(ops_[:, 0:128], lhsT=vtok[:, blk, :], rhs=am[:], start=True, stop=False),
                             r=["vtok", ("attm", d)], w=[opk])
                        for c4 in range(4):
                            P.op("pe", lambda e, dps=dps, c4=c4, blk=blk, d=d: e.matmul(dps[:, c4 * 128:(c4 + 1) * 128], lhsT=kdtok[d][:, blk, c4, :], rhs=vtok[:, blk, :], start=True, stop=True),
                                 r=[("kdtok", d), "vtok"], w=[dpk])
                        cs = range(4) if d == 0 else range(3, -1, -1)
                        for c4 in cs:
                            ch = blk * 4 + c4
                            csl = slice(ch * CH, (ch + 1) * CH)
                            P.op("pe", lambda e, ops_=ops_, c4=c4, csl=csl, d=d, cs=cs: e.matmul(ops_[:, c4 * CH:(c4 + 1) * CH], lhsT=Sb[d][:], rhs=qhat[d][:, csl], start=False, stop=(c4 == list(cs)[-1])),
                                 r=[("Sb", d), ("qhat", d)], w=[opk])
                            P.op("dve", lambda e, dps=dps, c4=c4, ch=ch, d=d: e.scalar_tensor_tensor(out=Sf[d][:], in0=Sf[d][:], scalar=adec[d][:, ch:ch + 1], in1=dps[:, c4 * 128:(c4 + 1) * 128], op0=ALU.mult, op1=ALU.add),
                                 r=[("Sf", d), ("adec", d), dpk], w=[("Sf", d)])
                            seg_end = (ch % 8 == 7) if d == 0 else (ch % 8 == 0)
                            if seg_end:
                                seg = ch // 8
                                P.dma(O["hg"][(seg * 2 + d) * 8 + h], Sf[d][:], r=[("Sf", d)], sem="hgout%d" % d)
                                P.op("dve", lambda e, d=d: e.tensor_scalar(out=Sf[d][:], in0=Sf[d][:], scalar1=keep[:, 0:1], scalar2=None, op0=ALU.mult), r=[("Sf", d), "keep"], w=[("Sf", d)])
                            P.op("act", lambda e, d=d: e.copy(out=Sb[d][:], in_=Sf[d][:]), r=[("Sf", d)], w=[("Sb", d)])
                        P.op("dve", lambda e, ops_=ops_, bsl=bsl: e.tensor_tensor(out=oacc[:, bsl], in0=ops_[:, 0:128], in1=oacc[:, bsl], op=ALU.add), r=[opk, "rstd"], w=["rstd"])
                P.op("act", lambda e: e.activation(out=sq[0][:], in_=oacc[:], func=AF.Square), r=["rstd"], w=[("sq", 0)])
                for th in range(2):
                    ps, pk = bank(6 + th)
                    P.op("pe", lambda e, ps=ps, th=th: e.matmul(ps, lhsT=ones_bf[:], rhs=sq[0][:, th * 512:(th + 1) * 512], start=True, stop=True), r=[("sq", 0), "ones_bf"], w=[pk])
                    P.op("act", lambda e, ps=ps, th=th: e.activation(out=tmpA[:, th * 512:(th + 1) * 512], in_=ps, func=AF.Sqrt, scale=1.0 / 128, bias=eps_t[:, 0:1]), r=[pk, "eps"], w=["tmpA"])
                P.op("dve", lambda e: e.reciprocal(out=tmpA[:], in_=tmpA[:]), r=["tmpA"], w=["tmpA"])
                P.op("dve", lambda e: e.scalar_tensor_tensor(out=tmpA[:], in0=oacc[:], scalar=hgn[:, 0:1], in1=tmpA[:], op0=ALU.mult, op1=ALU.mult), r=["rstd", "hgn", "tmpA"], w=["tmpA"])
                for th, (ps, pk) in enumerate(proj(wdn[0][:, 0:8, :], [("wdn", 0)], 2)):
                    P.op("act", lambda e, ps=ps, th=th: e.activation(out=tmpB[:, th * 512:(th + 1) * 512], in_=ps, func=AF.Silu), r=[pk], w=["tmpB"])
                P.op("dve", lambda e, h=h: e.tensor_tensor(out=aT[:, h, :], in0=tmpA[:], in1=tmpB[:], op=ALU.mult), r=["tmpA", "tmpB"], w=[("aT", h)])
            P.alias(["vlat"], [("kdtok", 0), ("kdtok", 1)])
            P.op("dve", lambda e: e.memset(vlat[:], 0.0), w=["vlat"])
            P.op("dve", lambda e: e.memset(vlat[:, :, :, 0, 64:65], 1.0), w=["vlat"])
            P.op("dve", lambda e: e.memset(vlat[:, :, :, 1, 0:1], 1.0), w=["vlat"])
            for dc in range(8):
                wb = wup[dc % 2]
                P.dma(wb[:, 0], I["hwo"][dc], w=[("wup", dc % 2, 0)], q="pool")
                for th in range(2):
                    ps, pk = bank((dc * 2 + th) % 4)
                    for kk in range(8):
                        P.op("pe", lambda e, ps=ps, kk=kk, th=th, wb=wb: e.matmul(ps, lhsT=wb[:, 0, kk, :], rhs=aT[:, kk, th * 512:(th + 1) * 512], start=(kk == 0), stop=(kk == 7)),
                             r=[("wup", dc % 2, 0), ("aT", kk)], w=[pk])
                    P.op("dve", lambda e, ps=ps, dc=dc, th=th: e.scalar_tensor_tensor(out=xT[:, dc, th * 512:(th + 1) * 512], in0=ps, scalar=mod[:, 16 + dc:17 + dc],
                                                                                     in1=xT[:, dc, th * 512:(th + 1) * 512], op0=ALU.mult, op1=ALU.add),
                         r=[pk, "mod", ("xT", dc)], w=[("xT", dc)])

        def ssm(l):
            SEG = 256
            rs32 = rstd[:]
            sr_ = [rs32[:, i * 64:(i + 1) * 64] for i in range(16)]
            sq32 = sq[0][:].bitcast(F32)
            sq_ = [sq32[:, i * 32:(i + 1) * 32] for i in range(14)]
            sCq = [sq[1][:, i * 512:(i + 1) * 512].rearrange("p (g i) -> p g i", g=32) for i in range(2)]
            P.alias(["srr"], ["rstd"])
            P.alias(["sqq"], [("sq", 0)])
            P.alias(["sCq"], [("sq", 1)])
            m256 = m32
            P.op("pool", lambda e: e.memset(m256[:], 1.0), r=["m32"], w=["m32"])
            P.op("pool", lambda e: e.memset(m256[:, 0:NT:256], 0.0), r=["m32"], w=["m32"])
            P.dma(smask[:], I["smask"], w=["smask"])
            P.dma(sD[:], I["sD"], w=["sD"])
            TT = lambda e, o, a, b, op: e.tensor_tensor(out=o, in0=a, in1=b, op=op)

            def vop(eng, o, a, b, op, r, w):
                P.op(eng, lambda e, o=o, a=a, b=b, op=op: e.tensor_tensor(out=o, in0=a, in1=b, op=op), r=r, w=w)

            def cmul(eng, ore, oim, are_, aim_, bre, bim, t1, t2, r, w, tk):
                vop(eng, t1, are_, bre, ALU.mult, r, [tk[0]])
                vop(eng, t2, aim_, bim, ALU.mult, r, [tk[1]])
                vop(eng, ore, t1, t2, ALU.subtract, [tk[0], tk[1]], w)
                vop(eng, t1, are_, bim, ALU.mult, r, [tk[0]])
                vop(eng, t2, aim_, bre, ALU.mult, r, [tk[1]])
                vop(eng, oim, t1, t2, ALU.add, [tk[0], tk[1]], w)

            def lam_params(are_ap, aim_ap, dt_scalar_or_ap, S, key, n, per_part_dt, need_inv=True, eng="dve"):
                arec, th, mag, c, s_, t1, t2, imag = S[0], S[1], S[2], S[3], S[4], S[5], S[6], S[7]
                K = [key]
                P.op(eng, lambda e: e.tensor_scalar(out=arec[:], in0=are_ap, scalar1=-1e-4, scalar2=None, op0=ALU.min), r=K, w=K)
                if per_part_dt:
                    dtx = S[8]
                    P.op("act", lambda e: e.activation(out=dtx[:, 0:1], in_=dt_scalar_or_ap, func=AF.Exp), r=K, w=K)
                    P.op(eng, lambda e: e.tensor_scalar(out=th[:], in0=aim_ap, scalar1=dtx[:, 0:1], scalar2=None, op0=ALU.mult), r=K, w=K)
                    P.op(eng, lambda e: e.tensor_scalar(out=mag[:], in0=arec[:], scalar1=dtx[:, 0:1], scalar2=None, op0=ALU.mult), r=K, w=K)
                else:
                    dtx = S[8]
                    P.op("act", lambda e: e.activation(out=dtx[:], in_=dt_scalar_or_ap, func=AF.Exp), r=K, w=K)
                    vop(eng, th[:], aim_ap, dtx[:], ALU.mult, K, K)
                    vop(eng, mag[:], arec[:], dtx[:], ALU.mult, K, K)
                P.op("act", lambda e: e.activation(out=imag[:], in_=mag[:], func=AF.Exp, scale=-1.0), r=K, w=K)
                P.op("act", lambda e: e.activation(out=mag[:], in_=mag[:], func=AF.Exp), r=K, w=K)
                P.op("act", lambda e: e.activation(out=s_[:], in_=th[:], func=AF.Sin, scale=1.0 / 64), r=K, w=K)
                P.op("act", lambda e: e.activation(out=c[:], in_=th[:], func=AF.Sin, scale=1.0 / 64, bias=halfpi[:, 0:1]), r=K + ["halfpi"], w=K)
                for _ in range(6):
                    vop(eng, t1[:], c[:], s_[:], ALU.mult, K, K)
                    vop(eng, c[:], c[:], c[:], ALU.mult, K, K)
                    vop(eng, s_[:], s_[:], s_[:], ALU.mult, K, K)
                    vop(eng, c[:], c[:], s_[:], ALU.subtract, K, K)
                    P.op(eng, lambda e: e.tensor_scalar(out=s_[:], in0=t1[:], scalar1=2.0, scalar2=None, op0=ALU.mult), r=K, w=K)
                L1re, L1im, Lm1re, Lm1im = S[9], S[10], S[11], S[12]
                vop(eng, L1re[:], mag[:], c[:], ALU.mult, K, K)
                vop(eng, L1im[:], mag[:], s_[:], ALU.mult, K, K)
                if need_inv:
                    vop(eng, Lm1re[:], imag[:], c[:], ALU.mult, K, K)
                    vop(eng, Lm1im[:], imag[:], s_[:], ALU.mult, K, K)
                    P.op(eng, lambda e: e.tensor_scalar(out=Lm1im[:], in0=Lm1im[:], scalar1=-1.0, scalar2=None, op0=ALU.mult), r=K, w=K)
                return dict(L1re=L1re, L1im=L1im, Lm1re=Lm1re, Lm1im=Lm1im, are=arec)

            A_, B_, C_, D_, Gr, Gi = cg[0], cg[1], cv[0], cv[1], tmpA, tmpB
            kA, kB, kC2, kD, kGr, kGi = ("cg", 0), ("cg", 1), ("cv", 0), ("cv", 1), "tmpA", "tmpB"
            vl = vlat[:].rearrange("p a g v n -> p (a g v n)").bitcast(F32)
            Tre_all = vl[:, 0:2048].rearrange("p (g t) -> p g t", g=8)
            Tim_all = vl[:, 2048:4096].rearrange("p (g t) -> p g t", g=8)
            Tp_re, Tm_re, Tp_im, Tm_im = Tre_all[:, 0:4], Tre_all[:, 4:8], Tim_all[:, 0:4], Tim_all[:, 4:8]
            P.alias(["stab"], ["vlat"])
            vc = vctx[:].rearrange("p a g v n -> p (a g v n)")
            W1pad = vc[:, 0:4096].rearrange("p (q r n) -> p q r n", q=16, r=2)
            kcf = kctxT[:].rearrange("p a n -> p (a n)")
            Ewpad = kcf[:, 0:2048].rearrange("p (q r n) -> p q r n", q=8, r=2)
            P.alias(["W1pad"], ["vctx"])
            P.alias(["Ewpad"], ["kctxT"])
            Hre_b = qhat[0][:].rearrange("p (g t) -> p g t", g=4)
            Him_b = qhat[1][:].rearrange("p (g t) -> p g t", g=4)
            ysb = aT

            def _body():
              for d in range(2):
                  P.dma(sq_[0], I["sAreQ"][d], w=["sqq"])
                  P.dma(sq_[1], I["sAimQ"][d], w=["sqq"])
                  P.dma(sq_[13], I["sDtQ"][d], w=["sqq"])
                  P.dma(sCq[0], I["sCreQ"][d], w=["sCq"], q="pool")
                  P.dma(sCq[1], I["sCimQ"][d], w=["sCq"], q="pool")
                  P.dma(sH[:], I["sH0"][d], w=["sH"])
                  Q = lam_params(sq_[0], sq_[1], sq_[13], sq_[2:13] + [sq_[0], sq_[1]], "sqq", 32, False)
                  for s in range(8):
                      k0, k1 = s // 2, 4 + s // 2
                      g8b = (4 * s) % 8
                      P.op("pool", lambda e: e.memset(kcf[:, 0:2048], 0.0), r=["Ewpad"], w=["Ewpad"])
                      if s % 2 == 0:
                          P.op("pool", lambda e: e.memset(vc[:], 0.0), r=["W1pad"], w=["W1pad"])
                      if s % 2 == 0:
                          for half, kk_ in enumerate((k0, k1)):
                              Rk = ["srr"]
                              P.dma(sr_[0], I["sAreR"][d, kk_], w=Rk)
                              P.dma(sr_[1], I["sAimR"][d, kk_], w=Rk)
                              P.dma(sr_[13][:, 0:1], I["sDtR"][d, kk_], w=Rk)
                              P.dma(sr_[14], I["sBreR"][d, kk_], w=Rk)
                              P.dma(sr_[15], I["sBimR"][d, kk_], w=Rk)
                              R_ = lam_params(sr_[0], sr_[1], sr_[13][:, 0:1], sr_[2:13] + [sr_[0], sr_[0]], "srr", 64, True, need_inv=False)
                              nre, den, cre, cim, t1, t2 = sr_[3], sr_[4], sr_[5], sr_[6], sr_[7], sr_[8]
                              aim_ = sr_[1]
                              P.op("dve", lambda e, R_=R_: e.tensor_scalar(out=nre[:], in0=R_["L1re"][:], scalar1=-1.0, scalar2=None, op0=ALU.add), r=Rk, w=Rk)
                              vop("dve", den[:], R_["are"][:], R_["are"][:], ALU.mult, Rk, Rk)
                              vop("dve", t1[:], aim_[:], aim_[:], ALU.mult, Rk, Rk)
                              vop("dve", den[:], den[:], t1[:], ALU.add, Rk, Rk)
                              P.op("dve", lambda e: e.reciprocal(out=den[:], in_=den[:]), r=Rk, w=Rk)
                              vop("dve", t1[:], nre[:], R_["are"][:], ALU.mult, Rk, Rk)
                              vop("dve", t2[:], R_["L1im"][:], aim_[:], ALU.mult, Rk, Rk)
                              vop("dve", cre[:], t1[:], t2[:], ALU.add, Rk, Rk)
                              vop("dve", cre[:], cre[:], den[:], ALU.mult, Rk, Rk)
                              vop("dve", t1[:], R_["L1im"][:], R_["are"][:], ALU.mult, Rk, Rk)
                              vop("dve", t2[:], nre[:], aim_[:], ALU.mult, Rk, Rk)
                              vop("dve", cim[:], t1[:], t2[:], ALU.subtract, Rk, Rk)
                              vop("dve", cim[:], cim[:], den[:], ALU.mult, Rk, Rk)
                              wre, wim = sr_[9], sr_[10]
                              cmul("dve", wre[:], wim[:], cre[:], cim[:], sr_[14][:], sr_[15][:], t1[:], t2[:], Rk, Rk, ["srr", "srr"])
                              for g8 in range(8):
                                  for ri, wsrc in enumerate((wre, wim)):
                                      P.op("act", lambda e, half=half, ri=ri, wsrc=wsrc, g8=g8: e.activation(out=W1pad[:, half * 8 + g8, ri, half * 64:(half + 1) * 64], in_=wsrc[:], func=AF.Identity, scale=smask[:, g8:g8 + 1]),
                                           r=Rk + ["smask"], w=["W1pad"])
                      for half, kk_ in enumerate((k0, k1)):
                          for q in range(4):
                              g8 = g8b + q
                              gq = 4 * s + q
                              P.op("act", lambda e, q=q, half=half, g8=g8, gq=gq: e.activation(out=Ewpad[:, half * 4 + q, 0, g8 * 16:(g8 + 1) * 16], in_=sCq[0][:, gq, :], func=AF.Identity, scale=smask[:, 8 + half:9 + half]),
                                   r=["sCq", "smask"], w=["Ewpad"])
                              P.op("act", lambda e, q=q, half=half, g8=g8, gq=gq: e.activation(out=Ewpad[:, half * 4 + q, 1, g8 * 16:(g8 + 1) * 16], in_=sCq[1][:, gq, :], func=AF.Identity, scale=smask[:, 10 + half:11 + half]),
                                   r=["sCq", "smask"], w=["Ewpad"])
                      gsl = slice(4 * s, 4 * s + 4)
                      i0 = 0 if d == 0 else SEG - 1
                      for (Tre, Tim, bre_, bim_) in ((Tp_re, Tp_im, Q["L1re"], Q["L1im"]), (Tm_re, Tm_im, Q["Lm1re"], Q["Lm1im"])):
                          P.op("dve", lambda e, Tre=Tre, bre_=bre_, i0=i0, gsl=gsl: e.tensor_copy(out=Tre[:, :, i0:i0 + 1], in_=bre_[:, gsl].unsqueeze(2)), r=["sqq"], w=["stab"])
                          P.op("dve", lambda e, Tim=Tim, bim_=bim_, i0=i0, gsl=gsl: e.tensor_copy(out=Tim[:, :, i0:i0 + 1], in_=bim_[:, gsl].unsqueeze(2)), r=["sqq"], w=["stab"])
                      L = 1
                      while L < SEG:
                          if d == 0:
                              src = slice(0, L); dst = slice(L, 2 * L); piv = L - 1
                          else:
                              src = slice(SEG - L, SEG); dst = slice(SEG - 2 * L, SEG - L); piv = SEG - L
                          zr = Tre_all[:, :, piv:piv + 1].to_broadcast([128, 8, L])
                          zi = Tim_all[:, :, piv:piv + 1].to_broadcast([128, 8, L])
                          cmul("dve", Tre_all[:, :, dst], Tim_all[:, :, dst], Tre_all[:, :, src], Tim_all[:, :, src], zr, zi,
                               A_[:, 0:8 * L].rearrange("p (g t) -> p g t", g=8), B_[:, 0:8 * L].rearrange("p (g t) -> p g t", g=8), ["stab"], ["stab"], [kA, kB])
                          L *= 2
                      P.op("dve", lambda e, gsl=gsl: e.tensor_copy(out=sHent[:], in_=sH[:, gsl, :]), r=["sH"], w=["sHent"])
                      segs = range(4) if d == 0 else range(3, -1, -1)
                      for seg in segs:
                          tsl = slice(seg * SEG, (seg + 1) * SEG)
                          Sre, Sim = pq[0], pq[1]
                          skr = [("pq", 0), ("pq", 1)]; ski = [("pq", 2), ("pq", 3)]
                          for ri, (St, sk) in enumerate(((Sre, skr), (Sim, ski))):
                              for q in range(4):
                                  for half, kk_ in enumerate((k0, k1)):
                                      P.op("pe", lambda e, St=St, q=q, half=half, kk_=kk_, ri=ri, tsl=tsl, g8b=g8b: e.matmul(St[:, q * SEG:(q + 1) * SEG], lhsT=W1pad[:, half * 8 + g8b + q, ri, :], rhs=hT[:, kk_, tsl], start=(half == 0), stop=(half == 1)),
                                           r=["W1pad", ("hT", kk_)], w=[sk[q // 2]])
                          S3r = Sre[:].rearrange("p (g t) -> p g t", g=4); S3i = Sim[:].rearrange("p (g t) -> p g t", g=4)
                          A3, B3, C3, D3 = [x[:].rearrange("p (g t) -> p g t", g=4) for x in (A_, B_, C_, D_)]
                          G3r, G3i = Gr[:].rearrange("p (g t) -> p g t", g=4), Gi[:].rearrange("p (g t) -> p g t", g=4)
                          vop("dve", A3, S3r, Tm_re, ALU.mult, skr + ["stab"], [kA])
                          vop("dve", B3, S3i, Tm_im, ALU.mult, ski + ["stab"], [kB])
                          vop("dve", A3, A3, B3, ALU.subtract, [kA, kB], [kA])
                          vop("dve", C3, S3i, Tm_re, ALU.mult, ski + ["stab"], [kC2])
                          vop("dve", D3, S3r, Tm_im, ALU.mult, skr + ["stab"], [kD])
                          vop("dve", C3, C3, D3, ALU.add, [kC2, kD], [kC2])
                          for (src_, dstt, ks, kd_) in ((A_, Gr, kA, kGr), (C_, Gi, kC2, kGi)):
                              P.op("dve", lambda e, src_=src_, dstt=dstt: e.tensor_tensor_scan(out=dstt[:], data0=m256[:], data1=src_[:], initial=0.0, op0=ALU.mult, op1=ALU.add), r=["m32", ks], w=[kd_])
                              if d == 1:
                                  s3 = src_[:].rearrange("p (g t) -> p g t", g=4); d3 = dstt[:].rearrange("p (g t) -> p g t", g=4)
                                  P.op("act", lambda e, dstt=dstt: e.copy(out=ctmp[:, 0:4], in_=dstt[:, SEG - 1:NT:SEG]), r=[kd_], w=["ctmp"])
                                  P.op("dve", lambda e, s3=s3, d3=d3: e.tensor_tensor(out=d3, in0=s3, in1=d3, op=ALU.subtract), r=[ks, kd_], w=[kd_])
                                  P.op("dve", lambda e, d3=d3: e.tensor_tensor(out=d3, in0=d3, in1=ctmp[:, 0:4].unsqueeze(2).to_broadcast([128, 4, SEG]), op=ALU.add), r=[kd_, "ctmp"], w=[kd_])
                          vop("dve", G3r, G3r, sHent[:, :, 0:1].to_broadcast([128, 4, SEG]), ALU.add, [kGr, "sHent"], [kGr])
                          vop("dve", G3i, G3i, sHent[:, :, 1:2].to_broadcast([128, 4, SEG]), ALU.add, [kGi, "sHent"], [kGi])
                          vop("dve", A3, G3r, Tp_re, ALU.mult, [kGr, "stab"], [kA])
                          vop("dve", B3, G3i, Tp_im, ALU.mult, [kGi, "stab"], [kB])
                          vop("dve", Hre_b, A3, B3, ALU.subtract, [kA, kB], [("qhat", 0)])
                          vop("dve", C3, G3r, Tp_im, ALU.mult, [kGr, "stab"], [kC2])
                          vop("dve", D3, G3i, Tp_re, ALU.mult, [kGi, "stab"], [kD])
                          vop("dve", Him_b, C3, D3, ALU.add, [kC2, kD], [("qhat", 1)])
                          xi = SEG - 1 if d == 0 else 0
                          vop("dve", sHx[:, :, 0:1], A3[:, :, xi:xi + 1], B3[:, :, xi:xi + 1], ALU.subtract, [kA, kB], ["sHx"])
                          vop("dve", sHx[:, :, 1:2], C3[:, :, xi:xi + 1], D3[:, :, xi:xi + 1], ALU.add, [kC2, kD], ["sHx"])
                          for half in range(2):
                              P.dma(O["ssm"][seg * 2 + d, half * 32 + 4 * s: half * 32 + 4 * s + 4].rearrange("g p r -> p g r"), sHx[half * 64:(half + 1) * 64, :, :], r=["sHx"], sem="ssmout")
                          P.op("dve", lambda e: e.tensor_scalar(out=sHent[:], in0=sHx[:], scalar1=keep[:, 0:1], scalar2=None, op0=ALU.mult), r=["sHx", "keep"], w=["sHent"])
                          if STAGE.get("ssm_dbg"):
                              P.dma(O["dbg2"][:, 0:1024], A_[:], r=[kA], sem="dbg")
                              P.dma(O["dbg2"][:, 1024:2048], B_[:], r=[kB], sem="dbg")
                              P.dma(O["dbg2"][:, 2048:3072], C_[:], r=[kC2], sem="dbg")
                              P.dma(O["dbg2"][:, 3072:4096], D_[:], r=[kD], sem="dbg")
                              P.dma(O["dbg2"][:, 4096:5120], Gr[:], r=[kGr], sem="dbg")
                              P.dma(O["dbg2"][:, 5120:6144], Gi[:], r=[kGi], sem="dbg")
                              P.dma(O["dbg3"], hT[:].rearrange("p k t -> p (k t)"), r=[("hT", kk) for kk in range(8)], sem="dbg")
                              P.dma(O["dbg"][:, 0:4096], vl, r=["stab"], sem="dbg")
                              P.dma(O["dbg"][:, 4096:5120], rs32, r=["srr"], sem="dbg")
                              P.dma(O["dbg"][:, 5120:5632], sq32, r=["sqq"], sem="dbg")
                              raise StopIteration
                          for half, kk_ in enumerate((k0, k1)):
                              yp_, ypk = bank(4 + half)
                              n = 0
                              for q in range(4):
                                  for ri, Hb in enumerate((Hre_b, Him_b)):
                                      P.op("pe", lambda e, yp_=yp_, q=q, half=half, ri=ri, Hb=Hb, n=n: e.matmul(yp_[:, 0:SEG], lhsT=Ewpad[:, half * 4 + q, ri, :], rhs=Hb[:, q, :], start=(n == 0), stop=(n == 7)),
                                           r=["Ewpad", ("qhat", ri)], w=[ypk])
                                      n += 1
                              first = (d == 0 and s % 2 == 0)
                              if first:
                                  P.op("dve", lambda e, yp_=yp_, kk_=kk_, tsl=tsl: e.scalar_tensor_tensor(out=ysb[:, kk_, tsl], in0=hT[:, kk_, tsl], scalar=sD[:, kk_:kk_ + 1], in1=yp_[:, 0:SEG], op0=ALU.mult, op1=ALU.add),
                                       r=[ypk, ("hT", kk_), "sD"], w=[("aT", kk_)])
                              else:
                                  P.op("dve", lambda e, yp_=yp_, kk_=kk_, tsl=tsl: e.tensor_tensor(out=ysb[:, kk_, tsl], in0=yp_[:, 0:SEG], in1=ysb[:, kk_, tsl], op=ALU.add),
                                       r=[ypk, ("aT", kk_)], w=[("aT", kk_)])

            try:
                _body()
            except StopIteration:
                pass
            P.alias(["vlat"], ["stab"])
            P.alias(["vctx"], ["W1pad"])
            P.alias(["kctxT"], ["Ewpad"])
            P.alias(["rstd"], ["srr"])
            P.alias([("sq", 0)], ["sqq"])
            P.alias([("sq", 1)], ["sCq"])
            P.op("dve", lambda e: e.memset(vlat[:], 0.0), w=["vlat"])
            P.op("dve", lambda e: e.memset(vlat[:, :, :, 0, 64:65], 1.0), w=["vlat"])
            P.op("dve", lambda e: e.memset(vlat[:, :, :, 1, 0:1], 1.0), w=["vlat"])
            for kk_ in range(8):
                yk = ysb[:, kk_, :]
                P.op("dve", lambda e, yk=yk: e.tensor_tensor(out=A_[:], in0=yk, in1=yk, op=ALU.mult), r=[("aT", kk_)], w=[kA])
                P.op("dve", lambda e: e.tensor_scalar(out=A_[:], in0=A_[:], scalar1=0.044715, scalar2=1.0, op0=ALU.mult, op1=ALU.add), r=[kA], w=[kA])
                P.op("pool", lambda e, yk=yk: e.tensor_tensor(out=A_[:], in0=A_[:], in1=yk, op=ALU.mult), r=[kA, ("aT", kk_)], w=[kA])
                P.op("act", lambda e: e.activation(out=A_[:], in_=A_[:], func=AF.Sigmoid, scale=1.5957691216057308), r=[kA], w=[kA])
                P.op("pool", lambda e, yk=yk: e.tensor_tensor(out=yk, in0=A_[:], in1=yk, op=ALU.mult), r=[kA, ("aT", kk_)], w=[("aT", kk_)])
            for dc in range(8):
                wb = wup[dc % 2]
                P.dma(wb[:, 0], I["wglu"][dc], w=[("wup", dc % 2, 0)], q="pool")
                P.dma(wb[:, 1], I["wglu"][8 + dc], w=[("wup", dc % 2, 1)], q="pool")
                for th in range(2):
                    vps, vpk = bank((dc * 2 + th) % 4)
                    gps, gpk = bank(4 + (dc * 2 + th) % 4)
                    for gv, (ps, pk) in enumerate(((vps, vpk), (gps, gpk))):
                        for kk_ in range(8):
                            P.op("pe", lambda e, ps=ps, kk_=kk_, th=th, wb=wb, gv=gv: e.matmul(ps, lhsT=wb[:, gv, kk_, :], rhs=ysb[:, kk_, th * 512:(th + 1) * 512], start=(kk_ == 0), stop=(kk_ == 7)),
                                 r=[("wup", dc % 2, gv), ("aT", kk_)], w=[pk])
                    sl = slice(th * 512, (th + 1) * 512)
                    P.op("act", lambda e, gps=gps, sl=sl: e.activation(out=B_[:, sl], in_=gps, func=AF.Sigmoid), r=[gpk], w=[kB])
                    P.op("dve", lambda e, vps=vps, sl=sl: e.tensor_tensor(out=B_[:, sl], in0=vps, in1=B_[:, sl], op=ALU.mult), r=[vpk, kB], w=[kB])
                    P.op("dve", lambda e, dc=dc, sl=sl: e.scalar_tensor_tensor(out=xT[:, dc, sl], in0=B_[:, sl], scalar=mod[:, 16 + dc:17 + dc], in1=xT[:, dc, sl], op0=ALU.mult, op1=ALU.add),
                         r=[kB, "mod", ("xT", dc)], w=[("xT", dc)])

        eps_t = P.sb("eps_t", [128, 1])
        P.op("dve", lambda e: e.memset(eps_t[:], EPS), w=["eps"])
        halfpi = P.sb("halfpi", [128, 1])
        P.op("dve", lambda e: e.memset(halfpi[:], 1.5707963267948966), w=["halfpi"])


        for l in range(STAGE["layers"]):
            ada_layer(l)
            norm_mod(0)
            if STAGE["mixers"]:
                if l % 3 == 0:
                    attention(l)
                elif l % 3 == 1 and STAGE.get("hgrn", True):
                    hgrn(l)
                elif l % 3 == 2 and STAGE.get("ssm", True):
                    ssm(l)
            norm_mod(1)
            ffn(l)

        rms_stats()
        for k in range(8):
            P.op("dve", lambda e, k=k: e.scalar_tensor_tensor(out=xT[:, k, :], in0=xT[:, k, :], scalar=fg[:, k:k + 1], in1=rstd[:], op0=ALU.mult, op1=ALU.mult),
                 r=[("xT", k), "fg", "rstd"], w=[("xT", k)])
        for k in range(8):
            P.dma(O["y"][k * 128:(k + 1) * 128, :], xT[:, k, :], r=[("xT", k)], sem="yout")
        P.op("pool", lambda e: e.memset(rstd[:], 0.0), r=["rstd"], w=["rstd"])
        if not (STAGE["mixers"] and STAGE.get("hgrn", True) and STAGE["layers"] > 1):
            for i in range(8):
                P.dma(O["hg"][i * 8:(i + 1) * 8].rearrange("a p n -> p a n"), rstd[:].rearrange("p (a n) -> p a n", a=8), r=["rstd"], sem="sout")
        if not (STAGE["mixers"] and STAGE.get("ssm", True) and STAGE["layers"] > 2):
            for a_ in range(8):
                P.dma(O["ssm"][a_].rearrange("g p r -> g (p r)"), rstd[0:64, 0:128], r=["rstd"], sem="sout")
        P.wait_all_dma()
        P.emit()
    return nc

def _c(a):
    return np.ascontiguousarray(a, dtype=np.float32)


def prep_inputs(inp):
    g = {k: np.asarray(v) for k, v in inp.items()}
    sh = {}
    sh["ident"] = np.eye(128, dtype=np.float32)
    sh["ada_w"] = _c(g["ada_w"].reshape(DEPTH, 8, 128, 12, 512).transpose(0, 3, 2, 1, 4))
    sh["ada_b"] = _c(g["ada_b"].reshape(DEPTH, 48, 128).transpose(0, 2, 1))
    sh["ng"] = _c(np.stack([g["norm1_g"], g["norm2_g"]]).reshape(2, DEPTH, 8, 128).transpose(3, 0, 1, 2))
    sh["final_g"] = _c(g["final_g"].reshape(8, 128).T)
    sh["ffn_w_up"] = _c(g["ffn_w_up"].reshape(DEPTH, 8, 128, 2, NFC, 128).transpose(0, 4, 2, 3, 1, 5))
    sh["ffn_w_down"] = _c(g["ffn_w_down"].reshape(DEPTH, NFC, 128, 8, 128).transpose(0, 3, 2, 1, 4))
    sh["ffn_conv_w"] = _c(g["ffn_conv_w"].reshape(DEPTH, 3, 2 * NFC, 128).transpose(0, 3, 1, 2))
    sh["ffn_conv_b"] = _c(g["ffn_conv_b"].reshape(DEPTH, 2 * NFC, 128).transpose(0, 2, 1))
    wqkv = g["attn_wqkv"]
    sh["wq"] = _c(wqkv[:, :, 0:1024].reshape(2, 8, 128, 8, 128).transpose(0, 3, 2, 1, 4))
    wk = wqkv[:, :, 1024:1280].reshape(2, 8, 128, 4, 1, 64)
    sh["wk"] = _c(np.broadcast_to(wk, (2, 8, 128, 4, 2, 64)).reshape(2, 8, 128, 4, 128).transpose(0, 3, 2, 1, 4))
    sh["wkv"] = _c(wqkv[:, :, 1024:1536].reshape(2, 8, 128, 512).transpose(0, 2, 1, 3))
    sh["wo"] = _c(g["attn_wo"].reshape(2, 8, 128, 8, 128).transpose(0, 3, 2, 1, 4))
    sh["sink"] = _c(np.broadcast_to(g["attn_sink"][:, None, :], (2, 128, 16)))
    hw = g["hgrn_w_in"][0]
    sh["hw_in"] = _c(hw.reshape(8, 128, 5, 8, 128).transpose(3, 2, 1, 0, 4))
    sh["hwo"] = _c(g["hgrn_wo"][0].reshape(8, 128, 8, 128).transpose(2, 1, 0, 3))
    sh["hlb"] = _c(g["hgrn_lb"].reshape(4, 2, 8, 128).transpose(3, 0, 1, 2))
    sh["hgn"] = _c(g["hgrn_g_norm"][0].reshape(128, 1))
    ii = np.arange(128)
    same = (ii[:, None] // 32) == (ii[None, :] // 32)
    sh["hmask"] = _c(np.stack([same & (ii[:, None] <= ii[None, :]), same & (ii[:, None] >= ii[None, :])], axis=1))
    are, aim, ldt = g["ssm_a_re"][0], g["ssm_a_im"][0], g["ssm_log_dt"][0]
    bre, bim, cre, cim = g["ssm_b_re"][0], g["ssm_b_im"][0], g["ssm_c_re"][0], g["ssm_c_im"][0]
    sh["sBreR"] = _c(bre.reshape(2, 8, 8, 64, 16).transpose(0, 1, 2, 4, 3).reshape(2, 8, 128, 64))
    sh["sBimR"] = _c(bim.reshape(2, 8, 8, 64, 16).transpose(0, 1, 2, 4, 3).reshape(2, 8, 128, 64))
    sh["sAreR"] = _c(np.broadcast_to(are.reshape(2, 8, 8, 1, 64), (2, 8, 8, 16, 64)).reshape(2, 8, 128, 64))
    sh["sAimR"] = _c(np.broadcast_to(aim.reshape(2, 8, 8, 1, 64), (2, 8, 8, 16, 64)).reshape(2, 8, 128, 64))
    sh["sDtR"] = _c(np.broadcast_to(ldt.reshape(2, 8, 8, 1, 1), (2, 8, 8, 16, 1)).reshape(2, 8, 128, 1))
    sh["sAreQ"] = _c(are.reshape(2, 2, 32, 64).transpose(0, 1, 3, 2).reshape(2, 128, 32))
    sh["sAimQ"] = _c(aim.reshape(2, 2, 32, 64).transpose(0, 1, 3, 2).reshape(2, 128, 32))
    sh["sDtQ"] = _c(np.broadcast_to(ldt.reshape(2, 2, 1, 32), (2, 2, 64, 32)).reshape(2, 128, 32))
    sh["sCreQ"] = _c(cre.reshape(2, 2, 32, 16, 64).transpose(0, 1, 4, 2, 3).reshape(2, 128, 32, 16))
    sh["sCimQ"] = _c(cim.reshape(2, 2, 32, 16, 64).transpose(0, 1, 4, 2, 3).reshape(2, 128, 32, 16))
    sh["sD"] = _c(g["ssm_d"][0].reshape(8, 128).T)
    sm = np.zeros((128, 12), np.float32)
    for q in range(8):
        sm[q * 16:(q + 1) * 16, q] = 1.0
    sm[0:64, 8] = 1.0; sm[64:128, 9] = 1.0; sm[0:64, 10] = -1.0; sm[64:128, 11] = -1.0
    sh["smask"] = sm
    sh["wglu"] = _c(g["ssm_w_glu"][0].reshape(8, 128, 16, 128).transpose(2, 1, 0, 3))
    tt = np.arange(NT)
    inv = 1.0 / (10000.0 ** (np.arange(0, 32, 2, dtype=np.float32) / np.float32(32)))
    ar = (tt // 64).astype(np.float32)[:, None] * inv.astype(np.float32)
    ac = (tt % 64).astype(np.float32)[:, None] * inv.astype(np.float32)
    ang = np.concatenate([ar, ar, ac, ac], axis=-1).astype(np.float32)
    cosS = _c(np.concatenate([np.cos(ang).T] * 2, axis=0)); sinS = _c(np.concatenate([np.sin(ang).T] * 2, axis=0))
    rm = np.zeros((128, 128), np.float32)
    for m in range(128):
        d = m % 64
        if (d % 32) < 16:
            rm[m + 16, m] = -1.0
        else:
            rm[m - 16, m] = 1.0
    sh["rmat"] = rm
    kk = np.arange(128)[:, None]; qq = np.arange(128)[None, :]
    mS = np.zeros((128, 8, 2, 128), np.float32); mP = np.zeros((128, 8, 2, 128), np.float32)
    for i in range(8):
        if i >= 1:
            mS[:, i, 0, :] = (kk >= qq)
        if i <= 6:
            mS[:, i, 1, :] = (kk <= qq)
        mP[:, i, 0, :] = 1.0 if i % 2 == 1 else 0.0
        mP[:, i, 1, :] = 1.0 if i % 2 == 0 else 0.0
    if STAGE.get("kv_only"):
        for kk_ in ("wq", "wk", "wo", "sink", "rmat"):
            sh.pop(kk_, None)
    maps = []
    for c in range(8):
        m = dict(sh)
        if c < 4:
            xc = g["x_sample"][c]
            cond = g["c"][c]
            kp = 1.0
        else:
            xc = g["x_prompt"][4 * (c - 4):4 * (c - 4) + 4].reshape(NT, D)
            cond = g["c_ctx"]
            kp = 0.0
        if c < 4:
            ropec = cosS; ropes = sinS; m["amask"] = mS
            ck = g["cache_k"][c]
            m["ckd"] = _c(np.broadcast_to(ck[:, :, :, None, :], (2, 512, 4, 2, 64)).reshape(2, 512, 512))
            cv = g["cache_v"][c]
            vp = np.zeros((2, 512, 4, 2, 128), np.float32)
            vp[:, :, :, 0, 0:64] = cv; vp[:, :, :, 0, 64] = 1.0
            vp[:, :, :, 1, 64:128] = cv; vp[:, :, :, 1, 0] = 1.0
            m["cvp"] = vp.reshape(2, 512, 1024)
        else:
            ropec = np.ones((128, NT), np.float32); ropes = np.zeros((128, NT), np.float32); m["amask"] = mP
            m["ckd"] = np.zeros((2, 512, 512), np.float32); m["cvp"] = np.zeros((2, 512, 1024), np.float32)
        if STAGE.get("kv_only"):
            for kk_ in ("amask", "ckd", "cvp"):
                m.pop(kk_, None)
        if c < 4:
            st = g["state_ssm"][c, 0]
            m["sH0"] = _c(st.reshape(2, 2, 32, 64, 2).transpose(0, 1, 3, 2, 4).reshape(2, 128, 32, 2))
        else:
            m["sH0"] = np.zeros((2, 128, 32, 2), np.float32)
        m["hs0"] = _c(g["state_hgrn"][c, 0]) if c < 4 else np.zeros((2, 8, 128, 128), np.float32)
        m["x"] = _c(np.concatenate([xc.T, ropec, ropes], axis=0))
        m["cond"] = _c(cond.reshape(8, 128).T)
        m["keep"] = _c(np.stack([np.full(128, kp), np.full(128, kp - 1.0)], axis=1))
        maps.append(m)
    return maps


def assemble(results):
    f = lambda a: np.asarray(a, dtype=np.float32)
    ys = np.stack([f(results[c]["y"]).T for c in range(4)])
    yp = np.concatenate([f(results[c]["y"]).T.reshape(4, 256, D) for c in range(4, 8)])
    nk = np.concatenate([f(results[c]["nk"]).reshape(2, 4, 256, 4, 64).transpose(1, 0, 2, 3, 4) for c in range(4, 8)])
    nv = np.concatenate([f(results[c]["nv"]).reshape(2, 4, 256, 4, 64).transpose(1, 0, 2, 3, 4) for c in range(4, 8)])
    hg = np.concatenate([f(results[c]["hg"]).reshape(4, 1, 2, 8, 128, 128) for c in range(4, 8)])
    ssm = np.concatenate([f(results[c]["ssm"]).reshape(4, 1, 2, 64, 64, 2) for c in range(4, 8)])
    return (np.ascontiguousarray(yp), np.ascontiguousarray(ys), np.ascontiguousarray(nk), np.ascontiguousarray(nv),
            np.ascontiguousarray(hg), np.ascontiguousarray(ssm))


def kernel(**inputs):
    nc = build_program()
    maps = prep_inputs(inputs)
    res = run_bass_kernel_spmd(nc, maps, core_ids=list(range(8)))
    return assemble(res.results)
```

```python
import numpy as np
from concourse.bass_utils import run_bass_kernel_spmd

from contextlib import ExitStack
import concourse.bass as bass
import concourse.mybir as mybir

F32 = mybir.dt.float32
F32R = mybir.dt.float32r
BF16 = mybir.dt.bfloat16
AF = mybir.ActivationFunctionType
ALU = mybir.AluOpType
AX = mybir.AxisListType


class Prog:
    ENGS = ("pe", "act", "dve", "pool", "sp")

    def __init__(self, nc, es: ExitStack):
        self.nc = nc
        self.es = es
        self.recs = {e: [] for e in self.ENGS}
        self.cnt = {e: 0 for e in self.ENGS}
        self.known = {e: {} for e in self.ENGS}
        self.state = {}
        self.sems = {}
        self.dcnt = {}
        for e in self.ENGS:
            self.sems[("e", e)] = es.enter_context(nc.semaphore("sem_" + e))
        self.psn = 0

    def sb(self, name, shape, dt=F32):
        return self.es.enter_context(self.nc.sbuf_tensor("sb_" + name, list(shape), dt))

    def ps(self, name, shape, dt=F32):
        return self.es.enter_context(self.nc.psum_tensor(name, list(shape), dt))

    def dsem(self, name):
        k = ("d", name)
        if k not in self.sems:
            self.sems[k] = self.es.enter_context(self.nc.semaphore("dsem_" + name))
            self.dcnt[k] = 0
        return k

    def _st(self, k):
        s = self.state.get(k)
        if s is None:
            s = {"w": {}, "r": {}}
            self.state[k] = s
        return s

    def _deps(self, eng, r, w):
        deps = {}
        def add(d):
            if d is None:
                return
            sk, v = d
            if deps.get(sk, 0) < v:
                deps[sk] = v
        for k in r:
            st = self._st(k)
            for sk, v in st["w"].items():
                add((sk, v))
            if isinstance(k, tuple) and k[0] == "pq":
                for sk, v in st["r"].items():
                    if sk != ("e", eng):
                        add((sk, v))
        for k in w:
            s = self._st(k)
            for sk, v in s["w"].items():
                add((sk, v))
            for sk, v in s["r"].items():
                add((sk, v))
        waits = []
        kn = self.known[eng]
        for sk, v in deps.items():
            if eng == "pe" and sk == ("e", "pe"):
                continue
            if kn.get(sk, 0) < v:
                waits.append((sk, v))
                kn[sk] = v
        return waits

    def _commit(self, comp, r, w):
        sk, v = comp
        for k in w:
            self.state[k] = {"w": {sk: v}, "r": {}}
        for k in r:
            s = self._st(k)
            if s["r"].get(sk, 0) < v:
                s["r"][sk] = v

    def alias(self, dst, src):
        mw, mr = {}, {}
        for k in src:
            st = self._st(k)
            for sk, v in st["w"].items():
                mw[sk] = max(mw.get(sk, 0), v)
            for sk, v in st["r"].items():
                mr[sk] = max(mr.get(sk, 0), v)
        for k in dst:
            self.state[k] = {"w": dict(mw), "r": dict(mr)}

    def op(self, eng, fn, r=(), w=(), inc=True):
        inc = True
        waits = self._deps(eng, r, w)
        sk = ("e", eng)
        comp = (sk, self.cnt[eng] + 1)
        if inc:
            self.cnt[eng] += 1
        self.recs[eng].append((waits, fn, (sk, 1) if inc else None))
        self._commit(comp, r, w)

    def dma(self, out, in_, r=(), w=(), sem=None, q="sp", **kw):
        if sem is None:
            sem = "_".join(str(x) for x in (w[0] if isinstance(w[0], tuple) else (w[0],)))
        waits = self._deps(q, r, w)
        sk = self.dsem(sem)
        self.dcnt[sk] += 16
        comp = (sk, self.dcnt[sk])
        self.recs[q].append((waits, (lambda e, o=out, i=in_, kw=kw: e.dma_start(out=o, in_=i, **kw)), (sk, 16)))
        self._commit(comp, r, w)

    def wait_all_dma(self, q="sp"):
        waits = []
        for sk, v in self.dcnt.items():
            if v > 0 and self.known[q].get(sk, 0) < v:
                waits.append((sk, v))
                self.known[q][sk] = v
        self.recs[q].append((waits, None, None))

    def barrier_all(self):
        tgt = {("e", e): self.cnt[e] for e in self.ENGS if self.cnt[e] > 0}
        for sk, v in self.dcnt.items():
            if v > 0:
                tgt[sk] = v
        for e in self.ENGS:
            waits = []
            for sk, v in tgt.items():
                if sk == ("e", e):
                    continue
                if self.known[e].get(sk, 0) < v:
                    waits.append((sk, v))
                    self.known[e][sk] = v
            if waits:
                self.recs[e].append((waits, None, None))

    def emit(self):
        nc = self.nc
        sems = self.sems
        recs = self.recs

        def replay(name):
            def f(e):
                for waits, fn, inc in recs[name]:
                    for sk, v in waits:
                        e.wait_ge(sems[sk], v)
                    if fn is not None:
                        ins = fn(e)
                        if inc is not None:
                            ins.then_inc(sems[inc[0]], inc[1])
            return f

        with nc.Block() as block:
            block.tensor(replay("pe"))
            block.scalar(replay("act"))
            block.vector(replay("dve"))
            block.gpsimd(replay("pool"))
            block.sync(replay("sp"))

    def stats(self):
        return {e: len(self.recs[e]) for e in self.ENGS}

D = 1024
NT = 1024
DFF = 2816
NFC = 22
DEPTH = 4
EPS = 1e-6

STAGE = {"mixers": True, "layers": 4, "kv_only": False}


def build_program():
    nc = bass.Bass("TRN2", target_bir_lowering=False)

    def din(name, shape, dt=F32):
        return nc.dram_tensor(name, list(shape), dt, kind="ExternalInput").ap()

    def dout(name, shape, dt=F32):
        return nc.dram_tensor(name, list(shape), dt, kind="ExternalOutput").ap()

    I = {}
    I["x"] = din("x", [D + 256, NT])
    I["cond"] = din("cond", [128, 8])
    I["keep"] = din("keep", [128, 2])
    I["ident"] = din("ident", [128, 128])
    I["ada_w"] = din("ada_w", [DEPTH, 12, 128, 8, 512])
    I["ada_b"] = din("ada_b", [DEPTH, 128, 48])
    I["ng"] = din("ng", [128, 2, DEPTH, 8])
    I["ffn_w_up"] = din("ffn_w_up", [DEPTH, NFC, 128, 2, 8, 128])
    I["ffn_conv_w"] = din("ffn_conv_w", [DEPTH, 128, 3, 2 * NFC])
    I["ffn_conv_b"] = din("ffn_conv_b", [DEPTH, 128, 2 * NFC])
    I["ffn_w_down"] = din("ffn_w_down", [DEPTH, 8, 128, NFC, 128])
    I["final_g"] = din("final_g", [128, 8])
    I["wkv"] = din("wkv", [2, 128, 8, 512])
    I["hw_in"] = din("hw_in", [8, 5, 128, 8, 128])
    I["hwo"] = din("hwo", [8, 128, 8, 128])
    I["hlb"] = din("hlb", [128, 4, 2, 8])
    I["hgn"] = din("hgn", [128, 1])
    I["hs0"] = din("hs0", [2, 8, 128, 128])
    I["hmask"] = din("hmask", [128, 2, 128])
    I["sBreR"] = din("sBreR", [2, 8, 128, 64]); I["sBimR"] = din("sBimR", [2, 8, 128, 64])
    I["sAreR"] = din("sAreR", [2, 8, 128, 64]); I["sAimR"] = din("sAimR", [2, 8, 128, 64])
    I["sDtR"] = din("sDtR", [2, 8, 128, 1])
    I["sAreQ"] = din("sAreQ", [2, 128, 32]); I["sAimQ"] = din("sAimQ", [2, 128, 32]); I["sDtQ"] = din("sDtQ", [2, 128, 32])
    I["sCreQ"] = din("sCreQ", [2, 128, 32, 16]); I["sCimQ"] = din("sCimQ", [2, 128, 32, 16])
    I["sH0"] = din("sH0", [2, 128, 32, 2])
    I["sD"] = din("sD", [128, 8])
    I["smask"] = din("smask", [128, 12])
    I["wglu"] = din("wglu", [16, 128, 8, 128])
    if not STAGE.get("kv_only"):
        I["wq"] = din("wq", [2, 8, 128, 8, 128])
        I["wk"] = din("wk", [2, 4, 128, 8, 128])
        I["wo"] = din("wo", [2, 8, 128, 8, 128])
        I["sink"] = din("sink", [2, 128, 16])
        I["rmat"] = din("rmat", [128, 128])
        I["amask"] = din("amask", [128, 8, 2, 128])
        I["ckd"] = din("ckd", [2, 512, 512])
        I["cvp"] = din("cvp", [2, 512, 1024])
    O = {}
    O["y"] = dout("y", [D, NT])
    O["nk"] = dout("nk", [2, NT, 256])
    O["nv"] = dout("nv", [2, NT, 256])
    O["hg"] = dout("hg", [64, 128, 128])
    O["ssm"] = dout("ssm", [8, 64, 64, 2])
    if STAGE.get("ssm_dbg"):
        O["dbg"] = dout("dbg", [128, 4096 + 1024 + 512])
        O["dbg2"] = dout("dbg2", [128, 6 * 1024])
        O["dbg3"] = dout("dbg3", [128, 8 * 1024], BF16)

    with ExitStack() as es:
        P = Prog(nc, es)
        xT = P.sb("xT", [128, 10, NT])
        hT = P.sb("hT", [128, 8, NT], BF16)
        aT = P.sb("aT", [128, 12, NT], BF16)
        rstd = P.sb("rstd", [128, NT])
        tmpA = P.sb("tmpA", [128, NT])
        tmpB = P.sb("tmpB", [128, NT])
        cg = [P.sb("cg%d" % i, [128, NT]) for i in range(2)]
        cv = [P.sb("cv%d" % i, [128, NT]) for i in range(2)]
        sq = [P.sb("sq%d" % i, [128, NT], BF16) for i in range(2)]
        ident = P.sb("ident", [128, 128])
        ones_bf = P.sb("ones_bf", [128, 128], BF16)
        one_f = P.sb("one_f", [128, 1])
        keep = P.sb("keep", [128, 2])
        cond = P.sb("cond", [128, 8])
        s_bf = P.sb("s_bf", [128, 8], BF16)
        mod = P.sb("mod", [128, 48])
        adab = P.sb("adab", [128, 48])
        ng = P.sb("ng", [128, 2, DEPTH, 8])
        fg = P.sb("fg", [128, 8])
        AB = P.sb("AB", [128, 4, 8])
        cw = P.sb("cw", [128, 3, 2 * NFC])
        cb = P.sb("cb", [128, 2 * NFC])
        cwk = P.sb("cwk", [128, 2, 2 * NFC])
        wada = [P.sb("wada%d" % i, [128, 8, 512], BF16) for i in range(2)]
        wup = [P.sb("wup%d" % i, [128, 2, 8, 128], BF16) for i in range(2)]
        wdn = [P.sb("wdn%d" % i, [128, 11, 128], BF16) for i in range(2)]
        vlat = P.sb("vlat", [128, 8, 4, 2, 128], BF16)
        vctx = P.sb("vctx", [128, 4, 4, 2, 128], BF16)
        kctxT = P.sb("kctxT", [128, 4, 512], BF16)
        ckd = P.sb("ckd", [128, 4, 512], BF16)
        amask = P.sb("amask", [128, 8, 2, 128], BF16)
        rmat = P.sb("rmat", [128, 128], BF16)
        ident_bf = P.sb("ident_bf", [128, 128], BF16)
        qb = [P.sb("qb%d" % i, [128, 512], BF16) for i in range(2)]
        esink = P.sb("esink", [128, 16])
        ones_f = P.sb("ones_f", [128, 128])
        m32 = P.sb("m32", [128, NT])
        hlb = P.sb("hlb", [128, 4, 2, 8])
        lbp = P.sb("lbp", [128, 2, 2, 8])
        hgn = P.sb("hgn", [128, 1])
        hmask = P.sb("hmask", [128, 2, 128], BF16)
        Sf = [P.sb("Sf%d" % i, [128, 128]) for i in range(2)]
        Sb = [P.sb("Sb%d" % i, [128, 128], BF16) for i in range(2)]
        adec = [P.sb("adec%d" % i, [128, 32]) for i in range(2)]
        rowm = P.sb("rowm", [128, 4])
        ctmp = P.sb("ctmp", [128, 32])
        vtok = P.sb("vtok", [128, 8, 128], BF16)
        qhat = [P.sb("qhat%d" % i, [128, NT], BF16) for i in range(2)]
        ktil = [P.sb("ktil%d" % i, [128, NT], BF16) for i in range(2)]
        kdT = [P.sb("kdT%d" % i, [128, NT], BF16) for i in range(2)]
        attm = [P.sb("attm%d" % i, [128, 128], BF16) for i in range(2)]
        smask = P.sb("smask", [128, 12])
        sD = P.sb("sD", [128, 8])
        sH = P.sb("sH", [128, 32, 2])
        sHent = P.sb("sHent", [128, 4, 2])
        sHx = P.sb("sHx", [128, 4, 2])
        pq = [P.ps("pq%d" % i, [128, 1024]) for i in range(4)]

        def bank(i):
            return pq[i // 2][:, (i % 2) * 512:(i % 2) * 512 + 512], ("pq", i)

        P.dma(ident[:], I["ident"], w=["ident"])
        P.dma(keep[:], I["keep"], w=["keep"])
        P.dma(cond[:], I["cond"], w=["cond"])
        P.dma(ng[:], I["ng"], w=["ng"])
        P.dma(fg[:], I["final_g"], w=["fg"])
        P.op("dve", lambda e: e.memset(ones_bf[:], 1.0), w=["ones_bf"])
        P.op("dve", lambda e: e.memset(one_f[:], 1.0), w=["one_f"])
        P.op("dve", lambda e: e.memset(ones_f[:], 1.0), w=["ones_f"])
        P.op("dve", lambda e: e.tensor_copy(out=ident_bf[:], in_=ident[:]), r=["ident"], w=["ident_bf"])
        if not STAGE.get("kv_only"):
            P.dma(rmat[:], I["rmat"], w=["rmat"], q="pool")
            P.dma(amask[:], I["amask"], w=["amask"], q="pool")
        P.op("pool", lambda e: e.memset(m32[:], 1.0), w=["m32"])
        P.op("pool", lambda e: e.memset(m32[:, 0:NT:32], 0.0), w=["m32"])
        P.op("pool", lambda e: e.memset(rowm[:], 0.0), w=["rowm"])
        for c4 in range(3):
            P.op("pool", lambda e, c4=c4: e.memset(rowm[c4 * 32:(c4 + 1) * 32, c4:c4 + 1], 1.0), w=["rowm"])
        P.op("pool", lambda e: e.memset(rowm[96:128, 3:4], 1.0), w=["rowm"])
        P.dma(hlb[:], I["hlb"], w=["hlb"])
        P.dma(hgn[:], I["hgn"], w=["hgn"])
        P.dma(hmask[:], I["hmask"], w=["hmask"], q="pool")
        P.op("dve", lambda e: e.memset(vlat[:], 0.0), w=["vlat"])
        P.op("dve", lambda e: e.memset(vlat[:, :, :, 0, 64:65], 1.0), w=["vlat"])
        P.op("dve", lambda e: e.memset(vlat[:, :, :, 1, 0:1], 1.0), w=["vlat"])
        P.op("act", lambda e: e.activation(out=s_bf[:], in_=cond[:], func=AF.Silu), r=["cond"], w=["s_bf"])

        P.dma(xT[:], I["x"].rearrange("(k p) t -> p k t", p=128), w=[("xT", k) for k in range(10)], sem="xin")

        def ada_layer(l):
            P.dma(adab[:], I["ada_b"][l], w=["adab"])
            ps, pk = bank(4)
            for n in range(12):
                wb = wada[n % 2]
                P.dma(wb[:], I["ada_w"][l, n],
                      w=[("wada", n % 2)], q="pool")
                for c4 in range(4):
                    c = n * 4 + c4
                    for k in range(8):
                        P.op("pe", lambda e, ps=ps, k=k, wb=wb, c=c, c4=c4: e.matmul(ps[:, c:c + 1], lhsT=wb[:, k, c4 * 128:(c4 + 1) * 128], rhs=s_bf[:, k:k + 1], start=(k == 0), stop=(k == 7)),
                             r=[("wada", n % 2), "s_bf"], w=[pk])
            P.op("dve", lambda e, ps=ps: e.tensor_tensor(out=mod[:], in0=ps[:, 0:48], in1=adab[:], op=ALU.add), r=[pk, "adab"], w=["mod"])
            for j in range(2):
                P.op("dve", lambda e, j=j: e.scalar_tensor_tensor(out=AB[:, 2 * j, :], in0=mod[:, (3 * j + 1) * 8:(3 * j + 2) * 8], scalar=1.0,
                                                                 in1=ng[:, j, l, :], op0=ALU.add, op1=ALU.mult),
                     r=["mod", "ng"], w=["AB"])
                P.op("dve", lambda e, j=j: e.tensor_copy(out=AB[:, 2 * j + 1, :], in_=mod[:, (3 * j) * 8:(3 * j + 1) * 8]), r=["mod"], w=["AB"])

        def rms_stats():
            b0, k0 = bank(6)
            b1, k1 = bank(7)
            for k in range(8):
                s = sq[k % 2]
                P.op("act", lambda e, s=s, k=k: e.activation(out=s[:], in_=xT[:, k, :], func=AF.Square), r=[("xT", k)], w=[("sq", k % 2)])
                for th, (b, bk) in enumerate(((b0, k0), (b1, k1))):
                    P.op("pe", lambda e, b=b, s=s, th=th, k=k: e.matmul(b, lhsT=ones_bf[:], rhs=s[:, th * 512:(th + 1) * 512], start=(k == 0), stop=(k == 7)),
                         r=[("sq", k % 2), "ones_bf"], w=[bk], inc=True)
            for th, (b, bk) in enumerate(((b0, k0), (b1, k1))):
                P.op("act", lambda e, b=b, th=th: e.activation(out=tmpA[:, th * 512:(th + 1) * 512], in_=b, func=AF.Ln, scale=1.0 / D, bias=eps_t[:, 0:1]),
                     r=[bk, "eps"], w=["tmpA"])
            P.op("act", lambda e: e.activation(out=rstd[:], in_=tmpA[:], func=AF.Exp, scale=-0.5), r=["tmpA"], w=["rstd"])

        def norm_mod(j):
            rms_stats()
            for k in range(8):
                t = tmpA if k % 2 == 0 else tmpB
                tk = "tmpA" if k % 2 == 0 else "tmpB"
                P.op("dve", lambda e, t=t, k=k: e.scalar_tensor_tensor(out=t[:], in0=xT[:, k, :], scalar=AB[:, 2 * j, k:k + 1], in1=rstd[:], op0=ALU.mult, op1=ALU.mult),
                     r=[("xT", k), "AB", "rstd"], w=[tk])
                P.op("act", lambda e, t=t, k=k: e.activation(out=hT[:, k, :], in_=t[:], func=AF.Identity, bias=AB[:, 2 * j + 1, k:k + 1], scale=1.0),
                     r=[tk, "AB"], w=[("hT", k)])

        def ffn(l):
            P.dma(cw[:], I["ffn_conv_w"][l], w=["cw"])
            P.dma(cb[:], I["ffn_conv_b"][l], w=["cb"])
            for jj, j in enumerate((0, 2)):
                P.op("dve", lambda e, jj=jj, j=j: e.tensor_scalar(out=cwk[:, jj, :], in0=cw[:, j, :], scalar1=keep[:, 1:2], scalar2=None, op0=ALU.mult),
                     r=["cw", "keep"], w=["cwk"])
            for grp in range(2):
                for fc in range(grp * 11, grp * 11 + 11):
                    wb = wup[fc % 2]
                    P.dma(wb[:], I["ffn_w_up"][l, fc], w=[("wup", fc % 2, 0), ("wup", fc % 2, 1)], q="pool")
                    outs = []
                    for gv in range(2):
                        pt = pq[(fc % 2) * 2 + gv]
                        pks = [("pq", ((fc % 2) * 2 + gv) * 2 + th) for th in range(2)]
                        for th in range(2):
                            for k in range(8):
                                P.op("pe", lambda e, pt=pt, th=th, k=k, gv=gv, wb=wb: e.matmul(pt[:, th * 512:(th + 1) * 512], lhsT=wb[:, gv, k, :], rhs=hT[:, k, th * 512:(th + 1) * 512],
                                                                                    start=(k == 0), stop=(k == 7)),
                                     r=[("wup", fc % 2, gv), ("hT", k)], w=[pks[th]], inc=(k == 7))
                        c = (cg if gv == 0 else cv)[fc % 2]
                        ck = ("cg" if gv == 0 else "cv", fc % 2)
                        col = gv * NFC + fc
                        P.op("act", lambda e, c=c, pt=pt, col=col: e.activation(out=c[:], in_=pt[:], func=AF.Identity, scale=cw[:, 1, col:col + 1], bias=cb[:, col:col + 1]),
                             r=pks + ["cw", "cb"], w=[ck])
                        P.op("dve", lambda e, c=c, pt=pt, col=col: e.scalar_tensor_tensor(out=c[:, 1:NT], in0=pt[:, 0:NT - 1], scalar=cw[:, 0, col:col + 1], in1=c[:, 1:NT], op0=ALU.mult, op1=ALU.add),
                             r=pks + ["cw", ck], w=[ck])
                        P.op("dve", lambda e, c=c, pt=pt, col=col: e.scalar_tensor_tensor(out=c[:, 0:NT - 1], in0=pt[:, 1:NT], scalar=cw[:, 2, col:col + 1], in1=c[:, 0:NT - 1], op0=ALU.mult, op1=ALU.add),
                             r=pks + ["cw", ck], w=[ck])
                        P.op("dve", lambda e, c=c, pt=pt, col=col: e.scalar_tensor_tensor(out=c[:, 256:NT:256], in0=pt[:, 255:NT - 1:256], scalar=cwk[:, 0, col:col + 1], in1=c[:, 256:NT:256], op0=ALU.mult, op1=ALU.add),
                             r=pks + ["cwk", ck], w=[ck])
                        P.op("dve", lambda e, c=c, pt=pt, col=col: e.scalar_tensor_tensor(out=c[:, 255:NT - 1:256], in0=pt[:, 256:NT:256], scalar=cwk[:, 1, col:col + 1], in1=c[:, 255:NT - 1:256], op0=ALU.mult, op1=ALU.add),
                             r=pks + ["cwk", ck], w=[ck])
                        outs.append((c, ck))
                    (cgt, cgk), (cvt, cvk) = outs
                    P.op("act", lambda e, cgt=cgt: e.activation(out=cgt[:], in_=cgt[:], func=AF.Silu), r=[cgk], w=[cgk])
                    P.op("dve", lambda e, cgt=cgt, cvt=cvt, fc=fc: e.tensor_tensor(out=aT[:, fc % 11, :], in0=cgt[:], in1=cvt[:], op=ALU.mult), r=[cgk, cvk], w=[("aT", fc % 11)])
                for dc in range(8):
                    wb = wdn[dc % 2]
                    P.dma(wb[:], I["ffn_w_down"][l, dc][:, grp * 11:grp * 11 + 11, :],
                          w=[("wdn", dc % 2)], q="pool")
                    for th in range(2):
                        ps, pk = bank((dc * 2 + th) % 8)
                        for fc in range(11):
                            P.op("pe", lambda e, ps=ps, fc=fc, th=th, wb=wb: e.matmul(ps, lhsT=wb[:, fc, :], rhs=aT[:, fc, th * 512:(th + 1) * 512], start=(fc == 0), stop=(fc == 10)),
                                 r=[("wdn", dc % 2), ("aT", fc)], w=[pk])
                        P.op("dve", lambda e, ps=ps, dc=dc, th=th: e.scalar_tensor_tensor(out=xT[:, dc, th * 512:(th + 1) * 512], in0=ps, scalar=mod[:, 40 + dc:41 + dc],
                                                                                         in1=xT[:, dc, th * 512:(th + 1) * 512], op0=ALU.mult, op1=ALU.add),
                             r=[pk, "mod", ("xT", dc)], w=[("xT", dc)])

        def attention(l):
            j = l // 3
            cosT, sinT = xT[:, 8, :], xT[:, 9, :]
            t1, t2 = cv[0], cv[1]
            if STAGE.get("kv_only"):
                wkv = wada[0]
                P.dma(wkv[:], I["wkv"][j], w=[("wada", 0)], q="pool")
                for tb in range(8):
                    ps, pk = bank(tb % 4)
                    for k in range(8):
                        P.op("pe", lambda e, ps=ps, k=k, tb=tb: e.matmul(ps, lhsT=hT[:, k, tb * 128:(tb + 1) * 128], rhs=wkv[:, k, :], start=(k == 0), stop=(k == 7)),
                             r=[("wada", 0), ("hT", k)], w=[pk])
                    kvt = tmpA if tb % 2 == 0 else tmpB
                    kvk = "tmpA" if tb % 2 == 0 else "tmpB"
                    P.op("act", lambda e, ps=ps, kvt=kvt: e.copy(out=kvt[:, 0:512], in_=ps), r=[pk], w=[kvk])
                    P.dma(O["nk"][j, tb * 128:(tb + 1) * 128, :], kvt[:, 0:256], r=[kvk], sem="kvout%d" % (tb % 2))
                    P.dma(O["nv"][j, tb * 128:(tb + 1) * 128, :], kvt[:, 256:512], r=[kvk], sem="kvout%d" % (tb % 2))
                return
            P.dma(esink[:], I["sink"][j], w=["esink"])
            P.op("act", lambda e: e.activation(out=esink[:], in_=esink[:], func=AF.Exp), r=["esink"], w=["esink"])
            if STAGE.get("attn_upto", 9) < 1:
                return
            P.dma(ckd[:], I["ckd"][j].rearrange("(kb p) n -> p kb n", p=128), w=["ckd"], q="pool")
            P.dma(vctx[:].rearrange("p kb g v n -> p kb (g v n)"), I["cvp"][j].rearrange("(kb p) n -> p kb n", p=128), w=["vctx"], q="pool")
            for g in range(4):
                ps, pk = bank(g)
                for kb in range(4):
                    P.op("pe", lambda e, ps=ps, kb=kb, g=g: e.matmul(ps[:, kb * 128:(kb + 1) * 128], lhsT=ckd[:, kb, g * 128:(g + 1) * 128], rhs=ident_bf[:], start=True, stop=True),
                         r=["ckd", "ident_bf"], w=[pk])
                P.op("act", lambda e, ps=ps, g=g: e.copy(out=kctxT[:, g, :], in_=ps), r=[pk], w=["kctxT"])
            if STAGE.get("attn_upto", 9) < 2:
                return
            def proj_rope(wsrc, dst, dkey, idx):
                wb = wup[idx % 2]
                P.dma(wb[:, 0], wsrc, w=[("wup", idx % 2, 0)], q="pool")
                for th in range(2):
                    ps, pk = bank((idx * 2 + th) % 4)
                    rps, rpk = bank(4 + (idx * 2 + th) % 2)
                    for k in range(8):
                        P.op("pe", lambda e, ps=ps, k=k, th=th, wb=wb: e.matmul(ps, lhsT=wb[:, 0, k, :], rhs=hT[:, k, th * 512:(th + 1) * 512], start=(k == 0), stop=(k == 7)),
                             r=[("wup", idx % 2, 0), ("hT", k)], w=[pk])
                    q_ = qb[th]
                    if STAGE.get("pr", 9) < 1:
                        continue
                    P.op("act", lambda e, ps=ps, q_=q_: e.copy(out=q_[:], in_=ps), r=[pk], w=[("qb", th)])
                    if STAGE.get("pr", 9) < 2:
                        continue
                    P.op("pe", lambda e, rps=rps, q_=q_: e.matmul(rps, lhsT=rmat[:], rhs=q_[:], start=True, stop=True), r=[("qb", th), "rmat"], w=[rpk])
                    sl = slice(th * 512, (th + 1) * 512)
                    if STAGE.get("pr", 9) < 3 or idx >= STAGE.get("pridx", 99):
                        continue
                    P.op("dve", lambda e, ps=ps, sl=sl: e.scalar_tensor_tensor(out=t1[:, sl], in0=ps, scalar=1.0, in1=cosT[:, sl], op0=ALU.mult, op1=ALU.mult), r=[pk, ("xT", 8), ("qb", th)], w=[("cv", 0)])
                    if STAGE.get("pr", 9) < 4:
                        continue
                    P.op("dve", lambda e, rps=rps, sl=sl: e.scalar_tensor_tensor(out=t2[:, sl], in0=rps, scalar=1.0, in1=sinT[:, sl], op0=ALU.mult, op1=ALU.mult), r=[rpk, ("xT", 9)], w=[("cv", 1)])
                    if STAGE.get("pr", 9) < 5:
                        continue
                    P.op("dve", lambda e, sl=sl, dst=dst: e.tensor_tensor(out=dst[:, sl], in0=t1[:, sl], in1=t2[:, sl], op=ALU.add), r=[("cv", 0), ("cv", 1)], w=[dkey])
            for qc in range(8):
                proj_rope(I["wq"][j, qc], aT[:, qc, :], ("aT", qc), qc)
            for g in range(4):
                proj_rope(I["wk"][j, g], aT[:, 8 + g, :], ("aT", 8 + g), 8 + g)
            if STAGE.get("attn_upto", 9) < 3:
                return
            wkv = wada[0]
            P.dma(wkv[:], I["wkv"][j], w=[("wada", 0)], q="pool")
            for tb in range(8):
                ps, pk = bank(tb % 4)
                for k in range(8):
                    P.op("pe", lambda e, ps=ps, k=k, tb=tb: e.matmul(ps, lhsT=hT[:, k, tb * 128:(tb + 1) * 128], rhs=wkv[:, k, :], start=(k == 0), stop=(k == 7)),
                         r=[("wada", 0), ("hT", k)], w=[pk])
                kvt = tmpA if tb % 2 == 0 else tmpB
                kvk = "tmpA" if tb % 2 == 0 else "tmpB"
                P.op("act", lambda e, ps=ps, kvt=kvt: e.copy(out=kvt[:, 0:512], in_=ps), r=[pk], w=[kvk])
                P.dma(O["nk"][j, tb * 128:(tb + 1) * 128, :], kvt[:, 0:256], r=[kvk], sem="kvout%d" % (tb % 2))
                P.dma(O["nv"][j, tb * 128:(tb + 1) * 128, :], kvt[:, 256:512], r=[kvk], sem="kvout%d" % (tb % 2))
                P.op("dve", lambda e, ps=ps, tb=tb: e.tensor_copy(out=vlat[:, tb, :, 0, 0:64], in_=ps[:, 256:512].rearrange("p (g d) -> p g d", g=4)), r=[pk], w=["vlat"])
                P.op("dve", lambda e, ps=ps, tb=tb: e.tensor_copy(out=vlat[:, tb, :, 1, 64:128], in_=ps[:, 256:512].rearrange("p (g d) -> p g d", g=4)), r=[pk], w=["vlat"])
            if STAGE.get("attn_upto", 9) < 4:
                return
            dsb = tmpA[:].rearrange("p (a n) -> p a n", a=2)
            osb = [cv[0][:, 0:512], cv[1][:, 0:512]]
            tb_bf = tmpB[:].bitcast(BF16)
            ebuf = [tb_bf[:, i * 512:(i + 1) * 512] for i in range(3)]
            P.alias(["dsb"], ["tmpA"])
            P.alias([("osb", 0)], [("cv", 0)])
            P.alias([("osb", 1)], [("cv", 1)])
            P.alias([("ebuf", i) for i in range(3)], ["tmpB"])
            sc = 0
            for h in range(STAGE.get("nheads", 16)):
                g, qc, pb, var = h // 4, h // 2, (h % 2) * 64, h % 2
                dr = 64 if var == 0 else 0
                qh = aT[pb:pb + 64, qc, :]
                kh = aT[pb:pb + 64, 8 + g, :]
                kch = kctxT[pb:pb + 64, g, :]
                for th in range(2):
                    it = h * 2 + th
                    OP, opk = bank(4 + it % 2)
                    first = True
                    for kb in range(4):
                        ps, pk = bank(sc % 4); eb = ebuf[sc % 3]; ek = ("ebuf", sc % 3); sc += 1
                        P.op("pe", lambda e, ps=ps, kb=kb, th=th, kch=kch, qh=qh: e.matmul(ps, lhsT=kch[:, kb * 128:(kb + 1) * 128], rhs=qh[:, th * 512:(th + 1) * 512], start=True, stop=True),
                             r=["kctxT", ("aT", qc)], w=[pk])
                        P.op("act", lambda e, ps=ps, eb=eb: e.activation(out=eb[:], in_=ps, func=AF.Exp, scale=0.125), r=[pk], w=[ek])
                        P.op("pe", lambda e, OP=OP, eb=eb, kb=kb, g=g, var=var, first=first: e.matmul(OP, lhsT=vctx[:, kb, g, var, :], rhs=eb[:], start=first, stop=False),
                             r=["vctx", ek], w=[opk])
                        first = False
                    jbs = [jb for jb in range(8) if max(jb - 1, 4 * th) <= min(jb + 1, 4 * th + 3)]
                    for jb in jbs:
                        i0 = max(jb - 1, 4 * th); i1 = min(jb + 1, 4 * th + 3)
                        n = (i1 - i0 + 1) * 128
                        ps, pk = bank(sc % 4); eb = ebuf[sc % 3]; ek = ("ebuf", sc % 3); sc += 1
                        P.op("pe", lambda e, ps=ps, jb=jb, i0=i0, n=n, kh=kh, qh=qh: e.matmul(ps[:, 0:n], lhsT=kh[:, jb * 128:(jb + 1) * 128], rhs=qh[:, i0 * 128:i0 * 128 + n], start=True, stop=True),
                             r=[("aT", 8 + g), ("aT", qc)], w=[pk])
                        P.op("act", lambda e, ps=ps, eb=eb, n=n: e.activation(out=eb[:, 0:n], in_=ps[:, 0:n], func=AF.Exp, scale=0.125), r=[pk], w=[ek])
                        for i in range(i0, i1 + 1):
                            if i == jb:
                                continue
                            off = 0 if i == jb + 1 else 1
                            c0 = (i - i0) * 128
                            P.op("dve", lambda e, eb=eb, c0=c0, i=i, off=off: e.tensor_tensor(out=eb[:, c0:c0 + 128], in0=eb[:, c0:c0 + 128], in1=amask[:, i, off, :], op=ALU.mult),
                                 r=[ek, "amask"], w=[ek])
                        o0 = (i0 - 4 * th) * 128
                        P.op("pe", lambda e, OP=OP, eb=eb, jb=jb, g=g, var=var, o0=o0, n=n, jbs=jbs: e.matmul(OP[:, o0:o0 + n], lhsT=vlat[:, jb, g, var, :], rhs=eb[:, 0:n], start=False, stop=(jb == jbs[-1])),
                             r=["vlat", ek], w=[opk])
                    P.op("dve", lambda e, OP=OP, dr=dr, h=h: e.tensor_scalar(out=dsb[dr:dr + 1, 0, :], in0=OP[dr:dr + 1, :], scalar1=esink[dr:dr + 1, h:h + 1], scalar2=None, op0=ALU.add),
                         r=[opk, "esink"], w=["dsb"])
                    P.op("act", lambda e, dr=dr: e.activation(out=dsb[dr:dr + 1, 0, :], in_=dsb[dr:dr + 1, 0, :], func=AF.Ln), r=["dsb"], w=["dsb"])
                    P.op("act", lambda e, dr=dr: e.activation(out=dsb[dr:dr + 1, 1, :], in_=dsb[dr:dr + 1, 0, :], func=AF.Exp, scale=-1.0), r=["dsb"], w=["dsb"])
                    BC, bck = bank(6 + it % 2)
                    P.op("pe", lambda e, BC=BC, dr=dr: e.matmul(BC, lhsT=ones_f[dr:dr + 1, :], rhs=dsb[dr:dr + 1, 1, :], start=True, stop=True), r=["dsb", "ones_f"], w=[bck])
                    ob = osb[it % 2]; obk = ("osb", it % 2)
                    P.op("act", lambda e, OP=OP, ob=ob, pb=pb: e.copy(out=ob[pb:pb + 64, :], in_=OP[pb:pb + 64, :]), r=[opk], w=[obk])
                    P.op("dve", lambda e, BC=BC, ob=ob, pb=pb, qc=qc, th=th: e.tensor_tensor(out=aT[pb:pb + 64, qc, th * 512:(th + 1) * 512], in0=ob[pb:pb + 64, :], in1=BC[pb:pb + 64, :], op=ALU.mult),
                         r=[obk, bck], w=[("aT", qc)])
            P.alias(["tmpA"], ["dsb"])
            P.alias([("cv", 0)], [("osb", 0)])
            P.alias([("cv", 1)], [("osb", 1)])
            P.alias(["tmpB"], [("ebuf", i) for i in range(3)])
            for dc in range(8):
                wb = wup[dc % 2]
                P.dma(wb[:, 0], I["wo"][j, dc], w=[("wup", dc % 2, 0)], q="pool")
                for th in range(2):
                    ps, pk = bank((dc * 2 + th) % 4)
                    for k in range(8):
                        P.op("pe", lambda e, ps=ps, k=k, th=th, wb=wb: e.matmul(ps, lhsT=wb[:, 0, k, :], rhs=aT[:, k, th * 512:(th + 1) * 512], start=(k == 0), stop=(k == 7)),
                             r=[("wup", dc % 2, 0), ("aT", k)], w=[pk])
                    P.op("dve", lambda e, ps=ps, dc=dc, th=th: e.scalar_tensor_tensor(out=xT[:, dc, th * 512:(th + 1) * 512], in0=ps, scalar=mod[:, 16 + dc:17 + dc],
                                                                                     in1=xT[:, dc, th * 512:(th + 1) * 512], op0=ALU.mult, op1=ALU.add),
                         r=[pk, "mod", ("xT", dc)], w=[("xT", dc)])

        def hgrn(l):
            CH = 32
            NCH = NT // CH
            P.op("act", lambda e: e.activation(out=hlb[:], in_=hlb[:], func=AF.Exp), r=["hlb"], w=["hlb"])
            P.op("dve", lambda e: e.tensor_tensor(out=lbp[:, 1], in0=hlb[:, 0], in1=hlb[:, 1], op=ALU.add), r=["hlb"], w=["lbp"])
            P.op("dve", lambda e: e.tensor_tensor(out=lbp[:, 1], in0=lbp[:, 1], in1=hlb[:, 2], op=ALU.add), r=["hlb", "lbp"], w=["lbp"])
            P.op("dve", lambda e: e.tensor_tensor(out=lbp[:, 1], in0=lbp[:, 1], in1=hlb[:, 3], op=ALU.add), r=["hlb", "lbp"], w=["lbp"])
            P.op("dve", lambda e: e.reciprocal(out=lbp[:, 1], in_=lbp[:, 1]), r=["lbp"], w=["lbp"])
            P.op("dve", lambda e: e.tensor_tensor(out=lbp[:, 0], in0=lbp[:, 1], in1=hlb[:, 1], op=ALU.mult), r=["hlb", "lbp"], w=["lbp"])
            P.op("dve", lambda e: e.tensor_scalar(out=lbp[:, 1], in0=lbp[:, 0], scalar1=-1.0, scalar2=1.0, op0=ALU.mult, op1=ALU.add), r=["lbp"], w=["lbp"])
            bufQ, bufF, bufK, bufC = cg[0], cg[1], cv[0], cv[1]
            kQ, kF, kK, kC = ("cg", 0), ("cg", 1), ("cv", 0), ("cv", 1)
            oacc = rstd
            kdtok = [vlat[:].rearrange("p a g v n -> p (a g v n)")[:, d * 4096:(d + 1) * 4096].rearrange("p (b c n) -> p b c n", b=8, c=4) for d in range(2)]
            P.alias([("kdtok", 0), ("kdtok", 1)], ["vlat"])
            for h in range(8):
                P.dma(wup[0][:], I["hw_in"][h, 0:2].rearrange("a p k n -> p a k n"), w=[("wup", 0, 0), ("wup", 0, 1)], q="pool")
                P.dma(wup[1][:], I["hw_in"][h, 2:4].rearrange("a p k n -> p a k n"), w=[("wup", 1, 0), ("wup", 1, 1)], q="pool")
                P.dma(wdn[0][:, 0:8, :], I["hw_in"][h, 4], w=[("wdn", 0)], q="pool")

                def proj(wap, wkeys, bi):
                    outs = []
                    for th in range(2):
                        ps, pk = bank(bi * 2 + th)
                        for kk in range(8):
                            P.op("pe", lambda e, ps=ps, kk=kk, th=th, wap=wap: e.matmul(ps, lhsT=wap[:, kk, :], rhs=hT[:, kk, th * 512:(th + 1) * 512], start=(kk == 0), stop=(kk == 7)),
                                 r=list(wkeys) + [("hT", kk)], w=[pk])
                        outs.append((ps, pk))
                    return outs
                for th, (ps, pk) in enumerate(proj(wup[0][:, 0], [("wup", 0, 0)], 0)):
                    P.op("act", lambda e, ps=ps, th=th: e.copy(out=bufQ[:, th * 512:(th + 1) * 512], in_=ps), r=[pk], w=[kQ])
                for blk in range(8):
                    ps, pk = bank(2 + blk % 2)
                    for kk in range(8):
                        P.op("pe", lambda e, ps=ps, kk=kk, blk=blk: e.matmul(ps[:, 0:128], lhsT=hT[:, kk, blk * 128:(blk + 1) * 128], rhs=wup[0][:, 1, kk, :], start=(kk == 0), stop=(kk == 7)),
                             r=[("wup", 0, 1), ("hT", kk)], w=[pk])
                    P.op("act", lambda e, ps=ps, blk=blk: e.copy(out=vtok[:, blk, :], in_=ps[:, 0:128]), r=[pk], w=["vtok"])
                for d in range(2):
                    for th, (ps, pk) in enumerate(proj(wup[1][:, d], [("wup", 1, d)], 2 + d)):
                        P.op("act", lambda e, ps=ps, th=th: e.activation(out=bufF[:, th * 512:(th + 1) * 512], in_=ps, func=AF.Sigmoid), r=[pk], w=[kF])
                    P.op("dve", lambda e, d=d, h=h: e.tensor_scalar(out=bufF[:], in0=bufF[:], scalar1=lbp[:, 1, d, h:h + 1], scalar2=lbp[:, 0, d, h:h + 1], op0=ALU.mult, op1=ALU.add),
                         r=[kF, "lbp"], w=[kF])
                    P.op("dve", lambda e: e.tensor_scalar(out=bufK[:], in0=bufF[:], scalar1=-1.0, scalar2=1.0, op0=ALU.mult, op1=ALU.add), r=[kF], w=[kK])
                    P.op("act", lambda e: e.activation(out=bufF[:], in_=bufF[:], func=AF.Ln), r=[kF], w=[kF])
                    P.op("dve", lambda e: e.tensor_tensor_scan(out=bufC[:], data0=m32[:], data1=bufF[:], initial=0.0, op0=ALU.mult, op1=ALU.add), r=["m32", kF], w=[kC])
                    if d == 1:
                        P.op("dve", lambda e: e.scalar_tensor_tensor(out=tmpA[:], in0=bufC[:], scalar=-1.0, in1=bufF[:], op0=ALU.mult, op1=ALU.add), r=[kC, kF], w=["tmpA"])
                        P.op("act", lambda e: e.copy(out=ctmp[:], in_=bufC[:, CH - 1:NT:CH]), r=[kC], w=["ctmp"])
                        P.op("dve", lambda e: e.tensor_tensor(out=bufC[:].rearrange("p (c t) -> p c t", t=CH), in0=tmpA[:].rearrange("p (c t) -> p c t", t=CH),
                                                              in1=ctmp[:].unsqueeze(2).to_broadcast([128, NCH, CH]), op=ALU.add),
                             r=["tmpA", "ctmp"], w=[kC])
                        ctot = bufC[:, 0:NT:CH]
                    else:
                        ctot = bufC[:, CH - 1:NT:CH]
                    P.op("act", lambda e, d=d, ctot=ctot: e.activation(out=adec[d][:], in_=ctot, func=AF.Exp), r=[kC], w=[("adec", d)])
                    P.op("act", lambda e: e.activation(out=tmpA[:], in_=bufC[:], func=AF.Exp), r=[kC], w=["tmpA"])
                    P.op("dve", lambda e, d=d: e.tensor_tensor(out=qhat[d][:], in0=bufQ[:], in1=tmpA[:], op=ALU.mult), r=[kQ, "tmpA"], w=[("qhat", d)])
                    P.op("dve", lambda e: e.tensor_scalar(out=tmpB[:], in0=bufC[:], scalar1=-1.0, scalar2=85.0, op0=ALU.mult, op1=ALU.min), r=[kC], w=["tmpB"])
                    P.op("act", lambda e: e.activation(out=tmpB[:], in_=tmpB[:], func=AF.Exp), r=["tmpB"], w=["tmpB"])
                    P.op("dve", lambda e: e.tensor_tensor(out=tmpB[:], in0=tmpB[:], in1=bufK[:], op=ALU.mult), r=["tmpB", kK], w=["tmpB"])
                    P.op("act", lambda e, d=d: e.copy(out=ktil[d][:], in_=tmpB[:]), r=["tmpB"], w=[("ktil", d)])
                    P.op("dve", lambda e, d=d: e.tensor_tensor(out=kdT[d][:].rearrange("p (c t) -> p c t", t=CH), in0=tmpB[:].rearrange("p (c t) -> p c t", t=CH),
                                                                in1=adec[d][:].unsqueeze(2).to_broadcast([128, NCH, CH]), op=ALU.mult),
                         r=["tmpB", ("adec", d)], w=[("kdT", d)])
                    for blk in range(8):
                        ps, pk = bank(6 + blk % 2)
                        pst = ps.bitcast(BF16)
                        P.op("pe", lambda e, pst=pst, blk=blk, d=d: e.transpose(pst[:, 0:128], kdT[d][:, blk * 128:(blk + 1) * 128], ident_bf[:]), r=[("kdT", d), "ident_bf"], w=[pk])
                        for c4 in range(4):
                            P.op("act", lambda e, pst=pst, blk=blk, c4=c4, d=d: e.activation(out=kdtok[d][:, blk, c4, :], in_=pst[:, 0:128], func=AF.Identity, scale=rowm[:, c4:c4 + 1]),
                                 r=[pk, "rowm"], w=[("kdtok", d)])
                    P.dma(Sf[d][:], I["hs0"][d, h], w=[("Sf", d)])
                    P.op("act", lambda e, d=d: e.copy(out=Sb[d][:], in_=Sf[d][:]), r=[("Sf", d)], w=[("Sb", d)])
                P.op("pool", lambda e: e.memset(oacc[:], 0.0), r=["rstd"], w=["rstd"])
                for bi_ in range(8):
                    for d in range(2):
                        blk = bi_ if d == 0 else 7 - bi_
                        bsl = slice(blk * 128, (blk + 1) * 128)
                        aps, apk = bank(0 + d * 2)
                        ops_, opk = bank(1 + d * 2)
                        dps, dpk = bank(4 + d)
                        P.op("pe", lambda e, aps=aps, d=d, bsl=bsl: e.matmul(aps[:, 0:128], lhsT=ktil[d][:, bsl], rhs=qhat[d][:, bsl], start=True, stop=True),
                             r=[("ktil", d), ("qhat", d)], w=[apk])
                        am = attm[d]
                        P.op("dve", lambda e, aps=aps, am=am, d=d: e.tensor_tensor(out=am[:], in0=aps[:, 0:128], in1=hmask[:, d, :], op=ALU.mult), r=[apk, "hmask"], w=[("attm", d)])
                        P.op("pe", lambda e, ops_=ops_, am=am, blk=blk: e.matmul(ops_[:, 0:128], lhsT=vtok[:, blk, :], rhs=am[:], start=True, stop=False),
                             r=["vtok", ("attm", d)], w=[opk])
                        for c4 in range(4):
                            P.op("pe", lambda e, dps=dps, c4=c4, blk=blk, d=d: e.matmul(dps[:, c4 * 128:(c4 + 1) * 128], lhsT=kdtok[d][:, blk, c4, :], rhs=vtok[:, blk, :], start=True, stop=True),
                                 r=[("kdtok", d), "vtok"], w=[dpk])
                        cs = range(4) if d == 0 else range(3, -1, -1)
                        for c4 in cs:
                            ch = blk * 4 + c4
                            csl = slice(ch * CH, (ch + 1) * CH)
                            P.op("pe", lambda e, ops_=ops_, c4=c4, csl=csl, d=d, cs=cs: e.matmul(ops_[:, c4 * CH:(c4 + 1) * CH], lhsT=Sb[d][:], rhs=qhat[d][:, csl], start=False, stop=(c4 == list(cs)[-1])),
                                 r=[("Sb", d), ("qhat", d)], w=[opk])
                            P.op("dve", lambda e, dps=dps, c4=c4, ch=ch, d=d: e.scalar_tensor_tensor(out=Sf[d][:], in0=Sf[d][:], scalar=adec[d][:, ch:ch + 1], in1=dps[:, c4 * 128:(c4 + 1) * 128], op0=ALU.mult, op1=ALU.add),
                                 r=[("Sf", d), ("adec", d), dpk], w=[("Sf", d)])
                            seg_end = (ch % 8 == 7) if d == 0 else (ch % 8 == 0)
                            if seg_end:
                                seg = ch // 8
                                P.dma(O["hg"][(seg * 2 + d) * 8 + h], Sf[d][:], r=[("Sf", d)], sem="hgout%d" % d)
                                P.op("dve", lambda e, d=d: e.tensor_scalar(out=Sf[d][:], in0=Sf[d][:], scalar1=keep[:, 0:1], scalar2=None, op0=ALU.mult), r=[("Sf", d), "keep"], w=[("Sf", d)])
                            P.op("act", lambda e, d=d: e.copy(out=Sb[d][:], in_=Sf[d][:]), r=[("Sf", d)], w=[("Sb", d)])
                        P.op("dve", lambda e, ops_=ops_, bsl=bsl: e.tensor_tensor(out=oacc[:, bsl], in0=ops_[:, 0:128], in1=oacc[:, bsl], op=ALU.add), r=[opk, "rstd"], w=["rstd"])
                P.op("act", lambda e: e.activation(out=sq[0][:], in_=oacc[:], func=AF.Square), r=["rstd"], w=[("sq", 0)])
                for th in range(2):
                    ps, pk = bank(6 + th)
                    P.op("pe", lambda e, ps=ps, th=th: e.matmul(ps, lhsT=ones_bf[:], rhs=sq[0][:, th * 512:(th + 1) * 512], start=True, stop=True), r=[("sq", 0), "ones_bf"], w=[pk])
                    P.op("act", lambda e, ps=ps, th=th: e.activation(out=tmpA[:, th * 512:(th + 1) * 512], in_=ps, func=AF.Ln, scale=1.0 / 128, bias=eps_t[:, 0:1]), r=[pk, "eps"], w=["tmpA"])
                P.op("act", lambda e: e.activation(out=tmpA[:], in_=tmpA[:], func=AF.Exp, scale=-0.5), r=["tmpA"], w=["tmpA"])
                P.op("dve", lambda e: e.scalar_tensor_tensor(out=tmpA[:], in0=oacc[:], scalar=hgn[:, 0:1], in1=tmpA[:], op0=ALU.mult, op1=ALU.mult), r=["rstd", "hgn", "tmpA"], w=["tmpA"])
                for th, (ps, pk) in enumerate(proj(wdn[0][:, 0:8, :], [("wdn", 0)], 2)):
                    P.op("act", lambda e, ps=ps, th=th: e.activation(out=tmpB[:, th * 512:(th + 1) * 512], in_=ps, func=AF.Silu), r=[pk], w=["tmpB"])
                P.op("dve", lambda e, h=h: e.tensor_tensor(out=aT[:, h, :], in0=tmpA[:], in1=tmpB[:], op=ALU.mult), r=["tmpA", "tmpB"], w=[("aT", h)])
            P.alias(["vlat"], [("kdtok", 0), ("kdtok", 1)])
            P.op("dve", lambda e: e.memset(vlat[:], 0.0), w=["vlat"])
            P.op("dve", lambda e: e.memset(vlat[:, :, :, 0, 64:65], 1.0), w=["vlat"])
            P.op("dve", lambda e: e.memset(vlat[:, :, :, 1, 0:1], 1.0), w=["vlat"])
            for dc in range(8):
                wb = wup[dc % 2]
                P.dma(wb[:, 0], I["hwo"][dc], w=[("wup", dc % 2, 0)], q="pool")
                for th in range(2):
                    ps, pk = bank((dc * 2 + th) % 4)
                    for kk in range(8):
                        P.op("pe", lambda e, ps=ps, kk=kk, th=th, wb=wb: e.matmul(ps, lhsT=wb[:, 0, kk, :], rhs=aT[:, kk, th * 512:(th + 1) * 512], start=(kk == 0), stop=(kk == 7)),
                             r=[("wup", dc % 2, 0), ("aT", kk)], w=[pk])
                    P.op("dve", lambda e, ps=ps, dc=dc, th=th: e.scalar_tensor_tensor(out=xT[:, dc, th * 512:(th + 1) * 512], in0=ps, scalar=mod[:, 16 + dc:17 + dc],
                                                                                     in1=xT[:, dc, th * 512:(th + 1) * 512], op0=ALU.mult, op1=ALU.add),
                         r=[pk, "mod", ("xT", dc)], w=[("xT", dc)])

        def ssm(l):
            SEG = 256
            rs32 = rstd[:]
            sr_ = [rs32[:, i * 64:(i + 1) * 64] for i in range(16)]
            sq32 = sq[0][:].bitcast(F32)
            sq_ = [sq32[:, i * 32:(i + 1) * 32] for i in range(14)]
            sCq = [sq[1][:, i * 512:(i + 1) * 512].rearrange("p (g i) -> p g i", g=32) for i in range(2)]
            P.alias(["srr"], ["rstd"])
            P.alias(["sqq"], [("sq", 0)])
            P.alias(["sCq"], [("sq", 1)])
            m256 = m32
            P.op("pool", lambda e: e.memset(m256[:], 1.0), r=["m32"], w=["m32"])
            P.op("pool", lambda e: e.memset(m256[:, 0:NT:256], 0.0), r=["m32"], w=["m32"])
            P.dma(smask[:], I["smask"], w=["smask"])
            P.dma(sD[:], I["sD"], w=["sD"])
            TT = lambda e, o, a, b, op: e.tensor_tensor(out=o, in0=a, in1=b, op=op)

            def vop(eng, o, a, b, op, r, w):
                P.op(eng, lambda e, o=o, a=a, b=b, op=op: e.tensor_tensor(out=o, in0=a, in1=b, op=op), r=r, w=w)

            def cmul(eng, ore, oim, are_, aim_, bre, bim, t1, t2, r, w, tk):
                vop(eng, t1, are_, bre, ALU.mult, r, [tk[0]])
                vop(eng, t2, aim_, bim, ALU.mult, r, [tk[1]])
                vop(eng, ore, t1, t2, ALU.subtract, [tk[0], tk[1]], w)
                vop(eng, t1, are_, bim, ALU.mult, r, [tk[0]])
                vop(eng, t2, aim_, bre, ALU.mult, r, [tk[1]])
                vop(eng, oim, t1, t2, ALU.add, [tk[0], tk[1]], w)

            def lam_params(are_ap, aim_ap, dt_scalar_or_ap, S, key, n, per_part_dt, need_inv=True, eng="dve"):
                arec, th, mag, c, s_, t1, t2, imag = S[0], S[1], S[2], S[3], S[4], S[5], S[6], S[7]
                K = [key]
                P.op(eng, lambda e: e.tensor_scalar(out=arec[:], in0=are_ap, scalar1=-1e-4, scalar2=None, op0=ALU.min), r=K, w=K)
                if per_part_dt:
                    dtx = S[8]
                    P.op("act", lambda e: e.activation(out=dtx[:, 0:1], in_=dt_scalar_or_ap, func=AF.Exp), r=K, w=K)
                    P.op(eng, lambda e: e.tensor_scalar(out=th[:], in0=aim_ap, scalar1=dtx[:, 0:1], scalar2=None, op0=ALU.mult), r=K, w=K)
                    P.op(eng, lambda e: e.tensor_scalar(out=mag[:], in0=arec[:], scalar1=dtx[:, 0:1], scalar2=None, op0=ALU.mult), r=K, w=K)
                else:
                    dtx = S[8]
                    P.op("act", lambda e: e.activation(out=dtx[:], in_=dt_scalar_or_ap, func=AF.Exp), r=K, w=K)
                    vop(eng, th[:], aim_ap, dtx[:], ALU.mult, K, K)
                    vop(eng, mag[:], arec[:], dtx[:], ALU.mult, K, K)
                P.op("act", lambda e: e.activation(out=imag[:], in_=mag[:], func=AF.Exp, scale=-1.0), r=K, w=K)
                P.op("act", lambda e: e.activation(out=mag[:], in_=mag[:], func=AF.Exp), r=K, w=K)
                P.op("act", lambda e: e.activation(out=s_[:], in_=th[:], func=AF.Sin, scale=1.0 / 64), r=K, w=K)
                P.op("act", lambda e: e.activation(out=c[:], in_=th[:], func=AF.Sin, scale=1.0 / 64, bias=halfpi[:, 0:1]), r=K + ["halfpi"], w=K)
                for _ in range(6):
                    vop(eng, t1[:], c[:], s_[:], ALU.mult, K, K)
                    vop(eng, c[:], c[:], c[:], ALU.mult, K, K)
                    vop(eng, s_[:], s_[:], s_[:], ALU.mult, K, K)
                    vop(eng, c[:], c[:], s_[:], ALU.subtract, K, K)
                    P.op(eng, lambda e: e.tensor_scalar(out=s_[:], in0=t1[:], scalar1=2.0, scalar2=None, op0=ALU.mult), r=K, w=K)
                L1re, L1im, Lm1re, Lm1im = S[9], S[10], S[11], S[12]
                vop(eng, L1re[:], mag[:], c[:], ALU.mult, K, K)
                vop(eng, L1im[:], mag[:], s_[:], ALU.mult, K, K)
                if need_inv:
                    vop(eng, Lm1re[:], imag[:], c[:], ALU.mult, K, K)
                    vop(eng, Lm1im[:], imag[:], s_[:], ALU.mult, K, K)
                    P.op(eng, lambda e: e.tensor_scalar(out=Lm1im[:], in0=Lm1im[:], scalar1=-1.0, scalar2=None, op0=ALU.mult), r=K, w=K)
                return dict(L1re=L1re, L1im=L1im, Lm1re=Lm1re, Lm1im=Lm1im, are=arec)

            A_, B_, C_, D_, Gr, Gi = cg[0], cg[1], cv[0], cv[1], tmpA, tmpB
            kA, kB, kC2, kD, kGr, kGi = ("cg", 0), ("cg", 1), ("cv", 0), ("cv", 1), "tmpA", "tmpB"
            vl = vlat[:].rearrange("p a g v n -> p (a g v n)").bitcast(F32)
            Tre_all = vl[:, 0:2048].rearrange("p (g t) -> p g t", g=8)
            Tim_all = vl[:, 2048:4096].rearrange("p (g t) -> p g t", g=8)
            Tp_re, Tm_re, Tp_im, Tm_im = Tre_all[:, 0:4], Tre_all[:, 4:8], Tim_all[:, 0:4], Tim_all[:, 4:8]
            P.alias(["stab"], ["vlat"])
            vc = vctx[:].rearrange("p a g v n -> p (a g v n)")
            W1pad = vc[:, 0:4096].rearrange("p (q r n) -> p q r n", q=16, r=2)
            kcf = kctxT[:].rearrange("p a n -> p (a n)")
            Ewpad = kcf[:, 0:2048].rearrange("p (q r n) -> p q r n", q=8, r=2)
            P.alias(["W1pad"], ["vctx"])
            P.alias(["Ewpad"], ["kctxT"])
            Hre_b = qhat[0][:].rearrange("p (g t) -> p g t", g=4)
            Him_b = qhat[1][:].rearrange("p (g t) -> p g t", g=4)
            ysb = aT

            def _body():
              for d in range(2):
                  P.dma(sq_[0], I["sAreQ"][d], w=["sqq"])
                  P.dma(sq_[1], I["sAimQ"][d], w=["sqq"])
                  P.dma(sq_[13], I["sDtQ"][d], w=["sqq"])
                  P.dma(sCq[0], I["sCreQ"][d], w=["sCq"], q="pool")
                  P.dma(sCq[1], I["sCimQ"][d], w=["sCq"], q="pool")
                  P.dma(sH[:], I["sH0"][d], w=["sH"])
                  Q = lam_params(sq_[0], sq_[1], sq_[13], sq_[2:13] + [sq_[0], sq_[1]], "sqq", 32, False)
                  for s in range(8):
                      k0, k1 = s // 2, 4 + s // 2
                      g8b = (4 * s) % 8
                      P.op("pool", lambda e: e.memset(kcf[:, 0:2048], 0.0), r=["Ewpad"], w=["Ewpad"])
                      if s % 2 == 0:
                          P.op("pool", lambda e: e.memset(vc[:], 0.0), r=["W1pad"], w=["W1pad"])
                      if s % 2 == 0:
                          for half, kk_ in enumerate((k0, k1)):
                              Rk = ["srr"]
                              P.dma(sr_[0], I["sAreR"][d, kk_], w=Rk)
                              P.dma(sr_[1], I["sAimR"][d, kk_], w=Rk)
                              P.dma(sr_[13][:, 0:1], I["sDtR"][d, kk_], w=Rk)
                              P.dma(sr_[14], I["sBreR"][d, kk_], w=Rk)
                              P.dma(sr_[15], I["sBimR"][d, kk_], w=Rk)
                              R_ = lam_params(sr_[0], sr_[1], sr_[13][:, 0:1], sr_[2:13] + [sr_[0], sr_[0]], "srr", 64, True, need_inv=False)
                              nre, den, cre, cim, t1, t2 = sr_[3], sr_[4], sr_[5], sr_[6], sr_[7], sr_[8]
                              aim_ = sr_[1]
                              P.op("dve", lambda e, R_=R_: e.tensor_scalar(out=nre[:], in0=R_["L1re"][:], scalar1=-1.0, scalar2=None, op0=ALU.add), r=Rk, w=Rk)
                              vop("dve", den[:], R_["are"][:], R_["are"][:], ALU.mult, Rk, Rk)
                              vop("dve", t1[:], aim_[:], aim_[:], ALU.mult, Rk, Rk)
                              vop("dve", den[:], den[:], t1[:], ALU.add, Rk, Rk)
                              P.op("dve", lambda e: e.reciprocal(out=den[:], in_=den[:]), r=Rk, w=Rk)
                              vop("dve", t1[:], nre[:], R_["are"][:], ALU.mult, Rk, Rk)
                              vop("dve", t2[:], R_["L1im"][:], aim_[:], ALU.mult, Rk, Rk)
                              vop("dve", cre[:], t1[:], t2[:], ALU.add, Rk, Rk)
                              vop("dve", cre[:], cre[:], den[:], ALU.mult, Rk, Rk)
                              vop("dve", t1[:], R_["L1im"][:], R_["are"][:], ALU.mult, Rk, Rk)
                              vop("dve", t2[:], nre[:], aim_[:], ALU.mult, Rk, Rk)
                              vop("dve", cim[:], t1[:], t2[:], ALU.subtract, Rk, Rk)
                              vop("dve", cim[:], cim[:], den[:], ALU.mult, Rk, Rk)
                              wre, wim = sr_[9], sr_[10]
                              cmul("dve", wre[:], wim[:], cre[:], cim[:], sr_[14][:], sr_[15][:], t1[:], t2[:], Rk, Rk, ["srr", "srr"])
                              for g8 in range(8):
                                  for ri, wsrc in enumerate((wre, wim)):
                                      P.op("act", lambda e, half=half, ri=ri, wsrc=wsrc, g8=g8: e.activation(out=W1pad[:, half * 8 + g8, ri, half * 64:(half + 1) * 64], in_=wsrc[:], func=AF.Identity, scale=smask[:, g8:g8 + 1]),
                                           r=Rk + ["smask"], w=["W1pad"])
                      for half, kk_ in enumerate((k0, k1)):
                          for q in range(4):
                              g8 = g8b + q
                              gq = 4 * s + q
                              P.op("act", lambda e, q=q, half=half, g8=g8, gq=gq: e.activation(out=Ewpad[:, half * 4 + q, 0, g8 * 16:(g8 + 1) * 16], in_=sCq[0][:, gq, :], func=AF.Identity, scale=smask[:, 8 + half:9 + half]),
                                   r=["sCq", "smask"], w=["Ewpad"])
                              P.op("act", lambda e, q=q, half=half, g8=g8, gq=gq: e.activation(out=Ewpad[:, half * 4 + q, 1, g8 * 16:(g8 + 1) * 16], in_=sCq[1][:, gq, :], func=AF.Identity, scale=smask[:, 10 + half:11 + half]),
                                   r=["sCq", "smask"], w=["Ewpad"])
                      gsl = slice(4 * s, 4 * s + 4)
                      i0 = 0 if d == 0 else SEG - 1
                      for (Tre, Tim, bre_, bim_) in ((Tp_re, Tp_im, Q["L1re"], Q["L1im"]), (Tm_re, Tm_im, Q["Lm1re"], Q["Lm1im"])):
                          P.op("dve", lambda e, Tre=Tre, bre_=bre_, i0=i0, gsl=gsl: e.tensor_copy(out=Tre[:, :, i0:i0 + 1], in_=bre_[:, gsl].unsqueeze(2)), r=["sqq"], w=["stab"])
                          P.op("dve", lambda e, Tim=Tim, bim_=bim_, i0=i0, gsl=gsl: e.tensor_copy(out=Tim[:, :, i0:i0 + 1], in_=bim_[:, gsl].unsqueeze(2)), r=["sqq"], w=["stab"])
                      L = 1
                      while L < SEG:
                          if d == 0:
                              src = slice(0, L); dst = slice(L, 2 * L); piv = L - 1
                          else:
                              src = slice(SEG - L, SEG); dst = slice(SEG - 2 * L, SEG - L); piv = SEG - L
                          zr = Tre_all[:, :, piv:piv + 1].to_broadcast([128, 8, L])
                          zi = Tim_all[:, :, piv:piv + 1].to_broadcast([128, 8, L])
                          cmul("dve", Tre_all[:, :, dst], Tim_all[:, :, dst], Tre_all[:, :, src], Tim_all[:, :, src], zr, zi,
                               A_[:, 0:8 * L].rearrange("p (g t) -> p g t", g=8), B_[:, 0:8 * L].rearrange("p (g t) -> p g t", g=8), ["stab"], ["stab"], [kA, kB])
                          L *= 2
                      P.op("dve", lambda e, gsl=gsl: e.tensor_copy(out=sHent[:], in_=sH[:, gsl, :]), r=["sH"], w=["sHent"])
                      segs = range(4) if d == 0 else range(3, -1, -1)
                      for seg in segs:
                          tsl = slice(seg * SEG, (seg + 1) * SEG)
                          Sre, Sim = pq[0], pq[1]
                          skr = [("pq", 0), ("pq", 1)]; ski = [("pq", 2), ("pq", 3)]
                          for ri, (St, sk) in enumerate(((Sre, skr), (Sim, ski))):
                              for q in range(4):
                                  for half, kk_ in enumerate((k0, k1)):
                                      P.op("pe", lambda e, St=St, q=q, half=half, kk_=kk_, ri=ri, tsl=tsl, g8b=g8b: e.matmul(St[:, q * SEG:(q + 1) * SEG], lhsT=W1pad[:, half * 8 + g8b + q, ri, :], rhs=hT[:, kk_, tsl], start=(half == 0), stop=(half == 1)),
                                           r=["W1pad", ("hT", kk_)], w=[sk[q // 2]])
                          S3r = Sre[:].rearrange("p (g t) -> p g t", g=4); S3i = Sim[:].rearrange("p (g t) -> p g t", g=4)
                          A3, B3, C3, D3 = [x[:].rearrange("p (g t) -> p g t", g=4) for x in (A_, B_, C_, D_)]
                          G3r, G3i = Gr[:].rearrange("p (g t) -> p g t", g=4), Gi[:].rearrange("p (g t) -> p g t", g=4)
                          vop("dve", A3, S3r, Tm_re, ALU.mult, skr + ["stab"], [kA])
                          vop("dve", B3, S3i, Tm_im, ALU.mult, ski + ["stab"], [kB])
                          vop("dve", A3, A3, B3, ALU.subtract, [kA, kB], [kA])
                          vop("dve", C3, S3i, Tm_re, ALU.mult, ski + ["stab"], [kC2])
                          vop("dve", D3, S3r, Tm_im, ALU.mult, skr + ["stab"], [kD])
                          vop("dve", C3, C3, D3, ALU.add, [kC2, kD], [kC2])
                          for (src_, dstt, ks, kd_) in ((A_, Gr, kA, kGr), (C_, Gi, kC2, kGi)):
                              P.op("dve", lambda e, src_=src_, dstt=dstt: e.tensor_tensor_scan(out=dstt[:], data0=m256[:], data1=src_[:], initial=0.0, op0=ALU.mult, op1=ALU.add), r=["m32", ks], w=[kd_])
                              if d == 1:
                                  s3 = src_[:].rearrange("p (g t) -> p g t", g=4); d3 = dstt[:].rearrange("p (g t) -> p g t", g=4)
                                  P.op("act", lambda e, dstt=dstt: e.copy(out=ctmp[:, 0:4], in_=dstt[:, SEG - 1:NT:SEG]), r=[kd_], w=["ctmp"])
                                  P.op("dve", lambda e, s3=s3, d3=d3: e.tensor_tensor(out=d3, in0=s3, in1=d3, op=ALU.subtract), r=[ks, kd_], w=[kd_])
                                  P.op("dve", lambda e, d3=d3: e.tensor_tensor(out=d3, in0=d3, in1=ctmp[:, 0:4].unsqueeze(2).to_broadcast([128, 4, SEG]), op=ALU.add), r=[kd_, "ctmp"], w=[kd_])
                          vop("dve", G3r, G3r, sHent[:, :, 0:1].to_broadcast([128, 4, SEG]), ALU.add, [kGr, "sHent"], [kGr])
                          vop("dve", G3i, G3i, sHent[:, :, 1:2].to_broadcast([128, 4, SEG]), ALU.add, [kGi, "sHent"], [kGi])
                          vop("dve", A3, G3r, Tp_re, ALU.mult, [kGr, "stab"], [kA])
                          vop("dve", B3, G3i, Tp_im, ALU.mult, [kGi, "stab"], [kB])
                          vop("dve", Hre_b, A3, B3, ALU.subtract, [kA, kB], [("qhat", 0)])
                          vop("dve", C3, G3r, Tp_im, ALU.mult, [kGr, "stab"], [kC2])
                          vop("dve", D3, G3i, Tp_re, ALU.mult, [kGi, "stab"], [kD])
                          vop("dve", Him_b, C3, D3, ALU.add, [kC2, kD], [("qhat", 1)])
                          xi = SEG - 1 if d == 0 else 0
                          vop("dve", sHx[:, :, 0:1], A3[:, :, xi:xi + 1], B3[:, :, xi:xi + 1], ALU.subtract, [kA, kB], ["sHx"])
                          vop("dve", sHx[:, :, 1:2], C3[:, :, xi:xi + 1], D3[:, :, xi:xi + 1], ALU.add, [kC2, kD], ["sHx"])
                          for half in range(2):
                              P.dma(O["ssm"][seg * 2 + d, half * 32 + 4 * s: half * 32 + 4 * s + 4].rearrange("g p r -> p g r"), sHx[half * 64:(half + 1) * 64, :, :], r=["sHx"], sem="ssmout")
                          P.op("dve", lambda e: e.tensor_scalar(out=sHent[:], in0=sHx[:], scalar1=keep[:, 0:1], scalar2=None, op0=ALU.mult), r=["sHx", "keep"], w=["sHent"])
                          if STAGE.get("ssm_dbg"):
                              P.dma(O["dbg2"][:, 0:1024], A_[:], r=[kA], sem="dbg")
                              P.dma(O["dbg2"][:, 1024:2048], B_[:], r=[kB], sem="dbg")
                              P.dma(O["dbg2"][:, 2048:3072], C_[:], r=[kC2], sem="dbg")
                              P.dma(O["dbg2"][:, 3072:4096], D_[:], r=[kD], sem="dbg")
                              P.dma(O["dbg2"][:, 4096:5120], Gr[:], r=[kGr], sem="dbg")
                              P.dma(O["dbg2"][:, 5120:6144], Gi[:], r=[kGi], sem="dbg")
                              P.dma(O["dbg3"], hT[:].rearrange("p k t -> p (k t)"), r=[("hT", kk) for kk in range(8)], sem="dbg")
                              P.dma(O["dbg"][:, 0:4096], vl, r=["stab"], sem="dbg")
                              P.dma(O["dbg"][:, 4096:5120], rs32, r=["srr"], sem="dbg")
                              P.dma(O["dbg"][:, 5120:5632], sq32, r=["sqq"], sem="dbg")
                              raise StopIteration
                          for half, kk_ in enumerate((k0, k1)):
                              yp_, ypk = bank(4 + half)
                              n = 0
                              for q in range(4):
                                  for ri, Hb in enumerate((Hre_b, Him_b)):
                                      P.op("pe", lambda e, yp_=yp_, q=q, half=half, ri=ri, Hb=Hb, n=n: e.matmul(yp_[:, 0:SEG], lhsT=Ewpad[:, half * 4 + q, ri, :], rhs=Hb[:, q, :], start=(n == 0), stop=(n == 7)),
                                           r=["Ewpad", ("qhat", ri)], w=[ypk])
                                      n += 1
                              first = (d == 0 and s % 2 == 0)
                              if first:
                                  P.op("dve", lambda e, yp_=yp_, kk_=kk_, tsl=tsl: e.scalar_tensor_tensor(out=ysb[:, kk_, tsl], in0=hT[:, kk_, tsl], scalar=sD[:, kk_:kk_ + 1], in1=yp_[:, 0:SEG], op0=ALU.mult, op1=ALU.add),
                                       r=[ypk, ("hT", kk_), "sD"], w=[("aT", kk_)])
                              else:
                                  P.op("dve", lambda e, yp_=yp_, kk_=kk_, tsl=tsl: e.tensor_tensor(out=ysb[:, kk_, tsl], in0=yp_[:, 0:SEG], in1=ysb[:, kk_, tsl], op=ALU.add),
                                       r=[ypk, ("aT", kk_)], w=[("aT", kk_)])

            try:
                _body()
            except StopIteration:
                pass
            P.alias(["vlat"], ["stab"])
            P.alias(["vctx"], ["W1pad"])
            P.alias(["kctxT"], ["Ewpad"])
            P.alias(["rstd"], ["srr"])
            P.alias([("sq", 0)], ["sqq"])
            P.alias([("sq", 1)], ["sCq"])
            P.op("dve", lambda e: e.memset(vlat[:], 0.0), w=["vlat"])
            P.op("dve", lambda e: e.memset(vlat[:, :, :, 0, 64:65], 1.0), w=["vlat"])
            P.op("dve", lambda e: e.memset(vlat[:, :, :, 1, 0:1], 1.0), w=["vlat"])
            for kk_ in range(8):
                yk = ysb[:, kk_, :]
                P.op("dve", lambda e, yk=yk: e.tensor_tensor(out=A_[:], in0=yk, in1=yk, op=ALU.mult), r=[("aT", kk_)], w=[kA])
                P.op("dve", lambda e: e.tensor_scalar(out=A_[:], in0=A_[:], scalar1=0.044715, scalar2=1.0, op0=ALU.mult, op1=ALU.add), r=[kA], w=[kA])
                P.op("pool", lambda e, yk=yk: e.tensor_tensor(out=A_[:], in0=A_[:], in1=yk, op=ALU.mult), r=[kA, ("aT", kk_)], w=[kA])
                P.op("act", lambda e: e.activation(out=A_[:], in_=A_[:], func=AF.Sigmoid, scale=1.5957691216057308), r=[kA], w=[kA])
                P.op("pool", lambda e, yk=yk: e.tensor_tensor(out=yk, in0=A_[:], in1=yk, op=ALU.mult), r=[kA, ("aT", kk_)], w=[("aT", kk_)])
            for dc in range(8):
                wb = wup[dc % 2]
                P.dma(wb[:, 0], I["wglu"][dc], w=[("wup", dc % 2, 0)], q="pool")
                P.dma(wb[:, 1], I["wglu"][8 + dc], w=[("wup", dc % 2, 1)], q="pool")
                for th in range(2):
                    vps, vpk = bank((dc * 2 + th) % 4)
                    gps, gpk = bank(4 + (dc * 2 + th) % 4)
                    for gv, (ps, pk) in enumerate(((vps, vpk), (gps, gpk))):
                        for kk_ in range(8):
                            P.op("pe", lambda e, ps=ps, kk_=kk_, th=th, wb=wb, gv=gv: e.matmul(ps, lhsT=wb[:, gv, kk_, :], rhs=ysb[:, kk_, th * 512:(th + 1) * 512], start=(kk_ == 0), stop=(kk_ == 7)),
                                 r=[("wup", dc % 2, gv), ("aT", kk_)], w=[pk])
                    sl = slice(th * 512, (th + 1) * 512)
                    P.op("act", lambda e, gps=gps, sl=sl: e.activation(out=B_[:, sl], in_=gps, func=AF.Sigmoid), r=[gpk], w=[kB])
                    P.op("dve", lambda e, vps=vps, sl=sl: e.tensor_tensor(out=B_[:, sl], in0=vps, in1=B_[:, sl], op=ALU.mult), r=[vpk, kB], w=[kB])
                    P.op("dve", lambda e, dc=dc, sl=sl: e.scalar_tensor_tensor(out=xT[:, dc, sl], in0=B_[:, sl], scalar=mod[:, 16 + dc:17 + dc], in1=xT[:, dc, sl], op0=ALU.mult, op1=ALU.add),
                         r=[kB, "mod", ("xT", dc)], w=[("xT", dc)])

        eps_t = P.sb("eps_t", [128, 1])
        P.op("dve", lambda e: e.memset(eps_t[:], EPS), w=["eps"])
        halfpi = P.sb("halfpi", [128, 1])
        P.op("dve", lambda e: e.memset(halfpi[:], 1.5707963267948966), w=["halfpi"])


        for l in range(STAGE["layers"]):
            ada_layer(l)
            norm_mod(0)
            if STAGE["mixers"]:
                if l % 3 == 0:
                    attention(l)
                elif l % 3 == 1 and STAGE.get("hgrn", True):
                    hgrn(l)
                elif l % 3 == 2 and STAGE.get("ssm", True):
                    ssm(l)
            norm_mod(1)
            ffn(l)

        rms_stats()
        for k in range(8):
            P.op("dve", lambda e, k=k: e.scalar_tensor_tensor(out=xT[:, k, :], in0=xT[:, k, :], scalar=fg[:, k:k + 1], in1=rstd[:], op0=ALU.mult, op1=ALU.mult),
                 r=[("xT", k), "fg", "rstd"], w=[("xT", k)])
        for k in range(8):
            P.dma(O["y"][k * 128:(k + 1) * 128, :], xT[:, k, :], r=[("xT", k)], sem="yout")
        P.op("pool", lambda e: e.memset(rstd[:], 0.0), r=["rstd"], w=["rstd"])
        if not (STAGE["mixers"] and STAGE.get("hgrn", True) and STAGE["layers"] > 1):
            for i in range(8):
                P.dma(O["hg"][i * 8:(i + 1) * 8].rearrange("a p n -> p a n"), rstd[:].rearrange("p (a n) -> p a n", a=8), r=["rstd"], sem="sout")
        if not (STAGE["mixers"] and STAGE.get("ssm", True) and STAGE["layers"] > 2):
            for a_ in range(8):
                P.dma(O["ssm"][a_].rearrange("g p r -> g (p r)"), rstd[0:64, 0:128], r=["rstd"], sem="sout")
        P.wait_all_dma()
        P.emit()
    return nc

def _c(a):
    return np.ascontiguousarray(a, dtype=np.float32)


def prep_inputs(inp):
    g = {k: np.asarray(v) for k, v in inp.items()}
    sh = {}
    sh["ident"] = np.eye(128, dtype=np.float32)
    sh["ada_w"] = _c(g["ada_w"].reshape(DEPTH, 8, 128, 12, 512).transpose(0, 3, 2, 1, 4))
    sh["ada_b"] = _c(g["ada_b"].reshape(DEPTH, 48, 128).transpose(0, 2, 1))
    sh["ng"] = _c(np.stack([g["norm1_g"], g["norm2_g"]]).reshape(2, DEPTH, 8, 128).transpose(3, 0, 1, 2))
    sh["final_g"] = _c(g["final_g"].reshape(8, 128).T)
    sh["ffn_w_up"] = _c(g["ffn_w_up"].reshape(DEPTH, 8, 128, 2, NFC, 128).transpose(0, 4, 2, 3, 1, 5))
    sh["ffn_w_down"] = _c(g["ffn_w_down"].reshape(DEPTH, NFC, 128, 8, 128).transpose(0, 3, 2, 1, 4))
    sh["ffn_conv_w"] = _c(g["ffn_conv_w"].reshape(DEPTH, 3, 2 * NFC, 128).transpose(0, 3, 1, 2))
    sh["ffn_conv_b"] = _c(g["ffn_conv_b"].reshape(DEPTH, 2 * NFC, 128).transpose(0, 2, 1))
    wqkv = g["attn_wqkv"]
    sh["wq"] = _c(wqkv[:, :, 0:1024].reshape(2, 8, 128, 8, 128).transpose(0, 3, 2, 1, 4))
    wk = wqkv[:, :, 1024:1280].reshape(2, 8, 128, 4, 1, 64)
    sh["wk"] = _c(np.broadcast_to(wk, (2, 8, 128, 4, 2, 64)).reshape(2, 8, 128, 4, 128).transpose(0, 3, 2, 1, 4))
    sh["wkv"] = _c(wqkv[:, :, 1024:1536].reshape(2, 8, 128, 512).transpose(0, 2, 1, 3))
    sh["wo"] = _c(g["attn_wo"].reshape(2, 8, 128, 8, 128).transpose(0, 3, 2, 1, 4))
    sh["sink"] = _c(np.broadcast_to(g["attn_sink"][:, None, :], (2, 128, 16)))
    hw = g["hgrn_w_in"][0]
    sh["hw_in"] = _c(hw.reshape(8, 128, 5, 8, 128).transpose(3, 2, 1, 0, 4))
    sh["hwo"] = _c(g["hgrn_wo"][0].reshape(8, 128, 8, 128).transpose(2, 1, 0, 3))
    sh["hlb"] = _c(g["hgrn_lb"].reshape(4, 2, 8, 128).transpose(3, 0, 1, 2))
    sh["hgn"] = _c(g["hgrn_g_norm"][0].reshape(128, 1))
    ii = np.arange(128)
    same = (ii[:, None] // 32) == (ii[None, :] // 32)
    sh["hmask"] = _c(np.stack([same & (ii[:, None] <= ii[None, :]), same & (ii[:, None] >= ii[None, :])], axis=1))
    are, aim, ldt = g["ssm_a_re"][0], g["ssm_a_im"][0], g["ssm_log_dt"][0]
    bre, bim, cre, cim = g["ssm_b_re"][0], g["ssm_b_im"][0], g["ssm_c_re"][0], g["ssm_c_im"][0]
    sh["sBreR"] = _c(bre.reshape(2, 8, 8, 64, 16).transpose(0, 1, 2, 4, 3).reshape(2, 8, 128, 64))
    sh["sBimR"] = _c(bim.reshape(2, 8, 8, 64, 16).transpose(0, 1, 2, 4, 3).reshape(2, 8, 128, 64))
    sh["sAreR"] = _c(np.broadcast_to(are.reshape(2, 8, 8, 1, 64), (2, 8, 8, 16, 64)).reshape(2, 8, 128, 64))
    sh["sAimR"] = _c(np.broadcast_to(aim.reshape(2, 8, 8, 1, 64), (2, 8, 8, 16, 64)).reshape(2, 8, 128, 64))
    sh["sDtR"] = _c(np.broadcast_to(ldt.reshape(2, 8, 8, 1, 1), (2, 8, 8, 16, 1)).reshape(2, 8, 128, 1))
    sh["sAreQ"] = _c(are.reshape(2, 2, 32, 64).transpose(0, 1, 3, 2).reshape(2, 128, 32))
    sh["sAimQ"] = _c(aim.reshape(2, 2, 32, 64).transpose(0, 1, 3, 2).reshape(2, 128, 32))
    sh["sDtQ"] = _c(np.broadcast_to(ldt.reshape(2, 2, 1, 32), (2, 2, 64, 32)).reshape(2, 128, 32))
    sh["sCreQ"] = _c(cre.reshape(2, 2, 32, 16, 64).transpose(0, 1, 4, 2, 3).reshape(2, 128, 32, 16))
    sh["sCimQ"] = _c(cim.reshape(2, 2, 32, 16, 64).transpose(0, 1, 4, 2, 3).reshape(2, 128, 32, 16))
    sh["sD"] = _c(g["ssm_d"][0].reshape(8, 128).T)
    sm = np.zeros((128, 12), np.float32)
    for q in range(8):
        sm[q * 16:(q + 1) * 16, q] = 1.0
    sm[0:64, 8] = 1.0; sm[64:128, 9] = 1.0; sm[0:64, 10] = -1.0; sm[64:128, 11] = -1.0
    sh["smask"] = sm
    sh["wglu"] = _c(g["ssm_w_glu"][0].reshape(8, 128, 16, 128).transpose(2, 1, 0, 3))
    tt = np.arange(NT)
    inv = 1.0 / (10000.0 ** (np.arange(0, 32, 2, dtype=np.float32) / np.float32(32)))
    ar = (tt // 64).astype(np.float32)[:, None] * inv.astype(np.float32)
    ac = (tt % 64).astype(np.float32)[:, None] * inv.astype(np.float32)
    ang = np.concatenate([ar, ar, ac, ac], axis=-1).astype(np.float32)
    cosS = _c(np.concatenate([np.cos(ang).T] * 2, axis=0)); sinS = _c(np.concatenate([np.sin(ang).T] * 2, axis=0))
    rm = np.zeros((128, 128), np.float32)
    for m in range(128):
        d = m % 64
        if (d % 32) < 16:
            rm[m + 16, m] = -1.0
        else:
            rm[m - 16, m] = 1.0
    sh["rmat"] = rm
    kk = np.arange(128)[:, None]; qq = np.arange(128)[None, :]
    mS = np.zeros((128, 8, 2, 128), np.float32); mP = np.zeros((128, 8, 2, 128), np.float32)
    for i in range(8):
        if i >= 1:
            mS[:, i, 0, :] = (kk >= qq)
        if i <= 6:
            mS[:, i, 1, :] = (kk <= qq)
        mP[:, i, 0, :] = 1.0 if i % 2 == 1 else 0.0
        mP[:, i, 1, :] = 1.0 if i % 2 == 0 else 0.0
    if STAGE.get("kv_only"):
        for kk_ in ("wq", "wk", "wo", "sink", "rmat"):
            sh.pop(kk_, None)
    maps = []
    for c in range(8):
        m = dict(sh)
        if c < 4:
            xc = g["x_sample"][c]
            cond = g["c"][c]
            kp = 1.0
        else:
            xc = g["x_prompt"][4 * (c - 4):4 * (c - 4) + 4].reshape(NT, D)
            cond = g["c_ctx"]
            kp = 0.0
        if c < 4:
            ropec = cosS; ropes = sinS; m["amask"] = mS
            ck = g["cache_k"][c]
            m["ckd"] = _c(np.broadcast_to(ck[:, :, :, None, :], (2, 512, 4, 2, 64)).reshape(2, 512, 512))
            cv = g["cache_v"][c]
            vp = np.zeros((2, 512, 4, 2, 128), np.float32)
            vp[:, :, :, 0, 0:64] = cv; vp[:, :, :, 0, 64] = 1.0
            vp[:, :, :, 1, 64:128] = cv; vp[:, :, :, 1, 0] = 1.0
            m["cvp"] = vp.reshape(2, 512, 1024)
        else:
            ropec = np.ones((128, NT), np.float32); ropes = np.zeros((128, NT), np.float32); m["amask"] = mP
            m["ckd"] = np.zeros((2, 512, 512), np.float32); m["cvp"] = np.zeros((2, 512, 1024), np.float32)
        if STAGE.get("kv_only"):
            for kk_ in ("amask", "ckd", "cvp"):
                m.pop(kk_, None)
        if c < 4:
            st = g["state_ssm"][c, 0]
            m["sH0"] = _c(st.reshape(2, 2, 32, 64, 2).transpose(0, 1, 3, 2, 4).reshape(2, 128, 32, 2))
        else:
            m["sH0"] = np.zeros((2, 128, 32, 2), np.float32)
        m["hs0"] = _c(g["state_hgrn"][c, 0]) if c < 4 else np.zeros((2, 8, 128, 128), np.float32)
        m["x"] = _c(np.concatenate([xc.T, ropec, ropes], axis=0))
        m["cond"] = _c(cond.reshape(8, 128).T)
        m["keep"] = _c(np.stack([np.full(128, kp), np.full(128, kp - 1.0)], axis=1))
        maps.append(m)
    return maps


def assemble(results):
    f = lambda a: np.asarray(a, dtype=np.float32)
    ys = np.stack([f(results[c]["y"]).T for c in range(4)])
    yp = np.concatenate([f(results[c]["y"]).T.reshape(4, 256, D) for c in range(4, 8)])
    nk = np.concatenate([f(results[c]["nk"]).reshape(2, 4, 256, 4, 64).transpose(1, 0, 2, 3, 4) for c in range(4, 8)])
    nv = np.concatenate([f(results[c]["nv"]).reshape(2, 4, 256, 4, 64).transpose(1, 0, 2, 3, 4) for c in range(4, 8)])
    hg = np.concatenate([f(results[c]["hg"]).reshape(4, 1, 2, 8, 128, 128) for c in range(4, 8)])
    ssm = np.concatenate([f(results[c]["ssm"]).reshape(4, 1, 2, 64, 64, 2) for c in range(4, 8)])
    return (np.ascontiguousarray(yp), np.ascontiguousarray(ys), np.ascontiguousarray(nk), np.ascontiguousarray(nv),
            np.ascontiguousarray(hg), np.ascontiguousarray(ssm))


def kernel(**inputs):
    nc = build_program()
    maps = prep_inputs(inputs)
    res = run_bass_kernel_spmd(nc, maps, core_ids=list(range(8)))
    return assemble(res.results)
```

```python
import numpy as np
from concourse.bass_utils import run_bass_kernel_spmd

from contextlib import ExitStack
import concourse.bass as bass
import concourse.mybir as mybir

F32 = mybir.dt.float32
F32R = mybir.dt.float32r
BF16 = mybir.dt.bfloat16
AF = mybir.ActivationFunctionType
ALU = mybir.AluOpType
AX = mybir.AxisListType


class Prog:
    ENGS = ("pe", "act", "dve", "pool", "sp")

    def __init__(self, nc, es: ExitStack):
        self.nc = nc
        self.es = es
        self.recs = {e: [] for e in self.ENGS}
        self.cnt = {e: 0 for e in self.ENGS}
        self.known = {e: {} for e in self.ENGS}
        self.state = {}
        self.sems = {}
        self.dcnt = {}
        for e in self.ENGS:
            self.sems[("e", e)] = es.enter_context(nc.semaphore("sem_" + e))
        self.psn = 0

    def sb(self, name, shape, dt=F32):
        return self.es.enter_context(self.nc.sbuf_tensor("sb_" + name, list(shape), dt))

    def ps(self, name, shape, dt=F32):
        return self.es.enter_context(self.nc.psum_tensor(name, list(shape), dt))

    def dsem(self, name):
        k = ("d", name)
        if k not in self.sems:
            self.sems[k] = self.es.enter_context(self.nc.semaphore("dsem_" + name))
            self.dcnt[k] = 0
        return k

    def _st(self, k):
        s = self.state.get(k)
        if s is None:
            s = {"w": {}, "r": {}}
            self.state[k] = s
        return s

    def _deps(self, eng, r, w):
        deps = {}
        def add(d):
            if d is None:
                return
            sk, v = d
            if deps.get(sk, 0) < v:
                deps[sk] = v
        for k in r:
            st = self._st(k)
            for sk, v in st["w"].items():
                add((sk, v))
            if isinstance(k, tuple) and k[0] == "pq":
                for sk, v in st["r"].items():
                    if sk != ("e", eng):
                        add((sk, v))
        for k in w:
            s = self._st(k)
            for sk, v in s["w"].items():
                add((sk, v))
            for sk, v in s["r"].items():
                add((sk, v))
        waits = []
        kn = self.known[eng]
        for sk, v in deps.items():
            if eng == "pe" and sk == ("e", "pe"):
                continue
            if kn.get(sk, 0) < v:
                waits.append((sk, v))
                kn[sk] = v
        return waits

    def _commit(self, comp, r, w):
        sk, v = comp
        for k in w:
            self.state[k] = {"w": {sk: v}, "r": {}}
        for k in r:
            s = self._st(k)
            if s["r"].get(sk, 0) < v:
                s["r"][sk] = v

    def alias(self, dst, src):
        mw, mr = {}, {}
        for k in src:
            st = self._st(k)
            for sk, v in st["w"].items():
                mw[sk] = max(mw.get(sk, 0), v)
            for sk, v in st["r"].items():
                mr[sk] = max(mr.get(sk, 0), v)
        for k in dst:
            self.state[k] = {"w": dict(mw), "r": dict(mr)}

    def op(self, eng, fn, r=(), w=(), inc=True):
        inc = True
        waits = self._deps(eng, r, w)
        sk = ("e", eng)
        comp = (sk, self.cnt[eng] + 1)
        if inc:
            self.cnt[eng] += 1
        self.recs[eng].append((waits, fn, (sk, 1) if inc else None))
        self._commit(comp, r, w)

    def dma(self, out, in_, r=(), w=(), sem=None, q="sp", **kw):
        if sem is None:
            sem = "_".join(str(x) for x in (w[0] if isinstance(w[0], tuple) else (w[0],)))
        waits = self._deps(q, r, w)
        sk = self.dsem(sem)
        self.dcnt[sk] += 16
        comp = (sk, self.dcnt[sk])
        self.recs[q].append((waits, (lambda e, o=out, i=in_, kw=kw: e.dma_start(out=o, in_=i, **kw)), (sk, 16)))
        self._commit(comp, r, w)

    def wait_all_dma(self, q="sp"):
        waits = []
        for sk, v in self.dcnt.items():
            if v > 0 and self.known[q].get(sk, 0) < v:
                waits.append((sk, v))
                self.known[q][sk] = v
        self.recs[q].append((waits, None, None))

    def barrier_all(self):
        tgt = {("e", e): self.cnt[e] for e in self.ENGS if self.cnt[e] > 0}
        for sk, v in self.dcnt.items():
            if v > 0:
                tgt[sk] = v
        for e in self.ENGS:
            waits = []
            for sk, v in tgt.items():
                if sk == ("e", e):
                    continue
                if self.known[e].get(sk, 0) < v:
                    waits.append((sk, v))
                    self.known[e][sk] = v
            if waits:
                self.recs[e].append((waits, None, None))

    def emit(self):
        nc = self.nc
        sems = self.sems
        recs = self.recs

        def replay(name):
            def f(e):
                for waits, fn, inc in recs[name]:
                    for sk, v in waits:
                        e.wait_ge(sems[sk], v)
                    if fn is not None:
                        ins = fn(e)
                        if inc is not None:
                            ins.then_inc(sems[inc[0]], inc[1])
            return f

        with nc.Block() as block:
            block.tensor(replay("pe"))
            block.scalar(replay("act"))
            block.vector(replay("dve"))
            block.gpsimd(replay("pool"))
            block.sync(replay("sp"))

    def stats(self):
        return {e: len(self.recs[e]) for e in self.ENGS}

D = 1024
NT = 1024
DFF = 2816
NFC = 22
DEPTH = 4
EPS = 1e-6

STAGE = {"mixers": True, "layers": 4, "kv_only": False}


def build_program():
    nc = bass.Bass("TRN2", target_bir_lowering=False)

    def din(name, shape, dt=F32):
        return nc.dram_tensor(name, list(shape), dt, kind="ExternalInput").ap()

    def dout(name, shape, dt=F32):
        return nc.dram_tensor(name, list(shape), dt, kind="ExternalOutput").ap()

    I = {}
    I["x"] = din("x", [D + 256, NT])
    I["cond"] = din("cond", [128, 8])
    I["keep"] = din("keep", [128, 2])
    I["ident"] = din("ident", [128, 128])
    I["ada_w"] = din("ada_w", [DEPTH, 12, 128, 8, 512])
    I["ada_b"] = din("ada_b", [DEPTH, 128, 48])
    I["ng"] = din("ng", [128, 2, DEPTH, 8])
    I["ffn_w_up"] = din("ffn_w_up", [DEPTH, NFC, 128, 2, 8, 128])
    I["ffn_conv_w"] = din("ffn_conv_w", [DEPTH, 128, 3, 2 * NFC])
    I["ffn_conv_b"] = din("ffn_conv_b", [DEPTH, 128, 2 * NFC])
    I["ffn_w_down"] = din("ffn_w_down", [DEPTH, 8, 128, NFC, 128])
    I["final_g"] = din("final_g", [128, 8])
    I["wkv"] = din("wkv", [2, 128, 8, 512])
    I["hw_in"] = din("hw_in", [8, 5, 128, 8, 128])
    I["hwo"] = din("hwo", [8, 128, 8, 128])
    I["hlb"] = din("hlb", [128, 4, 2, 8])
    I["hgn"] = din("hgn", [128, 1])
    I["hs0"] = din("hs0", [2, 8, 128, 128])
    I["hmask"] = din("hmask", [128, 2, 128])
    I["sBreR"] = din("sBreR", [2, 8, 128, 64]); I["sBimR"] = din("sBimR", [2, 8, 128, 64])
    I["sAreR"] = din("sAreR", [2, 8, 128, 64]); I["sAimR"] = din("sAimR", [2, 8, 128, 64])
    I["sDtR"] = din("sDtR", [2, 8, 128, 1])
    I["sAreQ"] = din("sAreQ", [2, 128, 32]); I["sAimQ"] = din("sAimQ", [2, 128, 32]); I["sDtQ"] = din("sDtQ", [2, 128, 32])
    I["sCreQ"] = din("sCreQ", [2, 128, 32, 16]); I["sCimQ"] = din("sCimQ", [2, 128, 32, 16])
    I["sH0"] = din("sH0", [2, 128, 32, 2])
    I["sD"] = din("sD", [128, 8])
    I["smask"] = din("smask", [128, 12])
    I["wglu"] = din("wglu", [16, 128, 8, 128])
    if not STAGE.get("kv_only"):
        I["wq"] = din("wq", [2, 8, 128, 8, 128])
        I["wk"] = din("wk", [2, 4, 128, 8, 128])
        I["wo"] = din("wo", [2, 8, 128, 8, 128])
        I["sink"] = din("sink", [2, 128, 16])
        I["rmat"] = din("rmat", [128, 128])
        I["amask"] = din("amask", [128, 8, 2, 128])
        I["ckd"] = din("ckd", [2, 512, 512])
        I["cvp"] = din("cvp", [2, 512, 1024])
    O = {}
    O["y"] = dout("y", [D, NT])
    O["nk"] = dout("nk", [2, NT, 256])
    O["nv"] = dout("nv", [2, NT, 256])
    O["hg"] = dout("hg", [64, 128, 128])
    O["ssm"] = dout("ssm", [8, 64, 64, 2])
    if STAGE.get("ssm_dbg"):
        O["dbg"] = dout("dbg", [128, 4096 + 1024 + 512])
        O["dbg2"] = dout("dbg2", [128, 6 * 1024])
        O["dbg3"] = dout("dbg3", [128, 8 * 1024], BF16)

    with ExitStack() as es:
        P = Prog(nc, es)
        xT = P.sb("xT", [128, 10, NT])
        hT = P.sb("hT", [128, 8, NT], BF16)
        aT = P.sb("aT", [128, 12, NT], BF16)
        rstd = P.sb("rstd", [128, NT])
        tmpA = P.sb("tmpA", [128, NT])
        tmpB = P.sb("tmpB", [128, NT])
        cg = [P.sb("cg%d" % i, [128, NT]) for i in range(2)]
        cv = [P.sb("cv%d" % i, [128, NT]) for i in range(2)]
        sq = [P.sb("sq%d" % i, [128, NT], BF16) for i in range(2)]
        ident = P.sb("ident", [128, 128])
        ones_bf = P.sb("ones_bf", [128, 128], BF16)
        one_f = P.sb("one_f", [128, 1])
        keep = P.sb("keep", [128, 2])
        cond = P.sb("cond", [128, 8])
        s_bf = P.sb("s_bf", [128, 8], BF16)
        mod = P.sb("mod", [128, 48])
        adab = P.sb("adab", [128, 48])
        ng = P.sb("ng", [128, 2, DEPTH, 8])
        fg = P.sb("fg", [128, 8])
        AB = P.sb("AB", [128, 4, 8])
        cw = P.sb("cw", [128, 3, 2 * NFC])
        cb = P.sb("cb", [128, 2 * NFC])
        cwk = P.sb("cwk", [128, 2, 2 * NFC])
        wada = [P.sb("wada%d" % i, [128, 8, 512], BF16) for i in range(2)]
        wup = [P.sb("wup%d" % i, [128, 2, 8, 128], BF16) for i in range(2)]
        wdn = [P.sb("wdn%d" % i, [128, 11, 128], BF16) for i in range(2)]
        vlat = P.sb("vlat", [128, 8, 4, 2, 128], BF16)
        vctx = P.sb("vctx", [128, 4, 4, 2, 128], BF16)
        kctxT = P.sb("kctxT", [128, 4, 512], BF16)
        ckd = P.sb("ckd", [128, 4, 512], BF16)
        amask = P.sb("amask", [128, 8, 2, 128], BF16)
        rmat = P.sb("rmat", [128, 128], BF16)
        ident_bf = P.sb("ident_bf", [128, 128], BF16)
        qb = [P.sb("qb%d" % i, [128, 512], BF16) for i in range(2)]
        esink = P.sb("esink", [128, 16])
        ones_f = P.sb("ones_f", [128, 128])
        m32 = P.sb("m32", [128, NT])
        hlb = P.sb("hlb", [128, 4, 2, 8])
        lbp = P.sb("lbp", [128, 2, 2, 8])
        hgn = P.sb("hgn", [128, 1])
        hmask = P.sb("hmask", [128, 2, 128], BF16)
        Sf = [P.sb("Sf%d" % i, [128, 128]) for i in range(2)]
        Sb = [P.sb("Sb%d" % i, [128, 128], BF16) for i in range(2)]
        adec = [P.sb("adec%d" % i, [128, 32]) for i in range(2)]
        rowm = P.sb("rowm", [128, 4])
        ctmp = P.sb("ctmp", [128, 32])
        vtok = P.sb("vtok", [128, 8, 128], BF16)
        qhat = [P.sb("qhat%d" % i, [128, NT], BF16) for i in range(2)]
        ktil = [P.sb("ktil%d" % i, [128, NT], BF16) for i in range(2)]
        kdT = [P.sb("kdT%d" % i, [128, NT], BF16) for i in range(2)]
        attm = [P.sb("attm%d" % i, [128, 128], BF16) for i in range(2)]
        smask = P.sb("smask", [128, 12])
        sD = P.sb("sD", [128, 8])
        sH = P.sb("sH", [128, 32, 2])
        sHent = P.sb("sHent", [128, 4, 2])
        sHx = P.sb("sHx", [128, 4, 2])
        pq = [P.ps("pq%d" % i, [128, 1024]) for i in range(4)]

        def bank(i):
            return pq[i // 2][:, (i % 2) * 512:(i % 2) * 512 + 512], ("pq", i)

        P.dma(ident[:], I["ident"], w=["ident"])
        P.dma(keep[:], I["keep"], w=["keep"])
        P.dma(cond[:], I["cond"], w=["cond"])
        P.dma(ng[:], I["ng"], w=["ng"])
        P.dma(fg[:], I["final_g"], w=["fg"])
        P.op("dve", lambda e: e.memset(ones_bf[:], 1.0), w=["ones_bf"])
        P.op("dve", lambda e: e.memset(one_f[:], 1.0), w=["one_f"])
        P.op("dve", lambda e: e.memset(ones_f[:], 1.0), w=["ones_f"])
        P.op("dve", lambda e: e.tensor_copy(out=ident_bf[:], in_=ident[:]), r=["ident"], w=["ident_bf"])
        if not STAGE.get("kv_only"):
            P.dma(rmat[:], I["rmat"], w=["rmat"], q="pool")
            P.dma(amask[:], I["amask"], w=["amask"], q="pool")
        P.op("pool", lambda e: e.memset(m32[:], 1.0), w=["m32"])
        P.op("pool", lambda e: e.memset(m32[:, 0:NT:32], 0.0), w=["m32"])
        P.op("pool", lambda e: e.memset(rowm[:], 0.0), w=["rowm"])
        for c4 in range(3):
            P.op("pool", lambda e, c4=c4: e.memset(rowm[c4 * 32:(c4 + 1) * 32, c4:c4 + 1], 1.0), w=["rowm"])
        P.op("pool", lambda e: e.memset(rowm[96:128, 3:4], 1.0), w=["rowm"])
        P.dma(hlb[:], I["hlb"], w=["hlb"])
        P.dma(hgn[:], I["hgn"], w=["hgn"])
        P.dma(hmask[:], I["hmask"], w=["hmask"], q="pool")
        P.op("dve", lambda e: e.memset(vlat[:], 0.0), w=["vlat"])
        P.op("dve", lambda e: e.memset(vlat[:, :, :, 0, 64:65], 1.0), w=["vlat"])
        P.op("dve", lambda e: e.memset(vlat[:, :, :, 1, 0:1], 1.0), w=["vlat"])
        P.op("act", lambda e: e.activation(out=s_bf[:], in_=cond[:], func=AF.Silu), r=["cond"], w=["s_bf"])

        P.dma(xT[:], I["x"].rearrange("(k p) t -> p k t", p=128), w=[("xT", k) for k in range(10)], sem="xin")

        def ada_layer(l):
            P.dma(adab[:], I["ada_b"][l], w=["adab"])
            ps, pk = bank(4)
            for n in range(12):
                wb = wada[n % 2]
                P.dma(wb[:], I["ada_w"][l, n],
                      w=[("wada", n % 2)], q="pool")
                for c4 in range(4):
                    c = n * 4 + c4
                    for k in range(8):
                        P.op("pe", lambda e, ps=ps, k=k, wb=wb, c=c, c4=c4: e.matmul(ps[:, c:c + 1], lhsT=wb[:, k, c4 * 128:(c4 + 1) * 128], rhs=s_bf[:, k:k + 1], start=(k == 0), stop=(k == 7)),
                             r=[("wada", n % 2), "s_bf"], w=[pk])
            P.op("dve", lambda e, ps=ps: e.tensor_tensor(out=mod[:], in0=ps[:, 0:48], in1=adab[:], op=ALU.add), r=[pk, "adab"], w=["mod"])
            for j in range(2):
                P.op("dve", lambda e, j=j: e.scalar_tensor_tensor(out=AB[:, 2 * j, :], in0=mod[:, (3 * j + 1) * 8:(3 * j + 2) * 8], scalar=1.0,
                                                                 in1=ng[:, j, l, :], op0=ALU.add, op1=ALU.mult),
                     r=["mod", "ng"], w=["AB"])
                P.op("dve", lambda e, j=j: e.tensor_copy(out=AB[:, 2 * j + 1, :], in_=mod[:, (3 * j) * 8:(3 * j + 1) * 8]), r=["mod"], w=["AB"])

        def rms_stats():
            b0, k0 = bank(6)
            b1, k1 = bank(7)
            for k in range(8):
                s = sq[k % 2]
                P.op("act", lambda e, s=s, k=k: e.activation(out=s[:], in_=xT[:, k, :], func=AF.Square), r=[("xT", k)], w=[("sq", k % 2)])
                for th, (b, bk) in enumerate(((b0, k0), (b1, k1))):
                    P.op("pe", lambda e, b=b, s=s, th=th, k=k: e.matmul(b, lhsT=ones_bf[:], rhs=s[:, th * 512:(th + 1) * 512], start=(k == 0), stop=(k == 7)),
                         r=[("sq", k % 2), "ones_bf"], w=[bk], inc=True)
            for th, (b, bk) in enumerate(((b0, k0), (b1, k1))):
                P.op("act", lambda e, b=b, th=th: e.activation(out=tmpA[:, th * 512:(th + 1) * 512], in_=b, func=AF.Ln, scale=1.0 / D, bias=eps_t[:, 0:1]),
                     r=[bk, "eps"], w=["tmpA"])
            P.op("act", lambda e: e.activation(out=rstd[:], in_=tmpA[:], func=AF.Exp, scale=-0.5), r=["tmpA"], w=["rstd"])

        def norm_mod(j):
            rms_stats()
            for k in range(8):
                t = tmpA if k % 2 == 0 else tmpB
                tk = "tmpA" if k % 2 == 0 else "tmpB"
                P.op("dve", lambda e, t=t, k=k: e.scalar_tensor_tensor(out=t[:], in0=xT[:, k, :], scalar=AB[:, 2 * j, k:k + 1], in1=rstd[:], op0=ALU.mult, op1=ALU.mult),
                     r=[("xT", k), "AB", "rstd"], w=[tk])
                P.op("act", lambda e, t=t, k=k: e.activation(out=hT[:, k, :], in_=t[:], func=AF.Identity, bias=AB[:, 2 * j + 1, k:k + 1], scale=1.0),
                     r=[tk, "AB"], w=[("hT", k)])

        def ffn(l):
            P.dma(cw[:], I["ffn_conv_w"][l], w=["cw"])
            P.dma(cb[:], I["ffn_conv_b"][l], w=["cb"])
            for jj, j in enumerate((0, 2)):
                P.op("dve", lambda e, jj=jj, j=j: e.tensor_scalar(out=cwk[:, jj, :], in0=cw[:, j, :], scalar1=keep[:, 1:2], scalar2=None, op0=ALU.mult),
                     r=["cw", "keep"], w=["cwk"])
            for grp in range(2):
                for fc in range(grp * 11, grp * 11 + 11):
                    wb = wup[fc % 2]
                    P.dma(wb[:], I["ffn_w_up"][l, fc], w=[("wup", fc % 2, 0), ("wup", fc % 2, 1)], q="pool")
                    outs = []
                    for gv in range(2):
                        pt = pq[(fc % 2) * 2 + gv]
                        pks = [("pq", ((fc % 2) * 2 + gv) * 2 + th) for th in range(2)]
                        for th in range(2):
                            for k in range(8):
                                P.op("pe", lambda e, pt=pt, th=th, k=k, gv=gv, wb=wb: e.matmul(pt[:, th * 512:(th + 1) * 512], lhsT=wb[:, gv, k, :], rhs=hT[:, k, th * 512:(th + 1) * 512],
                                                                                    start=(k == 0), stop=(k == 7)),
                                     r=[("wup", fc % 2, gv), ("hT", k)], w=[pks[th]], inc=(k == 7))
                        c = (cg if gv == 0 else cv)[fc % 2]
                        ck = ("cg" if gv == 0 else "cv", fc % 2)
                        col = gv * NFC + fc
                        P.op("act", lambda e, c=c, pt=pt, col=col: e.activation(out=c[:], in_=pt[:], func=AF.Identity, scale=cw[:, 1, col:col + 1], bias=cb[:, col:col + 1]),
                             r=pks + ["cw", "cb"], w=[ck])
                        P.op("dve", lambda e, c=c, pt=pt, col=col: e.scalar_tensor_tensor(out=c[:, 1:NT], in0=pt[:, 0:NT - 1], scalar=cw[:, 0, col:col + 1], in1=c[:, 1:NT], op0=ALU.mult, op1=ALU.add),
                             r=pks + ["cw", ck], w=[ck])
                        P.op("dve", lambda e, c=c, pt=pt, col=col: e.scalar_tensor_tensor(out=c[:, 0:NT - 1], in0=pt[:, 1:NT], scalar=cw[:, 2, col:col + 1], in1=c[:, 0:NT - 1], op0=ALU.mult, op1=ALU.add),
                             r=pks + ["cw", ck], w=[ck])
                        P.op("dve", lambda e, c=c, pt=pt, col=col: e.scalar_tensor_tensor(out=c[:, 256:NT:256], in0=pt[:, 255:NT - 1:256], scalar=cwk[:, 0, col:col + 1], in1=c[:, 256:NT:256], op0=ALU.mult, op1=ALU.add),
                             r=pks + ["cwk", ck], w=[ck])
                        P.op("dve", lambda e, c=c, pt=pt, col=col: e.scalar_tensor_tensor(out=c[:, 255:NT - 1:256], in0=pt[:, 256:NT:256], scalar=cwk[:, 1, col:col + 1], in1=c[:, 255:NT - 1:256], op0=ALU.mult, op1=ALU.add),
                             r=pks + ["cwk", ck], w=[ck])
                        outs.append((c, ck))
                    (cgt, cgk), (cvt, cvk) = outs
                    P.op("act", lambda e, cgt=cgt: e.activation(out=cgt[:], in_=cgt[:], func=AF.Silu), r=[cgk], w=[cgk])
                    P.op("dve", lambda e, cgt=cgt, cvt=cvt, fc=fc: e.tensor_tensor(out=aT[:, fc % 11, :], in0=cgt[:], in1=cvt[:], op=ALU.mult), r=[cgk, cvk], w=[("aT", fc % 11)])
                for dc in range(8):
                    wb = wdn[dc % 2]
                    P.dma(wb[:], I["ffn_w_down"][l, dc][:, grp * 11:grp * 11 + 11, :],
                          w=[("wdn", dc % 2)], q="pool")
                    for th in range(2):
                        ps, pk = bank((dc * 2 + th) % 8)
                        for fc in range(11):
                            P.op("pe", lambda e, ps=ps, fc=fc, th=th, wb=wb: e.matmul(ps, lhsT=wb[:, fc, :], rhs=aT[:, fc, th * 512:(th + 1) * 512], start=(fc == 0), stop=(fc == 10)),
                                 r=[("wdn", dc % 2), ("aT", fc)], w=[pk])
                        P.op("dve", lambda e, ps=ps, dc=dc, th=th: e.scalar_tensor_tensor(out=xT[:, dc, th * 512:(th + 1) * 512], in0=ps, scalar=mod[:, 40 + dc:41 + dc],
                                                                                         in1=xT[:, dc, th * 512:(th + 1) * 512], op0=ALU.mult, op1=ALU.add),
                             r=[pk, "mod", ("xT", dc)], w=[("xT", dc)])

        def attention(l):
            j = l // 3
            cosT, sinT = xT[:, 8, :], xT[:, 9, :]
            t1, t2 = cv[0], cv[1]
            if STAGE.get("kv_only"):
                wkv = wada[0]
                P.dma(wkv[:], I["wkv"][j], w=[("wada", 0)], q="pool")
                for tb in range(8):
                    ps, pk = bank(tb % 4)
                    for k in range(8):
                        P.op("pe", lambda e, ps=ps, k=k, tb=tb: e.matmul(ps, lhsT=hT[:, k, tb * 128:(tb + 1) * 128], rhs=wkv[:, k, :], start=(k == 0), stop=(k == 7)),
                             r=[("wada", 0), ("hT", k)], w=[pk])
                    kvt = tmpA if tb % 2 == 0 else tmpB
                    kvk = "tmpA" if tb % 2 == 0 else "tmpB"
                    P.op("act", lambda e, ps=ps, kvt=kvt: e.copy(out=kvt[:, 0:512], in_=ps), r=[pk], w=[kvk])
                    P.dma(O["nk"][j, tb * 128:(tb + 1) * 128, :], kvt[:, 0:256], r=[kvk], sem="kvout%d" % (tb % 2))
                    P.dma(O["nv"][j, tb * 128:(tb + 1) * 128, :], kvt[:, 256:512], r=[kvk], sem="kvout%d" % (tb % 2))
                return
            P.dma(esink[:], I["sink"][j], w=["esink"])
            P.op("act", lambda e: e.activation(out=esink[:], in_=esink[:], func=AF.Exp), r=["esink"], w=["esink"])
            if STAGE.get("attn_upto", 9) < 1:
                return
            P.dma(ckd[:], I["ckd"][j].rearrange("(kb p) n -> p kb n", p=128), w=["ckd"], q="pool")
            P.dma(vctx[:].rearrange("p kb g v n -> p kb (g v n)"), I["cvp"][j].rearrange("(kb p) n -> p kb n", p=128), w=["vctx"], q="pool")
            for g in range(4):
                ps, pk = bank(g)
                for kb in range(4):
                    P.op("pe", lambda e, ps=ps, kb=kb, g=g: e.matmul(ps[:, kb * 128:(kb + 1) * 128], lhsT=ckd[:, kb, g * 128:(g + 1) * 128], rhs=ident_bf[:], start=True, stop=True),
                         r=["ckd", "ident_bf"], w=[pk])
                P.op("act", lambda e, ps=ps, g=g: e.copy(out=kctxT[:, g, :], in_=ps), r=[pk], w=["kctxT"])
            if STAGE.get("attn_upto", 9) < 2:
                return
            def proj_rope(wsrc, dst, dkey, idx):
                wb = wup[idx % 2]
                P.dma(wb[:, 0], wsrc, w=[("wup", idx % 2, 0)], q="pool")
                for th in range(2):
                    ps, pk = bank((idx * 2 + th) % 4)
                    rps, rpk = bank(4 + (idx * 2 + th) % 2)
                    for k in range(8):
                        P.op("pe", lambda e, ps=ps, k=k, th=th, wb=wb: e.matmul(ps, lhsT=wb[:, 0, k, :], rhs=hT[:, k, th * 512:(th + 1) * 512], start=(k == 0), stop=(k == 7)),
                             r=[("wup", idx % 2, 0), ("hT", k)], w=[pk])
                    q_ = qb[th]
                    if STAGE.get("pr", 9) < 1:
                        continue
                    P.op("act", lambda e, ps=ps, q_=q_: e.copy(out=q_[:], in_=ps), r=[pk], w=[("qb", th)])
                    if STAGE.get("pr", 9) < 2:
                        continue
                    P.op("pe", lambda e, rps=rps, q_=q_: e.matmul(rps, lhsT=rmat[:], rhs=q_[:], start=True, stop=True), r=[("qb", th), "rmat"], w=[rpk])
                    sl = slice(th * 512, (th + 1) * 512)
                    if STAGE.get("pr", 9) < 3 or idx >= STAGE.get("pridx", 99):
                        continue
                    P.op("dve", lambda e, ps=ps, sl=sl: e.scalar_tensor_tensor(out=t1[:, sl], in0=ps, scalar=1.0, in1=cosT[:, sl], op0=ALU.mult, op1=ALU.mult), r=[pk, ("xT", 8), ("qb", th)], w=[("cv", 0)])
                    if STAGE.get("pr", 9) < 4:
                        continue
                    P.op("dve", lambda e, rps=rps, sl=sl: e.scalar_tensor_tensor(out=t2[:, sl], in0=rps, scalar=1.0, in1=sinT[:, sl], op0=ALU.mult, op1=ALU.mult), r=[rpk, ("xT", 9)], w=[("cv", 1)])
                    if STAGE.get("pr", 9) < 5:
                        continue
                    P.op("dve", lambda e, sl=sl, dst=dst: e.tensor_tensor(out=dst[:, sl], in0=t1[:, sl], in1=t2[:, sl], op=ALU.add), r=[("cv", 0), ("cv", 1)], w=[dkey])
            for qc in range(8):
                proj_rope(I["wq"][j, qc], aT[:, qc, :], ("aT", qc), qc)
            for g in range(4):
                proj_rope(I["wk"][j, g], aT[:, 8 + g, :], ("aT", 8 + g), 8 + g)
            if STAGE.get("attn_upto", 9) < 3:
                return
            wkv = wada[0]
            P.dma(wkv[:], I["wkv"][j], w=[("wada", 0)], q="pool")
            for tb in range(8):
                ps, pk = bank(tb % 4)
                for k in range(8):
                    P.op("pe", lambda e, ps=ps, k=k, tb=tb: e.matmul(ps, lhsT=hT[:, k, tb * 128:(tb + 1) * 128], rhs=wkv[:, k, :], start=(k == 0), stop=(k == 7)),
                         r=[("wada", 0), ("hT", k)], w=[pk])
                kvt = tmpA if tb % 2 == 0 else tmpB
                kvk = "tmpA" if tb % 2 == 0 else "tmpB"
                P.op("act", lambda e, ps=ps, kvt=kvt: e.copy(out=kvt[:, 0:512], in_=ps), r=[pk], w=[kvk])
                P.dma(O["nk"][j, tb * 128:(tb + 1) * 128, :], kvt[:, 0:256], r=[kvk], sem="kvout%d" % (tb % 2))
                P.dma(O["nv"][j, tb * 128:(tb + 1) * 128, :], kvt[:, 256:512], r=[kvk], sem="kvout%d" % (tb % 2))
                P.op("dve", lambda e, ps=ps, tb=tb: e.tensor_copy(out=vlat[:, tb, :, 0, 0:64], in_=ps[:, 256:512].rearrange("p (g d) -> p g d", g=4)), r=[pk], w=["vlat"])
                P.op("dve", lambda e, ps=ps, tb=tb: e.tensor_copy(out=vlat[:, tb, :, 1, 64:128], in_=ps[:, 256:512].rearrange("p (g d) -> p g d", g=4)), r=[pk], w=["vlat"])
            if STAGE.get("attn_upto", 9) < 4:
                return
            dsb = tmpA[:].rearrange("p (a n) -> p a n", a=2)
            osb = [cv[0][:, 0:512], cv[1][:, 0:512]]
            tb_bf = tmpB[:].bitcast(BF16)
            ebuf = [tb_bf[:, i * 512:(i + 1) * 512] for i in range(4)]
            P.alias(["dsb"], ["tmpA"])
            P.alias([("osb", 0)], [("cv", 0)])
            P.alias([("osb", 1)], [("cv", 1)])
            P.alias([("ebuf", i) for i in range(4)], ["tmpB"])
            sc = 0
            for h in range(STAGE.get("nheads", 16)):
                g, qc, pb, var = h // 4, h // 2, (h % 2) * 64, h % 2
                dr = 64 if var == 0 else 0
                qh = aT[pb:pb + 64, qc, :]
                kh = aT[pb:pb + 64, 8 + g, :]
                kch = kctxT[pb:pb + 64, g, :]
                for th in range(2):
                    it = h * 2 + th
                    OP, opk = bank(4 + it % 2)
                    jbs = [jb for jb in range(8) if max(jb - 1, 4 * th) <= min(jb + 1, 4 * th + 3)]
                    blocks = [("c", kb) for kb in range(4)] + [("l", jb) for jb in jbs]
                    LA = 2
                    pend = []

                    def emit_S(kind, ix):
                        nonlocal sc
                        ps, pk = bank(sc % 4); eb = ebuf[sc % 4]; ek = ("ebuf", sc % 4); sc += 1
                        if kind == "c":
                            kb = ix
                            P.op("pe", lambda e, ps=ps, kb=kb, th=th, kch=kch, qh=qh: e.matmul(ps, lhsT=kch[:, kb * 128:(kb + 1) * 128], rhs=qh[:, th * 512:(th + 1) * 512], start=True, stop=True),
                                 r=["kctxT", ("aT", qc)], w=[pk])
                            P.op("act", lambda e, ps=ps, eb=eb: e.activation(out=eb[:], in_=ps, func=AF.Exp, scale=0.125), r=[pk], w=[ek])
                            return (kind, ix, eb, ek, 0, 512)
                        jb = ix
                        i0_ = max(jb - 1, 4 * th); i1_ = min(jb + 1, 4 * th + 3)
                        n = (i1_ - i0_ + 1) * 128
                        P.op("pe", lambda e, ps=ps, jb=jb, i0_=i0_, n=n, kh=kh, qh=qh: e.matmul(ps[:, 0:n], lhsT=kh[:, jb * 128:(jb + 1) * 128], rhs=qh[:, i0_ * 128:i0_ * 128 + n], start=True, stop=True),
                             r=[("aT", 8 + g), ("aT", qc)], w=[pk])
                        P.op("act", lambda e, ps=ps, eb=eb, n=n: e.activation(out=eb[:, 0:n], in_=ps[:, 0:n], func=AF.Exp, scale=0.125), r=[pk], w=[ek])
                        for i in range(i0_, i1_ + 1):
                            if i == jb:
                                continue
                            off = 0 if i == jb + 1 else 1
                            c0 = (i - i0_) * 128
                            P.op("dve", lambda e, eb=eb, c0=c0, i=i, off=off: e.tensor_tensor(out=eb[:, c0:c0 + 128], in0=eb[:, c0:c0 + 128], in1=amask[:, i, off, :], op=ALU.mult),
                                 r=[ek, "amask"], w=[ek])
                        return (kind, ix, eb, ek, (i0_ - 4 * th) * 128, n)

                    def emit_PV(st, is_first, is_last):
                        kind, ix, eb, ek, o0, n = st
                        if kind == "c":
                            P.op("pe", lambda e, OP=OP, eb=eb, ix=ix, g=g, var=var, is_first=is_first: e.matmul(OP, lhsT=vctx[:, ix, g, var, :], rhs=eb[:], start=is_first, stop=False),
                                 r=["vctx", ek], w=[opk])
                        else:
                            P.op("pe", lambda e, OP=OP, eb=eb, ix=ix, g=g, var=var, o0=o0, n=n, is_last=is_last: e.matmul(OP[:, o0:o0 + n], lhsT=vlat[:, ix, g, var, :], rhs=eb[:, 0:n], start=False, stop=is_last),
                                 r=["vlat", ek], w=[opk])

                    nb = len(blocks)
                    for i in range(nb + LA):
                        if i < nb:
                            pend.append(emit_S(*blocks[i]))
                        if i - LA >= 0:
                            emit_PV(pend[i - LA], i - LA == 0, i - LA == nb - 1)
                    P.op("dve", lambda e, OP=OP, dr=dr, h=h: e.tensor_scalar(out=dsb[dr:dr + 1, 0, :], in0=OP[dr:dr + 1, :], scalar1=esink[dr:dr + 1, h:h + 1], scalar2=None, op0=ALU.add),
                         r=[opk, "esink"], w=["dsb"])
                    P.op("act", lambda e, dr=dr: e.activation(out=dsb[dr:dr + 1, 0, :], in_=dsb[dr:dr + 1, 0, :], func=AF.Ln), r=["dsb"], w=["dsb"])
                    P.op("act", lambda e, dr=dr: e.activation(out=dsb[dr:dr + 1, 1, :], in_=dsb[dr:dr + 1, 0, :], func=AF.Exp, scale=-1.0), r=["dsb"], w=["dsb"])
                    BC, bck = bank(6 + it % 2)
                    P.op("pe", lambda e, BC=BC, dr=dr: e.matmul(BC, lhsT=ones_f[dr:dr + 1, :], rhs=dsb[dr:dr + 1, 1, :], start=True, stop=True), r=["dsb", "ones_f"], w=[bck])
                    ob = osb[it % 2]; obk = ("osb", it % 2)
                    P.op("act", lambda e, OP=OP, ob=ob, pb=pb: e.copy(out=ob[pb:pb + 64, :], in_=OP[pb:pb + 64, :]), r=[opk], w=[obk])
                    P.op("dve", lambda e, BC=BC, ob=ob, pb=pb, qc=qc, th=th: e.tensor_tensor(out=aT[pb:pb + 64, qc, th * 512:(th + 1) * 512], in0=ob[pb:pb + 64, :], in1=BC[pb:pb + 64, :], op=ALU.mult),
                         r=[obk, bck], w=[("aT", qc)])
            P.alias(["tmpA"], ["dsb"])
            P.alias([("cv", 0)], [("osb", 0)])
            P.alias([("cv", 1)], [("osb", 1)])
            P.alias(["tmpB"], [("ebuf", i) for i in range(4)])
            for dc in range(8):
                wb = wup[dc % 2]
                P.dma(wb[:, 0], I["wo"][j, dc], w=[("wup", dc % 2, 0)], q="pool")
                for th in range(2):
                    ps, pk = bank((dc * 2 + th) % 4)
                    for k in range(8):
                        P.op("pe", lambda e, ps=ps, k=k, th=th, wb=wb: e.matmul(ps, lhsT=wb[:, 0, k, :], rhs=aT[:, k, th * 512:(th + 1) * 512], start=(k == 0), stop=(k == 7)),
                             r=[("wup", dc % 2, 0), ("aT", k)], w=[pk])
                    P.op("dve", lambda e, ps=ps, dc=dc, th=th: e.scalar_tensor_tensor(out=xT[:, dc, th * 512:(th + 1) * 512], in0=ps, scalar=mod[:, 16 + dc:17 + dc],
                                                                                     in1=xT[:, dc, th * 512:(th + 1) * 512], op0=ALU.mult, op1=ALU.add),
                         r=[pk, "mod", ("xT", dc)], w=[("xT", dc)])

        def hgrn(l):
            CH = 32
            NCH = NT // CH
            P.op("act", lambda e: e.activation(out=hlb[:], in_=hlb[:], func=AF.Exp), r=["hlb"], w=["hlb"])
            P.op("dve", lambda e: e.tensor_tensor(out=lbp[:, 1], in0=hlb[:, 0], in1=hlb[:, 1], op=ALU.add), r=["hlb"], w=["lbp"])
            P.op("dve", lambda e: e.tensor_tensor(out=lbp[:, 1], in0=lbp[:, 1], in1=hlb[:, 2], op=ALU.add), r=["hlb", "lbp"], w=["lbp"])
            P.op("dve", lambda e: e.tensor_tensor(out=lbp[:, 1], in0=lbp[:, 1], in1=hlb[:, 3], op=ALU.add), r=["hlb", "lbp"], w=["lbp"])
            P.op("dve", lambda e: e.reciprocal(out=lbp[:, 1], in_=lbp[:, 1]), r=["lbp"], w=["lbp"])
            P.op("dve", lambda e: e.tensor_tensor(out=lbp[:, 0], in0=lbp[:, 1], in1=hlb[:, 1], op=ALU.mult), r=["hlb", "lbp"], w=["lbp"])
            P.op("dve", lambda e: e.tensor_scalar(out=lbp[:, 1], in0=lbp[:, 0], scalar1=-1.0, scalar2=1.0, op0=ALU.mult, op1=ALU.add), r=["lbp"], w=["lbp"])
            bufQ, bufF, bufK, bufC = cg[0], cg[1], cv[0], cv[1]
            kQ, kF, kK, kC = ("cg", 0), ("cg", 1), ("cv", 0), ("cv", 1)
            oacc = rstd
            kdtok = [vlat[:].rearrange("p a g v n -> p (a g v n)")[:, d * 4096:(d + 1) * 4096].rearrange("p (b c n) -> p b c n", b=8, c=4) for d in range(2)]
            P.alias([("kdtok", 0), ("kdtok", 1)], ["vlat"])
            for h in range(8):
                P.dma(wup[0][:], I["hw_in"][h, 0:2].rearrange("a p k n -> p a k n"), w=[("wup", 0, 0), ("wup", 0, 1)], q="pool")
                P.dma(wup[1][:], I["hw_in"][h, 2:4].rearrange("a p k n -> p a k n"), w=[("wup", 1, 0), ("wup", 1, 1)], q="pool")
                P.dma(wdn[0][:, 0:8, :], I["hw_in"][h, 4], w=[("wdn", 0)], q="pool")

                def proj(wap, wkeys, bi):
                    outs = []
                    for th in range(2):
                        ps, pk = bank(bi * 2 + th)
                        for kk in range(8):
                            P.op("pe", lambda e, ps=ps, kk=kk, th=th, wap=wap: e.matmul(ps, lhsT=wap[:, kk, :], rhs=hT[:, kk, th * 512:(th + 1) * 512], start=(kk == 0), stop=(kk == 7)),
                                 r=list(wkeys) + [("hT", kk)], w=[pk])
                        outs.append((ps, pk))
                    return outs
                for th, (ps, pk) in enumerate(proj(wup[0][:, 0], [("wup", 0, 0)], 0)):
                    P.op("act", lambda e, ps=ps, th=th: e.copy(out=bufQ[:, th * 512:(th + 1) * 512], in_=ps), r=[pk], w=[kQ])
                for blk in range(8):
                    ps, pk = bank(2 + blk % 2)
                    for kk in range(8):
                        P.op("pe", lambda e, ps=ps, kk=kk, blk=blk: e.matmul(ps[:, 0:128], lhsT=hT[:, kk, blk * 128:(blk + 1) * 128], rhs=wup[0][:, 1, kk, :], start=(kk == 0), stop=(kk == 7)),
                             r=[("wup", 0, 1), ("hT", kk)], w=[pk])
                    P.op("act", lambda e, ps=ps, blk=blk: e.copy(out=vtok[:, blk, :], in_=ps[:, 0:128]), r=[pk], w=["vtok"])
                for d in range(2):
                    for th, (ps, pk) in enumerate(proj(wup[1][:, d], [("wup", 1, d)], 2 + d)):
                        P.op("act", lambda e, ps=ps, th=th: e.activation(out=bufF[:, th * 512:(th + 1) * 512], in_=ps, func=AF.Sigmoid), r=[pk], w=[kF])
                    P.op("dve", lambda e, d=d, h=h: e.tensor_scalar(out=bufF[:], in0=bufF[:], scalar1=lbp[:, 1, d, h:h + 1], scalar2=lbp[:, 0, d, h:h + 1], op0=ALU.mult, op1=ALU.add),
                         r=[kF, "lbp"], w=[kF])
                    P.op("dve", lambda e: e.tensor_scalar(out=bufK[:], in0=bufF[:], scalar1=-1.0, scalar2=1.0, op0=ALU.mult, op1=ALU.add), r=[kF], w=[kK])
                    P.op("act", lambda e: e.activation(out=bufF[:], in_=bufF[:], func=AF.Ln), r=[kF], w=[kF])
                    P.op("dve", lambda e: e.tensor_tensor_scan(out=bufC[:], data0=m32[:], data1=bufF[:], initial=0.0, op0=ALU.mult, op1=ALU.add), r=["m32", kF], w=[kC])
                    if d == 1:
                        P.op("dve", lambda e: e.scalar_tensor_tensor(out=tmpA[:], in0=bufC[:], scalar=-1.0, in1=bufF[:], op0=ALU.mult, op1=ALU.add), r=[kC, kF], w=["tmpA"])
                        P.op("act", lambda e: e.copy(out=ctmp[:], in_=bufC[:, CH - 1:NT:CH]), r=[kC], w=["ctmp"])
                        P.op("dve", lambda e: e.tensor_tensor(out=bufC[:].rearrange("p (c t) -> p c t", t=CH), in0=tmpA[:].rearrange("p (c t) -> p c t", t=CH),
                                                              in1=ctmp[:].unsqueeze(2).to_broadcast([128, NCH, CH]), op=ALU.add),
                             r=["tmpA", "ctmp"], w=[kC])
                        ctot = bufC[:, 0:NT:CH]
                    else:
                        ctot = bufC[:, CH - 1:NT:CH]
                    P.op("act", lambda e, d=d, ctot=ctot: e.activation(out=adec[d][:], in_=ctot, func=AF.Exp), r=[kC], w=[("adec", d)])
                    P.op("act", lambda e: e.activation(out=tmpA[:], in_=bufC[:], func=AF.Exp), r=[kC], w=["tmpA"])
                    P.op("dve", lambda e, d=d: e.tensor_tensor(out=qhat[d][:], in0=bufQ[:], in1=tmpA[:], op=ALU.mult), r=[kQ, "tmpA"], w=[("qhat", d)])
                    P.op("dve", lambda e: e.tensor_scalar(out=tmpB[:], in0=bufC[:], scalar1=-1.0, scalar2=85.0, op0=ALU.mult, op1=ALU.min), r=[kC], w=["tmpB"])
                    P.op("act", lambda e: e.activation(out=tmpB[:], in_=tmpB[:], func=AF.Exp), r=["tmpB"], w=["tmpB"])
                    P.op("dve", lambda e: e.tensor_tensor(out=tmpB[:], in0=tmpB[:], in1=bufK[:], op=ALU.mult), r=["tmpB", kK], w=["tmpB"])
                    P.op("act", lambda e, d=d: e.copy(out=ktil[d][:], in_=tmpB[:]), r=["tmpB"], w=[("ktil", d)])
                    P.op("dve", lambda e, d=d: e.tensor_tensor(out=kdT[d][:].rearrange("p (c t) -> p c t", t=CH), in0=tmpB[:].rearrange("p (c t) -> p c t", t=CH),
                                                                in1=adec[d][:].unsqueeze(2).to_broadcast([128, NCH, CH]), op=ALU.mult),
                         r=["tmpB", ("adec", d)], w=[("kdT", d)])
                    for blk in range(8):
                        ps, pk = bank(6 + blk % 2)
                        pst = ps.bitcast(BF16)
                        P.op("pe", lambda e, pst=pst, blk=blk, d=d: e.transpose(pst[:, 0:128], kdT[d][:, blk * 128:(blk + 1) * 128], ident_bf[:]), r=[("kdT", d), "ident_bf"], w=[pk])
                        for c4 in range(4):
                            P.op("act", lambda e, pst=pst, blk=blk, c4=c4, d=d: e.activation(out=kdtok[d][:, blk, c4, :], in_=pst[:, 0:128], func=AF.Identity, scale=rowm[:, c4:c4 + 1]),
                                 r=[pk, "rowm"], w=[("kdtok", d)])
                    P.dma(Sf[d][:], I["hs0"][d, h], w=[("Sf", d)])
                    P.op("act", lambda e, d=d: e.copy(out=Sb[d][:], in_=Sf[d][:]), r=[("Sf", d)], w=[("Sb", d)])
                P.op("pool", lambda e: e.memset(oacc[:], 0.0), r=["rstd"], w=["rstd"])
                for bi_ in range(8):
                    for d in range(2):
                        blk = bi_ if d == 0 else 7 - bi_
                        bsl = slice(blk * 128, (blk + 1) * 128)
                        aps, apk = bank(0 + d * 2)
                        ops_, opk = bank(1 + d * 2)
                        dps, dpk = bank(4 + d)
                        P.op("pe", lambda e, aps=aps, d=d, bsl=bsl: e.matmul(aps[:, 0:128], lhsT=ktil[d][:, bsl], rhs=qhat[d][:, bsl], start=True, stop=True),
                             r=[("ktil", d), ("qhat", d)], w=[apk])
                        am = attm[d]
                        P.op("dve", lambda e, aps=aps, am=am, d=d: e.tensor_tensor(out=am[:], in0=aps[:, 0:128], in1=hmask[:, d, :], op=ALU.mult), r=[apk, "hmask"], w=[("attm", d)])
                        P.op("pe", lambda e, ops_=ops_, am=am, blk=blk: e.matmul(ops_[:, 0:128], lhsT=vtok[:, blk, :], rhs=am[:], start=True, stop=False),
                             r=["vtok", ("attm", d)], w=[opk])
                        for c4 in range(4):
                            P.op("pe", lambda e, dps=dps, c4=c4, blk=blk, d=d: e.matmul(dps[:, c4 * 128:(c4 + 1) * 128], lhsT=kdtok[d][:, blk, c4, :], rhs=vtok[:, blk, :], start=True, stop=True),
                                 r=[("kdtok", d), "vtok"], w=[dpk])
                        cs = range(4) if d == 0 else range(3, -1, -1)
                        for c4 in cs:
                            ch = blk * 4 + c4
                            csl = slice(ch * CH, (ch + 1) * CH)
                            P.op("pe", lambda e, ops_=ops_, c4=c4, csl=csl, d=d, cs=cs: e.matmul(ops_[:, c4 * CH:(c4 + 1) * CH], lhsT=Sb[d][:], rhs=qhat[d][:, csl], start=False, stop=(c4 == list(cs)[-1])),
                                 r=[("Sb", d), ("qhat", d)], w=[opk])
                            P.op("dve", lambda e, dps=dps, c4=c4, ch=ch, d=d: e.scalar_tensor_tensor(out=Sf[d][:], in0=Sf[d][:], scalar=adec[d][:, ch:ch + 1], in1=dps[:, c4 * 128:(c4 + 1) * 128], op0=ALU.mult, op1=ALU.add),
                                 r=[("Sf", d), ("adec", d), dpk], w=[("Sf", d)])
                            seg_end = (ch % 8 == 7) if d == 0 else (ch % 8 == 0)
                            if seg_end:
                                seg = ch // 8
                                P.dma(O["hg"][(seg * 2 + d) * 8 + h], Sf[d][:], r=[("Sf", d)], sem="hgout%d" % d)
                                P.op("dve", lambda e, d=d: e.tensor_scalar(out=Sf[d][:], in0=Sf[d][:], scalar1=keep[:, 0:1], scalar2=None, op0=ALU.mult), r=[("Sf", d), "keep"], w=[("Sf", d)])
                            P.op("act", lambda e, d=d: e.copy(out=Sb[d][:], in_=Sf[d][:]), r=[("Sf", d)], w=[("Sb", d)])
                        P.op("dve", lambda e, ops_=ops_, bsl=bsl: e.tensor_tensor(out=oacc[:, bsl], in0=ops_[:, 0:128], in1=oacc[:, bsl], op=ALU.add), r=[opk, "rstd"], w=["rstd"])
                P.op("act", lambda e: e.activation(out=sq[0][:], in_=oacc[:], func=AF.Square), r=["rstd"], w=[("sq", 0)])
                for th in range(2):
                    ps, pk = bank(6 + th)
                    P.op("pe", lambda e, ps=ps, th=th: e.matmul(ps, lhsT=ones_bf[:], rhs=sq[0][:, th * 512:(th + 1) * 512], start=True, stop=True), r=[("sq", 0), "ones_bf"], w=[pk])
                    P.op("act", lambda e, ps=ps, th=th: e.activation(out=tmpA[:, th * 512:(th + 1) * 512], in_=ps, func=AF.Ln, scale=1.0 / 128, bias=eps_t[:, 0:1]), r=[pk, "eps"], w=["tmpA"])
                P.op("act", lambda e: e.activation(out=tmpA[:], in_=tmpA[:], func=AF.Exp, scale=-0.5), r=["tmpA"], w=["tmpA"])
                P.op("dve", lambda e: e.scalar_tensor_tensor(out=tmpA[:], in0=oacc[:], scalar=hgn[:, 0:1], in1=tmpA[:], op0=ALU.mult, op1=ALU.mult), r=["rstd", "hgn", "tmpA"], w=["tmpA"])
                for th, (ps, pk) in enumerate(proj(wdn[0][:, 0:8, :], [("wdn", 0)], 2)):
                    P.op("act", lambda e, ps=ps, th=th: e.activation(out=tmpB[:, th * 512:(th + 1) * 512], in_=ps, func=AF.Silu), r=[pk], w=["tmpB"])
                P.op("dve", lambda e, h=h: e.tensor_tensor(out=aT[:, h, :], in0=tmpA[:], in1=tmpB[:], op=ALU.mult), r=["tmpA", "tmpB"], w=[("aT", h)])
            P.alias(["vlat"], [("kdtok", 0), ("kdtok", 1)])
            P.op("dve", lambda e: e.memset(vlat[:], 0.0), w=["vlat"])
            P.op("dve", lambda e: e.memset(vlat[:, :, :, 0, 64:65], 1.0), w=["vlat"])
            P.op("dve", lambda e: e.memset(vlat[:, :, :, 1, 0:1], 1.0), w=["vlat"])
            for dc in range(8):
                wb = wup[dc % 2]
                P.dma(wb[:, 0], I["hwo"][dc], w=[("wup", dc % 2, 0)], q="pool")
                for th in range(2):
                    ps, pk = bank((dc * 2 + th) % 4)
                    for kk in range(8):
                        P.op("pe", lambda e, ps=ps, kk=kk, th=th, wb=wb: e.matmul(ps, lhsT=wb[:, 0, kk, :], rhs=aT[:, kk, th * 512:(th + 1) * 512], start=(kk == 0), stop=(kk == 7)),
                             r=[("wup", dc % 2, 0), ("aT", kk)], w=[pk])
                    P.op("dve", lambda e, ps=ps, dc=dc, th=th: e.scalar_tensor_tensor(out=xT[:, dc, th * 512:(th + 1) * 512], in0=ps, scalar=mod[:, 16 + dc:17 + dc],
                                                                                     in1=xT[:, dc, th * 512:(th + 1) * 512], op0=ALU.mult, op1=ALU.add),
                         r=[pk, "mod", ("xT", dc)], w=[("xT", dc)])

        def ssm(l):
            SEG = 256
            rs32 = rstd[:]
            sr_ = [rs32[:, i * 64:(i + 1) * 64] for i in range(16)]
            sq32 = sq[0][:].bitcast(F32)
            sq_ = [sq32[:, i * 32:(i + 1) * 32] for i in range(14)]
            sCq = [sq[1][:, i * 512:(i + 1) * 512].rearrange("p (g i) -> p g i", g=32) for i in range(2)]
            P.alias(["srr"], ["rstd"])
            P.alias(["sqq"], [("sq", 0)])
            P.alias(["sCq"], [("sq", 1)])
            m256 = m32
            P.op("pool", lambda e: e.memset(m256[:], 1.0), r=["m32"], w=["m32"])
            P.op("pool", lambda e: e.memset(m256[:, 0:NT:256], 0.0), r=["m32"], w=["m32"])
            P.dma(smask[:], I["smask"], w=["smask"])
            P.dma(sD[:], I["sD"], w=["sD"])
            TT = lambda e, o, a, b, op: e.tensor_tensor(out=o, in0=a, in1=b, op=op)

            def vop(eng, o, a, b, op, r, w):
                P.op(eng, lambda e, o=o, a=a, b=b, op=op: e.tensor_tensor(out=o, in0=a, in1=b, op=op), r=r, w=w)

            def cmul(eng, ore, oim, are_, aim_, bre, bim, t1, t2, r, w, tk):
                vop(eng, t1, are_, bre, ALU.mult, r, [tk[0]])
                vop(eng, t2, aim_, bim, ALU.mult, r, [tk[1]])
                vop(eng, ore, t1, t2, ALU.subtract, [tk[0], tk[1]], w)
                vop(eng, t1, are_, bim, ALU.mult, r, [tk[0]])
                vop(eng, t2, aim_, bre, ALU.mult, r, [tk[1]])
                vop(eng, oim, t1, t2, ALU.add, [tk[0], tk[1]], w)

            def lam_params(are_ap, aim_ap, dt_scalar_or_ap, S, key, n, per_part_dt, need_inv=True, eng="dve"):
                arec, th, mag, c, s_, t1, t2, imag = S[0], S[1], S[2], S[3], S[4], S[5], S[6], S[7]
                K = [key]
                P.op(eng, lambda e: e.tensor_scalar(out=arec[:], in0=are_ap, scalar1=-1e-4, scalar2=None, op0=ALU.min), r=K, w=K)
                if per_part_dt:
                    dtx = S[8]
                    P.op("act", lambda e: e.activation(out=dtx[:, 0:1], in_=dt_scalar_or_ap, func=AF.Exp), r=K, w=K)
                    P.op(eng, lambda e: e.tensor_scalar(out=th[:], in0=aim_ap, scalar1=dtx[:, 0:1], scalar2=None, op0=ALU.mult), r=K, w=K)
                    P.op(eng, lambda e: e.tensor_scalar(out=mag[:], in0=arec[:], scalar1=dtx[:, 0:1], scalar2=None, op0=ALU.mult), r=K, w=K)
                else:
                    dtx = S[8]
                    P.op("act", lambda e: e.activation(out=dtx[:], in_=dt_scalar_or_ap, func=AF.Exp), r=K, w=K)
                    vop(eng, th[:], aim_ap, dtx[:], ALU.mult, K, K)
                    vop(eng, mag[:], arec[:], dtx[:], ALU.mult, K, K)
                P.op("act", lambda e: e.activation(out=imag[:], in_=mag[:], func=AF.Exp, scale=-1.0), r=K, w=K)
                P.op("act", lambda e: e.activation(out=mag[:], in_=mag[:], func=AF.Exp), r=K, w=K)
                P.op("act", lambda e: e.activation(out=s_[:], in_=th[:], func=AF.Sin, scale=1.0 / 64), r=K, w=K)
                P.op("act", lambda e: e.activation(out=c[:], in_=th[:], func=AF.Sin, scale=1.0 / 64, bias=halfpi[:, 0:1]), r=K + ["halfpi"], w=K)
                for _ in range(6):
                    vop(eng, t1[:], c[:], s_[:], ALU.mult, K, K)
                    vop(eng, c[:], c[:], c[:], ALU.mult, K, K)
                    vop(eng, s_[:], s_[:], s_[:], ALU.mult, K, K)
                    vop(eng, c[:], c[:], s_[:], ALU.subtract, K, K)
                    P.op(eng, lambda e: e.tensor_scalar(out=s_[:], in0=t1[:], scalar1=2.0, scalar2=None, op0=ALU.mult), r=K, w=K)
                L1re, L1im, Lm1re, Lm1im = S[9], S[10], S[11], S[12]
                vop(eng, L1re[:], mag[:], c[:], ALU.mult, K, K)
                vop(eng, L1im[:], mag[:], s_[:], ALU.mult, K, K)
                if need_inv:
                    vop(eng, Lm1re[:], imag[:], c[:], ALU.mult, K, K)
                    vop(eng, Lm1im[:], imag[:], s_[:], ALU.mult, K, K)
                    P.op(eng, lambda e: e.tensor_scalar(out=Lm1im[:], in0=Lm1im[:], scalar1=-1.0, scalar2=None, op0=ALU.mult), r=K, w=K)
                return dict(L1re=L1re, L1im=L1im, Lm1re=Lm1re, Lm1im=Lm1im, are=arec)

            A_, B_, C_, D_, Gr, Gi = cg[0], cg[1], cv[0], cv[1], tmpA, tmpB
            kA, kB, kC2, kD, kGr, kGi = ("cg", 0), ("cg", 1), ("cv", 0), ("cv", 1), "tmpA", "tmpB"
            vl = vlat[:].rearrange("p a g v n -> p (a g v n)").bitcast(F32)
            Tre_all = vl[:, 0:2048].rearrange("p (g t) -> p g t", g=8)
            Tim_all = vl[:, 2048:4096].rearrange("p (g t) -> p g t", g=8)
            Tp_re, Tm_re, Tp_im, Tm_im = Tre_all[:, 0:4], Tre_all[:, 4:8], Tim_all[:, 0:4], Tim_all[:, 4:8]
            P.alias(["stab"], ["vlat"])
            vc = vctx[:].rearrange("p a g v n -> p (a g v n)")
            W1pad = vc[:, 0:4096].rearrange("p (q r n) -> p q r n", q=16, r=2)
            kcf = kctxT[:].rearrange("p a n -> p (a n)")
            Ewpad = kcf[:, 0:2048].rearrange("p (q r n) -> p q r n", q=8, r=2)
            P.alias(["W1pad"], ["vctx"])
            P.alias(["Ewpad"], ["kctxT"])
            Hre_b = qhat[0][:].rearrange("p (g t) -> p g t", g=4)
            Him_b = qhat[1][:].rearrange("p (g t) -> p g t", g=4)
            ysb = aT

            def _body():
              for d in range(2):
                  P.dma(sq_[0], I["sAreQ"][d], w=["sqq"])
                  P.dma(sq_[1], I["sAimQ"][d], w=["sqq"])
                  P.dma(sq_[13], I["sDtQ"][d], w=["sqq"])
                  P.dma(sCq[0], I["sCreQ"][d], w=["sCq"], q="pool")
                  P.dma(sCq[1], I["sCimQ"][d], w=["sCq"], q="pool")
                  P.dma(sH[:], I["sH0"][d], w=["sH"])
                  Q = lam_params(sq_[0], sq_[1], sq_[13], sq_[2:13] + [sq_[0], sq_[1]], "sqq", 32, False)
                  for s in range(8):
                      k0, k1 = s // 2, 4 + s // 2
                      g8b = (4 * s) % 8
                      P.op("pool", lambda e: e.memset(kcf[:, 0:2048], 0.0), r=["Ewpad"], w=["Ewpad"])
                      if s % 2 == 0:
                          P.op("pool", lambda e: e.memset(vc[:], 0.0), r=["W1pad"], w=["W1pad"])
                      if s % 2 == 0:
                          for half, kk_ in enumerate((k0, k1)):
                              Rk = ["srr"]
                              P.dma(sr_[0], I["sAreR"][d, kk_], w=Rk)
                              P.dma(sr_[1], I["sAimR"][d, kk_], w=Rk)
                              P.dma(sr_[13][:, 0:1], I["sDtR"][d, kk_], w=Rk)
                              P.dma(sr_[14], I["sBreR"][d, kk_], w=Rk)
                              P.dma(sr_[15], I["sBimR"][d, kk_], w=Rk)
                              R_ = lam_params(sr_[0], sr_[1], sr_[13][:, 0:1], sr_[2:13] + [sr_[0], sr_[0]], "srr", 64, True, need_inv=False)
                              nre, den, cre, cim, t1, t2 = sr_[3], sr_[4], sr_[5], sr_[6], sr_[7], sr_[8]
                              aim_ = sr_[1]
                              P.op("dve", lambda e, R_=R_: e.tensor_scalar(out=nre[:], in0=R_["L1re"][:], scalar1=-1.0, scalar2=None, op0=ALU.add), r=Rk, w=Rk)
                              vop("dve", den[:], R_["are"][:], R_["are"][:], ALU.mult, Rk, Rk)
                              vop("dve", t1[:], aim_[:], aim_[:], ALU.mult, Rk, Rk)
                              vop("dve", den[:], den[:], t1[:], ALU.add, Rk, Rk)
                              P.op("dve", lambda e: e.reciprocal(out=den[:], in_=den[:]), r=Rk, w=Rk)
                              vop("dve", t1[:], nre[:], R_["are"][:], ALU.mult, Rk, Rk)
                              vop("dve", t2[:], R_["L1im"][:], aim_[:], ALU.mult, Rk, Rk)
                              vop("dve", cre[:], t1[:], t2[:], ALU.add, Rk, Rk)
                              vop("dve", cre[:], cre[:], den[:], ALU.mult, Rk, Rk)
                              vop("dve", t1[:], R_["L1im"][:], R_["are"][:], ALU.mult, Rk, Rk)
                              vop("dve", t2[:], nre[:], aim_[:], ALU.mult, Rk, Rk)
                              vop("dve", cim[:], t1[:], t2[:], ALU.subtract, Rk, Rk)
                              vop("dve", cim[:], cim[:], den[:], ALU.mult, Rk, Rk)
                              wre, wim = sr_[9], sr_[10]
                              cmul("dve", wre[:], wim[:], cre[:], cim[:], sr_[14][:], sr_[15][:], t1[:], t2[:], Rk, Rk, ["srr", "srr"])
                              for g8 in range(8):
                                  for ri, wsrc in enumerate((wre, wim)):
                                      P.op("act", lambda e, half=half, ri=ri, wsrc=wsrc, g8=g8: e.activation(out=W1pad[:, half * 8 + g8, ri, half * 64:(half + 1) * 64], in_=wsrc[:], func=AF.Identity, scale=smask[:, g8:g8 + 1]),
                                           r=Rk + ["smask"], w=["W1pad"])
                      for half, kk_ in enumerate((k0, k1)):
                          for q in range(4):
                              g8 = g8b + q
                              gq = 4 * s + q
                              P.op("act", lambda e, q=q, half=half, g8=g8, gq=gq: e.activation(out=Ewpad[:, half * 4 + q, 0, g8 * 16:(g8 + 1) * 16], in_=sCq[0][:, gq, :], func=AF.Identity, scale=smask[:, 8 + half:9 + half]),
                                   r=["sCq", "smask"], w=["Ewpad"])
                              P.op("act", lambda e, q=q, half=half, g8=g8, gq=gq: e.activation(out=Ewpad[:, half * 4 + q, 1, g8 * 16:(g8 + 1) * 16], in_=sCq[1][:, gq, :], func=AF.Identity, scale=smask[:, 10 + half:11 + half]),
                                   r=["sCq", "smask"], w=["Ewpad"])
                      gsl = slice(4 * s, 4 * s + 4)
                      i0 = 0 if d == 0 else SEG - 1
                      for (Tre, Tim, bre_, bim_) in ((Tp_re, Tp_im, Q["L1re"], Q["L1im"]), (Tm_re, Tm_im, Q["Lm1re"], Q["Lm1im"])):
                          P.op("dve", lambda e, Tre=Tre, bre_=bre_, i0=i0, gsl=gsl: e.tensor_copy(out=Tre[:, :, i0:i0 + 1], in_=bre_[:, gsl].unsqueeze(2)), r=["sqq"], w=["stab"])
                          P.op("dve", lambda e, Tim=Tim, bim_=bim_, i0=i0, gsl=gsl: e.tensor_copy(out=Tim[:, :, i0:i0 + 1], in_=bim_[:, gsl].unsqueeze(2)), r=["sqq"], w=["stab"])
                      L = 1
                      while L < SEG:
                          if d == 0:
                              src = slice(0, L); dst = slice(L, 2 * L); piv = L - 1
                          else:
                              src = slice(SEG - L, SEG); dst = slice(SEG - 2 * L, SEG - L); piv = SEG - L
                          zr = Tre_all[:, :, piv:piv + 1].to_broadcast([128, 8, L])
                          zi = Tim_all[:, :, piv:piv + 1].to_broadcast([128, 8, L])
                          cmul("dve", Tre_all[:, :, dst], Tim_all[:, :, dst], Tre_all[:, :, src], Tim_all[:, :, src], zr, zi,
                               A_[:, 0:8 * L].rearrange("p (g t) -> p g t", g=8), B_[:, 0:8 * L].rearrange("p (g t) -> p g t", g=8), ["stab"], ["stab"], [kA, kB])
                          L *= 2
                      P.op("dve", lambda e, gsl=gsl: e.tensor_copy(out=sHent[:], in_=sH[:, gsl, :]), r=["sH"], w=["sHent"])
                      segs = range(4) if d == 0 else range(3, -1, -1)
                      for seg in segs:
                          tsl = slice(seg * SEG, (seg + 1) * SEG)
                          Sre, Sim = pq[0], pq[1]
                          skr = [("pq", 0), ("pq", 1)]; ski = [("pq", 2), ("pq", 3)]
                          for ri, (St, sk) in enumerate(((Sre, skr), (Sim, ski))):
                              for q in range(4):
                                  for half, kk_ in enumerate((k0, k1)):
                                      P.op("pe", lambda e, St=St, q=q, half=half, kk_=kk_, ri=ri, tsl=tsl, g8b=g8b: e.matmul(St[:, q * SEG:(q + 1) * SEG], lhsT=W1pad[:, half * 8 + g8b + q, ri, :], rhs=hT[:, kk_, tsl], start=(half == 0), stop=(half == 1)),
                                           r=["W1pad", ("hT", kk_)], w=[sk[q // 2]])
                          S3r = Sre[:].rearrange("p (g t) -> p g t", g=4); S3i = Sim[:].rearrange("p (g t) -> p g t", g=4)
                          A3, B3, C3, D3 = [x[:].rearrange("p (g t) -> p g t", g=4) for x in (A_, B_, C_, D_)]
                          G3r, G3i = Gr[:].rearrange("p (g t) -> p g t", g=4), Gi[:].rearrange("p (g t) -> p g t", g=4)
                          vop("dve", A3, S3r, Tm_re, ALU.mult, skr + ["stab"], [kA])
                          vop("dve", B3, S3i, Tm_im, ALU.mult, ski + ["stab"], [kB])
                          vop("dve", A3, A3, B3, ALU.subtract, [kA, kB], [kA])
                          vop("dve", C3, S3i, Tm_re, ALU.mult, ski + ["stab"], [kC2])
                          vop("dve", D3, S3r, Tm_im, ALU.mult, skr + ["stab"], [kD])
                          vop("dve", C3, C3, D3, ALU.add, [kC2, kD], [kC2])
                          for (src_, dstt, ks, kd_) in ((A_, Gr, kA, kGr), (C_, Gi, kC2, kGi)):
                              P.op("dve", lambda e, src_=src_, dstt=dstt: e.tensor_tensor_scan(out=dstt[:], data0=m256[:], data1=src_[:], initial=0.0, op0=ALU.mult, op1=ALU.add), r=["m32", ks], w=[kd_])
                              if d == 1:
                                  s3 = src_[:].rearrange("p (g t) -> p g t", g=4); d3 = dstt[:].rearrange("p (g t) -> p g t", g=4)
                                  P.op("act", lambda e, dstt=dstt: e.copy(out=ctmp[:, 0:4], in_=dstt[:, SEG - 1:NT:SEG]), r=[kd_], w=["ctmp"])
                                  P.op("dve", lambda e, s3=s3, d3=d3: e.tensor_tensor(out=d3, in0=s3, in1=d3, op=ALU.subtract), r=[ks, kd_], w=[kd_])
                                  P.op("dve", lambda e, d3=d3: e.tensor_tensor(out=d3, in0=d3, in1=ctmp[:, 0:4].unsqueeze(2).to_broadcast([128, 4, SEG]), op=ALU.add), r=[kd_, "ctmp"], w=[kd_])
                          vop("dve", G3r, G3r, sHent[:, :, 0:1].to_broadcast([128, 4, SEG]), ALU.add, [kGr, "sHent"], [kGr])
                          vop("dve", G3i, G3i, sHent[:, :, 1:2].to_broadcast([128, 4, SEG]), ALU.add, [kGi, "sHent"], [kGi])
                          vop("dve", A3, G3r, Tp_re, ALU.mult, [kGr, "stab"], [kA])
                          vop("dve", B3, G3i, Tp_im, ALU.mult, [kGi, "stab"], [kB])
                          vop("dve", Hre_b, A3, B3, ALU.subtract, [kA, kB], [("qhat", 0)])
                          vop("dve", C3, G3r, Tp_im, ALU.mult, [kGr, "stab"], [kC2])
                          vop("dve", D3, G3i, Tp_re, ALU.mult, [kGi, "stab"], [kD])
                          vop("dve", Him_b, C3, D3, ALU.add, [kC2, kD], [("qhat", 1)])
                          xi = SEG - 1 if d == 0 else 0
                          vop("dve", sHx[:, :, 0:1], A3[:, :, xi:xi + 1], B3[:, :, xi:xi + 1], ALU.subtract, [kA, kB], ["sHx"])
                          vop("dve", sHx[:, :, 1:2], C3[:, :, xi:xi + 1], D3[:, :, xi:xi + 1], ALU.add, [kC2, kD], ["sHx"])
                          for half in range(2):
                              P.dma(O["ssm"][seg * 2 + d, half * 32 + 4 * s: half * 32 + 4 * s + 4].rearrange("g p r -> p g r"), sHx[half * 64:(half + 1) * 64, :, :], r=["sHx"], sem="ssmout")
                          P.op("dve", lambda e: e.tensor_scalar(out=sHent[:], in0=sHx[:], scalar1=keep[:, 0:1], scalar2=None, op0=ALU.mult), r=["sHx", "keep"], w=["sHent"])
                          if STAGE.get("ssm_dbg"):
                              P.dma(O["dbg2"][:, 0:1024], A_[:], r=[kA], sem="dbg")
                              P.dma(O["dbg2"][:, 1024:2048], B_[:], r=[kB], sem="dbg")
                              P.dma(O["dbg2"][:, 2048:3072], C_[:], r=[kC2], sem="dbg")
                              P.dma(O["dbg2"][:, 3072:4096], D_[:], r=[kD], sem="dbg")
                              P.dma(O["dbg2"][:, 4096:5120], Gr[:], r=[kGr], sem="dbg")
                              P.dma(O["dbg2"][:, 5120:6144], Gi[:], r=[kGi], sem="dbg")
                              P.dma(O["dbg3"], hT[:].rearrange("p k t -> p (k t)"), r=[("hT", kk) for kk in range(8)], sem="dbg")
                              P.dma(O["dbg"][:, 0:4096], vl, r=["stab"], sem="dbg")
                              P.dma(O["dbg"][:, 4096:5120], rs32, r=["srr"], sem="dbg")
                              P.dma(O["dbg"][:, 5120:5632], sq32, r=["sqq"], sem="dbg")
                              raise StopIteration
                          for half, kk_ in enumerate((k0, k1)):
                              yp_, ypk = bank(4 + half)
                              n = 0
                              for q in range(4):
                                  for ri, Hb in enumerate((Hre_b, Him_b)):
                                      P.op("pe", lambda e, yp_=yp_, q=q, half=half, ri=ri, Hb=Hb, n=n: e.matmul(yp_[:, 0:SEG], lhsT=Ewpad[:, half * 4 + q, ri, :], rhs=Hb[:, q, :], start=(n == 0), stop=(n == 7)),
                                           r=["Ewpad", ("qhat", ri)], w=[ypk])
                                      n += 1
                              first = (d == 0 and s % 2 == 0)
                              if first:
                                  P.op("dve", lambda e, yp_=yp_, kk_=kk_, tsl=tsl: e.scalar_tensor_tensor(out=ysb[:, kk_, tsl], in0=hT[:, kk_, tsl], scalar=sD[:, kk_:kk_ + 1], in1=yp_[:, 0:SEG], op0=ALU.mult, op1=ALU.add),
                                       r=[ypk, ("hT", kk_), "sD"], w=[("aT", kk_)])
                              else:
                                  P.op("dve", lambda e, yp_=yp_, kk_=kk_, tsl=tsl: e.tensor_tensor(out=ysb[:, kk_, tsl], in0=yp_[:, 0:SEG], in1=ysb[:, kk_, tsl], op=ALU.add),
                                       r=[ypk, ("aT", kk_)], w=[("aT", kk_)])

            try:
                _body()
            except StopIteration:
                pass
            P.alias(["vlat"], ["stab"])
            P.alias(["vctx"], ["W1pad"])
            P.alias(["kctxT"], ["Ewpad"])
            P.alias(["rstd"], ["srr"])
            P.alias([("sq", 0)], ["sqq"])
            P.alias([("sq", 1)], ["sCq"])
            P.op("dve", lambda e: e.memset(vlat[:], 0.0), w=["vlat"])
            P.op("dve", lambda e: e.memset(vlat[:, :, :, 0, 64:65], 1.0), w=["vlat"])
            P.op("dve", lambda e: e.memset(vlat[:, :, :, 1, 0:1], 1.0), w=["vlat"])
            for kk_ in range(8):
                yk = ysb[:, kk_, :]
                P.op("dve", lambda e, yk=yk: e.tensor_tensor(out=A_[:], in0=yk, in1=yk, op=ALU.mult), r=[("aT", kk_)], w=[kA])
                P.op("dve", lambda e: e.tensor_scalar(out=A_[:], in0=A_[:], scalar1=0.044715, scalar2=1.0, op0=ALU.mult, op1=ALU.add), r=[kA], w=[kA])
                P.op("pool", lambda e, yk=yk: e.tensor_tensor(out=A_[:], in0=A_[:], in1=yk, op=ALU.mult), r=[kA, ("aT", kk_)], w=[kA])
                P.op("act", lambda e: e.activation(out=A_[:], in_=A_[:], func=AF.Sigmoid, scale=1.5957691216057308), r=[kA], w=[kA])
                P.op("pool", lambda e, yk=yk: e.tensor_tensor(out=yk, in0=A_[:], in1=yk, op=ALU.mult), r=[kA, ("aT", kk_)], w=[("aT", kk_)])
            for dc in range(8):
                wb = wup[dc % 2]
                P.dma(wb[:, 0], I["wglu"][dc], w=[("wup", dc % 2, 0)], q="pool")
                P.dma(wb[:, 1], I["wglu"][8 + dc], w=[("wup", dc % 2, 1)], q="pool")
                for th in range(2):
                    vps, vpk = bank((dc * 2 + th) % 4)
                    gps, gpk = bank(4 + (dc * 2 + th) % 4)
                    for gv, (ps, pk) in enumerate(((vps, vpk), (gps, gpk))):
                        for kk_ in range(8):
                            P.op("pe", lambda e, ps=ps, kk_=kk_, th=th, wb=wb, gv=gv: e.matmul(ps, lhsT=wb[:, gv, kk_, :], rhs=ysb[:, kk_, th * 512:(th + 1) * 512], start=(kk_ == 0), stop=(kk_ == 7)),
                                 r=[("wup", dc % 2, gv), ("aT", kk_)], w=[pk])
                    sl = slice(th * 512, (th + 1) * 512)
                    P.op("act", lambda e, gps=gps, sl=sl: e.activation(out=B_[:, sl], in_=gps, func=AF.Sigmoid), r=[gpk], w=[kB])
                    P.op("dve", lambda e, vps=vps, sl=sl: e.tensor_tensor(out=B_[:, sl], in0=vps, in1=B_[:, sl], op=ALU.mult), r=[vpk, kB], w=[kB])
                    P.op("dve", lambda e, dc=dc, sl=sl: e.scalar_tensor_tensor(out=xT[:, dc, sl], in0=B_[:, sl], scalar=mod[:, 16 + dc:17 + dc], in1=xT[:, dc, sl], op0=ALU.mult, op1=ALU.add),
                         r=[kB, "mod", ("xT", dc)], w=[("xT", dc)])

        eps_t = P.sb("eps_t", [128, 1])
        P.op("dve", lambda e: e.memset(eps_t[:], EPS), w=["eps"])
        halfpi = P.sb("halfpi", [128, 1])
        P.op("dve", lambda e: e.memset(halfpi[:], 1.5707963267948966), w=["halfpi"])


        for l in range(STAGE["layers"]):
            ada_layer(l)
            norm_mod(0)
            if STAGE["mixers"]:
                if l % 3 == 0:
                    attention(l)
                elif l % 3 == 1 and STAGE.get("hgrn", True):
                    hgrn(l)
                elif l % 3 == 2 and STAGE.get("ssm", True):
                    ssm(l)
            norm_mod(1)
            ffn(l)

        rms_stats()
        for k in range(8):
            P.op("dve", lambda e, k=k: e.scalar_tensor_tensor(out=xT[:, k, :], in0=xT[:, k, :], scalar=fg[:, k:k + 1], in1=rstd[:], op0=ALU.mult, op1=ALU.mult),
                 r=[("xT", k), "fg", "rstd"], w=[("xT", k)])
        for k in range(8):
            P.dma(O["y"][k * 128:(k + 1) * 128, :], xT[:, k, :], r=[("xT", k)], sem="yout")
        P.op("pool", lambda e: e.memset(rstd[:], 0.0), r=["rstd"], w=["rstd"])
        if not (STAGE["mixers"] and STAGE.get("hgrn", True) and STAGE["layers"] > 1):
            for i in range(8):
                P.dma(O["hg"][i * 8:(i + 1) * 8].rearrange("a p n -> p a n"), rstd[:].rearrange("p (a n) -> p a n", a=8), r=["rstd"], sem="sout")
        if not (STAGE["mixers"] and STAGE.get("ssm", True) and STAGE["layers"] > 2):
            for a_ in range(8):
                P.dma(O["ssm"][a_].rearrange("g p r -> g (p r)"), rstd[0:64, 0:128], r=["rstd"], sem="sout")
        P.wait_all_dma()
        P.emit()
    return nc

def _c(a):
    return np.ascontiguousarray(a, dtype=np.float32)


def prep_inputs(inp):
    g = {k: np.asarray(v) for k, v in inp.items()}
    sh = {}
    sh["ident"] = np.eye(128, dtype=np.float32)
    sh["ada_w"] = _c(g["ada_w"].reshape(DEPTH, 8, 128, 12, 512).transpose(0, 3, 2, 1, 4))
    sh["ada_b"] = _c(g["ada_b"].reshape(DEPTH, 48, 128).transpose(0, 2, 1))
    sh["ng"] = _c(np.stack([g["norm1_g"], g["norm2_g"]]).reshape(2, DEPTH, 8, 128).transpose(3, 0, 1, 2))
    sh["final_g"] = _c(g["final_g"].reshape(8, 128).T)
    sh["ffn_w_up"] = _c(g["ffn_w_up"].reshape(DEPTH, 8, 128, 2, NFC, 128).transpose(0, 4, 2, 3, 1, 5))
    sh["ffn_w_down"] = _c(g["ffn_w_down"].reshape(DEPTH, NFC, 128, 8, 128).transpose(0, 3, 2, 1, 4))
    sh["ffn_conv_w"] = _c(g["ffn_conv_w"].reshape(DEPTH, 3, 2 * NFC, 128).transpose(0, 3, 1, 2))
    sh["ffn_conv_b"] = _c(g["ffn_conv_b"].reshape(DEPTH, 2 * NFC, 128).transpose(0, 2, 1))
    wqkv = g["attn_wqkv"]
    sh["wq"] = _c(wqkv[:, :, 0:1024].reshape(2, 8, 128, 8, 128).transpose(0, 3, 2, 1, 4))
    wk = wqkv[:, :, 1024:1280].reshape(2, 8, 128, 4, 1, 64)
    sh["wk"] = _c(np.broadcast_to(wk, (2, 8, 128, 4, 2, 64)).reshape(2, 8, 128, 4, 128).transpose(0, 3, 2, 1, 4))
    sh["wkv"] = _c(wqkv[:, :, 1024:1536].reshape(2, 8, 128, 512).transpose(0, 2, 1, 3))
    sh["wo"] = _c(g["attn_wo"].reshape(2, 8, 128, 8, 128).transpose(0, 3, 2, 1, 4))
    sh["sink"] = _c(np.broadcast_to(g["attn_sink"][:, None, :], (2, 128, 16)))
    hw = g["hgrn_w_in"][0]
    sh["hw_in"] = _c(hw.reshape(8, 128, 5, 8, 128).transpose(3, 2, 1, 0, 4))
    sh["hwo"] = _c(g["hgrn_wo"][0].reshape(8, 128, 8, 128).transpose(2, 1, 0, 3))
    sh["hlb"] = _c(g["hgrn_lb"].reshape(4, 2, 8, 128).transpose(3, 0, 1, 2))
    sh["hgn"] = _c(g["hgrn_g_norm"][0].reshape(128, 1))
    ii = np.arange(128)
    same = (ii[:, None] // 32) == (ii[None, :] // 32)
    sh["hmask"] = _c(np.stack([same & (ii[:, None] <= ii[None, :]), same & (ii[:, None] >= ii[None, :])], axis=1))
    are, aim, ldt = g["ssm_a_re"][0], g["ssm_a_im"][0], g["ssm_log_dt"][0]
    bre, bim, cre, cim = g["ssm_b_re"][0], g["ssm_b_im"][0], g["ssm_c_re"][0], g["ssm_c_im"][0]
    sh["sBreR"] = _c(bre.reshape(2, 8, 8, 64, 16).transpose(0, 1, 2, 4, 3).reshape(2, 8, 128, 64))
    sh["sBimR"] = _c(bim.reshape(2, 8, 8, 64, 16).transpose(0, 1, 2, 4, 3).reshape(2, 8, 128, 64))
    sh["sAreR"] = _c(np.broadcast_to(are.reshape(2, 8, 8, 1, 64), (2, 8, 8, 16, 64)).reshape(2, 8, 128, 64))
    sh["sAimR"] = _c(np.broadcast_to(aim.reshape(2, 8, 8, 1, 64), (2, 8, 8, 16, 64)).reshape(2, 8, 128, 64))
    sh["sDtR"] = _c(np.broadcast_to(ldt.reshape(2, 8, 8, 1, 1), (2, 8, 8, 16, 1)).reshape(2, 8, 128, 1))
    sh["sAreQ"] = _c(are.reshape(2, 2, 32, 64).transpose(0, 1, 3, 2).reshape(2, 128, 32))
    sh["sAimQ"] = _c(aim.reshape(2, 2, 32, 64).transpose(0, 1, 3, 2).reshape(2, 128, 32))
    sh["sDtQ"] = _c(np.broadcast_to(ldt.reshape(2, 2, 1, 32), (2, 2, 64, 32)).reshape(2, 128, 32))
    sh["sCreQ"] = _c(cre.reshape(2, 2, 32, 16, 64).transpose(0, 1, 4, 2, 3).reshape(2, 128, 32, 16))
    sh["sCimQ"] = _c(cim.reshape(2, 2, 32, 16, 64).transpose(0, 1, 4, 2, 3).reshape(2, 128, 32, 16))
    sh["sD"] = _c(g["ssm_d"][0].reshape(8, 128).T)
    sm = np.zeros((128, 12), np.float32)
    for q in range(8):
        sm[q * 16:(q + 1) * 16, q] = 1.0
    sm[0:64, 8] = 1.0; sm[64:128, 9] = 1.0; sm[0:64, 10] = -1.0; sm[64:128, 11] = -1.0
    sh["smask"] = sm
    sh["wglu"] = _c(g["ssm_w_glu"][0].reshape(8, 128, 16, 128).transpose(2, 1, 0, 3))
    tt = np.arange(NT)
    inv = 1.0 / (10000.0 ** (np.arange(0, 32, 2, dtype=np.float32) / np.float32(32)))
    ar = (tt // 64).astype(np.float32)[:, None] * inv.astype(np.float32)
    ac = (tt % 64).astype(np.float32)[:, None] * inv.astype(np.float32)
    ang = np.concatenate([ar, ar, ac, ac], axis=-1).astype(np.float32)
    cosS = _c(np.concatenate([np.cos(ang).T] * 2, axis=0)); sinS = _c(np.concatenate([np.sin(ang).T] * 2, axis=0))
    rm = np.zeros((128, 128), np.float32)
    for m in range(128):
        d = m % 64
        if (d % 32) < 16:
            rm[m + 16, m] = -1.0
        else:
            rm[m - 16, m] = 1.0
    sh["rmat"] = rm
    kk = np.arange(128)[:, None]; qq = np.arange(128)[None, :]
    mS = np.zeros((128, 8, 2, 128), np.float32); mP = np.zeros((128, 8, 2, 128), np.float32)
    for i in range(8):
        if i >= 1:
            mS[:, i, 0, :] = (kk >= qq)
        if i <= 6:
            mS[:, i, 1, :] = (kk <= qq)
        mP[:, i, 0, :] = 1.0 if i % 2 == 1 else 0.0
        mP[:, i, 1, :] = 1.0 if i % 2 == 0 else 0.0
    if STAGE.get("kv_only"):
        for kk_ in ("wq", "wk", "wo", "sink", "rmat"):
            sh.pop(kk_, None)
    maps = []
    for c in range(8):
        m = dict(sh)
        if c < 4:
            xc = g["x_sample"][c]
            cond = g["c"][c]
            kp = 1.0
        else:
            xc = g["x_prompt"][4 * (c - 4):4 * (c - 4) + 4].reshape(NT, D)
            cond = g["c_ctx"]
            kp = 0.0
        if c < 4:
            ropec = cosS; ropes = sinS; m["amask"] = mS
            ck = g["cache_k"][c]
            m["ckd"] = _c(np.broadcast_to(ck[:, :, :, None, :], (2, 512, 4, 2, 64)).reshape(2, 512, 512))
            cv = g["cache_v"][c]
            vp = np.zeros((2, 512, 4, 2, 128), np.float32)
            vp[:, :, :, 0, 0:64] = cv; vp[:, :, :, 0, 64] = 1.0
            vp[:, :, :, 1, 64:128] = cv; vp[:, :, :, 1, 0] = 1.0
            m["cvp"] = vp.reshape(2, 512, 1024)
        else:
            ropec = np.ones((128, NT), np.float32); ropes = np.zeros((128, NT), np.float32); m["amask"] = mP
            m["ckd"] = np.zeros((2, 512, 512), np.float32); m["cvp"] = np.zeros((2, 512, 1024), np.float32)
        if STAGE.get("kv_only"):
            for kk_ in ("amask", "ckd", "cvp"):
                m.pop(kk_, None)
        if c < 4:
            st = g["state_ssm"][c, 0]
            m["sH0"] = _c(st.reshape(2, 2, 32, 64, 2).transpose(0, 1, 3, 2, 4).reshape(2, 128, 32, 2))
        else:
            m["sH0"] = np.zeros((2, 128, 32, 2), np.float32)
        m["hs0"] = _c(g["state_hgrn"][c, 0]) if c < 4 else np.zeros((2, 8, 128, 128), np.float32)
        m["x"] = _c(np.concatenate([xc.T, ropec, ropes], axis=0))
        m["cond"] = _c(cond.reshape(8, 128).T)
        m["keep"] = _c(np.stack([np.full(128, kp), np.full(128, kp - 1.0)], axis=1))
        maps.append(m)
    return maps


def assemble(results):
    f = lambda a: np.asarray(a, dtype=np.float32)
    ys = np.stack([f(results[c]["y"]).T for c in range(4)])
    yp = np.concatenate([f(results[c]["y"]).T.reshape(4, 256, D) for c in range(4, 8)])
    nk = np.concatenate([f(results[c]["nk"]).reshape(2, 4, 256, 4, 64).transpose(1, 0, 2, 3, 4) for c in range(4, 8)])
    nv = np.concatenate([f(results[c]["nv"]).reshape(2, 4, 256, 4, 64).transpose(1, 0, 2, 3, 4) for c in range(4, 8)])
    hg = np.concatenate([f(results[c]["hg"]).reshape(4, 1, 2, 8, 128, 128) for c in range(4, 8)])
    ssm = np.concatenate([f(results[c]["ssm"]).reshape(4, 1, 2, 64, 64, 2) for c in range(4, 8)])
    return (np.ascontiguousarray(yp), np.ascontiguousarray(ys), np.ascontiguousarray(nk), np.ascontiguousarray(nv),
            np.ascontiguousarray(hg), np.ascontiguousarray(ssm))


def kernel(**inputs):
    nc = build_program()
    maps = prep_inputs(inputs)
    res = run_bass_kernel_spmd(nc, maps, core_ids=list(range(8)))
    return assemble(res.results)
```

```python
import numpy as np
from concourse.bass_utils import run_bass_kernel_spmd

from contextlib import ExitStack
import concourse.bass as bass
import concourse.mybir as mybir

F32 = mybir.dt.float32
F32R = mybir.dt.float32r
BF16 = mybir.dt.bfloat16
AF = mybir.ActivationFunctionType
ALU = mybir.AluOpType
AX = mybir.AxisListType


class Prog:
    ENGS = ("pe", "act", "dve", "pool", "sp")

    def __init__(self, nc, es: ExitStack):
        self.nc = nc
        self.es = es
        self.recs = {e: [] for e in self.ENGS}
        self.cnt = {e: 0 for e in self.ENGS}
        self.known = {e: {} for e in self.ENGS}
        self.state = {}
        self.sems = {}
        self.dcnt = {}
        for e in self.ENGS:
            self.sems[("e", e)] = es.enter_context(nc.semaphore("sem_" + e))
        self.psn = 0

    def sb(self, name, shape, dt=F32):
        return self.es.enter_context(self.nc.sbuf_tensor("sb_" + name, list(shape), dt))

    def ps(self, name, shape, dt=F32):
        return self.es.enter_context(self.nc.psum_tensor(name, list(shape), dt))

    def dsem(self, name):
        k = ("d", name)
        if k not in self.sems:
            self.sems[k] = self.es.enter_context(self.nc.semaphore("dsem_" + name))
            self.dcnt[k] = 0
        return k

    def _st(self, k):
        s = self.state.get(k)
        if s is None:
            s = {"w": {}, "r": {}}
            self.state[k] = s
        return s

    def _deps(self, eng, r, w):
        deps = {}
        def add(d):
            if d is None:
                return
            sk, v = d
            if deps.get(sk, 0) < v:
                deps[sk] = v
        for k in r:
            st = self._st(k)
            for sk, v in st["w"].items():
                add((sk, v))
            if isinstance(k, tuple) and k[0] == "pq":
                for sk, v in st["r"].items():
                    if sk != ("e", eng):
                        add((sk, v))
        for k in w:
            s = self._st(k)
            for sk, v in s["w"].items():
                add((sk, v))
            for sk, v in s["r"].items():
                add((sk, v))
        waits = []
        kn = self.known[eng]
        for sk, v in deps.items():
            if eng == "pe" and sk == ("e", "pe"):
                continue
            if kn.get(sk, 0) < v:
                waits.append((sk, v))
                kn[sk] = v
        return waits

    def _commit(self, comp, r, w):
        sk, v = comp
        for k in w:
            self.state[k] = {"w": {sk: v}, "r": {}}
        for k in r:
            s = self._st(k)
            if s["r"].get(sk, 0) < v:
                s["r"][sk] = v

    def alias(self, dst, src):
        mw, mr = {}, {}
        for k in src:
            st = self._st(k)
            for sk, v in st["w"].items():
                mw[sk] = max(mw.get(sk, 0), v)
            for sk, v in st["r"].items():
                mr[sk] = max(mr.get(sk, 0), v)
        for k in dst:
            self.state[k] = {"w": dict(mw), "r": dict(mr)}

    def op(self, eng, fn, r=(), w=(), inc=True):
        inc = True
        waits = self._deps(eng, r, w)
        sk = ("e", eng)
        comp = (sk, self.cnt[eng] + 1)
        if inc:
            self.cnt[eng] += 1
        self.recs[eng].append((waits, fn, (sk, 1) if inc else None))
        self._commit(comp, r, w)

    def dma(self, out, in_, r=(), w=(), sem=None, q="sp", **kw):
        if sem is None:
            sem = "_".join(str(x) for x in (w[0] if isinstance(w[0], tuple) else (w[0],)))
        waits = self._deps(q, r, w)
        sk = self.dsem(sem)
        self.dcnt[sk] += 16
        comp = (sk, self.dcnt[sk])
        self.recs[q].append((waits, (lambda e, o=out, i=in_, kw=kw: e.dma_start(out=o, in_=i, **kw)), (sk, 16)))
        self._commit(comp, r, w)

    def wait_all_dma(self, q="sp"):
        waits = []
        for sk, v in self.dcnt.items():
            if v > 0 and self.known[q].get(sk, 0) < v:
                waits.append((sk, v))
                self.known[q][sk] = v
        self.recs[q].append((waits, None, None))

    def barrier_all(self):
        tgt = {("e", e): self.cnt[e] for e in self.ENGS if self.cnt[e] > 0}
        for sk, v in self.dcnt.items():
            if v > 0:
                tgt[sk] = v
        for e in self.ENGS:
            waits = []
            for sk, v in tgt.items():
                if sk == ("e", e):
                    continue
                if self.known[e].get(sk, 0) < v:
                    waits.append((sk, v))
                    self.known[e][sk] = v
            if waits:
                self.recs[e].append((waits, None, None))

    def emit(self):
        nc = self.nc
        sems = self.sems
        recs = self.recs

        def replay(name):
            def f(e):
                for waits, fn, inc in recs[name]:
                    for sk, v in waits:
                        e.wait_ge(sems[sk], v)
                    if fn is not None:
                        ins = fn(e)
                        if inc is not None:
                            ins.then_inc(sems[inc[0]], inc[1])
            return f

        with nc.Block() as block:
            block.tensor(replay("pe"))
            block.scalar(replay("act"))
            block.vector(replay("dve"))
            block.gpsimd(replay("pool"))
            block.sync(replay("sp"))

    def stats(self):
        return {e: len(self.recs[e]) for e in self.ENGS}

D = 1024
NT = 1024
DFF = 2816
NFC = 22
DEPTH = 4
EPS = 1e-6

STAGE = {"mixers": True, "layers": 4, "kv_only": False}


def build_program():
    nc = bass.Bass("TRN2", target_bir_lowering=False)

    def din(name, shape, dt=F32):
        return nc.dram_tensor(name, list(shape), dt, kind="ExternalInput").ap()

    def dout(name, shape, dt=F32):
        return nc.dram_tensor(name, list(shape), dt, kind="ExternalOutput").ap()

    I = {}
    I["x"] = din("x", [D + 256, NT])
    I["cond"] = din("cond", [128, 8])
    I["keep"] = din("keep", [128, 2])
    I["ident"] = din("ident", [128, 128])
    I["ada_w"] = din("ada_w", [DEPTH, 12, 128, 8, 512])
    I["ada_b"] = din("ada_b", [DEPTH, 128, 48])
    I["ng"] = din("ng", [128, 2, DEPTH, 8])
    I["ffn_w_up"] = din("ffn_w_up", [DEPTH, NFC, 128, 2, 8, 128])
    I["ffn_conv_w"] = din("ffn_conv_w", [DEPTH, 128, 3, 2 * NFC])
    I["ffn_conv_b"] = din("ffn_conv_b", [DEPTH, 128, 2 * NFC])
    I["ffn_w_down"] = din("ffn_w_down", [DEPTH, 8, 128, NFC, 128])
    I["final_g"] = din("final_g", [128, 8])
    I["wkv"] = din("wkv", [2, 128, 8, 512])
    I["hw_in"] = din("hw_in", [8, 5, 128, 8, 128])
    I["hwo"] = din("hwo", [8, 128, 8, 128])
    I["hlb"] = din("hlb", [128, 4, 2, 8])
    I["hgn"] = din("hgn", [128, 1])
    I["hs0"] = din("hs0", [2, 8, 128, 128])
    I["hmask"] = din("hmask", [128, 2, 128])
    I["sBreR"] = din("sBreR", [2, 8, 128, 64]); I["sBimR"] = din("sBimR", [2, 8, 128, 64])
    I["sAreR"] = din("sAreR", [2, 8, 128, 64]); I["sAimR"] = din("sAimR", [2, 8, 128, 64])
    I["sDtR"] = din("sDtR", [2, 8, 128, 1])
    I["sAreQ"] = din("sAreQ", [2, 128, 32]); I["sAimQ"] = din("sAimQ", [2, 128, 32]); I["sDtQ"] = din("sDtQ", [2, 128, 32])
    I["sCreQ"] = din("sCreQ", [2, 128, 32, 16]); I["sCimQ"] = din("sCimQ", [2, 128, 32, 16])
    I["sH0"] = din("sH0", [2, 128, 32, 2])
    I["sD"] = din("sD", [128, 8])
    I["smask"] = din("smask", [128, 12])
    I["wglu"] = din("wglu", [16, 128, 8, 128])
    if not STAGE.get("kv_only"):
        I["wq"] = din("wq", [2, 8, 128, 8, 128])
        I["wk"] = din("wk", [2, 4, 128, 8, 128])
        I["wo"] = din("wo", [2, 8, 128, 8, 128])
        I["sink"] = din("sink", [2, 128, 16])
        I["rmat"] = din("rmat", [128, 128])
        I["amask"] = din("amask", [128, 8, 2, 128])
        I["ckd"] = din("ckd", [2, 512, 512])
        I["cvp"] = din("cvp", [2, 512, 1024])
    O = {}
    O["y"] = dout("y", [D, NT])
    O["nk"] = dout("nk", [2, NT, 256])
    O["nv"] = dout("nv", [2, NT, 256])
    O["hg"] = dout("hg", [64, 128, 128])
    O["ssm"] = dout("ssm", [8, 64, 64, 2])
    if STAGE.get("ssm_dbg"):
        O["dbg"] = dout("dbg", [128, 4096 + 1024 + 512])
        O["dbg2"] = dout("dbg2", [128, 6 * 1024])
        O["dbg3"] = dout("dbg3", [128, 8 * 1024], BF16)

    with ExitStack() as es:
        P = Prog(nc, es)
        xT = P.sb("xT", [128, 10, NT])
        hT = P.sb("hT", [128, 8, NT], BF16)
        aT = P.sb("aT", [128, 12, NT], BF16)
        rstd = P.sb("rstd", [128, NT])
        tmpA = P.sb("tmpA", [128, NT])
        tmpB = P.sb("tmpB", [128, NT])
        cg = [P.sb("cg%d" % i, [128, NT]) for i in range(2)]
        cv = [P.sb("cv%d" % i, [128, NT]) for i in range(2)]
        sq = [P.sb("sq%d" % i, [128, NT], BF16) for i in range(2)]
        ident = P.sb("ident", [128, 128])
        ones_bf = P.sb("ones_bf", [128, 128], BF16)
        one_f = P.sb("one_f", [128, 1])
        keep = P.sb("keep", [128, 2])
        cond = P.sb("cond", [128, 8])
        s_bf = P.sb("s_bf", [128, 8], BF16)
        mod = P.sb("mod", [128, 48])
        adab = P.sb("adab", [128, 48])
        ng = P.sb("ng", [128, 2, DEPTH, 8])
        fg = P.sb("fg", [128, 8])
        AB = P.sb("AB", [128, 4, 8])
        cw = P.sb("cw", [128, 3, 2 * NFC])
        cb = P.sb("cb", [128, 2 * NFC])
        cwk = P.sb("cwk", [128, 2, 2 * NFC])
        wada = [P.sb("wada%d" % i, [128, 8, 512], BF16) for i in range(2)]
        wup = [P.sb("wup%d" % i, [128, 2, 8, 128], BF16) for i in range(2)]
        wdn = [P.sb("wdn%d" % i, [128, 11, 128], BF16) for i in range(2)]
        vlat = P.sb("vlat", [128, 8, 4, 2, 128], BF16)
        vctx = P.sb("vctx", [128, 4, 4, 2, 128], BF16)
        kctxT = P.sb("kctxT", [128, 4, 512], BF16)
        ckd = P.sb("ckd", [128, 4, 512], BF16)
        amask = P.sb("amask", [128, 8, 2, 128], BF16)
        rmat = P.sb("rmat", [128, 128], BF16)
        ident_bf = P.sb("ident_bf", [128, 128], BF16)
        qb = [P.sb("qb%d" % i, [128, 512], BF16) for i in range(2)]
        esink = P.sb("esink", [128, 16])
        ones_f = P.sb("ones_f", [128, 128])
        m32 = P.sb("m32", [128, NT])
        hlb = P.sb("hlb", [128, 4, 2, 8])
        lbp = P.sb("lbp", [128, 2, 2, 8])
        hgn = P.sb("hgn", [128, 1])
        hmask = P.sb("hmask", [128, 2, 128], BF16)
        Sf = [P.sb("Sf%d" % i, [128, 128]) for i in range(2)]
        Sb = [P.sb("Sb%d" % i, [128, 128], BF16) for i in range(2)]
        adec = [P.sb("adec%d" % i, [128, 32]) for i in range(2)]
        rowm = P.sb("rowm", [128, 4])
        ctmp = P.sb("ctmp", [128, 32])
        vtok = P.sb("vtok", [128, 8, 128], BF16)
        qhat = [P.sb("qhat%d" % i, [128, NT], BF16) for i in range(2)]
        ktil = [P.sb("ktil%d" % i, [128, NT], BF16) for i in range(2)]
        kdT = [P.sb("kdT%d" % i, [128, NT], BF16) for i in range(2)]
        attm = [P.sb("attm%d" % i, [128, 128], BF16) for i in range(2)]
        smask = P.sb("smask", [128, 12])
        sD = P.sb("sD", [128, 8])
        sH = P.sb("sH", [128, 32, 2])
        sHent = P.sb("sHent", [128, 4, 2])
        sHx = P.sb("sHx", [128, 4, 2])
        pq = [P.ps("pq%d" % i, [128, 1024]) for i in range(4)]

        def bank(i):
            return pq[i // 2][:, (i % 2) * 512:(i % 2) * 512 + 512], ("pq", i)

        P.dma(ident[:], I["ident"], w=["ident"])
        P.dma(keep[:], I["keep"], w=["keep"])
        P.dma(cond[:], I["cond"], w=["cond"])
        P.dma(ng[:], I["ng"], w=["ng"])
        P.dma(fg[:], I["final_g"], w=["fg"])
        P.op("dve", lambda e: e.memset(ones_bf[:], 1.0), w=["ones_bf"])
        P.op("dve", lambda e: e.memset(one_f[:], 1.0), w=["one_f"])
        P.op("dve", lambda e: e.memset(ones_f[:], 1.0), w=["ones_f"])
        P.op("dve", lambda e: e.tensor_copy(out=ident_bf[:], in_=ident[:]), r=["ident"], w=["ident_bf"])
        if not STAGE.get("kv_only"):
            P.dma(rmat[:], I["rmat"], w=["rmat"], q="pool")
            P.dma(amask[:], I["amask"], w=["amask"], q="pool")
        P.op("pool", lambda e: e.memset(m32[:], 1.0), w=["m32"])
        P.op("pool", lambda e: e.memset(m32[:, 0:NT:32], 0.0), w=["m32"])
        P.op("pool", lambda e: e.memset(rowm[:], 0.0), w=["rowm"])
        for c4 in range(3):
            P.op("pool", lambda e, c4=c4: e.memset(rowm[c4 * 32:(c4 + 1) * 32, c4:c4 + 1], 1.0), w=["rowm"])
        P.op("pool", lambda e: e.memset(rowm[96:128, 3:4], 1.0), w=["rowm"])
        P.dma(hlb[:], I["hlb"], w=["hlb"])
        P.dma(hgn[:], I["hgn"], w=["hgn"])
        P.dma(hmask[:], I["hmask"], w=["hmask"], q="pool")
        P.op("dve", lambda e: e.memset(vlat[:], 0.0), w=["vlat"])
        P.op("dve", lambda e: e.memset(vlat[:, :, :, 0, 64:65], 1.0), w=["vlat"])
        P.op("dve", lambda e: e.memset(vlat[:, :, :, 1, 0:1], 1.0), w=["vlat"])
        P.op("act", lambda e: e.activation(out=s_bf[:], in_=cond[:], func=AF.Silu), r=["cond"], w=["s_bf"])

        P.dma(xT[:], I["x"].rearrange("(k p) t -> p k t", p=128), w=[("xT", k) for k in range(10)], sem="xin")

        def ada_layer(l):
            P.dma(adab[:], I["ada_b"][l], w=["adab"])
            ps, pk = bank(4)
            for n in range(12):
                wb = wada[n % 2]
                P.dma(wb[:], I["ada_w"][l, n],
                      w=[("wada", n % 2)], q="pool")
                for c4 in range(4):
                    c = n * 4 + c4
                    for k in range(8):
                        P.op("pe", lambda e, ps=ps, k=k, wb=wb, c=c, c4=c4: e.matmul(ps[:, c:c + 1], lhsT=wb[:, k, c4 * 128:(c4 + 1) * 128], rhs=s_bf[:, k:k + 1], start=(k == 0), stop=(k == 7)),
                             r=[("wada", n % 2), "s_bf"], w=[pk])
            P.op("dve", lambda e, ps=ps: e.tensor_tensor(out=mod[:], in0=ps[:, 0:48], in1=adab[:], op=ALU.add), r=[pk, "adab"], w=["mod"])
            for j in range(2):
                P.op("dve", lambda e, j=j: e.scalar_tensor_tensor(out=AB[:, 2 * j, :], in0=mod[:, (3 * j + 1) * 8:(3 * j + 2) * 8], scalar=1.0,
                                                                 in1=ng[:, j, l, :], op0=ALU.add, op1=ALU.mult),
                     r=["mod", "ng"], w=["AB"])
                P.op("dve", lambda e, j=j: e.tensor_copy(out=AB[:, 2 * j + 1, :], in_=mod[:, (3 * j) * 8:(3 * j + 1) * 8]), r=["mod"], w=["AB"])

        def rms_stats():
            b0, k0 = bank(6)
            b1, k1 = bank(7)
            for k in range(8):
                s = sq[k % 2]
                P.op("act", lambda e, s=s, k=k: e.activation(out=s[:], in_=xT[:, k, :], func=AF.Square), r=[("xT", k)], w=[("sq", k % 2)])
                for th, (b, bk) in enumerate(((b0, k0), (b1, k1))):
                    P.op("pe", lambda e, b=b, s=s, th=th, k=k: e.matmul(b, lhsT=ones_bf[:], rhs=s[:, th * 512:(th + 1) * 512], start=(k == 0), stop=(k == 7)),
                         r=[("sq", k % 2), "ones_bf"], w=[bk], inc=True)
            for th, (b, bk) in enumerate(((b0, k0), (b1, k1))):
                P.op("act", lambda e, b=b, th=th: e.activation(out=tmpA[:, th * 512:(th + 1) * 512], in_=b, func=AF.Ln, scale=1.0 / D, bias=eps_t[:, 0:1]),
                     r=[bk, "eps"], w=["tmpA"])
            P.op("act", lambda e: e.activation(out=rstd[:], in_=tmpA[:], func=AF.Exp, scale=-0.5), r=["tmpA"], w=["rstd"])

        def norm_mod(j):
            rms_stats()
            for k in range(8):
                t = tmpA if k % 2 == 0 else tmpB
                tk = "tmpA" if k % 2 == 0 else "tmpB"
                P.op("dve", lambda e, t=t, k=k: e.scalar_tensor_tensor(out=t[:], in0=xT[:, k, :], scalar=AB[:, 2 * j, k:k + 1], in1=rstd[:], op0=ALU.mult, op1=ALU.mult),
                     r=[("xT", k), "AB", "rstd"], w=[tk])
                P.op("act", lambda e, t=t, k=k: e.activation(out=hT[:, k, :], in_=t[:], func=AF.Identity, bias=AB[:, 2 * j + 1, k:k + 1], scale=1.0),
                     r=[tk, "AB"], w=[("hT", k)])

        def ffn(l):
            P.dma(cw[:], I["ffn_conv_w"][l], w=["cw"])
            P.dma(cb[:], I["ffn_conv_b"][l], w=["cb"])
            for jj, j in enumerate((0, 2)):
                P.op("dve", lambda e, jj=jj, j=j: e.tensor_scalar(out=cwk[:, jj, :], in0=cw[:, j, :], scalar1=keep[:, 1:2], scalar2=None, op0=ALU.mult),
                     r=["cw", "keep"], w=["cwk"])
            for grp in range(2):
                for fc in range(grp * 11, grp * 11 + 11):
                    wb = wup[fc % 2]
                    P.dma(wb[:], I["ffn_w_up"][l, fc], w=[("wup", fc % 2, 0), ("wup", fc % 2, 1)], q="pool")
                    outs = []
                    for gv in range(2):
                        pt = pq[(fc % 2) * 2 + gv]
                        pks = [("pq", ((fc % 2) * 2 + gv) * 2 + th) for th in range(2)]
                        for th in range(2):
                            for k in range(8):
                                P.op("pe", lambda e, pt=pt, th=th, k=k, gv=gv, wb=wb: e.matmul(pt[:, th * 512:(th + 1) * 512], lhsT=wb[:, gv, k, :], rhs=hT[:, k, th * 512:(th + 1) * 512],
                                                                                    start=(k == 0), stop=(k == 7)),
                                     r=[("wup", fc % 2, gv), ("hT", k)], w=[pks[th]], inc=(k == 7))
                        c = (cg if gv == 0 else cv)[fc % 2]
                        ck = ("cg" if gv == 0 else "cv", fc % 2)
                        col = gv * NFC + fc
                        P.op("act", lambda e, c=c, pt=pt, col=col: e.activation(out=c[:], in_=pt[:], func=AF.Identity, scale=cw[:, 1, col:col + 1], bias=cb[:, col:col + 1]),
                             r=pks + ["cw", "cb"], w=[ck])
                        P.op("dve", lambda e, c=c, pt=pt, col=col: e.scalar_tensor_tensor(out=c[:, 1:NT], in0=pt[:, 0:NT - 1], scalar=cw[:, 0, col:col + 1], in1=c[:, 1:NT], op0=ALU.mult, op1=ALU.add),
                             r=pks + ["cw", ck], w=[ck])
                        P.op("dve", lambda e, c=c, pt=pt, col=col: e.scalar_tensor_tensor(out=c[:, 0:NT - 1], in0=pt[:, 1:NT], scalar=cw[:, 2, col:col + 1], in1=c[:, 0:NT - 1], op0=ALU.mult, op1=ALU.add),
                             r=pks + ["cw", ck], w=[ck])
                        P.op("dve", lambda e, c=c, pt=pt, col=col: e.scalar_tensor_tensor(out=c[:, 256:NT:256], in0=pt[:, 255:NT - 1:256], scalar=cwk[:, 0, col:col + 1], in1=c[:, 256:NT:256], op0=ALU.mult, op1=ALU.add),
                             r=pks + ["cwk", ck], w=[ck])
                        P.op("dve", lambda e, c=c, pt=pt, col=col: e.scalar_tensor_tensor(out=c[:, 255:NT - 1:256], in0=pt[:, 256:NT:256], scalar=cwk[:, 1, col:col + 1], in1=c[:, 255:NT - 1:256], op0=ALU.mult, op1=ALU.add),
                             r=pks + ["cwk", ck], w=[ck])
                        outs.append((c, ck))
                    (cgt, cgk), (cvt, cvk) = outs
                    P.op("act", lambda e, cgt=cgt: e.activation(out=cgt[:], in_=cgt[:], func=AF.Silu), r=[cgk], w=[cgk])
                    P.op("dve", lambda e, cgt=cgt, cvt=cvt, fc=fc: e.tensor_tensor(out=aT[:, fc % 11, :], in0=cgt[:], in1=cvt[:], op=ALU.mult), r=[cgk, cvk], w=[("aT", fc % 11)])
                for dc in range(8):
                    wb = wdn[dc % 2]
                    P.dma(wb[:], I["ffn_w_down"][l, dc][:, grp * 11:grp * 11 + 11, :],
                          w=[("wdn", dc % 2)], q="pool")
                    for th in range(2):
                        ps, pk = bank((dc * 2 + th) % 8)
                        for fc in range(11):
                            P.op("pe", lambda e, ps=ps, fc=fc, th=th, wb=wb: e.matmul(ps, lhsT=wb[:, fc, :], rhs=aT[:, fc, th * 512:(th + 1) * 512], start=(fc == 0), stop=(fc == 10)),
                                 r=[("wdn", dc % 2), ("aT", fc)], w=[pk])
                        P.op("dve", lambda e, ps=ps, dc=dc, th=th: e.scalar_tensor_tensor(out=xT[:, dc, th * 512:(th + 1) * 512], in0=ps, scalar=mod[:, 40 + dc:41 + dc],
                                                                                         in1=xT[:, dc, th * 512:(th + 1) * 512], op0=ALU.mult, op1=ALU.add),
                             r=[pk, "mod", ("xT", dc)], w=[("xT", dc)])

        def attention(l):
            j = l // 3
            cosT, sinT = xT[:, 8, :], xT[:, 9, :]
            t1, t2 = cv[0], cv[1]
            if STAGE.get("kv_only"):
                wkv = wada[0]
                P.dma(wkv[:], I["wkv"][j], w=[("wada", 0)], q="pool")
                for tb in range(8):
                    ps, pk = bank(tb % 4)
                    for k in range(8):
                        P.op("pe", lambda e, ps=ps, k=k, tb=tb: e.matmul(ps, lhsT=hT[:, k, tb * 128:(tb + 1) * 128], rhs=wkv[:, k, :], start=(k == 0), stop=(k == 7)),
                             r=[("wada", 0), ("hT", k)], w=[pk])
                    kvt = tmpA if tb % 2 == 0 else tmpB
                    kvk = "tmpA" if tb % 2 == 0 else "tmpB"
                    P.op("act", lambda e, ps=ps, kvt=kvt: e.copy(out=kvt[:, 0:512], in_=ps), r=[pk], w=[kvk])
                    P.dma(O["nk"][j, tb * 128:(tb + 1) * 128, :], kvt[:, 0:256], r=[kvk], sem="kvout%d" % (tb % 2))
                    P.dma(O["nv"][j, tb * 128:(tb + 1) * 128, :], kvt[:, 256:512], r=[kvk], sem="kvout%d" % (tb % 2))
                return
            P.dma(esink[:], I["sink"][j], w=["esink"])
            P.op("act", lambda e: e.activation(out=esink[:], in_=esink[:], func=AF.Exp), r=["esink"], w=["esink"])
            if STAGE.get("attn_upto", 9) < 1:
                return
            P.dma(ckd[:], I["ckd"][j].rearrange("(kb p) n -> p kb n", p=128), w=["ckd"], q="pool")
            P.dma(vctx[:].rearrange("p kb g v n -> p kb (g v n)"), I["cvp"][j].rearrange("(kb p) n -> p kb n", p=128), w=["vctx"], q="pool")
            for g in range(4):
                ps, pk = bank(g)
                for kb in range(4):
                    P.op("pe", lambda e, ps=ps, kb=kb, g=g: e.matmul(ps[:, kb * 128:(kb + 1) * 128], lhsT=ckd[:, kb, g * 128:(g + 1) * 128], rhs=ident_bf[:], start=True, stop=True),
                         r=["ckd", "ident_bf"], w=[pk])
                P.op("act", lambda e, ps=ps, g=g: e.copy(out=kctxT[:, g, :], in_=ps), r=[pk], w=["kctxT"])
            if STAGE.get("attn_upto", 9) < 2:
                return
            def proj_rope(wsrc, dst, dkey, idx):
                wb = wup[idx % 2]
                P.dma(wb[:, 0], wsrc, w=[("wup", idx % 2, 0)], q="pool")
                for th in range(2):
                    ps, pk = bank((idx * 2 + th) % 4)
                    rps, rpk = bank(4 + (idx * 2 + th) % 2)
                    for k in range(8):
                        P.op("pe", lambda e, ps=ps, k=k, th=th, wb=wb: e.matmul(ps, lhsT=wb[:, 0, k, :], rhs=hT[:, k, th * 512:(th + 1) * 512], start=(k == 0), stop=(k == 7)),
                             r=[("wup", idx % 2, 0), ("hT", k)], w=[pk])
                    q_ = qb[th]
                    if STAGE.get("pr", 9) < 1:
                        continue
                    P.op("act", lambda e, ps=ps, q_=q_: e.copy(out=q_[:], in_=ps), r=[pk], w=[("qb", th)])
                    if STAGE.get("pr", 9) < 2:
                        continue
                    P.op("pe", lambda e, rps=rps, q_=q_: e.matmul(rps, lhsT=rmat[:], rhs=q_[:], start=True, stop=True), r=[("qb", th), "rmat"], w=[rpk])
                    sl = slice(th * 512, (th + 1) * 512)
                    if STAGE.get("pr", 9) < 3 or idx >= STAGE.get("pridx", 99):
                        continue
                    P.op("dve", lambda e, ps=ps, sl=sl: e.scalar_tensor_tensor(out=t1[:, sl], in0=ps, scalar=1.0, in1=cosT[:, sl], op0=ALU.mult, op1=ALU.mult), r=[pk, ("xT", 8), ("qb", th)], w=[("cv", 0)])
                    if STAGE.get("pr", 9) < 4:
                        continue
                    P.op("dve", lambda e, rps=rps, sl=sl: e.scalar_tensor_tensor(out=t2[:, sl], in0=rps, scalar=1.0, in1=sinT[:, sl], op0=ALU.mult, op1=ALU.mult), r=[rpk, ("xT", 9)], w=[("cv", 1)])
                    if STAGE.get("pr", 9) < 5:
                        continue
                    P.op("dve", lambda e, sl=sl, dst=dst: e.tensor_tensor(out=dst[:, sl], in0=t1[:, sl], in1=t2[:, sl], op=ALU.add), r=[("cv", 0), ("cv", 1)], w=[dkey])
            for qc in range(8):
                proj_rope(I["wq"][j, qc], aT[:, qc, :], ("aT", qc), qc)
            for g in range(4):
                proj_rope(I["wk"][j, g], aT[:, 8 + g, :], ("aT", 8 + g), 8 + g)
            if STAGE.get("attn_upto", 9) < 3:
                return
            wkv = wada[0]
            P.dma(wkv[:], I["wkv"][j], w=[("wada", 0)], q="pool")
            for tb in range(8):
                ps, pk = bank(tb % 4)
                for k in range(8):
                    P.op("pe", lambda e, ps=ps, k=k, tb=tb: e.matmul(ps, lhsT=hT[:, k, tb * 128:(tb + 1) * 128], rhs=wkv[:, k, :], start=(k == 0), stop=(k == 7)),
                         r=[("wada", 0), ("hT", k)], w=[pk])
                kvt = tmpA if tb % 2 == 0 else tmpB
                kvk = "tmpA" if tb % 2 == 0 else "tmpB"
                P.op("act", lambda e, ps=ps, kvt=kvt: e.copy(out=kvt[:, 0:512], in_=ps), r=[pk], w=[kvk])
                P.dma(O["nk"][j, tb * 128:(tb + 1) * 128, :], kvt[:, 0:256], r=[kvk], sem="kvout%d" % (tb % 2))
                P.dma(O["nv"][j, tb * 128:(tb + 1) * 128, :], kvt[:, 256:512], r=[kvk], sem="kvout%d" % (tb % 2))
                P.op("dve", lambda e, ps=ps, tb=tb: e.tensor_copy(out=vlat[:, tb, :, 0, 0:64], in_=ps[:, 256:512].rearrange("p (g d) -> p g d", g=4)), r=[pk], w=["vlat"])
                P.op("dve", lambda e, ps=ps, tb=tb: e.tensor_copy(out=vlat[:, tb, :, 1, 64:128], in_=ps[:, 256:512].rearrange("p (g d) -> p g d", g=4)), r=[pk], w=["vlat"])
            if STAGE.get("attn_upto", 9) < 4:
                return
            dsb = tmpA[:].rearrange("p (a n) -> p a n", a=2)
            osb = [cv[0][:, 0:512], cv[1][:, 0:512]]
            tb_bf = tmpB[:].bitcast(BF16)
            ebuf = [tb_bf[:, i * 512:(i + 1) * 512] for i in range(4)]
            P.alias(["dsb"], ["tmpA"])
            P.alias([("osb", 0)], [("cv", 0)])
            P.alias([("osb", 1)], [("cv", 1)])
            P.alias([("ebuf", i) for i in range(4)], ["tmpB"])
            sc = 0
            for h in range(STAGE.get("nheads", 16)):
                g, qc, pb, var = h // 4, h // 2, (h % 2) * 64, h % 2
                dr = 64 if var == 0 else 0
                qh = aT[pb:pb + 64, qc, :]
                kh = aT[pb:pb + 64, 8 + g, :]
                kch = kctxT[pb:pb + 64, g, :]
                for th in range(2):
                    it = h * 2 + th
                    OP, opk = bank(4 + it % 2)
                    jbs = [jb for jb in range(8) if max(jb - 1, 4 * th) <= min(jb + 1, 4 * th + 3)]
                    blocks = [("c", kb) for kb in range(4)] + [("l", jb) for jb in jbs]
                    LA = 2
                    pend = []

                    def emit_S(kind, ix):
                        nonlocal sc
                        ps, pk = bank(sc % 4); eb = ebuf[sc % 4]; ek = ("ebuf", sc % 4); sc += 1
                        if kind == "c":
                            kb = ix
                            P.op("pe", lambda e, ps=ps, kb=kb, th=th, kch=kch, qh=qh: e.matmul(ps, lhsT=kch[:, kb * 128:(kb + 1) * 128], rhs=qh[:, th * 512:(th + 1) * 512], start=True, stop=True),
                                 r=["kctxT", ("aT", qc)], w=[pk])
                            P.op("act", lambda e, ps=ps, eb=eb: e.activation(out=eb[:], in_=ps, func=AF.Exp, scale=0.125), r=[pk], w=[ek])
                            return (kind, ix, eb, ek, 0, 512)
                        jb = ix
                        i0_ = max(jb - 1, 4 * th); i1_ = min(jb + 1, 4 * th + 3)
                        n = (i1_ - i0_ + 1) * 128
                        P.op("pe", lambda e, ps=ps, jb=jb, i0_=i0_, n=n, kh=kh, qh=qh: e.matmul(ps[:, 0:n], lhsT=kh[:, jb * 128:(jb + 1) * 128], rhs=qh[:, i0_ * 128:i0_ * 128 + n], start=True, stop=True),
                             r=[("aT", 8 + g), ("aT", qc)], w=[pk])
                        P.op("act", lambda e, ps=ps, eb=eb, n=n: e.activation(out=eb[:, 0:n], in_=ps[:, 0:n], func=AF.Exp, scale=0.125), r=[pk], w=[ek])
                        for i in range(i0_, i1_ + 1):
                            if i == jb:
                                continue
                            off = 0 if i == jb + 1 else 1
                            c0 = (i - i0_) * 128
                            P.op("dve", lambda e, eb=eb, c0=c0, i=i, off=off: e.tensor_tensor(out=eb[:, c0:c0 + 128], in0=eb[:, c0:c0 + 128], in1=amask[:, i, off, :], op=ALU.mult),
                                 r=[ek, "amask"], w=[ek])
                        return (kind, ix, eb, ek, (i0_ - 4 * th) * 128, n)

                    def emit_PV(st, is_first, is_last):
                        kind, ix, eb, ek, o0, n = st
                        if kind == "c":
                            P.op("pe", lambda e, OP=OP, eb=eb, ix=ix, g=g, var=var, is_first=is_first: e.matmul(OP, lhsT=vctx[:, ix, g, var, :], rhs=eb[:], start=is_first, stop=False),
                                 r=["vctx", ek], w=[opk])
                        else:
                            P.op("pe", lambda e, OP=OP, eb=eb, ix=ix, g=g, var=var, o0=o0, n=n, is_last=is_last: e.matmul(OP[:, o0:o0 + n], lhsT=vlat[:, ix, g, var, :], rhs=eb[:, 0:n], start=False, stop=is_last),
                                 r=["vlat", ek], w=[opk])

                    nb = len(blocks)
                    for i in range(nb + LA):
                        if i < nb:
                            pend.append(emit_S(*blocks[i]))
                        if i - LA >= 0:
                            emit_PV(pend[i - LA], i - LA == 0, i - LA == nb - 1)
                    P.op("dve", lambda e, OP=OP, dr=dr, h=h: e.tensor_scalar(out=dsb[dr:dr + 1, 0, :], in0=OP[dr:dr + 1, :], scalar1=esink[dr:dr + 1, h:h + 1], scalar2=None, op0=ALU.add),
                         r=[opk, "esink"], w=["dsb"])
                    P.op("act", lambda e, dr=dr: e.activation(out=dsb[dr:dr + 1, 0, :], in_=dsb[dr:dr + 1, 0, :], func=AF.Ln), r=["dsb"], w=["dsb"])
                    P.op("act", lambda e, dr=dr: e.activation(out=dsb[dr:dr + 1, 1, :], in_=dsb[dr:dr + 1, 0, :], func=AF.Exp, scale=-1.0), r=["dsb"], w=["dsb"])
                    BC, bck = bank(6 + it % 2)
                    P.op("pe", lambda e, BC=BC, dr=dr: e.matmul(BC, lhsT=ones_f[dr:dr + 1, :], rhs=dsb[dr:dr + 1, 1, :], start=True, stop=True), r=["dsb", "ones_f"], w=[bck])
                    ob = osb[it % 2]; obk = ("osb", it % 2)
                    P.op("act", lambda e, OP=OP, ob=ob, pb=pb: e.copy(out=ob[pb:pb + 64, :], in_=OP[pb:pb + 64, :]), r=[opk], w=[obk])
                    P.op("dve", lambda e, BC=BC, ob=ob, pb=pb, qc=qc, th=th: e.tensor_tensor(out=aT[pb:pb + 64, qc, th * 512:(th + 1) * 512], in0=ob[pb:pb + 64, :], in1=BC[pb:pb + 64, :], op=ALU.mult),
                         r=[obk, bck], w=[("aT", qc)])
            P.alias(["tmpA"], ["dsb"])
            P.alias([("cv", 0)], [("osb", 0)])
            P.alias([("cv", 1)], [("osb", 1)])
            P.alias(["tmpB"], [("ebuf", i) for i in range(4)])
            for dc in range(8):
                wb = wup[dc % 2]
                P.dma(wb[:, 0], I["wo"][j, dc], w=[("wup", dc % 2, 0)], q="pool")
                for th in range(2):
                    ps, pk = bank((dc * 2 + th) % 4)
                    for k in range(8):
                        P.op("pe", lambda e, ps=ps, k=k, th=th, wb=wb: e.matmul(ps, lhsT=wb[:, 0, k, :], rhs=aT[:, k, th * 512:(th + 1) * 512], start=(k == 0), stop=(k == 7)),
                             r=[("wup", dc % 2, 0), ("aT", k)], w=[pk])
                    P.op("dve", lambda e, ps=ps, dc=dc, th=th: e.scalar_tensor_tensor(out=xT[:, dc, th * 512:(th + 1) * 512], in0=ps, scalar=mod[:, 16 + dc:17 + dc],
                                                                                     in1=xT[:, dc, th * 512:(th + 1) * 512], op0=ALU.mult, op1=ALU.add),
                         r=[pk, "mod", ("xT", dc)], w=[("xT", dc)])

        def hgrn(l):
            CH = 32
            NCH = NT // CH
            P.op("act", lambda e: e.activation(out=hlb[:], in_=hlb[:], func=AF.Exp), r=["hlb"], w=["hlb"])
            P.op("dve", lambda e: e.tensor_tensor(out=lbp[:, 1], in0=hlb[:, 0], in1=hlb[:, 1], op=ALU.add), r=["hlb"], w=["lbp"])
            P.op("dve", lambda e: e.tensor_tensor(out=lbp[:, 1], in0=lbp[:, 1], in1=hlb[:, 2], op=ALU.add), r=["hlb", "lbp"], w=["lbp"])
            P.op("dve", lambda e: e.tensor_tensor(out=lbp[:, 1], in0=lbp[:, 1], in1=hlb[:, 3], op=ALU.add), r=["hlb", "lbp"], w=["lbp"])
            P.op("dve", lambda e: e.reciprocal(out=lbp[:, 1], in_=lbp[:, 1]), r=["lbp"], w=["lbp"])
            P.op("dve", lambda e: e.tensor_tensor(out=lbp[:, 0], in0=lbp[:, 1], in1=hlb[:, 1], op=ALU.mult), r=["hlb", "lbp"], w=["lbp"])
            P.op("dve", lambda e: e.tensor_scalar(out=lbp[:, 1], in0=lbp[:, 0], scalar1=-1.0, scalar2=1.0, op0=ALU.mult, op1=ALU.add), r=["lbp"], w=["lbp"])
            bufQ, bufF, bufK, bufC = cg[0], cg[1], cv[0], cv[1]
            kQ, kF, kK, kC = ("cg", 0), ("cg", 1), ("cv", 0), ("cv", 1)
            oacc = rstd
            kdtok = [vlat[:].rearrange("p a g v n -> p (a g v n)")[:, d * 4096:(d + 1) * 4096].rearrange("p (b c n) -> p b c n", b=8, c=4) for d in range(2)]
            P.alias([("kdtok", 0), ("kdtok", 1)], ["vlat"])
            for h in range(8):
                P.dma(wup[0][:], I["hw_in"][h, 0:2].rearrange("a p k n -> p a k n"), w=[("wup", 0, 0), ("wup", 0, 1)], q="pool")
                P.dma(wup[1][:], I["hw_in"][h, 2:4].rearrange("a p k n -> p a k n"), w=[("wup", 1, 0), ("wup", 1, 1)], q="pool")
                P.dma(wdn[0][:, 0:8, :], I["hw_in"][h, 4], w=[("wdn", 0)], q="pool")

                def proj(wap, wkeys, bi):
                    outs = []
                    for th in range(2):
                        ps, pk = bank(bi * 2 + th)
                        for kk in range(8):
                            P.op("pe", lambda e, ps=ps, kk=kk, th=th, wap=wap: e.matmul(ps, lhsT=wap[:, kk, :], rhs=hT[:, kk, th * 512:(th + 1) * 512], start=(kk == 0), stop=(kk == 7)),
                                 r=list(wkeys) + [("hT", kk)], w=[pk])
                        outs.append((ps, pk))
                    return outs
                for th, (ps, pk) in enumerate(proj(wup[0][:, 0], [("wup", 0, 0)], 0)):
                    P.op("act", lambda e, ps=ps, th=th: e.copy(out=bufQ[:, th * 512:(th + 1) * 512], in_=ps), r=[pk], w=[kQ])
                for blk in range(8):
                    ps, pk = bank(2 + blk % 2)
                    for kk in range(8):
                        P.op("pe", lambda e, ps=ps, kk=kk, blk=blk: e.matmul(ps[:, 0:128], lhsT=hT[:, kk, blk * 128:(blk + 1) * 128], rhs=wup[0][:, 1, kk, :], start=(kk == 0), stop=(kk == 7)),
                             r=[("wup", 0, 1), ("hT", kk)], w=[pk])
                    P.op("act", lambda e, ps=ps, blk=blk: e.copy(out=vtok[:, blk, :], in_=ps[:, 0:128]), r=[pk], w=["vtok"])
                for d in range(2):
                    for th, (ps, pk) in enumerate(proj(wup[1][:, d], [("wup", 1, d)], 2 + d)):
                        P.op("act", lambda e, ps=ps, th=th: e.activation(out=bufF[:, th * 512:(th + 1) * 512], in_=ps, func=AF.Sigmoid), r=[pk], w=[kF])
                    P.op("dve", lambda e, d=d, h=h: e.tensor_scalar(out=bufF[:], in0=bufF[:], scalar1=lbp[:, 1, d, h:h + 1], scalar2=lbp[:, 0, d, h:h + 1], op0=ALU.mult, op1=ALU.add),
                         r=[kF, "lbp"], w=[kF])
                    P.op("dve", lambda e: e.tensor_scalar(out=bufK[:], in0=bufF[:], scalar1=-1.0, scalar2=1.0, op0=ALU.mult, op1=ALU.add), r=[kF], w=[kK])
                    P.op("act", lambda e: e.activation(out=bufF[:], in_=bufF[:], func=AF.Ln), r=[kF], w=[kF])
                    P.op("dve", lambda e: e.tensor_tensor_scan(out=bufC[:], data0=m32[:], data1=bufF[:], initial=0.0, op0=ALU.mult, op1=ALU.add), r=["m32", kF], w=[kC])
                    if d == 1:
                        P.op("dve", lambda e: e.scalar_tensor_tensor(out=tmpA[:], in0=bufC[:], scalar=-1.0, in1=bufF[:], op0=ALU.mult, op1=ALU.add), r=[kC, kF], w=["tmpA"])
                        P.op("act", lambda e: e.copy(out=ctmp[:], in_=bufC[:, CH - 1:NT:CH]), r=[kC], w=["ctmp"])
                        P.op("dve", lambda e: e.tensor_tensor(out=bufC[:].rearrange("p (c t) -> p c t", t=CH), in0=tmpA[:].rearrange("p (c t) -> p c t", t=CH),
                                                              in1=ctmp[:].unsqueeze(2).to_broadcast([128, NCH, CH]), op=ALU.add),
                             r=["tmpA", "ctmp"], w=[kC])
                        ctot = bufC[:, 0:NT:CH]
                    else:
                        ctot = bufC[:, CH - 1:NT:CH]
                    P.op("act", lambda e, d=d, ctot=ctot: e.activation(out=adec[d][:], in_=ctot, func=AF.Exp), r=[kC], w=[("adec", d)])
                    P.op("act", lambda e: e.activation(out=tmpA[:], in_=bufC[:], func=AF.Exp), r=[kC], w=["tmpA"])
                    P.op("dve", lambda e, d=d: e.tensor_tensor(out=qhat[d][:], in0=bufQ[:], in1=tmpA[:], op=ALU.mult), r=[kQ, "tmpA"], w=[("qhat", d)])
                    P.op("dve", lambda e: e.tensor_scalar(out=tmpB[:], in0=bufC[:], scalar1=-1.0, scalar2=85.0, op0=ALU.mult, op1=ALU.min), r=[kC], w=["tmpB"])
                    P.op("act", lambda e: e.activation(out=tmpB[:], in_=tmpB[:], func=AF.Exp), r=["tmpB"], w=["tmpB"])
                    P.op("dve", lambda e: e.tensor_tensor(out=tmpB[:], in0=tmpB[:], in1=bufK[:], op=ALU.mult), r=["tmpB", kK], w=["tmpB"])
                    P.op("act", lambda e, d=d: e.copy(out=ktil[d][:], in_=tmpB[:]), r=["tmpB"], w=[("ktil", d)])
                    P.op("dve", lambda e, d=d: e.tensor_tensor(out=kdT[d][:].rearrange("p (c t) -> p c t", t=CH), in0=tmpB[:].rearrange("p (c t) -> p c t", t=CH),
                                                                in1=adec[d][:].unsqueeze(2).to_broadcast([128, NCH, CH]), op=ALU.mult),
                         r=["tmpB", ("adec", d)], w=[("kdT", d)])
                    for blk in range(8):
                        ps, pk = bank(6 + blk % 2)
                        pst = ps.bitcast(BF16)
                        P.op("pe", lambda e, pst=pst, blk=blk, d=d: e.transpose(pst[:, 0:128], kdT[d][:, blk * 128:(blk + 1) * 128], ident_bf[:]), r=[("kdT", d), "ident_bf"], w=[pk])
                        for c4 in range(4):
                            P.op("act", lambda e, pst=pst, blk=blk, c4=c4, d=d: e.activation(out=kdtok[d][:, blk, c4, :], in_=pst[:, 0:128], func=AF.Identity, scale=rowm[:, c4:c4 + 1]),
                                 r=[pk, "rowm"], w=[("kdtok", d)])
                    P.dma(Sf[d][:], I["hs0"][d, h], w=[("Sf", d)])
                    P.op("act", lambda e, d=d: e.copy(out=Sb[d][:], in_=Sf[d][:]), r=[("Sf", d)], w=[("Sb", d)])
                P.op("pool", lambda e: e.memset(oacc[:], 0.0), r=["rstd"], w=["rstd"])
                for bi_ in range(8):
                    ctxs = []
                    for d in range(2):
                        blk = bi_ if d == 0 else 7 - bi_
                        bsl = slice(blk * 128, (blk + 1) * 128)
                        aps, apk = bank(0 + d * 2)
                        ops_, opk = bank(1 + d * 2)
                        dps, dpk = bank(4 + d)
                        P.op("pe", lambda e, aps=aps, d=d, bsl=bsl: e.matmul(aps[:, 0:128], lhsT=ktil[d][:, bsl], rhs=qhat[d][:, bsl], start=True, stop=True),
                             r=[("ktil", d), ("qhat", d)], w=[apk])
                        am = attm[d]
                        P.op("dve", lambda e, aps=aps, am=am, d=d: e.tensor_tensor(out=am[:], in0=aps[:, 0:128], in1=hmask[:, d, :], op=ALU.mult), r=[apk, "hmask"], w=[("attm", d)])
                        for c4 in range(4):
                            P.op("pe", lambda e, dps=dps, c4=c4, blk=blk, d=d: e.matmul(dps[:, c4 * 128:(c4 + 1) * 128], lhsT=kdtok[d][:, blk, c4, :], rhs=vtok[:, blk, :], start=True, stop=True),
                                 r=[("kdtok", d), "vtok"], w=[dpk])
                        P.op("pe", lambda e, ops_=ops_, am=am, blk=blk: e.matmul(ops_[:, 0:128], lhsT=vtok[:, blk, :], rhs=am[:], start=True, stop=False),
                             r=["vtok", ("attm", d)], w=[opk])
                        ctxs.append((blk, bsl, ops_, opk, dps, dpk))
                    for step in range(4):
                        for d in range(2):
                            blk, bsl, ops_, opk, dps, dpk = ctxs[d]
                            c4 = step if d == 0 else 3 - step
                            last = (step == 3)
                            ch = blk * 4 + c4
                            csl = slice(ch * CH, (ch + 1) * CH)
                            P.op("pe", lambda e, ops_=ops_, c4=c4, csl=csl, d=d, last=last: e.matmul(ops_[:, c4 * CH:(c4 + 1) * CH], lhsT=Sb[d][:], rhs=qhat[d][:, csl], start=False, stop=last),
                                 r=[("Sb", d), ("qhat", d)], w=[opk])
                            P.op("dve", lambda e, dps=dps, c4=c4, ch=ch, d=d: e.scalar_tensor_tensor(out=Sf[d][:], in0=Sf[d][:], scalar=adec[d][:, ch:ch + 1], in1=dps[:, c4 * 128:(c4 + 1) * 128], op0=ALU.mult, op1=ALU.add),
                                 r=[("Sf", d), ("adec", d), dpk], w=[("Sf", d)])
                            seg_end = (ch % 8 == 7) if d == 0 else (ch % 8 == 0)
                            if seg_end:
                                seg = ch // 8
                                P.dma(O["hg"][(seg * 2 + d) * 8 + h], Sf[d][:], r=[("Sf", d)], sem="hgout%d" % d)
                                P.op("dve", lambda e, d=d: e.tensor_scalar(out=Sf[d][:], in0=Sf[d][:], scalar1=keep[:, 0:1], scalar2=None, op0=ALU.mult), r=[("Sf", d), "keep"], w=[("Sf", d)])
                            P.op("act", lambda e, d=d: e.copy(out=Sb[d][:], in_=Sf[d][:]), r=[("Sf", d)], w=[("Sb", d)])
                    for d in range(2):
                        blk, bsl, ops_, opk, dps, dpk = ctxs[d]
                        P.op("dve", lambda e, ops_=ops_, bsl=bsl: e.tensor_tensor(out=oacc[:, bsl], in0=ops_[:, 0:128], in1=oacc[:, bsl], op=ALU.add), r=[opk, "rstd"], w=["rstd"])
                P.op("act", lambda e: e.activation(out=sq[0][:], in_=oacc[:], func=AF.Square), r=["rstd"], w=[("sq", 0)])
                for th in range(2):
                    ps, pk = bank(6 + th)
                    P.op("pe", lambda e, ps=ps, th=th: e.matmul(ps, lhsT=ones_bf[:], rhs=sq[0][:, th * 512:(th + 1) * 512], start=True, stop=True), r=[("sq", 0), "ones_bf"], w=[pk])
                    P.op("act", lambda e, ps=ps, th=th: e.activation(out=tmpA[:, th * 512:(th + 1) * 512], in_=ps, func=AF.Ln, scale=1.0 / 128, bias=eps_t[:, 0:1]), r=[pk, "eps"], w=["tmpA"])
                P.op("act", lambda e: e.activation(out=tmpA[:], in_=tmpA[:], func=AF.Exp, scale=-0.5), r=["tmpA"], w=["tmpA"])
                P.op("dve", lambda e: e.scalar_tensor_tensor(out=tmpA[:], in0=oacc[:], scalar=hgn[:, 0:1], in1=tmpA[:], op0=ALU.mult, op1=ALU.mult), r=["rstd", "hgn", "tmpA"], w=["tmpA"])
                for th, (ps, pk) in enumerate(proj(wdn[0][:, 0:8, :], [("wdn", 0)], 2)):
                    P.op("act", lambda e, ps=ps, th=th: e.activation(out=tmpB[:, th * 512:(th + 1) * 512], in_=ps, func=AF.Silu), r=[pk], w=["tmpB"])
                P.op("dve", lambda e, h=h: e.tensor_tensor(out=aT[:, h, :], in0=tmpA[:], in1=tmpB[:], op=ALU.mult), r=["tmpA", "tmpB"], w=[("aT", h)])
            P.alias(["vlat"], [("kdtok", 0), ("kdtok", 1)])
            P.op("dve", lambda e: e.memset(vlat[:], 0.0), w=["vlat"])
            P.op("dve", lambda e: e.memset(vlat[:, :, :, 0, 64:65], 1.0), w=["vlat"])
            P.op("dve", lambda e: e.memset(vlat[:, :, :, 1, 0:1], 1.0), w=["vlat"])
            for dc in range(8):
                wb = wup[dc % 2]
                P.dma(wb[:, 0], I["hwo"][dc], w=[("wup", dc % 2, 0)], q="pool")
                for th in range(2):
                    ps, pk = bank((dc * 2 + th) % 4)
                    for kk in range(8):
                        P.op("pe", lambda e, ps=ps, kk=kk, th=th, wb=wb: e.matmul(ps, lhsT=wb[:, 0, kk, :], rhs=aT[:, kk, th * 512:(th + 1) * 512], start=(kk == 0), stop=(kk == 7)),
                             r=[("wup", dc % 2, 0), ("aT", kk)], w=[pk])
                    P.op("dve", lambda e, ps=ps, dc=dc, th=th: e.scalar_tensor_tensor(out=xT[:, dc, th * 512:(th + 1) * 512], in0=ps, scalar=mod[:, 16 + dc:17 + dc],
                                                                                     in1=xT[:, dc, th * 512:(th + 1) * 512], op0=ALU.mult, op1=ALU.add),
                         r=[pk, "mod", ("xT", dc)], w=[("xT", dc)])

        def ssm(l):
            SEG = 256
            rs32 = rstd[:]
            sr_ = [rs32[:, i * 64:(i + 1) * 64] for i in range(16)]
            sq32 = sq[0][:].bitcast(F32)
            sq_ = [sq32[:, i * 32:(i + 1) * 32] for i in range(14)]
            sCq = [sq[1][:, i * 512:(i + 1) * 512].rearrange("p (g i) -> p g i", g=32) for i in range(2)]
            P.alias(["srr"], ["rstd"])
            P.alias(["sqq"], [("sq", 0)])
            P.alias(["sCq"], [("sq", 1)])
            m256 = m32
            P.op("pool", lambda e: e.memset(m256[:], 1.0), r=["m32"], w=["m32"])
            P.op("pool", lambda e: e.memset(m256[:, 0:NT:256], 0.0), r=["m32"], w=["m32"])
            P.dma(smask[:], I["smask"], w=["smask"])
            P.dma(sD[:], I["sD"], w=["sD"])
            TT = lambda e, o, a, b, op: e.tensor_tensor(out=o, in0=a, in1=b, op=op)

            def vop(eng, o, a, b, op, r, w):
                P.op(eng, lambda e, o=o, a=a, b=b, op=op: e.tensor_tensor(out=o, in0=a, in1=b, op=op), r=r, w=w)

            def cmul(eng, ore, oim, are_, aim_, bre, bim, t1, t2, r, w, tk):
                vop(eng, t1, are_, bre, ALU.mult, r, [tk[0]])
                vop(eng, t2, aim_, bim, ALU.mult, r, [tk[1]])
                vop(eng, ore, t1, t2, ALU.subtract, [tk[0], tk[1]], w)
                vop(eng, t1, are_, bim, ALU.mult, r, [tk[0]])
                vop(eng, t2, aim_, bre, ALU.mult, r, [tk[1]])
                vop(eng, oim, t1, t2, ALU.add, [tk[0], tk[1]], w)

            def lam_params(are_ap, aim_ap, dt_scalar_or_ap, S, key, n, per_part_dt, need_inv=True, eng="dve"):
                arec, th, mag, c, s_, t1, t2, imag = S[0], S[1], S[2], S[3], S[4], S[5], S[6], S[7]
                K = [key]
                P.op(eng, lambda e: e.tensor_scalar(out=arec[:], in0=are_ap, scalar1=-1e-4, scalar2=None, op0=ALU.min), r=K, w=K)
                if per_part_dt:
                    dtx = S[8]
                    P.op("act", lambda e: e.activation(out=dtx[:, 0:1], in_=dt_scalar_or_ap, func=AF.Exp), r=K, w=K)
                    P.op(eng, lambda e: e.tensor_scalar(out=th[:], in0=aim_ap, scalar1=dtx[:, 0:1], scalar2=None, op0=ALU.mult), r=K, w=K)
                    P.op(eng, lambda e: e.tensor_scalar(out=mag[:], in0=arec[:], scalar1=dtx[:, 0:1], scalar2=None, op0=ALU.mult), r=K, w=K)
                else:
                    dtx = S[8]
                    P.op("act", lambda e: e.activation(out=dtx[:], in_=dt_scalar_or_ap, func=AF.Exp), r=K, w=K)
                    vop(eng, th[:], aim_ap, dtx[:], ALU.mult, K, K)
                    vop(eng, mag[:], arec[:], dtx[:], ALU.mult, K, K)
                P.op("act", lambda e: e.activation(out=imag[:], in_=mag[:], func=AF.Exp, scale=-1.0), r=K, w=K)
                P.op("act", lambda e: e.activation(out=mag[:], in_=mag[:], func=AF.Exp), r=K, w=K)
                P.op("act", lambda e: e.activation(out=s_[:], in_=th[:], func=AF.Sin, scale=1.0 / 64), r=K, w=K)
                P.op("act", lambda e: e.activation(out=c[:], in_=th[:], func=AF.Sin, scale=1.0 / 64, bias=halfpi[:, 0:1]), r=K + ["halfpi"], w=K)
                for _ in range(6):
                    vop(eng, t1[:], c[:], s_[:], ALU.mult, K, K)
                    vop(eng, c[:], c[:], c[:], ALU.mult, K, K)
                    vop(eng, s_[:], s_[:], s_[:], ALU.mult, K, K)
                    vop(eng, c[:], c[:], s_[:], ALU.subtract, K, K)
                    P.op(eng, lambda e: e.tensor_scalar(out=s_[:], in0=t1[:], scalar1=2.0, scalar2=None, op0=ALU.mult), r=K, w=K)
                L1re, L1im, Lm1re, Lm1im = S[9], S[10], S[11], S[12]
                vop(eng, L1re[:], mag[:], c[:], ALU.mult, K, K)
                vop(eng, L1im[:], mag[:], s_[:], ALU.mult, K, K)
                if need_inv:
                    vop(eng, Lm1re[:], imag[:], c[:], ALU.mult, K, K)
                    vop(eng, Lm1im[:], imag[:], s_[:], ALU.mult, K, K)
                    P.op(eng, lambda e: e.tensor_scalar(out=Lm1im[:], in0=Lm1im[:], scalar1=-1.0, scalar2=None, op0=ALU.mult), r=K, w=K)
                return dict(L1re=L1re, L1im=L1im, Lm1re=Lm1re, Lm1im=Lm1im, are=arec)

            A_, B_, C_, D_, Gr, Gi = cg[0], cg[1], cv[0], cv[1], tmpA, tmpB
            kA, kB, kC2, kD, kGr, kGi = ("cg", 0), ("cg", 1), ("cv", 0), ("cv", 1), "tmpA", "tmpB"
            vl = vlat[:].rearrange("p a g v n -> p (a g v n)").bitcast(F32)
            Tre_all = vl[:, 0:2048].rearrange("p (g t) -> p g t", g=8)
            Tim_all = vl[:, 2048:4096].rearrange("p (g t) -> p g t", g=8)
            Tp_re, Tm_re, Tp_im, Tm_im = Tre_all[:, 0:4], Tre_all[:, 4:8], Tim_all[:, 0:4], Tim_all[:, 4:8]
            P.alias(["stab"], ["vlat"])
            vc = vctx[:].rearrange("p a g v n -> p (a g v n)")
            W1pad = vc[:, 0:4096].rearrange("p (q r n) -> p q r n", q=16, r=2)
            kcf = kctxT[:].rearrange("p a n -> p (a n)")
            Ewpad = kcf[:, 0:2048].rearrange("p (q r n) -> p q r n", q=8, r=2)
            P.alias(["W1pad"], ["vctx"])
            P.alias(["Ewpad"], ["kctxT"])
            Hre_b = qhat[0][:].rearrange("p (g t) -> p g t", g=4)
            Him_b = qhat[1][:].rearrange("p (g t) -> p g t", g=4)
            ysb = aT

            def _body():
              for d in range(2):
                  P.dma(sq_[0], I["sAreQ"][d], w=["sqq"])
                  P.dma(sq_[1], I["sAimQ"][d], w=["sqq"])
                  P.dma(sq_[13], I["sDtQ"][d], w=["sqq"])
                  P.dma(sCq[0], I["sCreQ"][d], w=["sCq"], q="pool")
                  P.dma(sCq[1], I["sCimQ"][d], w=["sCq"], q="pool")
                  P.dma(sH[:], I["sH0"][d], w=["sH"])
                  Q = lam_params(sq_[0], sq_[1], sq_[13], sq_[2:13] + [sq_[0], sq_[1]], "sqq", 32, False)
                  for s in range(8):
                      k0, k1 = s // 2, 4 + s // 2
                      g8b = (4 * s) % 8
                      P.op("pool", lambda e: e.memset(kcf[:, 0:2048], 0.0), r=["Ewpad"], w=["Ewpad"])
                      if s % 2 == 0:
                          P.op("pool", lambda e: e.memset(vc[:], 0.0), r=["W1pad"], w=["W1pad"])
                      if s % 2 == 0:
                          for half, kk_ in enumerate((k0, k1)):
                              Rk = ["srr"]
                              P.dma(sr_[0], I["sAreR"][d, kk_], w=Rk)
                              P.dma(sr_[1], I["sAimR"][d, kk_], w=Rk)
                              P.dma(sr_[13][:, 0:1], I["sDtR"][d, kk_], w=Rk)
                              P.dma(sr_[14], I["sBreR"][d, kk_], w=Rk)
                              P.dma(sr_[15], I["sBimR"][d, kk_], w=Rk)
                              R_ = lam_params(sr_[0], sr_[1], sr_[13][:, 0:1], sr_[2:13] + [sr_[0], sr_[0]], "srr", 64, True, need_inv=False)
                              nre, den, cre, cim, t1, t2 = sr_[3], sr_[4], sr_[5], sr_[6], sr_[7], sr_[8]
                              aim_ = sr_[1]
                              P.op("dve", lambda e, R_=R_: e.tensor_scalar(out=nre[:], in0=R_["L1re"][:], scalar1=-1.0, scalar2=None, op0=ALU.add), r=Rk, w=Rk)
                              vop("dve", den[:], R_["are"][:], R_["are"][:], ALU.mult, Rk, Rk)
                              vop("dve", t1[:], aim_[:], aim_[:], ALU.mult, Rk, Rk)
                              vop("dve", den[:], den[:], t1[:], ALU.add, Rk, Rk)
                              P.op("dve", lambda e: e.reciprocal(out=den[:], in_=den[:]), r=Rk, w=Rk)
                              vop("dve", t1[:], nre[:], R_["are"][:], ALU.mult, Rk, Rk)
                              vop("dve", t2[:], R_["L1im"][:], aim_[:], ALU.mult, Rk, Rk)
                              vop("dve", cre[:], t1[:], t2[:], ALU.add, Rk, Rk)
                              vop("dve", cre[:], cre[:], den[:], ALU.mult, Rk, Rk)
                              vop("dve", t1[:], R_["L1im"][:], R_["are"][:], ALU.mult, Rk, Rk)
                              vop("dve", t2[:], nre[:], aim_[:], ALU.mult, Rk, Rk)
                              vop("dve", cim[:], t1[:], t2[:], ALU.subtract, Rk, Rk)
                              vop("dve", cim[:], cim[:], den[:], ALU.mult, Rk, Rk)
                              wre, wim = sr_[9], sr_[10]
                              cmul("dve", wre[:], wim[:], cre[:], cim[:], sr_[14][:], sr_[15][:], t1[:], t2[:], Rk, Rk, ["srr", "srr"])
                              for g8 in range(8):
                                  for ri, wsrc in enumerate((wre, wim)):
                                      P.op("act", lambda e, half=half, ri=ri, wsrc=wsrc, g8=g8: e.activation(out=W1pad[:, half * 8 + g8, ri, half * 64:(half + 1) * 64], in_=wsrc[:], func=AF.Identity, scale=smask[:, g8:g8 + 1]),
                                           r=Rk + ["smask"], w=["W1pad"])
                      for half, kk_ in enumerate((k0, k1)):
                          for q in range(4):
                              g8 = g8b + q
                              gq = 4 * s + q
                              P.op("act", lambda e, q=q, half=half, g8=g8, gq=gq: e.activation(out=Ewpad[:, half * 4 + q, 0, g8 * 16:(g8 + 1) * 16], in_=sCq[0][:, gq, :], func=AF.Identity, scale=smask[:, 8 + half:9 + half]),
                                   r=["sCq", "smask"], w=["Ewpad"])
                              P.op("act", lambda e, q=q, half=half, g8=g8, gq=gq: e.activation(out=Ewpad[:, half * 4 + q, 1, g8 * 16:(g8 + 1) * 16], in_=sCq[1][:, gq, :], func=AF.Identity, scale=smask[:, 10 + half:11 + half]),
                                   r=["sCq", "smask"], w=["Ewpad"])
                      gsl = slice(4 * s, 4 * s + 4)
                      i0 = 0 if d == 0 else SEG - 1
                      for (Tre, Tim, bre_, bim_) in ((Tp_re, Tp_im, Q["L1re"], Q["L1im"]), (Tm_re, Tm_im, Q["Lm1re"], Q["Lm1im"])):
                          P.op("dve", lambda e, Tre=Tre, bre_=bre_, i0=i0, gsl=gsl: e.tensor_copy(out=Tre[:, :, i0:i0 + 1], in_=bre_[:, gsl].unsqueeze(2)), r=["sqq"], w=["stab"])
                          P.op("dve", lambda e, Tim=Tim, bim_=bim_, i0=i0, gsl=gsl: e.tensor_copy(out=Tim[:, :, i0:i0 + 1], in_=bim_[:, gsl].unsqueeze(2)), r=["sqq"], w=["stab"])
                      L = 1
                      while L < SEG:
                          if d == 0:
                              src = slice(0, L); dst = slice(L, 2 * L); piv = L - 1
                          else:
                              src = slice(SEG - L, SEG); dst = slice(SEG - 2 * L, SEG - L); piv = SEG - L
                          zr = Tre_all[:, :, piv:piv + 1].to_broadcast([128, 8, L])
                          zi = Tim_all[:, :, piv:piv + 1].to_broadcast([128, 8, L])
                          cmul("dve", Tre_all[:, :, dst], Tim_all[:, :, dst], Tre_all[:, :, src], Tim_all[:, :, src], zr, zi,
                               A_[:, 0:8 * L].rearrange("p (g t) -> p g t", g=8), B_[:, 0:8 * L].rearrange("p (g t) -> p g t", g=8), ["stab"], ["stab"], [kA, kB])
                          L *= 2
                      P.op("dve", lambda e, gsl=gsl: e.tensor_copy(out=sHent[:], in_=sH[:, gsl, :]), r=["sH"], w=["sHent"])
                      segs = range(4) if d == 0 else range(3, -1, -1)
                      for seg in segs:
                          tsl = slice(seg * SEG, (seg + 1) * SEG)
                          Sre, Sim = pq[0], pq[1]
                          skr = [("pq", 0), ("pq", 1)]; ski = [("pq", 2), ("pq", 3)]
                          for ri, (St, sk) in enumerate(((Sre, skr), (Sim, ski))):
                              for q in range(4):
                                  for half, kk_ in enumerate((k0, k1)):
                                      P.op("pe", lambda e, St=St, q=q, half=half, kk_=kk_, ri=ri, tsl=tsl, g8b=g8b: e.matmul(St[:, q * SEG:(q + 1) * SEG], lhsT=W1pad[:, half * 8 + g8b + q, ri, :], rhs=hT[:, kk_, tsl], start=(half == 0), stop=(half == 1)),
                                           r=["W1pad", ("hT", kk_)], w=[sk[q // 2]])
                          S3r = Sre[:].rearrange("p (g t) -> p g t", g=4); S3i = Sim[:].rearrange("p (g t) -> p g t", g=4)
                          A3, B3, C3, D3 = [x[:].rearrange("p (g t) -> p g t", g=4) for x in (A_, B_, C_, D_)]
                          G3r, G3i = Gr[:].rearrange("p (g t) -> p g t", g=4), Gi[:].rearrange("p (g t) -> p g t", g=4)
                          vop("dve", A3, S3r, Tm_re, ALU.mult, skr + ["stab"], [kA])
                          vop("dve", B3, S3i, Tm_im, ALU.mult, ski + ["stab"], [kB])
                          vop("dve", A3, A3, B3, ALU.subtract, [kA, kB], [kA])
                          vop("dve", C3, S3i, Tm_re, ALU.mult, ski + ["stab"], [kC2])
                          vop("dve", D3, S3r, Tm_im, ALU.mult, skr + ["stab"], [kD])
                          vop("dve", C3, C3, D3, ALU.add, [kC2, kD], [kC2])
                          for (src_, dstt, ks, kd_) in ((A_, Gr, kA, kGr), (C_, Gi, kC2, kGi)):
                              P.op("dve", lambda e, src_=src_, dstt=dstt: e.tensor_tensor_scan(out=dstt[:], data0=m256[:], data1=src_[:], initial=0.0, op0=ALU.mult, op1=ALU.add), r=["m32", ks], w=[kd_])
                              if d == 1:
                                  s3 = src_[:].rearrange("p (g t) -> p g t", g=4); d3 = dstt[:].rearrange("p (g t) -> p g t", g=4)
                                  P.op("act", lambda e, dstt=dstt: e.copy(out=ctmp[:, 0:4], in_=dstt[:, SEG - 1:NT:SEG]), r=[kd_], w=["ctmp"])
                                  P.op("dve", lambda e, s3=s3, d3=d3: e.tensor_tensor(out=d3, in0=s3, in1=d3, op=ALU.subtract), r=[ks, kd_], w=[kd_])
                                  P.op("dve", lambda e, d3=d3: e.tensor_tensor(out=d3, in0=d3, in1=ctmp[:, 0:4].unsqueeze(2).to_broadcast([128, 4, SEG]), op=ALU.add), r=[kd_, "ctmp"], w=[kd_])
                          vop("dve", G3r, G3r, sHent[:, :, 0:1].to_broadcast([128, 4, SEG]), ALU.add, [kGr, "sHent"], [kGr])
                          vop("dve", G3i, G3i, sHent[:, :, 1:2].to_broadcast([128, 4, SEG]), ALU.add, [kGi, "sHent"], [kGi])
                          vop("dve", A3, G3r, Tp_re, ALU.mult, [kGr, "stab"], [kA])
                          vop("dve", B3, G3i, Tp_im, ALU.mult, [kGi, "stab"], [kB])
                          vop("dve", Hre_b, A3, B3, ALU.subtract, [kA, kB], [("qhat", 0)])
                          vop("dve", C3, G3r, Tp_im, ALU.mult, [kGr, "stab"], [kC2])
                          vop("dve", D3, G3i, Tp_re, ALU.mult, [kGi, "stab"], [kD])
                          vop("dve", Him_b, C3, D3, ALU.add, [kC2, kD], [("qhat", 1)])
                          xi = SEG - 1 if d == 0 else 0
                          vop("dve", sHx[:, :, 0:1], A3[:, :, xi:xi + 1], B3[:, :, xi:xi + 1], ALU.subtract, [kA, kB], ["sHx"])
                          vop("dve", sHx[:, :, 1:2], C3[:, :, xi:xi + 1], D3[:, :, xi:xi + 1], ALU.add, [kC2, kD], ["sHx"])
                          for half in range(2):
                              P.dma(O["ssm"][seg * 2 + d, half * 32 + 4 * s: half * 32 + 4 * s + 4].rearrange("g p r -> p g r"), sHx[half * 64:(half + 1) * 64, :, :], r=["sHx"], sem="ssmout")
                          P.op("dve", lambda e: e.tensor_scalar(out=sHent[:], in0=sHx[:], scalar1=keep[:, 0:1], scalar2=None, op0=ALU.mult), r=["sHx", "keep"], w=["sHent"])
                          if STAGE.get("ssm_dbg"):
                              P.dma(O["dbg2"][:, 0:1024], A_[:], r=[kA], sem="dbg")
                              P.dma(O["dbg2"][:, 1024:2048], B_[:], r=[kB], sem="dbg")
                              P.dma(O["dbg2"][:, 2048:3072], C_[:], r=[kC2], sem="dbg")
                              P.dma(O["dbg2"][:, 3072:4096], D_[:], r=[kD], sem="dbg")
                              P.dma(O["dbg2"][:, 4096:5120], Gr[:], r=[kGr], sem="dbg")
                              P.dma(O["dbg2"][:, 5120:6144], Gi[:], r=[kGi], sem="dbg")
                              P.dma(O["dbg3"], hT[:].rearrange("p k t -> p (k t)"), r=[("hT", kk) for kk in range(8)], sem="dbg")
                              P.dma(O["dbg"][:, 0:4096], vl, r=["stab"], sem="dbg")
                              P.dma(O["dbg"][:, 4096:5120], rs32, r=["srr"], sem="dbg")
                              P.dma(O["dbg"][:, 5120:5632], sq32, r=["sqq"], sem="dbg")
                              raise StopIteration
                          for half, kk_ in enumerate((k0, k1)):
                              yp_, ypk = bank(4 + half)
                              n = 0
                              for q in range(4):
                                  for ri, Hb in enumerate((Hre_b, Him_b)):
                                      P.op("pe", lambda e, yp_=yp_, q=q, half=half, ri=ri, Hb=Hb, n=n: e.matmul(yp_[:, 0:SEG], lhsT=Ewpad[:, half * 4 + q, ri, :], rhs=Hb[:, q, :], start=(n == 0), stop=(n == 7)),
                                           r=["Ewpad", ("qhat", ri)], w=[ypk])
                                      n += 1
                              first = (d == 0 and s % 2 == 0)
                              if first:
                                  P.op("dve", lambda e, yp_=yp_, kk_=kk_, tsl=tsl: e.scalar_tensor_tensor(out=ysb[:, kk_, tsl], in0=hT[:, kk_, tsl], scalar=sD[:, kk_:kk_ + 1], in1=yp_[:, 0:SEG], op0=ALU.mult, op1=ALU.add),
                                       r=[ypk, ("hT", kk_), "sD"], w=[("aT", kk_)])
                              else:
                                  P.op("dve", lambda e, yp_=yp_, kk_=kk_, tsl=tsl: e.tensor_tensor(out=ysb[:, kk_, tsl], in0=yp_[:, 0:SEG], in1=ysb[:, kk_, tsl], op=ALU.add),
                                       r=[ypk, ("aT", kk_)], w=[("aT", kk_)])

            try:
                _body()
            except StopIteration:
                pass
            P.alias(["vlat"], ["stab"])
            P.alias(["vctx"], ["W1pad"])
            P.alias(["kctxT"], ["Ewpad"])
            P.alias(["rstd"], ["srr"])
            P.alias([("sq", 0)], ["sqq"])
            P.alias([("sq", 1)], ["sCq"])
            P.op("dve", lambda e: e.memset(vlat[:], 0.0), w=["vlat"])
            P.op("dve", lambda e: e.memset(vlat[:, :, :, 0, 64:65], 1.0), w=["vlat"])
            P.op("dve", lambda e: e.memset(vlat[:, :, :, 1, 0:1], 1.0), w=["vlat"])
            for kk_ in range(8):
                yk = ysb[:, kk_, :]
                P.op("dve", lambda e, yk=yk: e.tensor_tensor(out=A_[:], in0=yk, in1=yk, op=ALU.mult), r=[("aT", kk_)], w=[kA])
                P.op("dve", lambda e: e.tensor_scalar(out=A_[:], in0=A_[:], scalar1=0.044715, scalar2=1.0, op0=ALU.mult, op1=ALU.add), r=[kA], w=[kA])
                P.op("pool", lambda e, yk=yk: e.tensor_tensor(out=A_[:], in0=A_[:], in1=yk, op=ALU.mult), r=[kA, ("aT", kk_)], w=[kA])
                P.op("act", lambda e: e.activation(out=A_[:], in_=A_[:], func=AF.Sigmoid, scale=1.5957691216057308), r=[kA], w=[kA])
                P.op("pool", lambda e, yk=yk: e.tensor_tensor(out=yk, in0=A_[:], in1=yk, op=ALU.mult), r=[kA, ("aT", kk_)], w=[("aT", kk_)])
            for dc in range(8):
                wb = wup[dc % 2]
                P.dma(wb[:, 0], I["wglu"][dc], w=[("wup", dc % 2, 0)], q="pool")
                P.dma(wb[:, 1], I["wglu"][8 + dc], w=[("wup", dc % 2, 1)], q="pool")
                for th in range(2):
                    vps, vpk = bank((dc * 2 + th) % 4)
                    gps, gpk = bank(4 + (dc * 2 + th) % 4)
                    for gv, (ps, pk) in enumerate(((vps, vpk), (gps, gpk))):
                        for kk_ in range(8):
                            P.op("pe", lambda e, ps=ps, kk_=kk_, th=th, wb=wb, gv=gv: e.matmul(ps, lhsT=wb[:, gv, kk_, :], rhs=ysb[:, kk_, th * 512:(th + 1) * 512], start=(kk_ == 0), stop=(kk_ == 7)),
                                 r=[("wup", dc % 2, gv), ("aT", kk_)], w=[pk])
                    sl = slice(th * 512, (th + 1) * 512)
                    P.op("act", lambda e, gps=gps, sl=sl: e.activation(out=B_[:, sl], in_=gps, func=AF.Sigmoid), r=[gpk], w=[kB])
                    P.op("dve", lambda e, vps=vps, sl=sl: e.tensor_tensor(out=B_[:, sl], in0=vps, in1=B_[:, sl], op=ALU.mult), r=[vpk, kB], w=[kB])
                    P.op("dve", lambda e, dc=dc, sl=sl: e.scalar_tensor_tensor(out=xT[:, dc, sl], in0=B_[:, sl], scalar=mod[:, 16 + dc:17 + dc], in1=xT[:, dc, sl], op0=ALU.mult, op1=ALU.add),
                         r=[kB, "mod", ("xT", dc)], w=[("xT", dc)])

        eps_t = P.sb("eps_t", [128, 1])
        P.op("dve", lambda e: e.memset(eps_t[:], EPS), w=["eps"])
        halfpi = P.sb("halfpi", [128, 1])
        P.op("dve", lambda e: e.memset(halfpi[:], 1.5707963267948966), w=["halfpi"])


        for l in range(STAGE["layers"]):
            ada_layer(l)
            norm_mod(0)
            if STAGE["mixers"]:
                if l % 3 == 0:
                    attention(l)
                elif l % 3 == 1 and STAGE.get("hgrn", True):
                    hgrn(l)
                elif l % 3 == 2 and STAGE.get("ssm", True):
                    ssm(l)
            norm_mod(1)
            ffn(l)

        rms_stats()
        for k in range(8):
            P.op("dve", lambda e, k=k: e.scalar_tensor_tensor(out=xT[:, k, :], in0=xT[:, k, :], scalar=fg[:, k:k + 1], in1=rstd[:], op0=ALU.mult, op1=ALU.mult),
                 r=[("xT", k), "fg", "rstd"], w=[("xT", k)])
        for k in range(8):
            P.dma(O["y"][k * 128:(k + 1) * 128, :], xT[:, k, :], r=[("xT", k)], sem="yout")
        P.op("pool", lambda e: e.memset(rstd[:], 0.0), r=["rstd"], w=["rstd"])
        if not (STAGE["mixers"] and STAGE.get("hgrn", True) and STAGE["layers"] > 1):
            for i in range(8):
                P.dma(O["hg"][i * 8:(i + 1) * 8].rearrange("a p n -> p a n"), rstd[:].rearrange("p (a n) -> p a n", a=8), r=["rstd"], sem="sout")
        if not (STAGE["mixers"] and STAGE.get("ssm", True) and STAGE["layers"] > 2):
            for a_ in range(8):
                P.dma(O["ssm"][a_].rearrange("g p r -> g (p r)"), rstd[0:64, 0:128], r=["rstd"], sem="sout")
        P.wait_all_dma()
        P.emit()
    return nc

def _c(a):
    return np.ascontiguousarray(a, dtype=np.float32)


def prep_inputs(inp):
    g = {k: np.asarray(v) for k, v in inp.items()}
    sh = {}
    sh["ident"] = np.eye(128, dtype=np.float32)
    sh["ada_w"] = _c(g["ada_w"].reshape(DEPTH, 8, 128, 12, 512).transpose(0, 3, 2, 1, 4))
    sh["ada_b"] = _c(g["ada_b"].reshape(DEPTH, 48, 128).transpose(0, 2, 1))
    sh["ng"] = _c(np.stack([g["norm1_g"], g["norm2_g"]]).reshape(2, DEPTH, 8, 128).transpose(3, 0, 1, 2))
    sh["final_g"] = _c(g["final_g"].reshape(8, 128).T)
    sh["ffn_w_up"] = _c(g["ffn_w_up"].reshape(DEPTH, 8, 128, 2, NFC, 128).transpose(0, 4, 2, 3, 1, 5))
    sh["ffn_w_down"] = _c(g["ffn_w_down"].reshape(DEPTH, NFC, 128, 8, 128).transpose(0, 3, 2, 1, 4))
    sh["ffn_conv_w"] = _c(g["ffn_conv_w"].reshape(DEPTH, 3, 2 * NFC, 128).transpose(0, 3, 1, 2))
    sh["ffn_conv_b"] = _c(g["ffn_conv_b"].reshape(DEPTH, 2 * NFC, 128).transpose(0, 2, 1))
    wqkv = g["attn_wqkv"]
    sh["wq"] = _c(wqkv[:, :, 0:1024].reshape(2, 8, 128, 8, 128).transpose(0, 3, 2, 1, 4))
    wk = wqkv[:, :, 1024:1280].reshape(2, 8, 128, 4, 1, 64)
    sh["wk"] = _c(np.broadcast_to(wk, (2, 8, 128, 4, 2, 64)).reshape(2, 8, 128, 4, 128).transpose(0, 3, 2, 1, 4))
    sh["wkv"] = _c(wqkv[:, :, 1024:1536].reshape(2, 8, 128, 512).transpose(0, 2, 1, 3))
    sh["wo"] = _c(g["attn_wo"].reshape(2, 8, 128, 8, 128).transpose(0, 3, 2, 1, 4))
    sh["sink"] = _c(np.broadcast_to(g["attn_sink"][:, None, :], (2, 128, 16)))
    hw = g["hgrn_w_in"][0]
    sh["hw_in"] = _c(hw.reshape(8, 128, 5, 8, 128).transpose(3, 2, 1, 0, 4))
    sh["hwo"] = _c(g["hgrn_wo"][0].reshape(8, 128, 8, 128).transpose(2, 1, 0, 3))
    sh["hlb"] = _c(g["hgrn_lb"].reshape(4, 2, 8, 128).transpose(3, 0, 1, 2))
    sh["hgn"] = _c(g["hgrn_g_norm"][0].reshape(128, 1))
    ii = np.arange(128)
    same = (ii[:, None] // 32) == (ii[None, :] // 32)
    sh["hmask"] = _c(np.stack([same & (ii[:, None] <= ii[None, :]), same & (ii[:, None] >= ii[None, :])], axis=1))
    are, aim, ldt = g["ssm_a_re"][0], g["ssm_a_im"][0], g["ssm_log_dt"][0]
    bre, bim, cre, cim = g["ssm_b_re"][0], g["ssm_b_im"][0], g["ssm_c_re"][0], g["ssm_c_im"][0]
    sh["sBreR"] = _c(bre.reshape(2, 8, 8, 64, 16).transpose(0, 1, 2, 4, 3).reshape(2, 8, 128, 64))
    sh["sBimR"] = _c(bim.reshape(2, 8, 8, 64, 16).transpose(0, 1, 2, 4, 3).reshape(2, 8, 128, 64))
    sh["sAreR"] = _c(np.broadcast_to(are.reshape(2, 8, 8, 1, 64), (2, 8, 8, 16, 64)).reshape(2, 8, 128, 64))
    sh["sAimR"] = _c(np.broadcast_to(aim.reshape(2, 8, 8, 1, 64), (2, 8, 8, 16, 64)).reshape(2, 8, 128, 64))
    sh["sDtR"] = _c(np.broadcast_to(ldt.reshape(2, 8, 8, 1, 1), (2, 8, 8, 16, 1)).reshape(2, 8, 128, 1))
    sh["sAreQ"] = _c(are.reshape(2, 2, 32, 64).transpose(0, 1, 3, 2).reshape(2, 128, 32))
    sh["sAimQ"] = _c(aim.reshape(2, 2, 32, 64).transpose(0, 1, 3, 2).reshape(2, 128, 32))
    sh["sDtQ"] = _c(np.broadcast_to(ldt.reshape(2, 2, 1, 32), (2, 2, 64, 32)).reshape(2, 128, 32))
    sh["sCreQ"] = _c(cre.reshape(2, 2, 32, 16, 64).transpose(0, 1, 4, 2, 3).reshape(2, 128, 32, 16))
    sh["sCimQ"] = _c(cim.reshape(2, 2, 32, 16, 64).transpose(0, 1, 4, 2, 3).reshape(2, 128, 32, 16))
    sh["sD"] = _c(g["ssm_d"][0].reshape(8, 128).T)
    sm = np.zeros((128, 12), np.float32)
    for q in range(8):
        sm[q * 16:(q + 1) * 16, q] = 1.0
    sm[0:64, 8] = 1.0; sm[64:128, 9] = 1.0; sm[0:64, 10] = -1.0; sm[64:128, 11] = -1.0
    sh["smask"] = sm
    sh["wglu"] = _c(g["ssm_w_glu"][0].reshape(8, 128, 16, 128).transpose(2, 1, 0, 3))
    tt = np.arange(NT)
    inv = 1.0 / (10000.0 ** (np.arange(0, 32, 2, dtype=np.float32) / np.float32(32)))
    ar = (tt // 64).astype(np.float32)[:, None] * inv.astype(np.float32)
    ac = (tt % 64).astype(np.float32)[:, None] * inv.astype(np.float32)
    ang = np.concatenate([ar, ar, ac, ac], axis=-1).astype(np.float32)
    cosS = _c(np.concatenate([np.cos(ang).T] * 2, axis=0)); sinS = _c(np.concatenate([np.sin(ang).T] * 2, axis=0))
    rm = np.zeros((128, 128), np.float32)
    for m in range(128):
        d = m % 64
        if (d % 32) < 16:
            rm[m + 16, m] = -1.0
        else:
            rm[m - 16, m] = 1.0
    sh["rmat"] = rm
    kk = np.arange(128)[:, None]; qq = np.arange(128)[None, :]
    mS = np.zeros((128, 8, 2, 128), np.float32); mP = np.zeros((128, 8, 2, 128), np.float32)
    for i in range(8):
        if i >= 1:
            mS[:, i, 0, :] = (kk >= qq)
        if i <= 6:
            mS[:, i, 1, :] = (kk <= qq)
        mP[:, i, 0, :] = 1.0 if i % 2 == 1 else 0.0
        mP[:, i, 1, :] = 1.0 if i % 2 == 0 else 0.0
    if STAGE.get("kv_only"):
        for kk_ in ("wq", "wk", "wo", "sink", "rmat"):
            sh.pop(kk_, None)
    maps = []
    for c in range(8):
        m = dict(sh)
        if c < 4:
            xc = g["x_sample"][c]
            cond = g["c"][c]
            kp = 1.0
        else:
            xc = g["x_prompt"][4 * (c - 4):4 * (c - 4) + 4].reshape(NT, D)
            cond = g["c_ctx"]
            kp = 0.0
        if c < 4:
            ropec = cosS; ropes = sinS; m["amask"] = mS
            ck = g["cache_k"][c]
            m["ckd"] = _c(np.broadcast_to(ck[:, :, :, None, :], (2, 512, 4, 2, 64)).reshape(2, 512, 512))
            cv = g["cache_v"][c]
            vp = np.zeros((2, 512, 4, 2, 128), np.float32)
            vp[:, :, :, 0, 0:64] = cv; vp[:, :, :, 0, 64] = 1.0
            vp[:, :, :, 1, 64:128] = cv; vp[:, :, :, 1, 0] = 1.0
            m["cvp"] = vp.reshape(2, 512, 1024)
        else:
            ropec = np.ones((128, NT), np.float32); ropes = np.zeros((128, NT), np.float32); m["amask"] = mP
            m["ckd"] = np.zeros((2, 512, 512), np.float32); m["cvp"] = np.zeros((2, 512, 1024), np.float32)
        if STAGE.get("kv_only"):
            for kk_ in ("amask", "ckd", "cvp"):
                m.pop(kk_, None)
        if c < 4:
            st = g["state_ssm"][c, 0]
            m["sH0"] = _c(st.reshape(2, 2, 32, 64, 2).transpose(0, 1, 3, 2, 4).reshape(2, 128, 32, 2))
        else:
            m["sH0"] = np.zeros((2, 128, 32, 2), np.float32)
        m["hs0"] = _c(g["state_hgrn"][c, 0]) if c < 4 else np.zeros((2, 8, 128, 128), np.float32)
        m["x"] = _c(np.concatenate([xc.T, ropec, ropes], axis=0))
        m["cond"] = _c(cond.reshape(8, 128).T)
        m["keep"] = _c(np.stack([np.full(128, kp), np.full(128, kp - 1.0)], axis=1))
        maps.append(m)
    return maps


def assemble(results):
    f = lambda a: np.asarray(a, dtype=np.float32)
    ys = np.stack([f(results[c]["y"]).T for c in range(4)])
    yp = np.concatenate([f(results[c]["y"]).T.reshape(4, 256, D) for c in range(4, 8)])
    nk = np.concatenate([f(results[c]["nk"]).reshape(2, 4, 256, 4, 64).transpose(1, 0, 2, 3, 4) for c in range(4, 8)])
    nv = np.concatenate([f(results[c]["nv"]).reshape(2, 4, 256, 4, 64).transpose(1, 0, 2, 3, 4) for c in range(4, 8)])
    hg = np.concatenate([f(results[c]["hg"]).reshape(4, 1, 2, 8, 128, 128) for c in range(4, 8)])
    ssm = np.concatenate([f(results[c]["ssm"]).reshape(4, 1, 2, 64, 64, 2) for c in range(4, 8)])
    return (np.ascontiguousarray(yp), np.ascontiguousarray(ys), np.ascontiguousarray(nk), np.ascontiguousarray(nv),
            np.ascontiguousarray(hg), np.ascontiguousarray(ssm))


def kernel(**inputs):
    nc = build_program()
    maps = prep_inputs(inputs)
    res = run_bass_kernel_spmd(nc, maps, core_ids=list(range(8)))
    return assemble(res.results)
```

```python
import numpy as np
from concourse.bass_utils import run_bass_kernel_spmd

from contextlib import ExitStack
import concourse.bass as bass
import concourse.mybir as mybir

F32 = mybir.dt.float32
F32R = mybir.dt.float32r
BF16 = mybir.dt.bfloat16
AF = mybir.ActivationFunctionType
ALU = mybir.AluOpType
AX = mybir.AxisListType


class Prog:
    ENGS = ("pe", "act", "dve", "pool", "sp")

    def __init__(self, nc, es: ExitStack):
        self.nc = nc
        self.es = es
        self.recs = {e: [] for e in self.ENGS}
        self.cnt = {e: 0 for e in self.ENGS}
        self.known = {e: {} for e in self.ENGS}
        self.state = {}
        self.sems = {}
        self.dcnt = {}
        for e in self.ENGS:
            self.sems[("e", e)] = es.enter_context(nc.semaphore("sem_" + e))
        self.psn = 0

    def sb(self, name, shape, dt=F32):
        return self.es.enter_context(self.nc.sbuf_tensor("sb_" + name, list(shape), dt))

    def ps(self, name, shape, dt=F32):
        return self.es.enter_context(self.nc.psum_tensor(name, list(shape), dt))

    def dsem(self, name):
        k = ("d", name)
        if k not in self.sems:
            self.sems[k] = self.es.enter_context(self.nc.semaphore("dsem_" + name))
            self.dcnt[k] = 0
        return k

    def _st(self, k):
        s = self.state.get(k)
        if s is None:
            s = {"w": {}, "r": {}}
            self.state[k] = s
        return s

    def _deps(self, eng, r, w):
        deps = {}
        def add(d):
            if d is None:
                return
            sk, v = d
            if deps.get(sk, 0) < v:
                deps[sk] = v
        for k in r:
            st = self._st(k)
            for sk, v in st["w"].items():
                add((sk, v))
            if isinstance(k, tuple) and k[0] == "pq":
                for sk, v in st["r"].items():
                    if sk != ("e", eng):
                        add((sk, v))
        for k in w:
            s = self._st(k)
            for sk, v in s["w"].items():
                add((sk, v))
            for sk, v in s["r"].items():
                add((sk, v))
        waits = []
        kn = self.known[eng]
        for sk, v in deps.items():
            if eng == "pe" and sk == ("e", "pe"):
                continue
            if kn.get(sk, 0) < v:
                waits.append((sk, v))
                kn[sk] = v
        return waits

    def _commit(self, comp, r, w):
        sk, v = comp
        for k in w:
            self.state[k] = {"w": {sk: v}, "r": {}}
        for k in r:
            s = self._st(k)
            if s["r"].get(sk, 0) < v:
                s["r"][sk] = v

    def alias(self, dst, src):
        mw, mr = {}, {}
        for k in src:
            st = self._st(k)
            for sk, v in st["w"].items():
                mw[sk] = max(mw.get(sk, 0), v)
            for sk, v in st["r"].items():
                mr[sk] = max(mr.get(sk, 0), v)
        for k in dst:
            self.state[k] = {"w": dict(mw), "r": dict(mr)}

    def op(self, eng, fn, r=(), w=(), inc=True):
        inc = True
        waits = self._deps(eng, r, w)
        sk = ("e", eng)
        comp = (sk, self.cnt[eng] + 1)
        if inc:
            self.cnt[eng] += 1
        self.recs[eng].append((waits, fn, (sk, 1) if inc else None))
        self._commit(comp, r, w)

    def dma(self, out, in_, r=(), w=(), sem=None, q="sp", **kw):
        if sem is None:
            sem = "_".join(str(x) for x in (w[0] if isinstance(w[0], tuple) else (w[0],)))
        waits = self._deps(q, r, w)
        sk = self.dsem(sem)
        self.dcnt[sk] += 16
        comp = (sk, self.dcnt[sk])
        self.recs[q].append((waits, (lambda e, o=out, i=in_, kw=kw: e.dma_start(out=o, in_=i, **kw)), (sk, 16)))
        self._commit(comp, r, w)

    def wait_all_dma(self, q="sp"):
        waits = []
        for sk, v in self.dcnt.items():
            if v > 0 and self.known[q].get(sk, 0) < v:
                waits.append((sk, v))
                self.known[q][sk] = v
        self.recs[q].append((waits, None, None))

    def barrier_all(self):
        tgt = {("e", e): self.cnt[e] for e in self.ENGS if self.cnt[e] > 0}
        for sk, v in self.dcnt.items():
            if v > 0:
                tgt[sk] = v
        for e in self.ENGS:
            waits = []
            for sk, v in tgt.items():
                if sk == ("e", e):
                    continue
                if self.known[e].get(sk, 0) < v:
                    waits.append((sk, v))
                    self.known[e][sk] = v
            if waits:
                self.recs[e].append((waits, None, None))

    def emit(self):
        nc = self.nc
        sems = self.sems
        recs = self.recs

        def replay(name):
            def f(e):
                for waits, fn, inc in recs[name]:
                    for sk, v in waits:
                        e.wait_ge(sems[sk], v)
                    if fn is not None:
                        ins = fn(e)
                        if inc is not None:
                            ins.then_inc(sems[inc[0]], inc[1])
            return f

        with nc.Block() as block:
            block.tensor(replay("pe"))
            block.scalar(replay("act"))
            block.vector(replay("dve"))
            block.gpsimd(replay("pool"))
            block.sync(replay("sp"))

    def stats(self):
        return {e: len(self.recs[e]) for e in self.ENGS}

D = 1024
NT = 1024
DFF = 2816
NFC = 22
DEPTH = 4
EPS = 1e-6

STAGE = {"mixers": True, "layers": 4, "kv_only": False}


def build_program():
    nc = bass.Bass("TRN2", target_bir_lowering=False)

    def din(name, shape, dt=F32):
        return nc.dram_tensor(name, list(shape), dt, kind="ExternalInput").ap()

    def dout(name, shape, dt=F32):
        return nc.dram_tensor(name, list(shape), dt, kind="ExternalOutput").ap()

    I = {}
    I["x"] = din("x", [D + 256, NT])
    I["cond"] = din("cond", [128, 8])
    I["keep"] = din("keep", [128, 2])
    I["ident"] = din("ident", [128, 128])
    I["ada_w"] = din("ada_w", [DEPTH, 12, 128, 8, 512])
    I["ada_b"] = din("ada_b", [DEPTH, 128, 48])
    I["ng"] = din("ng", [128, 2, DEPTH, 8])
    I["ffn_w_up"] = din("ffn_w_up", [DEPTH, NFC, 128, 2, 8, 128])
    I["ffn_conv_w"] = din("ffn_conv_w", [DEPTH, 128, 3, 2 * NFC])
    I["ffn_conv_b"] = din("ffn_conv_b", [DEPTH, 128, 2 * NFC])
    I["ffn_w_down"] = din("ffn_w_down", [DEPTH, 8, 128, NFC, 128])
    I["final_g"] = din("final_g", [128, 8])
    I["wkv"] = din("wkv", [2, 128, 8, 512])
    I["hw_in"] = din("hw_in", [8, 5, 128, 8, 128])
    I["hwo"] = din("hwo", [8, 128, 8, 128])
    I["hlb"] = din("hlb", [128, 4, 2, 8])
    I["hgn"] = din("hgn", [128, 1])
    I["hs0"] = din("hs0", [2, 8, 128, 128])
    I["hmask"] = din("hmask", [128, 2, 128])
    I["sBreR"] = din("sBreR", [2, 8, 128, 64]); I["sBimR"] = din("sBimR", [2, 8, 128, 64])
    I["sAreR"] = din("sAreR", [2, 8, 128, 64]); I["sAimR"] = din("sAimR", [2, 8, 128, 64])
    I["sDtR"] = din("sDtR", [2, 8, 128, 1])
    I["sAreQ"] = din("sAreQ", [2, 128, 32]); I["sAimQ"] = din("sAimQ", [2, 128, 32]); I["sDtQ"] = din("sDtQ", [2, 128, 32])
    I["sCreQ"] = din("sCreQ", [2, 128, 32, 16]); I["sCimQ"] = din("sCimQ", [2, 128, 32, 16])
    I["sH0"] = din("sH0", [2, 128, 32, 2])
    I["sD"] = din("sD", [128, 8])
    I["smask"] = din("smask", [128, 12])
    I["wglu"] = din("wglu", [16, 128, 8, 128])
    if not STAGE.get("kv_only"):
        I["wq"] = din("wq", [2, 8, 128, 8, 128])
        I["wk"] = din("wk", [2, 4, 128, 8, 128])
        I["wo"] = din("wo", [2, 8, 128, 8, 128])
        I["sink"] = din("sink", [2, 128, 16])
        I["rmat"] = din("rmat", [128, 128])
        I["amask"] = din("amask", [128, 8, 2, 128])
        I["ckd"] = din("ckd", [2, 512, 512])
        I["cvp"] = din("cvp", [2, 512, 1024])
    O = {}
    O["y"] = dout("y", [D, NT])
    O["nk"] = dout("nk", [2, NT, 256])
    O["nv"] = dout("nv", [2, NT, 256])
    O["hg"] = dout("hg", [64, 128, 128])
    O["ssm"] = dout("ssm", [8, 64, 64, 2])
    if STAGE.get("ssm_dbg"):
        O["dbg"] = dout("dbg", [128, 4096 + 1024 + 512])
        O["dbg2"] = dout("dbg2", [128, 6 * 1024])
        O["dbg3"] = dout("dbg3", [128, 8 * 1024], BF16)

    with ExitStack() as es:
        P = Prog(nc, es)
        xT = P.sb("xT", [128, 10, NT])
        hT = P.sb("hT", [128, 8, NT], BF16)
        aT = P.sb("aT", [128, 12, NT], BF16)
        rstd = P.sb("rstd", [128, NT])
        tmpA = P.sb("tmpA", [128, NT])
        tmpB = P.sb("tmpB", [128, NT])
        cg = [P.sb("cg%d" % i, [128, NT]) for i in range(2)]
        cv = [P.sb("cv%d" % i, [128, NT]) for i in range(2)]
        sq = [P.sb("sq%d" % i, [128, NT], BF16) for i in range(2)]
        ident = P.sb("ident", [128, 128])
        ones_bf = P.sb("ones_bf", [128, 128], BF16)
        one_f = P.sb("one_f", [128, 1])
        keep = P.sb("keep", [128, 2])
        cond = P.sb("cond", [128, 8])
        s_bf = P.sb("s_bf", [128, 8], BF16)
        mod = P.sb("mod", [128, 48])
        adab = P.sb("adab", [128, 48])
        ng = P.sb("ng", [128, 2, DEPTH, 8])
        fg = P.sb("fg", [128, 8])
        AB = P.sb("AB", [128, 4, 8])
        cw = P.sb("cw", [128, 3, 2 * NFC])
        cb = P.sb("cb", [128, 2 * NFC])
        cwk = P.sb("cwk", [128, 2, 2 * NFC])
        wada = [P.sb("wada%d" % i, [128, 8, 512], BF16) for i in range(2)]
        wup = [P.sb("wup%d" % i, [128, 2, 8, 128], BF16) for i in range(2)]
        wdn = [P.sb("wdn%d" % i, [128, 11, 128], BF16) for i in range(2)]
        vlat = P.sb("vlat", [128, 8, 4, 2, 128], BF16)
        vctx = P.sb("vctx", [128, 4, 4, 2, 128], BF16)
        kctxT = P.sb("kctxT", [128, 4, 512], BF16)
        ckd = P.sb("ckd", [128, 4, 512], BF16)
        amask = P.sb("amask", [128, 8, 2, 128], BF16)
        rmat = P.sb("rmat", [128, 128], BF16)
        ident_bf = P.sb("ident_bf", [128, 128], BF16)
        qb = [P.sb("qb%d" % i, [128, 512], BF16) for i in range(2)]
        esink = P.sb("esink", [128, 16])
        ones_f = P.sb("ones_f", [128, 128])
        m32 = P.sb("m32", [128, NT])
        hlb = P.sb("hlb", [128, 4, 2, 8])
        lbp = P.sb("lbp", [128, 2, 2, 8])
        hgn = P.sb("hgn", [128, 1])
        hmask = P.sb("hmask", [128, 2, 128], BF16)
        Sf = [P.sb("Sf%d" % i, [128, 128]) for i in range(2)]
        Sb = [P.sb("Sb%d" % i, [128, 128], BF16) for i in range(2)]
        adec = [P.sb("adec%d" % i, [128, 32]) for i in range(2)]
        rowm = P.sb("rowm", [128, 4])
        ctmp = P.sb("ctmp", [128, 32])
        vtok = P.sb("vtok", [128, 8, 128], BF16)
        qhat = [P.sb("qhat%d" % i, [128, NT], BF16) for i in range(2)]
        ktil = [P.sb("ktil%d" % i, [128, NT], BF16) for i in range(2)]
        kdT = [P.sb("kdT%d" % i, [128, NT], BF16) for i in range(2)]
        attm = [P.sb("attm%d" % i, [128, 128], BF16) for i in range(2)]
        smask = P.sb("smask", [128, 12])
        sD = P.sb("sD", [128, 8])
        sH = P.sb("sH", [128, 32, 2])
        sHent = P.sb("sHent", [128, 4, 2])
        sHx = P.sb("sHx", [128, 4, 2])
        pq = [P.ps("pq%d" % i, [128, 1024]) for i in range(4)]

        def bank(i):
            return pq[i // 2][:, (i % 2) * 512:(i % 2) * 512 + 512], ("pq", i)

        P.dma(ident[:], I["ident"], w=["ident"])
        P.dma(keep[:], I["keep"], w=["keep"])
        P.dma(cond[:], I["cond"], w=["cond"])
        P.dma(ng[:], I["ng"], w=["ng"])
        P.dma(fg[:], I["final_g"], w=["fg"])
        P.op("dve", lambda e: e.memset(ones_bf[:], 1.0), w=["ones_bf"])
        P.op("dve", lambda e: e.memset(one_f[:], 1.0), w=["one_f"])
        P.op("dve", lambda e: e.memset(ones_f[:], 1.0), w=["ones_f"])
        P.op("dve", lambda e: e.tensor_copy(out=ident_bf[:], in_=ident[:]), r=["ident"], w=["ident_bf"])
        if not STAGE.get("kv_only"):
            P.dma(rmat[:], I["rmat"], w=["rmat"], q="pool")
            P.dma(amask[:], I["amask"], w=["amask"], q="pool")
        P.op("pool", lambda e: e.memset(m32[:], 1.0), w=["m32"])
        P.op("pool", lambda e: e.memset(m32[:, 0:NT:32], 0.0), w=["m32"])
        P.op("pool", lambda e: e.memset(rowm[:], 0.0), w=["rowm"])
        for c4 in range(3):
            P.op("pool", lambda e, c4=c4: e.memset(rowm[c4 * 32:(c4 + 1) * 32, c4:c4 + 1], 1.0), w=["rowm"])
        P.op("pool", lambda e: e.memset(rowm[96:128, 3:4], 1.0), w=["rowm"])
        P.dma(hlb[:], I["hlb"], w=["hlb"])
        P.dma(hgn[:], I["hgn"], w=["hgn"])
        P.dma(hmask[:], I["hmask"], w=["hmask"], q="pool")
        P.op("dve", lambda e: e.memset(vlat[:], 0.0), w=["vlat"])
        P.op("dve", lambda e: e.memset(vlat[:, :, :, 0, 64:65], 1.0), w=["vlat"])
        P.op("dve", lambda e: e.memset(vlat[:, :, :, 1, 0:1], 1.0), w=["vlat"])
        P.op("act", lambda e: e.activation(out=s_bf[:], in_=cond[:], func=AF.Silu), r=["cond"], w=["s_bf"])

        P.dma(xT[:], I["x"].rearrange("(k p) t -> p k t", p=128), w=[("xT", k) for k in range(10)], sem="xin")

        def ada_layer(l):
            P.dma(adab[:], I["ada_b"][l], w=["adab"])
            ps, pk = bank(4)
            for n in range(12):
                wb = wada[n % 2]
                P.dma(wb[:], I["ada_w"][l, n],
                      w=[("wada", n % 2)], q="pool")
                for c4 in range(4):
                    c = n * 4 + c4
                    for k in range(8):
                        P.op("pe", lambda e, ps=ps, k=k, wb=wb, c=c, c4=c4: e.matmul(ps[:, c:c + 1], lhsT=wb[:, k, c4 * 128:(c4 + 1) * 128], rhs=s_bf[:, k:k + 1], start=(k == 0), stop=(k == 7)),
                             r=[("wada", n % 2), "s_bf"], w=[pk])
            P.op("dve", lambda e, ps=ps: e.tensor_tensor(out=mod[:], in0=ps[:, 0:48], in1=adab[:], op=ALU.add), r=[pk, "adab"], w=["mod"])
            for j in range(2):
                P.op("dve", lambda e, j=j: e.scalar_tensor_tensor(out=AB[:, 2 * j, :], in0=mod[:, (3 * j + 1) * 8:(3 * j + 2) * 8], scalar=1.0,
                                                                 in1=ng[:, j, l, :], op0=ALU.add, op1=ALU.mult),
                     r=["mod", "ng"], w=["AB"])
                P.op("dve", lambda e, j=j: e.tensor_copy(out=AB[:, 2 * j + 1, :], in_=mod[:, (3 * j) * 8:(3 * j + 1) * 8]), r=["mod"], w=["AB"])

        def rms_stats():
            b0, k0 = bank(6)
            b1, k1 = bank(7)
            for k in range(8):
                s = sq[k % 2]
                P.op("act", lambda e, s=s, k=k: e.activation(out=s[:], in_=xT[:, k, :], func=AF.Square), r=[("xT", k)], w=[("sq", k % 2)])
                for th, (b, bk) in enumerate(((b0, k0), (b1, k1))):
                    P.op("pe", lambda e, b=b, s=s, th=th, k=k: e.matmul(b, lhsT=ones_bf[:], rhs=s[:, th * 512:(th + 1) * 512], start=(k == 0), stop=(k == 7)),
                         r=[("sq", k % 2), "ones_bf"], w=[bk], inc=True)
            for th, (b, bk) in enumerate(((b0, k0), (b1, k1))):
                P.op("act", lambda e, b=b, th=th: e.activation(out=tmpA[:, th * 512:(th + 1) * 512], in_=b, func=AF.Ln, scale=1.0 / D, bias=eps_t[:, 0:1]),
                     r=[bk, "eps"], w=["tmpA"])
            P.op("act", lambda e: e.activation(out=rstd[:], in_=tmpA[:], func=AF.Exp, scale=-0.5), r=["tmpA"], w=["rstd"])

        def norm_mod(j):
            rms_stats()
            for k in range(8):
                t = tmpA if k % 2 == 0 else tmpB
                tk = "tmpA" if k % 2 == 0 else "tmpB"
                P.op("dve", lambda e, t=t, k=k: e.scalar_tensor_tensor(out=t[:], in0=xT[:, k, :], scalar=AB[:, 2 * j, k:k + 1], in1=rstd[:], op0=ALU.mult, op1=ALU.mult),
                     r=[("xT", k), "AB", "rstd"], w=[tk])
                P.op("act", lambda e, t=t, k=k: e.activation(out=hT[:, k, :], in_=t[:], func=AF.Identity, bias=AB[:, 2 * j + 1, k:k + 1], scale=1.0),
                     r=[tk, "AB"], w=[("hT", k)])

        def ffn(l):
            P.dma(cw[:], I["ffn_conv_w"][l], w=["cw"])
            P.dma(cb[:], I["ffn_conv_b"][l], w=["cb"])
            for jj, j in enumerate((0, 2)):
                P.op("dve", lambda e, jj=jj, j=j: e.tensor_scalar(out=cwk[:, jj, :], in0=cw[:, j, :], scalar1=keep[:, 1:2], scalar2=None, op0=ALU.mult),
                     r=["cw", "keep"], w=["cwk"])
            for grp in range(2):
                for fc in range(grp * 11, grp * 11 + 11):
                    wb = wup[fc % 2]
                    P.dma(wb[:], I["ffn_w_up"][l, fc], w=[("wup", fc % 2, 0), ("wup", fc % 2, 1)], q="pool")
                    outs = []
                    for gv in range(2):
                        pt = pq[(fc % 2) * 2 + gv]
                        pks = [("pq", ((fc % 2) * 2 + gv) * 2 + th) for th in range(2)]
                        for th in range(2):
                            for k in range(8):
                                P.op("pe", lambda e, pt=pt, th=th, k=k, gv=gv, wb=wb: e.matmul(pt[:, th * 512:(th + 1) * 512], lhsT=wb[:, gv, k, :], rhs=hT[:, k, th * 512:(th + 1) * 512],
                                                                                    start=(k == 0), stop=(k == 7)),
                                     r=[("wup", fc % 2, gv), ("hT", k)], w=[pks[th]], inc=(k == 7))
                        c = (cg if gv == 0 else cv)[fc % 2]
                        ck = ("cg" if gv == 0 else "cv", fc % 2)
                        col = gv * NFC + fc
                        P.op("act", lambda e, c=c, pt=pt, col=col: e.activation(out=c[:], in_=pt[:], func=AF.Identity, scale=cw[:, 1, col:col + 1], bias=cb[:, col:col + 1]),
                             r=pks + ["cw", "cb"], w=[ck])
                        P.op("dve", lambda e, c=c, pt=pt, col=col: e.scalar_tensor_tensor(out=c[:, 1:NT], in0=pt[:, 0:NT - 1], scalar=cw[:, 0, col:col + 1], in1=c[:, 1:NT], op0=ALU.mult, op1=ALU.add),
                             r=pks + ["cw", ck], w=[ck])
                        P.op("dve", lambda e, c=c, pt=pt, col=col: e.scalar_tensor_tensor(out=c[:, 0:NT - 1], in0=pt[:, 1:NT], scalar=cw[:, 2, col:col + 1], in1=c[:, 0:NT - 1], op0=ALU.mult, op1=ALU.add),
                             r=pks + ["cw", ck], w=[ck])
                        P.op("dve", lambda e, c=c, pt=pt, col=col: e.scalar_tensor_tensor(out=c[:, 256:NT:256], in0=pt[:, 255:NT - 1:256], scalar=cwk[:, 0, col:col + 1], in1=c[:, 256:NT:256], op0=ALU.mult, op1=ALU.add),
                             r=pks + ["cwk", ck], w=[ck])
                        P.op("dve", lambda e, c=c, pt=pt, col=col: e.scalar_tensor_tensor(out=c[:, 255:NT - 1:256], in0=pt[:, 256:NT:256], scalar=cwk[:, 1, col:col + 1], in1=c[:, 255:NT - 1:256], op0=ALU.mult, op1=ALU.add),
                             r=pks + ["cwk", ck], w=[ck])
                        outs.append((c, ck))
                    (cgt, cgk), (cvt, cvk) = outs
                    P.op("act", lambda e, cgt=cgt: e.activation(out=cgt[:], in_=cgt[:], func=AF.Silu), r=[cgk], w=[cgk])
                    P.op("dve", lambda e, cgt=cgt, cvt=cvt, fc=fc: e.tensor_tensor(out=aT[:, fc % 11, :], in0=cgt[:], in1=cvt[:], op=ALU.mult), r=[cgk, cvk], w=[("aT", fc % 11)])
                for dc in range(8):
                    wb = wdn[dc % 2]
                    P.dma(wb[:], I["ffn_w_down"][l, dc][:, grp * 11:grp * 11 + 11, :],
                          w=[("wdn", dc % 2)], q="pool")
                    for th in range(2):
                        ps, pk = bank((dc * 2 + th) % 8)
                        for fc in range(11):
                            P.op("pe", lambda e, ps=ps, fc=fc, th=th, wb=wb: e.matmul(ps, lhsT=wb[:, fc, :], rhs=aT[:, fc, th * 512:(th + 1) * 512], start=(fc == 0), stop=(fc == 10)),
                                 r=[("wdn", dc % 2), ("aT", fc)], w=[pk])
                        P.op("dve", lambda e, ps=ps, dc=dc, th=th: e.scalar_tensor_tensor(out=xT[:, dc, th * 512:(th + 1) * 512], in0=ps, scalar=mod[:, 40 + dc:41 + dc],
                                                                                         in1=xT[:, dc, th * 512:(th + 1) * 512], op0=ALU.mult, op1=ALU.add),
                             r=[pk, "mod", ("xT", dc)], w=[("xT", dc)])

        def attention(l):
            j = l // 3
            cosT, sinT = xT[:, 8, :], xT[:, 9, :]
            t1, t2 = cv[0], cv[1]
            if STAGE.get("kv_only"):
                wkv = wada[0]
                P.dma(wkv[:], I["wkv"][j], w=[("wada", 0)], q="pool")
                for tb in range(8):
                    ps, pk = bank(tb % 4)
                    for k in range(8):
                        P.op("pe", lambda e, ps=ps, k=k, tb=tb: e.matmul(ps, lhsT=hT[:, k, tb * 128:(tb + 1) * 128], rhs=wkv[:, k, :], start=(k == 0), stop=(k == 7)),
                             r=[("wada", 0), ("hT", k)], w=[pk])
                    kvt = tmpA if tb % 2 == 0 else tmpB
                    kvk = "tmpA" if tb % 2 == 0 else "tmpB"
                    P.op("act", lambda e, ps=ps, kvt=kvt: e.copy(out=kvt[:, 0:512], in_=ps), r=[pk], w=[kvk])
                    P.dma(O["nk"][j, tb * 128:(tb + 1) * 128, :], kvt[:, 0:256], r=[kvk], sem="kvout%d" % (tb % 2))
                    P.dma(O["nv"][j, tb * 128:(tb + 1) * 128, :], kvt[:, 256:512], r=[kvk], sem="kvout%d" % (tb % 2))
                return
            P.dma(esink[:], I["sink"][j], w=["esink"])
            P.op("act", lambda e: e.activation(out=esink[:], in_=esink[:], func=AF.Exp), r=["esink"], w=["esink"])
            if STAGE.get("attn_upto", 9) < 1:
                return
            P.dma(ckd[:], I["ckd"][j].rearrange("(kb p) n -> p kb n", p=128), w=["ckd"], q="pool")
            P.dma(vctx[:].rearrange("p kb g v n -> p kb (g v n)"), I["cvp"][j].rearrange("(kb p) n -> p kb n", p=128), w=["vctx"], q="pool")
            for g in range(4):
                ps, pk = bank(g)
                for kb in range(4):
                    P.op("pe", lambda e, ps=ps, kb=kb, g=g: e.matmul(ps[:, kb * 128:(kb + 1) * 128], lhsT=ckd[:, kb, g * 128:(g + 1) * 128], rhs=ident_bf[:], start=True, stop=True),
                         r=["ckd", "ident_bf"], w=[pk])
                P.op("act", lambda e, ps=ps, g=g: e.copy(out=kctxT[:, g, :], in_=ps), r=[pk], w=["kctxT"])
            if STAGE.get("attn_upto", 9) < 2:
                return
            def proj_rope(wsrc, dst, dkey, idx):
                wb = wup[idx % 2]
                P.dma(wb[:, 0], wsrc, w=[("wup", idx % 2, 0)], q="pool")
                for th in range(2):
                    ps, pk = bank((idx * 2 + th) % 4)
                    rps, rpk = bank(4 + (idx * 2 + th) % 2)
                    for k in range(8):
                        P.op("pe", lambda e, ps=ps, k=k, th=th, wb=wb: e.matmul(ps, lhsT=wb[:, 0, k, :], rhs=hT[:, k, th * 512:(th + 1) * 512], start=(k == 0), stop=(k == 7)),
                             r=[("wup", idx % 2, 0), ("hT", k)], w=[pk])
                    q_ = qb[th]
                    if STAGE.get("pr", 9) < 1:
                        continue
                    P.op("act", lambda e, ps=ps, q_=q_: e.copy(out=q_[:], in_=ps), r=[pk], w=[("qb", th)])
                    if STAGE.get("pr", 9) < 2:
                        continue
                    P.op("pe", lambda e, rps=rps, q_=q_: e.matmul(rps, lhsT=rmat[:], rhs=q_[:], start=True, stop=True), r=[("qb", th), "rmat"], w=[rpk])
                    sl = slice(th * 512, (th + 1) * 512)
                    if STAGE.get("pr", 9) < 3 or idx >= STAGE.get("pridx", 99):
                        continue
                    P.op("dve", lambda e, ps=ps, sl=sl: e.scalar_tensor_tensor(out=t1[:, sl], in0=ps, scalar=1.0, in1=cosT[:, sl], op0=ALU.mult, op1=ALU.mult), r=[pk, ("xT", 8), ("qb", th)], w=[("cv", 0)])
                    if STAGE.get("pr", 9) < 4:
                        continue
                    P.op("dve", lambda e, rps=rps, sl=sl: e.scalar_tensor_tensor(out=t2[:, sl], in0=rps, scalar=1.0, in1=sinT[:, sl], op0=ALU.mult, op1=ALU.mult), r=[rpk, ("xT", 9)], w=[("cv", 1)])
                    if STAGE.get("pr", 9) < 5:
                        continue
                    P.op("dve", lambda e, sl=sl, dst=dst: e.tensor_tensor(out=dst[:, sl], in0=t1[:, sl], in1=t2[:, sl], op=ALU.add), r=[("cv", 0), ("cv", 1)], w=[dkey])
            for qc in range(8):
                proj_rope(I["wq"][j, qc], aT[:, qc, :], ("aT", qc), qc)
            for g in range(4):
                proj_rope(I["wk"][j, g], aT[:, 8 + g, :], ("aT", 8 + g), 8 + g)
            if STAGE.get("attn_upto", 9) < 3:
                return
            wkv = wada[0]
            P.dma(wkv[:], I["wkv"][j], w=[("wada", 0)], q="pool")
            for tb in range(8):
                ps, pk = bank(tb % 4)
                for k in range(8):
                    P.op("pe", lambda e, ps=ps, k=k, tb=tb: e.matmul(ps, lhsT=hT[:, k, tb * 128:(tb + 1) * 128], rhs=wkv[:, k, :], start=(k == 0), stop=(k == 7)),
                         r=[("wada", 0), ("hT", k)], w=[pk])
                kvt = tmpA if tb % 2 == 0 else tmpB
                kvk = "tmpA" if tb % 2 == 0 else "tmpB"
                P.op("act", lambda e, ps=ps, kvt=kvt: e.copy(out=kvt[:, 0:512], in_=ps), r=[pk], w=[kvk])
                P.dma(O["nk"][j, tb * 128:(tb + 1) * 128, :], kvt[:, 0:256], r=[kvk], sem="kvout%d" % (tb % 2))
                P.dma(O["nv"][j, tb * 128:(tb + 1) * 128, :], kvt[:, 256:512], r=[kvk], sem="kvout%d" % (tb % 2))
                P.op("dve", lambda e, ps=ps, tb=tb: e.tensor_copy(out=vlat[:, tb, :, 0, 0:64], in_=ps[:, 256:512].rearrange("p (g d) -> p g d", g=4)), r=[pk], w=["vlat"])
                P.op("dve", lambda e, ps=ps, tb=tb: e.tensor_copy(out=vlat[:, tb, :, 1, 64:128], in_=ps[:, 256:512].rearrange("p (g d) -> p g d", g=4)), r=[pk], w=["vlat"])
            if STAGE.get("attn_upto", 9) < 4:
                return
            dsb = tmpA[:].rearrange("p (a n) -> p a n", a=2)
            osb = [cv[0][:, 0:512], cv[1][:, 0:512]]
            tb_bf = tmpB[:].bitcast(BF16)
            ebuf = [tb_bf[:, i * 512:(i + 1) * 512] for i in range(4)]
            P.alias(["dsb"], ["tmpA"])
            P.alias([("osb", 0)], [("cv", 0)])
            P.alias([("osb", 1)], [("cv", 1)])
            P.alias([("ebuf", i) for i in range(4)], ["tmpB"])
            sc = 0
            for h in range(STAGE.get("nheads", 16)):
                g, qc, pb, var = h // 4, h // 2, (h % 2) * 64, h % 2
                dr = 64 if var == 0 else 0
                qh = aT[pb:pb + 64, qc, :]
                kh = aT[pb:pb + 64, 8 + g, :]
                kch = kctxT[pb:pb + 64, g, :]
                for th in range(2):
                    it = h * 2 + th
                    OP, opk = bank(4 + it % 2)
                    jbs = [jb for jb in range(8) if max(jb - 1, 4 * th) <= min(jb + 1, 4 * th + 3)]
                    blocks = [("c", kb) for kb in range(4)] + [("l", jb) for jb in jbs]
                    LA = 2
                    pend = []

                    def emit_S(kind, ix):
                        nonlocal sc
                        ps, pk = bank(sc % 4); eb = ebuf[sc % 4]; ek = ("ebuf", sc % 4); sc += 1
                        if kind == "c":
                            kb = ix
                            P.op("pe", lambda e, ps=ps, kb=kb, th=th, kch=kch, qh=qh: e.matmul(ps, lhsT=kch[:, kb * 128:(kb + 1) * 128], rhs=qh[:, th * 512:(th + 1) * 512], start=True, stop=True),
                                 r=["kctxT", ("aT", qc)], w=[pk])
                            P.op("act", lambda e, ps=ps, eb=eb: e.activation(out=eb[:], in_=ps, func=AF.Exp, scale=0.125), r=[pk], w=[ek])
                            return (kind, ix, eb, ek, 0, 512)
                        jb = ix
                        i0_ = max(jb - 1, 4 * th); i1_ = min(jb + 1, 4 * th + 3)
                        n = (i1_ - i0_ + 1) * 128
                        P.op("pe", lambda e, ps=ps, jb=jb, i0_=i0_, n=n, kh=kh, qh=qh: e.matmul(ps[:, 0:n], lhsT=kh[:, jb * 128:(jb + 1) * 128], rhs=qh[:, i0_ * 128:i0_ * 128 + n], start=True, stop=True),
                             r=[("aT", 8 + g), ("aT", qc)], w=[pk])
                        P.op("act", lambda e, ps=ps, eb=eb, n=n: e.activation(out=eb[:, 0:n], in_=ps[:, 0:n], func=AF.Exp, scale=0.125), r=[pk], w=[ek])
                        for i in range(i0_, i1_ + 1):
                            if i == jb:
                                continue
                            off = 0 if i == jb + 1 else 1
                            c0 = (i - i0_) * 128
                            P.op("dve", lambda e, eb=eb, c0=c0, i=i, off=off: e.tensor_tensor(out=eb[:, c0:c0 + 128], in0=eb[:, c0:c0 + 128], in1=amask[:, i, off, :], op=ALU.mult),
                                 r=[ek, "amask"], w=[ek])
                        return (kind, ix, eb, ek, (i0_ - 4 * th) * 128, n)

                    def emit_PV(st, is_first, is_last):
                        kind, ix, eb, ek, o0, n = st
                        if kind == "c":
                            P.op("pe", lambda e, OP=OP, eb=eb, ix=ix, g=g, var=var, is_first=is_first: e.matmul(OP, lhsT=vctx[:, ix, g, var, :], rhs=eb[:], start=is_first, stop=False),
                                 r=["vctx", ek], w=[opk])
                        else:
                            P.op("pe", lambda e, OP=OP, eb=eb, ix=ix, g=g, var=var, o0=o0, n=n, is_last=is_last: e.matmul(OP[:, o0:o0 + n], lhsT=vlat[:, ix, g, var, :], rhs=eb[:, 0:n], start=False, stop=is_last),
                                 r=["vlat", ek], w=[opk])

                    nb = len(blocks)
                    for i in range(nb + LA):
                        if i < nb:
                            pend.append(emit_S(*blocks[i]))
                        if i - LA >= 0:
                            emit_PV(pend[i - LA], i - LA == 0, i - LA == nb - 1)
                    P.op("dve", lambda e, OP=OP, dr=dr, h=h: e.tensor_scalar(out=dsb[dr:dr + 1, 0, :], in0=OP[dr:dr + 1, :], scalar1=esink[dr:dr + 1, h:h + 1], scalar2=None, op0=ALU.add),
                         r=[opk, "esink"], w=["dsb"])
                    P.op("act", lambda e, dr=dr: e.activation(out=dsb[dr:dr + 1, 0, :], in_=dsb[dr:dr + 1, 0, :], func=AF.Ln), r=["dsb"], w=["dsb"])
                    P.op("act", lambda e, dr=dr: e.activation(out=dsb[dr:dr + 1, 1, :], in_=dsb[dr:dr + 1, 0, :], func=AF.Exp, scale=-1.0), r=["dsb"], w=["dsb"])
                    BC, bck = bank(6 + it % 2)
                    P.op("pe", lambda e, BC=BC, dr=dr: e.matmul(BC, lhsT=ones_f[dr:dr + 1, :], rhs=dsb[dr:dr + 1, 1, :], start=True, stop=True), r=["dsb", "ones_f"], w=[bck])
                    ob = osb[it % 2]; obk = ("osb", it % 2)
                    P.op("act", lambda e, OP=OP, ob=ob, pb=pb: e.copy(out=ob[pb:pb + 64, :], in_=OP[pb:pb + 64, :]), r=[opk], w=[obk])
                    P.op("dve", lambda e, BC=BC, ob=ob, pb=pb, qc=qc, th=th: e.tensor_tensor(out=aT[pb:pb + 64, qc, th * 512:(th + 1) * 512], in0=ob[pb:pb + 64, :], in1=BC[pb:pb + 64, :], op=ALU.mult),
                         r=[obk, bck], w=[("aT", qc)])
            P.alias(["tmpA"], ["dsb"])
            P.alias([("cv", 0)], [("osb", 0)])
            P.alias([("cv", 1)], [("osb", 1)])
            P.alias(["tmpB"], [("ebuf", i) for i in range(4)])
            for dc in range(8):
                wb = wup[dc % 2]
                P.dma(wb[:, 0], I["wo"][j, dc], w=[("wup", dc % 2, 0)], q="pool")
                for th in range(2):
                    ps, pk = bank((dc * 2 + th) % 4)
                    for k in range(8):
                        P.op("pe", lambda e, ps=ps, k=k, th=th, wb=wb: e.matmul(ps, lhsT=wb[:, 0, k, :], rhs=aT[:, k, th * 512:(th + 1) * 512], start=(k == 0), stop=(k == 7)),
                             r=[("wup", dc % 2, 0), ("aT", k)], w=[pk])
                    P.op("dve", lambda e, ps=ps, dc=dc, th=th: e.scalar_tensor_tensor(out=xT[:, dc, th * 512:(th + 1) * 512], in0=ps, scalar=mod[:, 16 + dc:17 + dc],
                                                                                     in1=xT[:, dc, th * 512:(th + 1) * 512], op0=ALU.mult, op1=ALU.add),
                         r=[pk, "mod", ("xT", dc)], w=[("xT", dc)])

        def hgrn(l):
            CH = 32
            NCH = NT // CH
            P.op("act", lambda e: e.activation(out=hlb[:], in_=hlb[:], func=AF.Exp), r=["hlb"], w=["hlb"])
            P.op("dve", lambda e: e.tensor_tensor(out=lbp[:, 1], in0=hlb[:, 0], in1=hlb[:, 1], op=ALU.add), r=["hlb"], w=["lbp"])
            P.op("dve", lambda e: e.tensor_tensor(out=lbp[:, 1], in0=lbp[:, 1], in1=hlb[:, 2], op=ALU.add), r=["hlb", "lbp"], w=["lbp"])
            P.op("dve", lambda e: e.tensor_tensor(out=lbp[:, 1], in0=lbp[:, 1], in1=hlb[:, 3], op=ALU.add), r=["hlb", "lbp"], w=["lbp"])
            P.op("dve", lambda e: e.reciprocal(out=lbp[:, 1], in_=lbp[:, 1]), r=["lbp"], w=["lbp"])
            P.op("dve", lambda e: e.tensor_tensor(out=lbp[:, 0], in0=lbp[:, 1], in1=hlb[:, 1], op=ALU.mult), r=["hlb", "lbp"], w=["lbp"])
            P.op("dve", lambda e: e.tensor_scalar(out=lbp[:, 1], in0=lbp[:, 0], scalar1=-1.0, scalar2=1.0, op0=ALU.mult, op1=ALU.add), r=["lbp"], w=["lbp"])
            bufQ, bufF, bufK, bufC = cg[0], cg[1], cv[0], cv[1]
            kQ, kF, kK, kC = ("cg", 0), ("cg", 1), ("cv", 0), ("cv", 1)
            oacc = rstd
            kdtok = [vlat[:].rearrange("p a g v n -> p (a g v n)")[:, d * 4096:(d + 1) * 4096].rearrange("p (b c n) -> p b c n", b=8, c=4) for d in range(2)]
            P.alias([("kdtok", 0), ("kdtok", 1)], ["vlat"])
            for h in range(8):
                P.dma(wup[0][:], I["hw_in"][h, 0:2].rearrange("a p k n -> p a k n"), w=[("wup", 0, 0), ("wup", 0, 1)], q="pool")
                P.dma(wup[1][:], I["hw_in"][h, 2:4].rearrange("a p k n -> p a k n"), w=[("wup", 1, 0), ("wup", 1, 1)], q="pool")
                P.dma(wdn[0][:, 0:8, :], I["hw_in"][h, 4], w=[("wdn", 0)], q="pool")

                def proj(wap, wkeys, bi):
                    outs = []
                    for th in range(2):
                        ps, pk = bank(bi * 2 + th)
                        for kk in range(8):
                            P.op("pe", lambda e, ps=ps, kk=kk, th=th, wap=wap: e.matmul(ps, lhsT=wap[:, kk, :], rhs=hT[:, kk, th * 512:(th + 1) * 512], start=(kk == 0), stop=(kk == 7)),
                                 r=list(wkeys) + [("hT", kk)], w=[pk])
                        outs.append((ps, pk))
                    return outs
                for th, (ps, pk) in enumerate(proj(wup[0][:, 0], [("wup", 0, 0)], 0)):
                    P.op("act", lambda e, ps=ps, th=th: e.copy(out=bufQ[:, th * 512:(th + 1) * 512], in_=ps), r=[pk], w=[kQ])
                for blk in range(8):
                    ps, pk = bank(2 + blk % 2)
                    for kk in range(8):
                        P.op("pe", lambda e, ps=ps, kk=kk, blk=blk: e.matmul(ps[:, 0:128], lhsT=hT[:, kk, blk * 128:(blk + 1) * 128], rhs=wup[0][:, 1, kk, :], start=(kk == 0), stop=(kk == 7)),
                             r=[("wup", 0, 1), ("hT", kk)], w=[pk])
                    P.op("act", lambda e, ps=ps, blk=blk: e.copy(out=vtok[:, blk, :], in_=ps[:, 0:128]), r=[pk], w=["vtok"])
                for d in range(2):
                    for th, (ps, pk) in enumerate(proj(wup[1][:, d], [("wup", 1, d)], 2 + d)):
                        P.op("act", lambda e, ps=ps, th=th: e.activation(out=bufF[:, th * 512:(th + 1) * 512], in_=ps, func=AF.Sigmoid), r=[pk], w=[kF])
                    P.op("dve", lambda e, d=d, h=h: e.tensor_scalar(out=bufF[:], in0=bufF[:], scalar1=lbp[:, 1, d, h:h + 1], scalar2=lbp[:, 0, d, h:h + 1], op0=ALU.mult, op1=ALU.add),
                         r=[kF, "lbp"], w=[kF])
                    P.op("dve", lambda e: e.tensor_scalar(out=bufK[:], in0=bufF[:], scalar1=-1.0, scalar2=1.0, op0=ALU.mult, op1=ALU.add), r=[kF], w=[kK])
                    P.op("act", lambda e: e.activation(out=bufF[:], in_=bufF[:], func=AF.Ln), r=[kF], w=[kF])
                    P.op("dve", lambda e: e.tensor_tensor_scan(out=bufC[:], data0=m32[:], data1=bufF[:], initial=0.0, op0=ALU.mult, op1=ALU.add), r=["m32", kF], w=[kC])
                    if d == 1:
                        P.op("dve", lambda e: e.scalar_tensor_tensor(out=tmpA[:], in0=bufC[:], scalar=-1.0, in1=bufF[:], op0=ALU.mult, op1=ALU.add), r=[kC, kF], w=["tmpA"])
                        P.op("act", lambda e: e.copy(out=ctmp[:], in_=bufC[:, CH - 1:NT:CH]), r=[kC], w=["ctmp"])
                        P.op("dve", lambda e: e.tensor_tensor(out=bufC[:].rearrange("p (c t) -> p c t", t=CH), in0=tmpA[:].rearrange("p (c t) -> p c t", t=CH),
                                                              in1=ctmp[:].unsqueeze(2).to_broadcast([128, NCH, CH]), op=ALU.add),
                             r=["tmpA", "ctmp"], w=[kC])
                        ctot = bufC[:, 0:NT:CH]
                    else:
                        ctot = bufC[:, CH - 1:NT:CH]
                    P.op("act", lambda e, d=d, ctot=ctot: e.activation(out=adec[d][:], in_=ctot, func=AF.Exp), r=[kC], w=[("adec", d)])
                    P.op("act", lambda e: e.activation(out=tmpA[:], in_=bufC[:], func=AF.Exp), r=[kC], w=["tmpA"])
                    P.op("dve", lambda e, d=d: e.tensor_tensor(out=qhat[d][:], in0=bufQ[:], in1=tmpA[:], op=ALU.mult), r=[kQ, "tmpA"], w=[("qhat", d)])
                    P.op("dve", lambda e: e.tensor_scalar(out=tmpB[:], in0=bufC[:], scalar1=-1.0, scalar2=85.0, op0=ALU.mult, op1=ALU.min), r=[kC], w=["tmpB"])
                    P.op("act", lambda e: e.activation(out=tmpB[:], in_=tmpB[:], func=AF.Exp), r=["tmpB"], w=["tmpB"])
                    P.op("dve", lambda e: e.tensor_tensor(out=tmpB[:], in0=tmpB[:], in1=bufK[:], op=ALU.mult), r=["tmpB", kK], w=["tmpB"])
                    P.op("act", lambda e, d=d: e.copy(out=ktil[d][:], in_=tmpB[:]), r=["tmpB"], w=[("ktil", d)])
                    P.op("dve", lambda e, d=d: e.tensor_tensor(out=kdT[d][:].rearrange("p (c t) -> p c t", t=CH), in0=tmpB[:].rearrange("p (c t) -> p c t", t=CH),
                                                                in1=adec[d][:].unsqueeze(2).to_broadcast([128, NCH, CH]), op=ALU.mult),
                         r=["tmpB", ("adec", d)], w=[("kdT", d)])
                    for blk in range(8):
                        ps, pk = bank(6 + blk % 2)
                        pst = ps.bitcast(BF16)
                        P.op("pe", lambda e, pst=pst, blk=blk, d=d: e.transpose(pst[:, 0:128], kdT[d][:, blk * 128:(blk + 1) * 128], ident_bf[:]), r=[("kdT", d), "ident_bf"], w=[pk])
                        for c4 in range(4):
                            P.op("act", lambda e, pst=pst, blk=blk, c4=c4, d=d: e.activation(out=kdtok[d][:, blk, c4, :], in_=pst[:, 0:128], func=AF.Identity, scale=rowm[:, c4:c4 + 1]),
                                 r=[pk, "rowm"], w=[("kdtok", d)])
                    P.dma(Sf[d][:], I["hs0"][d, h], w=[("Sf", d)])
                    P.op("act", lambda e, d=d: e.copy(out=Sb[d][:], in_=Sf[d][:]), r=[("Sf", d)], w=[("Sb", d)])
                P.op("pool", lambda e: e.memset(oacc[:], 0.0), r=["rstd"], w=["rstd"])
                for bi_ in range(8):
                    ctxs = []
                    for d in range(2):
                        blk = bi_ if d == 0 else 7 - bi_
                        bsl = slice(blk * 128, (blk + 1) * 128)
                        aps, apk = bank(0 + d * 2)
                        ops_, opk = bank(1 + d * 2)
                        dps, dpk = bank(4 + d)
                        P.op("pe", lambda e, aps=aps, d=d, bsl=bsl: e.matmul(aps[:, 0:128], lhsT=ktil[d][:, bsl], rhs=qhat[d][:, bsl], start=True, stop=True),
                             r=[("ktil", d), ("qhat", d)], w=[apk])
                        am = attm[d]
                        P.op("dve", lambda e, aps=aps, am=am, d=d: e.tensor_tensor(out=am[:], in0=aps[:, 0:128], in1=hmask[:, d, :], op=ALU.mult), r=[apk, "hmask"], w=[("attm", d)])
                        for c4 in range(4):
                            P.op("pe", lambda e, dps=dps, c4=c4, blk=blk, d=d: e.matmul(dps[:, c4 * 128:(c4 + 1) * 128], lhsT=kdtok[d][:, blk, c4, :], rhs=vtok[:, blk, :], start=True, stop=True),
                                 r=[("kdtok", d), "vtok"], w=[dpk])
                        P.op("pe", lambda e, ops_=ops_, am=am, blk=blk: e.matmul(ops_[:, 0:128], lhsT=vtok[:, blk, :], rhs=am[:], start=True, stop=False),
                             r=["vtok", ("attm", d)], w=[opk])
                        ctxs.append((blk, bsl, ops_, opk, dps, dpk))
                    for step in range(4):
                        for d in range(2):
                            blk, bsl, ops_, opk, dps, dpk = ctxs[d]
                            c4 = step if d == 0 else 3 - step
                            last = (step == 3)
                            ch = blk * 4 + c4
                            csl = slice(ch * CH, (ch + 1) * CH)
                            P.op("pe", lambda e, ops_=ops_, c4=c4, csl=csl, d=d, last=last: e.matmul(ops_[:, c4 * CH:(c4 + 1) * CH], lhsT=Sb[d][:], rhs=qhat[d][:, csl], start=False, stop=last),
                                 r=[("Sb", d), ("qhat", d)], w=[opk])
                            P.op("dve", lambda e, dps=dps, c4=c4, ch=ch, d=d: e.scalar_tensor_tensor(out=Sf[d][:], in0=Sf[d][:], scalar=adec[d][:, ch:ch + 1], in1=dps[:, c4 * 128:(c4 + 1) * 128], op0=ALU.mult, op1=ALU.add),
                                 r=[("Sf", d), ("adec", d), dpk], w=[("Sf", d)])
                            seg_end = (ch % 8 == 7) if d == 0 else (ch % 8 == 0)
                            if seg_end:
                                seg = ch // 8
                                P.dma(O["hg"][(seg * 2 + d) * 8 + h], Sf[d][:], r=[("Sf", d)], sem="hgout%d" % d)
                                P.op("dve", lambda e, d=d: e.tensor_scalar(out=Sf[d][:], in0=Sf[d][:], scalar1=keep[:, 0:1], scalar2=None, op0=ALU.mult), r=[("Sf", d), "keep"], w=[("Sf", d)])
                            P.op("act", lambda e, d=d: e.copy(out=Sb[d][:], in_=Sf[d][:]), r=[("Sf", d)], w=[("Sb", d)])
                    for d in range(2):
                        blk, bsl, ops_, opk, dps, dpk = ctxs[d]
                        P.op("dve", lambda e, ops_=ops_, bsl=bsl: e.tensor_tensor(out=oacc[:, bsl], in0=ops_[:, 0:128], in1=oacc[:, bsl], op=ALU.add), r=[opk, "rstd"], w=["rstd"])
                P.op("act", lambda e: e.activation(out=sq[0][:], in_=oacc[:], func=AF.Square), r=["rstd"], w=[("sq", 0)])
                for th in range(2):
                    ps, pk = bank(6 + th)
                    P.op("pe", lambda e, ps=ps, th=th: e.matmul(ps, lhsT=ones_bf[:], rhs=sq[0][:, th * 512:(th + 1) * 512], start=True, stop=True), r=[("sq", 0), "ones_bf"], w=[pk])
                    P.op("act", lambda e, ps=ps, th=th: e.activation(out=tmpA[:, th * 512:(th + 1) * 512], in_=ps, func=AF.Ln, scale=1.0 / 128, bias=eps_t[:, 0:1]), r=[pk, "eps"], w=["tmpA"])
                P.op("act", lambda e: e.activation(out=tmpA[:], in_=tmpA[:], func=AF.Exp, scale=-0.5), r=["tmpA"], w=["tmpA"])
                P.op("dve", lambda e: e.scalar_tensor_tensor(out=tmpA[:], in0=oacc[:], scalar=hgn[:, 0:1], in1=tmpA[:], op0=ALU.mult, op1=ALU.mult), r=["rstd", "hgn", "tmpA"], w=["tmpA"])
                for th, (ps, pk) in enumerate(proj(wdn[0][:, 0:8, :], [("wdn", 0)], 2)):
                    P.op("act", lambda e, ps=ps, th=th: e.activation(out=tmpB[:, th * 512:(th + 1) * 512], in_=ps, func=AF.Silu), r=[pk], w=["tmpB"])
                P.op("dve", lambda e, h=h: e.tensor_tensor(out=aT[:, h, :], in0=tmpA[:], in1=tmpB[:], op=ALU.mult), r=["tmpA", "tmpB"], w=[("aT", h)])
            P.alias(["vlat"], [("kdtok", 0), ("kdtok", 1)])
            P.op("dve", lambda e: e.memset(vlat[:], 0.0), w=["vlat"])
            P.op("dve", lambda e: e.memset(vlat[:, :, :, 0, 64:65], 1.0), w=["vlat"])
            P.op("dve", lambda e: e.memset(vlat[:, :, :, 1, 0:1], 1.0), w=["vlat"])
            for dc in range(8):
                wb = wup[dc % 2]
                P.dma(wb[:, 0], I["hwo"][dc], w=[("wup", dc % 2, 0)], q="pool")
                for th in range(2):
                    ps, pk = bank((dc * 2 + th) % 4)
                    for kk in range(8):
                        P.op("pe", lambda e, ps=ps, kk=kk, th=th, wb=wb: e.matmul(ps, lhsT=wb[:, 0, kk, :], rhs=aT[:, kk, th * 512:(th + 1) * 512], start=(kk == 0), stop=(kk == 7)),
                             r=[("wup", dc % 2, 0), ("aT", kk)], w=[pk])
                    P.op("dve", lambda e, ps=ps, dc=dc, th=th: e.scalar_tensor_tensor(out=xT[:, dc, th * 512:(th + 1) * 512], in0=ps, scalar=mod[:, 16 + dc:17 + dc],
                                                                                     in1=xT[:, dc, th * 512:(th + 1) * 512], op0=ALU.mult, op1=ALU.add),
                         r=[pk, "mod", ("xT", dc)], w=[("xT", dc)])

        def ssm(l):
            SEG = 256
            rs32 = rstd[:]
            sr_ = [rs32[:, i * 64:(i + 1) * 64] for i in range(16)]
            sq32 = sq[0][:].bitcast(F32)
            sq_ = [sq32[:, i * 32:(i + 1) * 32] for i in range(14)]
            sCq = [sq[1][:, i * 512:(i + 1) * 512].rearrange("p (g i) -> p g i", g=32) for i in range(2)]
            P.alias(["srr"], ["rstd"])
            P.alias(["sqq"], [("sq", 0)])
            P.alias(["sCq"], [("sq", 1)])
            m256 = m32
            P.op("pool", lambda e: e.memset(m256[:], 1.0), r=["m32"], w=["m32"])
            P.op("pool", lambda e: e.memset(m256[:, 0:NT:256], 0.0), r=["m32"], w=["m32"])
            P.dma(smask[:], I["smask"], w=["smask"])
            P.dma(sD[:], I["sD"], w=["sD"])
            TT = lambda e, o, a, b, op: e.tensor_tensor(out=o, in0=a, in1=b, op=op)

            def vop(eng, o, a, b, op, r, w):
                P.op(eng, lambda e, o=o, a=a, b=b, op=op: e.tensor_tensor(out=o, in0=a, in1=b, op=op), r=r, w=w)

            def cmul(eng, ore, oim, are_, aim_, bre, bim, t1, t2, r, w, tk):
                vop(eng, t1, are_, bre, ALU.mult, r, [tk[0]])
                vop(eng, t2, aim_, bim, ALU.mult, r, [tk[1]])
                vop(eng, ore, t1, t2, ALU.subtract, [tk[0], tk[1]], w)
                vop(eng, t1, are_, bim, ALU.mult, r, [tk[0]])
                vop(eng, t2, aim_, bre, ALU.mult, r, [tk[1]])
                vop(eng, oim, t1, t2, ALU.add, [tk[0], tk[1]], w)

            def lam_params(are_ap, aim_ap, dt_scalar_or_ap, S, key, n, per_part_dt, need_inv=True, eng="dve"):
                arec, th, mag, c, s_, t1, t2, imag = S[0], S[1], S[2], S[3], S[4], S[5], S[6], S[7]
                K = [key]
                P.op(eng, lambda e: e.tensor_scalar(out=arec[:], in0=are_ap, scalar1=-1e-4, scalar2=None, op0=ALU.min), r=K, w=K)
                if per_part_dt:
                    dtx = S[8]
                    P.op("act", lambda e: e.activation(out=dtx[:, 0:1], in_=dt_scalar_or_ap, func=AF.Exp), r=K, w=K)
                    P.op(eng, lambda e: e.tensor_scalar(out=th[:], in0=aim_ap, scalar1=dtx[:, 0:1], scalar2=None, op0=ALU.mult), r=K, w=K)
                    P.op(eng, lambda e: e.tensor_scalar(out=mag[:], in0=arec[:], scalar1=dtx[:, 0:1], scalar2=None, op0=ALU.mult), r=K, w=K)
                else:
                    dtx = S[8]
                    P.op("act", lambda e: e.activation(out=dtx[:], in_=dt_scalar_or_ap, func=AF.Exp), r=K, w=K)
                    vop(eng, th[:], aim_ap, dtx[:], ALU.mult, K, K)
                    vop(eng, mag[:], arec[:], dtx[:], ALU.mult, K, K)
                P.op("act", lambda e: e.activation(out=imag[:], in_=mag[:], func=AF.Exp, scale=-1.0), r=K, w=K)
                P.op("act", lambda e: e.activation(out=mag[:], in_=mag[:], func=AF.Exp), r=K, w=K)
                P.op("act", lambda e: e.activation(out=s_[:], in_=th[:], func=AF.Sin, scale=1.0 / 64), r=K, w=K)
                P.op("act", lambda e: e.activation(out=c[:], in_=th[:], func=AF.Sin, scale=1.0 / 64, bias=halfpi[:, 0:1]), r=K + ["halfpi"], w=K)
                for _ in range(6):
                    vop(eng, t1[:], c[:], s_[:], ALU.mult, K, K)
                    vop(eng, c[:], c[:], c[:], ALU.mult, K, K)
                    vop(eng, s_[:], s_[:], s_[:], ALU.mult, K, K)
                    vop(eng, c[:], c[:], s_[:], ALU.subtract, K, K)
                    P.op(eng, lambda e: e.tensor_scalar(out=s_[:], in0=t1[:], scalar1=2.0, scalar2=None, op0=ALU.mult), r=K, w=K)
                L1re, L1im, Lm1re, Lm1im = S[9], S[10], S[11], S[12]
                vop(eng, L1re[:], mag[:], c[:], ALU.mult, K, K)
                vop(eng, L1im[:], mag[:], s_[:], ALU.mult, K, K)
                if need_inv:
                    vop(eng, Lm1re[:], imag[:], c[:], ALU.mult, K, K)
                    vop(eng, Lm1im[:], imag[:], s_[:], ALU.mult, K, K)
                    P.op(eng, lambda e: e.tensor_scalar(out=Lm1im[:], in0=Lm1im[:], scalar1=-1.0, scalar2=None, op0=ALU.mult), r=K, w=K)
                return dict(L1re=L1re, L1im=L1im, Lm1re=Lm1re, Lm1im=Lm1im, are=arec)

            A_, B_, C_, D_, Gr, Gi = cg[0], cg[1], cv[0], cv[1], tmpA, tmpB
            kA, kB, kC2, kD, kGr, kGi = ("cg", 0), ("cg", 1), ("cv", 0), ("cv", 1), "tmpA", "tmpB"
            vl = vlat[:].rearrange("p a g v n -> p (a g v n)").bitcast(F32)
            Tre_all = vl[:, 0:2048].rearrange("p (g t) -> p g t", g=8)
            Tim_all = vl[:, 2048:4096].rearrange("p (g t) -> p g t", g=8)
            Tp_re, Tm_re, Tp_im, Tm_im = Tre_all[:, 0:4], Tre_all[:, 4:8], Tim_all[:, 0:4], Tim_all[:, 4:8]
            P.alias(["stab"], ["vlat"])
            vc = vctx[:].rearrange("p a g v n -> p (a g v n)")
            W1pad = vc[:, 0:4096].rearrange("p (q r n) -> p q r n", q=16, r=2)
            kcf = kctxT[:].rearrange("p a n -> p (a n)")
            Ewpad = kcf[:, 0:2048].rearrange("p (q r n) -> p q r n", q=8, r=2)
            P.alias(["W1pad"], ["vctx"])
            P.alias(["Ewpad"], ["kctxT"])
            Hre_b = qhat[0][:].rearrange("p (g t) -> p g t", g=4)
            Him_b = qhat[1][:].rearrange("p (g t) -> p g t", g=4)
            ysb = aT

            def _body():
              for d in range(2):
                  P.dma(sq_[0], I["sAreQ"][d], w=["sqq"])
                  P.dma(sq_[1], I["sAimQ"][d], w=["sqq"])
                  P.dma(sq_[13], I["sDtQ"][d], w=["sqq"])
                  P.dma(sCq[0], I["sCreQ"][d], w=["sCq"], q="pool")
                  P.dma(sCq[1], I["sCimQ"][d], w=["sCq"], q="pool")
                  P.dma(sH[:], I["sH0"][d], w=["sH"])
                  Q = lam_params(sq_[0], sq_[1], sq_[13], sq_[2:13] + [sq_[0], sq_[1]], "sqq", 32, False)
                  for s in range(8):
                      k0, k1 = s // 2, 4 + s // 2
                      g8b = (4 * s) % 8
                      P.op("pool", lambda e: e.memset(kcf[:, 0:2048], 0.0), r=["Ewpad"], w=["Ewpad"])
                      if s % 2 == 0:
                          P.op("pool", lambda e: e.memset(vc[:], 0.0), r=["W1pad"], w=["W1pad"])
                      if s % 2 == 0:
                          for half, kk_ in enumerate((k0, k1)):
                              Rk = ["srr"]
                              P.dma(sr_[0], I["sAreR"][d, kk_], w=Rk)
                              P.dma(sr_[1], I["sAimR"][d, kk_], w=Rk)
                              P.dma(sr_[13][:, 0:1], I["sDtR"][d, kk_], w=Rk)
                              P.dma(sr_[14], I["sBreR"][d, kk_], w=Rk)
                              P.dma(sr_[15], I["sBimR"][d, kk_], w=Rk)
                              R_ = lam_params(sr_[0], sr_[1], sr_[13][:, 0:1], sr_[2:13] + [sr_[0], sr_[0]], "srr", 64, True, need_inv=False)
                              nre, den, cre, cim, t1, t2 = sr_[3], sr_[4], sr_[5], sr_[6], sr_[7], sr_[8]
                              aim_ = sr_[1]
                              P.op("dve", lambda e, R_=R_: e.tensor_scalar(out=nre[:], in0=R_["L1re"][:], scalar1=-1.0, scalar2=None, op0=ALU.add), r=Rk, w=Rk)
                              vop("dve", den[:], R_["are"][:], R_["are"][:], ALU.mult, Rk, Rk)
                              vop("dve", t1[:], aim_[:], aim_[:], ALU.mult, Rk, Rk)
                              vop("dve", den[:], den[:], t1[:], ALU.add, Rk, Rk)
                              P.op("dve", lambda e: e.reciprocal(out=den[:], in_=den[:]), r=Rk, w=Rk)
                              vop("dve", t1[:], nre[:], R_["are"][:], ALU.mult, Rk, Rk)
                              vop("dve", t2[:], R_["L1im"][:], aim_[:], ALU.mult, Rk, Rk)
                              vop("dve", cre[:], t1[:], t2[:], ALU.add, Rk, Rk)
                              vop("dve", cre[:], cre[:], den[:], ALU.mult, Rk, Rk)
                              vop("dve", t1[:], R_["L1im"][:], R_["are"][:], ALU.mult, Rk, Rk)
                              vop("dve", t2[:], nre[:], aim_[:], ALU.mult, Rk, Rk)
                              vop("dve", cim[:], t1[:], t2[:], ALU.subtract, Rk, Rk)
                              vop("dve", cim[:], cim[:], den[:], ALU.mult, Rk, Rk)
                              wre, wim = sr_[9], sr_[10]
                              cmul("dve", wre[:], wim[:], cre[:], cim[:], sr_[14][:], sr_[15][:], t1[:], t2[:], Rk, Rk, ["srr", "srr"])
                              for g8 in range(8):
                                  for ri, wsrc in enumerate((wre, wim)):
                                      P.op("act", lambda e, half=half, ri=ri, wsrc=wsrc, g8=g8: e.activation(out=W1pad[:, half * 8 + g8, ri, half * 64:(half + 1) * 64], in_=wsrc[:], func=AF.Identity, scale=smask[:, g8:g8 + 1]),
                                           r=Rk + ["smask"], w=["W1pad"])
                      for half, kk_ in enumerate((k0, k1)):
                          for q in range(4):
                              g8 = g8b + q
                              gq = 4 * s + q
                              P.op("act", lambda e, q=q, half=half, g8=g8, gq=gq: e.activation(out=Ewpad[:, half * 4 + q, 0, g8 * 16:(g8 + 1) * 16], in_=sCq[0][:, gq, :], func=AF.Identity, scale=smask[:, 8 + half:9 + half]),
                                   r=["sCq", "smask"], w=["Ewpad"])
                              P.op("act", lambda e, q=q, half=half, g8=g8, gq=gq: e.activation(out=Ewpad[:, half * 4 + q, 1, g8 * 16:(g8 + 1) * 16], in_=sCq[1][:, gq, :], func=AF.Identity, scale=smask[:, 10 + half:11 + half]),
                                   r=["sCq", "smask"], w=["Ewpad"])
                      gsl = slice(4 * s, 4 * s + 4)
                      i0 = 0 if d == 0 else SEG - 1
                      for (Tre, Tim, bre_, bim_) in ((Tp_re, Tp_im, Q["L1re"], Q["L1im"]), (Tm_re, Tm_im, Q["Lm1re"], Q["Lm1im"])):
                          P.op("dve", lambda e, Tre=Tre, bre_=bre_, i0=i0, gsl=gsl: e.tensor_copy(out=Tre[:, :, i0:i0 + 1], in_=bre_[:, gsl].unsqueeze(2)), r=["sqq"], w=["stab"])
                          P.op("dve", lambda e, Tim=Tim, bim_=bim_, i0=i0, gsl=gsl: e.tensor_copy(out=Tim[:, :, i0:i0 + 1], in_=bim_[:, gsl].unsqueeze(2)), r=["sqq"], w=["stab"])
                      L = 1
                      while L < SEG:
                          if d == 0:
                              src = slice(0, L); dst = slice(L, 2 * L); piv = L - 1
                          else:
                              src = slice(SEG - L, SEG); dst = slice(SEG - 2 * L, SEG - L); piv = SEG - L
                          zr = Tre_all[:, :, piv:piv + 1].to_broadcast([128, 8, L])
                          zi = Tim_all[:, :, piv:piv + 1].to_broadcast([128, 8, L])
                          cmul("dve", Tre_all[:, :, dst], Tim_all[:, :, dst], Tre_all[:, :, src], Tim_all[:, :, src], zr, zi,
                               A_[:, 0:8 * L].rearrange("p (g t) -> p g t", g=8), B_[:, 0:8 * L].rearrange("p (g t) -> p g t", g=8), ["stab"], ["stab"], [kA, kB])
                          L *= 2
                      P.op("dve", lambda e, gsl=gsl: e.tensor_copy(out=sHent[:], in_=sH[:, gsl, :]), r=["sH"], w=["sHent"])
                      segs = range(4) if d == 0 else range(3, -1, -1)
                      Sre, Sim = pq[0], pq[1]
                      skr = [("pq", 0), ("pq", 1)]; ski = [("pq", 2), ("pq", 3)]

                      def emit_S(seg_):
                          tsl_ = slice(seg_ * SEG, (seg_ + 1) * SEG)
                          for ri, (St, sk) in enumerate(((Sre, skr), (Sim, ski))):
                              for q in range(4):
                                  for half, kk_ in enumerate((k0, k1)):
                                      P.op("pe", lambda e, St=St, q=q, half=half, kk_=kk_, ri=ri, tsl_=tsl_, g8b=g8b: e.matmul(St[:, q * SEG:(q + 1) * SEG], lhsT=W1pad[:, half * 8 + g8b + q, ri, :], rhs=hT[:, kk_, tsl_], start=(half == 0), stop=(half == 1)),
                                           r=["W1pad", ("hT", kk_)], w=[sk[q // 2]])

                      def emit_yevac(pend_):
                          for (yp_, ypk, kk_, tsl_, first_) in pend_:
                              if first_:
                                  P.op("dve", lambda e, yp_=yp_, kk_=kk_, tsl_=tsl_: e.scalar_tensor_tensor(out=ysb[:, kk_, tsl_], in0=hT[:, kk_, tsl_], scalar=sD[:, kk_:kk_ + 1], in1=yp_[:, 0:SEG], op0=ALU.mult, op1=ALU.add),
                                       r=[ypk, ("hT", kk_), "sD"], w=[("aT", kk_)])
                              else:
                                  P.op("dve", lambda e, yp_=yp_, kk_=kk_, tsl_=tsl_: e.tensor_tensor(out=ysb[:, kk_, tsl_], in0=yp_[:, 0:SEG], in1=ysb[:, kk_, tsl_], op=ALU.add),
                                       r=[ypk, ("aT", kk_)], w=[("aT", kk_)])
                      segl = list(segs)
                      ypend = []
                      emit_S(segl[0])
                      for si_, seg in enumerate(segl):
                          tsl = slice(seg * SEG, (seg + 1) * SEG)
                          S3r = Sre[:].rearrange("p (g t) -> p g t", g=4); S3i = Sim[:].rearrange("p (g t) -> p g t", g=4)
                          A3, B3, C3, D3 = [x[:].rearrange("p (g t) -> p g t", g=4) for x in (A_, B_, C_, D_)]
                          G3r, G3i = Gr[:].rearrange("p (g t) -> p g t", g=4), Gi[:].rearrange("p (g t) -> p g t", g=4)
                          vop("dve", A3, S3r, Tm_re, ALU.mult, skr + ["stab"], [kA])
                          vop("dve", B3, S3i, Tm_im, ALU.mult, ski + ["stab"], [kB])
                          vop("dve", A3, A3, B3, ALU.subtract, [kA, kB], [kA])
                          vop("dve", C3, S3i, Tm_re, ALU.mult, ski + ["stab"], [kC2])
                          vop("dve", D3, S3r, Tm_im, ALU.mult, skr + ["stab"], [kD])
                          vop("dve", C3, C3, D3, ALU.add, [kC2, kD], [kC2])
                          if si_ + 1 < len(segl):
                              emit_S(segl[si_ + 1])
                          emit_yevac(ypend); ypend = []
                          for (src_, dstt, ks, kd_) in ((A_, Gr, kA, kGr), (C_, Gi, kC2, kGi)):
                              P.op("dve", lambda e, src_=src_, dstt=dstt: e.tensor_tensor_scan(out=dstt[:], data0=m256[:], data1=src_[:], initial=0.0, op0=ALU.mult, op1=ALU.add), r=["m32", ks], w=[kd_])
                              if d == 1:
                                  s3 = src_[:].rearrange("p (g t) -> p g t", g=4); d3 = dstt[:].rearrange("p (g t) -> p g t", g=4)
                                  P.op("act", lambda e, dstt=dstt: e.copy(out=ctmp[:, 0:4], in_=dstt[:, SEG - 1:NT:SEG]), r=[kd_], w=["ctmp"])
                                  P.op("dve", lambda e, s3=s3, d3=d3: e.tensor_tensor(out=d3, in0=s3, in1=d3, op=ALU.subtract), r=[ks, kd_], w=[kd_])
                                  P.op("dve", lambda e, d3=d3: e.tensor_tensor(out=d3, in0=d3, in1=ctmp[:, 0:4].unsqueeze(2).to_broadcast([128, 4, SEG]), op=ALU.add), r=[kd_, "ctmp"], w=[kd_])
                          vop("dve", G3r, G3r, sHent[:, :, 0:1].to_broadcast([128, 4, SEG]), ALU.add, [kGr, "sHent"], [kGr])
                          vop("dve", G3i, G3i, sHent[:, :, 1:2].to_broadcast([128, 4, SEG]), ALU.add, [kGi, "sHent"], [kGi])
                          vop("dve", A3, G3r, Tp_re, ALU.mult, [kGr, "stab"], [kA])
                          vop("dve", B3, G3i, Tp_im, ALU.mult, [kGi, "stab"], [kB])
                          vop("dve", Hre_b, A3, B3, ALU.subtract, [kA, kB], [("qhat", 0)])
                          vop("dve", C3, G3r, Tp_im, ALU.mult, [kGr, "stab"], [kC2])
                          vop("dve", D3, G3i, Tp_re, ALU.mult, [kGi, "stab"], [kD])
                          vop("dve", Him_b, C3, D3, ALU.add, [kC2, kD], [("qhat", 1)])
                          xi = SEG - 1 if d == 0 else 0
                          vop("dve", sHx[:, :, 0:1], A3[:, :, xi:xi + 1], B3[:, :, xi:xi + 1], ALU.subtract, [kA, kB], ["sHx"])
                          vop("dve", sHx[:, :, 1:2], C3[:, :, xi:xi + 1], D3[:, :, xi:xi + 1], ALU.add, [kC2, kD], ["sHx"])
                          for half in range(2):
                              P.dma(O["ssm"][seg * 2 + d, half * 32 + 4 * s: half * 32 + 4 * s + 4].rearrange("g p r -> p g r"), sHx[half * 64:(half + 1) * 64, :, :], r=["sHx"], sem="ssmout")
                          P.op("dve", lambda e: e.tensor_scalar(out=sHent[:], in0=sHx[:], scalar1=keep[:, 0:1], scalar2=None, op0=ALU.mult), r=["sHx", "keep"], w=["sHent"])
                          if STAGE.get("ssm_dbg"):
                              P.dma(O["dbg2"][:, 0:1024], A_[:], r=[kA], sem="dbg")
                              P.dma(O["dbg2"][:, 1024:2048], B_[:], r=[kB], sem="dbg")
                              P.dma(O["dbg2"][:, 2048:3072], C_[:], r=[kC2], sem="dbg")
                              P.dma(O["dbg2"][:, 3072:4096], D_[:], r=[kD], sem="dbg")
                              P.dma(O["dbg2"][:, 4096:5120], Gr[:], r=[kGr], sem="dbg")
                              P.dma(O["dbg2"][:, 5120:6144], Gi[:], r=[kGi], sem="dbg")
                              P.dma(O["dbg3"], hT[:].rearrange("p k t -> p (k t)"), r=[("hT", kk) for kk in range(8)], sem="dbg")
                              P.dma(O["dbg"][:, 0:4096], vl, r=["stab"], sem="dbg")
                              P.dma(O["dbg"][:, 4096:5120], rs32, r=["srr"], sem="dbg")
                              P.dma(O["dbg"][:, 5120:5632], sq32, r=["sqq"], sem="dbg")
                              raise StopIteration
                          for half, kk_ in enumerate((k0, k1)):
                              yp_, ypk = bank(4 + 2 * (si_ % 2) + half)
                              n = 0
                              for q in range(4):
                                  for ri, Hb in enumerate((Hre_b, Him_b)):
                                      P.op("pe", lambda e, yp_=yp_, q=q, half=half, ri=ri, Hb=Hb, n=n: e.matmul(yp_[:, 0:SEG], lhsT=Ewpad[:, half * 4 + q, ri, :], rhs=Hb[:, q, :], start=(n == 0), stop=(n == 7)),
                                           r=["Ewpad", ("qhat", ri)], w=[ypk])
                                      n += 1
                              ypend.append((yp_, ypk, kk_, tsl, (d == 0 and s % 2 == 0)))
                      emit_yevac(ypend); ypend = []

            try:
                _body()
            except StopIteration:
                pass
            P.alias(["vlat"], ["stab"])
            P.alias(["vctx"], ["W1pad"])
            P.alias(["kctxT"], ["Ewpad"])
            P.alias(["rstd"], ["srr"])
            P.alias([("sq", 0)], ["sqq"])
            P.alias([("sq", 1)], ["sCq"])
            P.op("dve", lambda e: e.memset(vlat[:], 0.0), w=["vlat"])
            P.op("dve", lambda e: e.memset(vlat[:, :, :, 0, 64:65], 1.0), w=["vlat"])
            P.op("dve", lambda e: e.memset(vlat[:, :, :, 1, 0:1], 1.0), w=["vlat"])
            for kk_ in range(8):
                yk = ysb[:, kk_, :]
                P.op("dve", lambda e, yk=yk: e.tensor_tensor(out=A_[:], in0=yk, in1=yk, op=ALU.mult), r=[("aT", kk_)], w=[kA])
                P.op("dve", lambda e: e.tensor_scalar(out=A_[:], in0=A_[:], scalar1=0.044715, scalar2=1.0, op0=ALU.mult, op1=ALU.add), r=[kA], w=[kA])
                P.op("pool", lambda e, yk=yk: e.tensor_tensor(out=A_[:], in0=A_[:], in1=yk, op=ALU.mult), r=[kA, ("aT", kk_)], w=[kA])
                P.op("act", lambda e: e.activation(out=A_[:], in_=A_[:], func=AF.Sigmoid, scale=1.5957691216057308), r=[kA], w=[kA])
                P.op("pool", lambda e, yk=yk: e.tensor_tensor(out=yk, in0=A_[:], in1=yk, op=ALU.mult), r=[kA, ("aT", kk_)], w=[("aT", kk_)])
            for dc in range(8):
                wb = wup[dc % 2]
                P.dma(wb[:, 0], I["wglu"][dc], w=[("wup", dc % 2, 0)], q="pool")
                P.dma(wb[:, 1], I["wglu"][8 + dc], w=[("wup", dc % 2, 1)], q="pool")
                for th in range(2):
                    vps, vpk = bank((dc * 2 + th) % 4)
                    gps, gpk = bank(4 + (dc * 2 + th) % 4)
                    for gv, (ps, pk) in enumerate(((vps, vpk), (gps, gpk))):
                        for kk_ in range(8):
                            P.op("pe", lambda e, ps=ps, kk_=kk_, th=th, wb=wb, gv=gv: e.matmul(ps, lhsT=wb[:, gv, kk_, :], rhs=ysb[:, kk_, th * 512:(th + 1) * 512], start=(kk_ == 0), stop=(kk_ == 7)),
                                 r=[("wup", dc % 2, gv), ("aT", kk_)], w=[pk])
                    sl = slice(th * 512, (th + 1) * 512)
                    P.op("act", lambda e, gps=gps, sl=sl: e.activation(out=B_[:, sl], in_=gps, func=AF.Sigmoid), r=[gpk], w=[kB])
                    P.op("dve", lambda e, vps=vps, sl=sl: e.tensor_tensor(out=B_[:, sl], in0=vps, in1=B_[:, sl], op=ALU.mult), r=[vpk, kB], w=[kB])
                    P.op("dve", lambda e, dc=dc, sl=sl: e.scalar_tensor_tensor(out=xT[:, dc, sl], in0=B_[:, sl], scalar=mod[:, 16 + dc:17 + dc], in1=xT[:, dc, sl], op0=ALU.mult, op1=ALU.add),
                         r=[kB, "mod", ("xT", dc)], w=[("xT", dc)])

        eps_t = P.sb("eps_t", [128, 1])
        P.op("dve", lambda e: e.memset(eps_t[:], EPS), w=["eps"])
        halfpi = P.sb("halfpi", [128, 1])
        P.op("dve", lambda e: e.memset(halfpi[:], 1.5707963267948966), w=["halfpi"])


        for l in range(STAGE["layers"]):
            ada_layer(l)
            norm_mod(0)
            if STAGE["mixers"]:
                if l % 3 == 0:
                    attention(l)
                elif l % 3 == 1 and STAGE.get("hgrn", True):
                    hgrn(l)
                elif l % 3 == 2 and STAGE.get("ssm", True):
                    ssm(l)
            norm_mod(1)
            ffn(l)

        rms_stats()
        for k in range(8):
            P.op("dve", lambda e, k=k: e.scalar_tensor_tensor(out=xT[:, k, :], in0=xT[:, k, :], scalar=fg[:, k:k + 1], in1=rstd[:], op0=ALU.mult, op1=ALU.mult),
                 r=[("xT", k), "fg", "rstd"], w=[("xT", k)])
        for k in range(8):
            P.dma(O["y"][k * 128:(k + 1) * 128, :], xT[:, k, :], r=[("xT", k)], sem="yout")
        P.op("pool", lambda e: e.memset(rstd[:], 0.0), r=["rstd"], w=["rstd"])
        if not (STAGE["mixers"] and STAGE.get("hgrn", True) and STAGE["layers"] > 1):
            for i in range(8):
                P.dma(O["hg"][i * 8:(i + 1) * 8].rearrange("a p n -> p a n"), rstd[:].rearrange("p (a n) -> p a n", a=8), r=["rstd"], sem="sout")
        if not (STAGE["mixers"] and STAGE.get("ssm", True) and STAGE["layers"] > 2):
            for a_ in range(8):
                P.dma(O["ssm"][a_].rearrange("g p r -> g (p r)"), rstd[0:64, 0:128], r=["rstd"], sem="sout")
        P.wait_all_dma()
        P.emit()
    return nc

def _c(a):
    return np.ascontiguousarray(a, dtype=np.float32)


def prep_inputs(inp):
    g = {k: np.asarray(v) for k, v in inp.items()}
    sh = {}
    sh["ident"] = np.eye(128, dtype=np.float32)
    sh["ada_w"] = _c(g["ada_w"].reshape(DEPTH, 8, 128, 12, 512).transpose(0, 3, 2, 1, 4))
    sh["ada_b"] = _c(g["ada_b"].reshape(DEPTH, 48, 128).transpose(0, 2, 1))
    sh["ng"] = _c(np.stack([g["norm1_g"], g["norm2_g"]]).reshape(2, DEPTH, 8, 128).transpose(3, 0, 1, 2))
    sh["final_g"] = _c(g["final_g"].reshape(8, 128).T)
    sh["ffn_w_up"] = _c(g["ffn_w_up"].reshape(DEPTH, 8, 128, 2, NFC, 128).transpose(0, 4, 2, 3, 1, 5))
    sh["ffn_w_down"] = _c(g["ffn_w_down"].reshape(DEPTH, NFC, 128, 8, 128).transpose(0, 3, 2, 1, 4))
    sh["ffn_conv_w"] = _c(g["ffn_conv_w"].reshape(DEPTH, 3, 2 * NFC, 128).transpose(0, 3, 1, 2))
    sh["ffn_conv_b"] = _c(g["ffn_conv_b"].reshape(DEPTH, 2 * NFC, 128).transpose(0, 2, 1))
    wqkv = g["attn_wqkv"]
    sh["wq"] = _c(wqkv[:, :, 0:1024].reshape(2, 8, 128, 8, 128).transpose(0, 3, 2, 1, 4))
    wk = wqkv[:, :, 1024:1280].reshape(2, 8, 128, 4, 1, 64)
    sh["wk"] = _c(np.broadcast_to(wk, (2, 8, 128, 4, 2, 64)).reshape(2, 8, 128, 4, 128).transpose(0, 3, 2, 1, 4))
    sh["wkv"] = _c(wqkv[:, :, 1024:1536].reshape(2, 8, 128, 512).transpose(0, 2, 1, 3))
    sh["wo"] = _c(g["attn_wo"].reshape(2, 8, 128, 8, 128).transpose(0, 3, 2, 1, 4))
    sh["sink"] = _c(np.broadcast_to(g["attn_sink"][:, None, :], (2, 128, 16)))
    hw = g["hgrn_w_in"][0]
    sh["hw_in"] = _c(hw.reshape(8, 128, 5, 8, 128).transpose(3, 2, 1, 0, 4))
    sh["hwo"] = _c(g["hgrn_wo"][0].reshape(8, 128, 8, 128).transpose(2, 1, 0, 3))
    sh["hlb"] = _c(g["hgrn_lb"].reshape(4, 2, 8, 128).transpose(3, 0, 1, 2))
    sh["hgn"] = _c(g["hgrn_g_norm"][0].reshape(128, 1))
    ii = np.arange(128)
    same = (ii[:, None] // 32) == (ii[None, :] // 32)
    sh["hmask"] = _c(np.stack([same & (ii[:, None] <= ii[None, :]), same & (ii[:, None] >= ii[None, :])], axis=1))
    are, aim, ldt = g["ssm_a_re"][0], g["ssm_a_im"][0], g["ssm_log_dt"][0]
    bre, bim, cre, cim = g["ssm_b_re"][0], g["ssm_b_im"][0], g["ssm_c_re"][0], g["ssm_c_im"][0]
    sh["sBreR"] = _c(bre.reshape(2, 8, 8, 64, 16).transpose(0, 1, 2, 4, 3).reshape(2, 8, 128, 64))
    sh["sBimR"] = _c(bim.reshape(2, 8, 8, 64, 16).transpose(0, 1, 2, 4, 3).reshape(2, 8, 128, 64))
    sh["sAreR"] = _c(np.broadcast_to(are.reshape(2, 8, 8, 1, 64), (2, 8, 8, 16, 64)).reshape(2, 8, 128, 64))
    sh["sAimR"] = _c(np.broadcast_to(aim.reshape(2, 8, 8, 1, 64), (2, 8, 8, 16, 64)).reshape(2, 8, 128, 64))
    sh["sDtR"] = _c(np.broadcast_to(ldt.reshape(2, 8, 8, 1, 1), (2, 8, 8, 16, 1)).reshape(2, 8, 128, 1))
    sh["sAreQ"] = _c(are.reshape(2, 2, 32, 64).transpose(0, 1, 3, 2).reshape(2, 128, 32))
    sh["sAimQ"] = _c(aim.reshape(2, 2, 32, 64).transpose(0, 1, 3, 2).reshape(2, 128, 32))
    sh["sDtQ"] = _c(np.broadcast_to(ldt.reshape(2, 2, 1, 32), (2, 2, 64, 32)).reshape(2, 128, 32))
    sh["sCreQ"] = _c(cre.reshape(2, 2, 32, 16, 64).transpose(0, 1, 4, 2, 3).reshape(2, 128, 32, 16))
    sh["sCimQ"] = _c(cim.reshape(2, 2, 32, 16, 64).transpose(0, 1, 4, 2, 3).reshape(2, 128, 32, 16))
    sh["sD"] = _c(g["ssm_d"][0].reshape(8, 128).T)
    sm = np.zeros((128, 12), np.float32)
    for q in range(8):
        sm[q * 16:(q + 1) * 16, q] = 1.0
    sm[0:64, 8] = 1.0; sm[64:128, 9] = 1.0; sm[0:64, 10] = -1.0; sm[64:128, 11] = -1.0
    sh["smask"] = sm
    sh["wglu"] = _c(g["ssm_w_glu"][0].reshape(8, 128, 16, 128).transpose(2, 1, 0, 3))
    tt = np.arange(NT)
    inv = 1.0 / (10000.0 ** (np.arange(0, 32, 2, dtype=np.float32) / np.float32(32)))
    ar = (tt // 64).astype(np.float32)[:, None] * inv.astype(np.float32)
    ac = (tt % 64).astype(np.float32)[:, None] * inv.astype(np.float32)
    ang = np.concatenate([ar, ar, ac, ac], axis=-1).astype(np.float32)
    cosS = _c(np.concatenate([np.cos(ang).T] * 2, axis=0)); sinS = _c(np.concatenate([np.sin(ang).T] * 2, axis=0))
    rm = np.zeros((128, 128), np.float32)
    for m in range(128):
        d = m % 64
        if (d % 32) < 16:
            rm[m + 16, m] = -1.0
        else:
            rm[m - 16, m] = 1.0
    sh["rmat"] = rm
    kk = np.arange(128)[:, None]; qq = np.arange(128)[None, :]
    mS = np.zeros((128, 8, 2, 128), np.float32); mP = np.zeros((128, 8, 2, 128), np.float32)
    for i in range(8):
        if i >= 1:
            mS[:, i, 0, :] = (kk >= qq)
        if i <= 6:
            mS[:, i, 1, :] = (kk <= qq)
        mP[:, i, 0, :] = 1.0 if i % 2 == 1 else 0.0
        mP[:, i, 1, :] = 1.0 if i % 2 == 0 else 0.0
    if STAGE.get("kv_only"):
        for kk_ in ("wq", "wk", "wo", "sink", "rmat"):
            sh.pop(kk_, None)
    maps = []
    for c in range(8):
        m = dict(sh)
        if c < 4:
            xc = g["x_sample"][c]
            cond = g["c"][c]
            kp = 1.0
        else:
            xc = g["x_prompt"][4 * (c - 4):4 * (c - 4) + 4].reshape(NT, D)
            cond = g["c_ctx"]
            kp = 0.0
        if c < 4:
            ropec = cosS; ropes = sinS; m["amask"] = mS
            ck = g["cache_k"][c]
            m["ckd"] = _c(np.broadcast_to(ck[:, :, :, None, :], (2, 512, 4, 2, 64)).reshape(2, 512, 512))
            cv = g["cache_v"][c]
            vp = np.zeros((2, 512, 4, 2, 128), np.float32)
            vp[:, :, :, 0, 0:64] = cv; vp[:, :, :, 0, 64] = 1.0
            vp[:, :, :, 1, 64:128] = cv; vp[:, :, :, 1, 0] = 1.0
            m["cvp"] = vp.reshape(2, 512, 1024)
        else:
            ropec = np.ones((128, NT), np.float32); ropes = np.zeros((128, NT), np.float32); m["amask"] = mP
            m["ckd"] = np.zeros((2, 512, 512), np.float32); m["cvp"] = np.zeros((2, 512, 1024), np.float32)
        if STAGE.get("kv_only"):
            for kk_ in ("amask", "ckd", "cvp"):
                m.pop(kk_, None)
        if c < 4:
            st = g["state_ssm"][c, 0]
            m["sH0"] = _c(st.reshape(2, 2, 32, 64, 2).transpose(0, 1, 3, 2, 4).reshape(2, 128, 32, 2))
        else:
            m["sH0"] = np.zeros((2, 128, 32, 2), np.float32)
        m["hs0"] = _c(g["state_hgrn"][c, 0]) if c < 4 else np.zeros((2, 8, 128, 128), np.float32)
        m["x"] = _c(np.concatenate([xc.T, ropec, ropes], axis=0))
        m["cond"] = _c(cond.reshape(8, 128).T)
        m["keep"] = _c(np.stack([np.full(128, kp), np.full(128, kp - 1.0)], axis=1))
        maps.append(m)
    return maps


def assemble(results):
    f = lambda a: np.asarray(a, dtype=np.float32)
    ys = np.stack([f(results[c]["y"]).T for c in range(4)])
    yp = np.concatenate([f(results[c]["y"]).T.reshape(4, 256, D) for c in range(4, 8)])
    nk = np.concatenate([f(results[c]["nk"]).reshape(2, 4, 256, 4, 64).transpose(1, 0, 2, 3, 4) for c in range(4, 8)])
    nv = np.concatenate([f(results[c]["nv"]).reshape(2, 4, 256, 4, 64).transpose(1, 0, 2, 3, 4) for c in range(4, 8)])
    hg = np.concatenate([f(results[c]["hg"]).reshape(4, 1, 2, 8, 128, 128) for c in range(4, 8)])
    ssm = np.concatenate([f(results[c]["ssm"]).reshape(4, 1, 2, 64, 64, 2) for c in range(4, 8)])
    return (np.ascontiguousarray(yp), np.ascontiguousarray(ys), np.ascontiguousarray(nk), np.ascontiguousarray(nv),
            np.ascontiguousarray(hg), np.ascontiguousarray(ssm))


def kernel(**inputs):
    nc = build_program()
    maps = prep_inputs(inputs)
    res = run_bass_kernel_spmd(nc, maps, core_ids=list(range(8)))
    return assemble(res.results)
```

```python
import numpy as np
from concourse.bass_utils import run_bass_kernel_spmd

from contextlib import ExitStack
import concourse.bass as bass
import concourse.mybir as mybir

F32 = mybir.dt.float32
F32R = mybir.dt.float32r
BF16 = mybir.dt.bfloat16
AF = mybir.ActivationFunctionType
ALU = mybir.AluOpType
AX = mybir.AxisListType


class Prog:
    ENGS = ("pe", "act", "dve", "pool", "sp")

    def __init__(self, nc, es: ExitStack):
        self.nc = nc
        self.es = es
        self.recs = {e: [] for e in self.ENGS}
        self.cnt = {e: 0 for e in self.ENGS}
        self.known = {e: {} for e in self.ENGS}
        self.state = {}
        self.sems = {}
        self.dcnt = {}
        for e in self.ENGS:
            self.sems[("e", e)] = es.enter_context(nc.semaphore("sem_" + e))
        self.psn = 0

    def sb(self, name, shape, dt=F32):
        return self.es.enter_context(self.nc.sbuf_tensor("sb_" + name, list(shape), dt))

    def ps(self, name, shape, dt=F32):
        return self.es.enter_context(self.nc.psum_tensor(name, list(shape), dt))

    def dsem(self, name):
        k = ("d", name)
        if k not in self.sems:
            self.sems[k] = self.es.enter_context(self.nc.semaphore("dsem_" + name))
            self.dcnt[k] = 0
        return k

    def _st(self, k):
        s = self.state.get(k)
        if s is None:
            s = {"w": {}, "r": {}}
            self.state[k] = s
        return s

    def _deps(self, eng, r, w):
        deps = {}
        def add(d):
            if d is None:
                return
            sk, v = d
            if deps.get(sk, 0) < v:
                deps[sk] = v
        for k in r:
            st = self._st(k)
            for sk, v in st["w"].items():
                add((sk, v))
            if isinstance(k, tuple) and k[0] == "pq":
                for sk, v in st["r"].items():
                    if sk != ("e", eng):
                        add((sk, v))
        for k in w:
            s = self._st(k)
            for sk, v in s["w"].items():
                add((sk, v))
            for sk, v in s["r"].items():
                add((sk, v))
        waits = []
        kn = self.known[eng]
        for sk, v in deps.items():
            if eng == "pe" and sk == ("e", "pe"):
                continue
            if kn.get(sk, 0) < v:
                waits.append((sk, v))
                kn[sk] = v
        return waits

    def _commit(self, comp, r, w):
        sk, v = comp
        for k in w:
            self.state[k] = {"w": {sk: v}, "r": {}}
        for k in r:
            s = self._st(k)
            if s["r"].get(sk, 0) < v:
                s["r"][sk] = v

    def alias(self, dst, src):
        mw, mr = {}, {}
        for k in src:
            st = self._st(k)
            for sk, v in st["w"].items():
                mw[sk] = max(mw.get(sk, 0), v)
            for sk, v in st["r"].items():
                mr[sk] = max(mr.get(sk, 0), v)
        for k in dst:
            self.state[k] = {"w": dict(mw), "r": dict(mr)}

    def op(self, eng, fn, r=(), w=(), inc=True):
        inc = True
        waits = self._deps(eng, r, w)
        sk = ("e", eng)
        comp = (sk, self.cnt[eng] + 1)
        if inc:
            self.cnt[eng] += 1
        self.recs[eng].append((waits, fn, (sk, 1) if inc else None))
        self._commit(comp, r, w)

    def dma(self, out, in_, r=(), w=(), sem=None, q="sp", **kw):
        if sem is None:
            sem = "_".join(str(x) for x in (w[0] if isinstance(w[0], tuple) else (w[0],)))
        waits = self._deps(q, r, w)
        sk = self.dsem(sem)
        self.dcnt[sk] += 16
        comp = (sk, self.dcnt[sk])
        self.recs[q].append((waits, (lambda e, o=out, i=in_, kw=kw: e.dma_start(out=o, in_=i, **kw)), (sk, 16)))
        self._commit(comp, r, w)

    def wait_all_dma(self, q="sp"):
        waits = []
        for sk, v in self.dcnt.items():
            if v > 0 and self.known[q].get(sk, 0) < v:
                waits.append((sk, v))
                self.known[q][sk] = v
        self.recs[q].append((waits, None, None))

    def barrier_all(self):
        tgt = {("e", e): self.cnt[e] for e in self.ENGS if self.cnt[e] > 0}
        for sk, v in self.dcnt.items():
            if v > 0:
                tgt[sk] = v
        for e in self.ENGS:
            waits = []
            for sk, v in tgt.items():
                if sk == ("e", e):
                    continue
                if self.known[e].get(sk, 0) < v:
                    waits.append((sk, v))
                    self.known[e][sk] = v
            if waits:
                self.recs[e].append((waits, None, None))

    def emit(self):
        nc = self.nc
        sems = self.sems
        recs = self.recs

        def replay(name):
            def f(e):
                for waits, fn, inc in recs[name]:
                    for sk, v in waits:
                        e.wait_ge(sems[sk], v)
                    if fn is not None:
                        ins = fn(e)
                        if inc is not None:
                            ins.then_inc(sems[inc[0]], inc[1])
            return f

        with nc.Block() as block:
            block.tensor(replay("pe"))
            block.scalar(replay("act"))
            block.vector(replay("dve"))
            block.gpsimd(replay("pool"))
            block.sync(replay("sp"))

    def stats(self):
        return {e: len(self.recs[e]) for e in self.ENGS}

D = 1024
NT = 1024
DFF = 2816
NFC = 22
DEPTH = 4
EPS = 1e-6

STAGE = {"mixers": True, "layers": 4, "kv_only": False}


def build_program():
    nc = bass.Bass("TRN2", target_bir_lowering=False)

    def din(name, shape, dt=F32):
        return nc.dram_tensor(name, list(shape), dt, kind="ExternalInput").ap()

    def dout(name, shape, dt=F32):
        return nc.dram_tensor(name, list(shape), dt, kind="ExternalOutput").ap()

    I = {}
    I["x"] = din("x", [D + 256, NT])
    I["cond"] = din("cond", [128, 8])
    I["keep"] = din("keep", [128, 2])
    I["ident"] = din("ident", [128, 128])
    I["ada_w"] = din("ada_w", [DEPTH, 12, 128, 8, 512])
    I["ada_b"] = din("ada_b", [DEPTH, 128, 48])
    I["ng"] = din("ng", [128, 2, DEPTH, 8])
    I["ffn_w_up"] = din("ffn_w_up", [DEPTH, NFC, 128, 2, 8, 128])
    I["ffn_conv_w"] = din("ffn_conv_w", [DEPTH, 128, 3, 2 * NFC])
    I["ffn_conv_b"] = din("ffn_conv_b", [DEPTH, 128, 2 * NFC])
    I["ffn_w_down"] = din("ffn_w_down", [DEPTH, 8, 128, NFC, 128])
    I["final_g"] = din("final_g", [128, 8])
    I["wkv"] = din("wkv", [2, 128, 8, 512])
    I["hw_in"] = din("hw_in", [8, 5, 128, 8, 128])
    I["hwo"] = din("hwo", [8, 128, 8, 128])
    I["hlb"] = din("hlb", [128, 4, 2, 8])
    I["hgn"] = din("hgn", [128, 1])
    I["hs0"] = din("hs0", [2, 8, 128, 128])
    I["hmask"] = din("hmask", [128, 2, 128])
    I["sBreR"] = din("sBreR", [2, 8, 128, 64]); I["sBimR"] = din("sBimR", [2, 8, 128, 64])
    I["sAreR"] = din("sAreR", [2, 8, 128, 64]); I["sAimR"] = din("sAimR", [2, 8, 128, 64])
    I["sDtR"] = din("sDtR", [2, 8, 128, 1])
    I["sAreQ"] = din("sAreQ", [2, 128, 32]); I["sAimQ"] = din("sAimQ", [2, 128, 32]); I["sDtQ"] = din("sDtQ", [2, 128, 32])
    I["sCreQ"] = din("sCreQ", [2, 128, 32, 16]); I["sCimQ"] = din("sCimQ", [2, 128, 32, 16])
    I["sH0"] = din("sH0", [2, 128, 32, 2])
    I["sD"] = din("sD", [128, 8])
    I["smask"] = din("smask", [128, 12])
    I["wglu"] = din("wglu", [16, 128, 8, 128])
    if not STAGE.get("kv_only"):
        I["wq"] = din("wq", [2, 8, 128, 8, 128])
        I["wk"] = din("wk", [2, 4, 128, 8, 128])
        I["wo"] = din("wo", [2, 8, 128, 8, 128])
        I["sink"] = din("sink", [2, 128, 16])
        I["rmat"] = din("rmat", [128, 128])
        I["amask"] = din("amask", [128, 8, 2, 128])
        I["ckd"] = din("ckd", [2, 512, 512])
        I["cvp"] = din("cvp", [2, 512, 1024])
    O = {}
    O["y"] = dout("y", [D, NT])
    O["nk"] = dout("nk", [2, NT, 256])
    O["nv"] = dout("nv", [2, NT, 256])
    O["hg"] = dout("hg", [64, 128, 128])
    O["ssm"] = dout("ssm", [8, 64, 64, 2])
    if STAGE.get("ssm_dbg"):
        O["dbg"] = dout("dbg", [128, 4096 + 1024 + 512])
        O["dbg2"] = dout("dbg2", [128, 6 * 1024])
        O["dbg3"] = dout("dbg3", [128, 8 * 1024], BF16)

    with ExitStack() as es:
        P = Prog(nc, es)
        xT = P.sb("xT", [128, 10, NT])
        hT = P.sb("hT", [128, 8, NT], BF16)
        aT = P.sb("aT", [128, 12, NT], BF16)
        rstd = P.sb("rstd", [128, NT])
        tmpA = P.sb("tmpA", [128, NT])
        tmpB = P.sb("tmpB", [128, NT])
        cg = [P.sb("cg%d" % i, [128, NT]) for i in range(2)]
        cv = [P.sb("cv%d" % i, [128, NT]) for i in range(2)]
        sq = [P.sb("sq%d" % i, [128, NT], BF16) for i in range(2)]
        ident = P.sb("ident", [128, 128])
        ones_bf = P.sb("ones_bf", [128, 128], BF16)
        one_f = P.sb("one_f", [128, 1])
        keep = P.sb("keep", [128, 2])
        cond = P.sb("cond", [128, 8])
        s_bf = P.sb("s_bf", [128, 8], BF16)
        mod = P.sb("mod", [128, 48])
        adab = P.sb("adab", [128, 48])
        ng = P.sb("ng", [128, 2, DEPTH, 8])
        fg = P.sb("fg", [128, 8])
        AB = P.sb("AB", [128, 4, 8])
        cw = P.sb("cw", [128, 3, 2 * NFC])
        cb = P.sb("cb", [128, 2 * NFC])
        cwk = P.sb("cwk", [128, 2, 2 * NFC])
        wada = [P.sb("wada%d" % i, [128, 8, 512], BF16) for i in range(2)]
        wup = [P.sb("wup%d" % i, [128, 2, 8, 128], BF16) for i in range(2)]
        wdn = [P.sb("wdn%d" % i, [128, 11, 128], BF16) for i in range(2)]
        vlat = P.sb("vlat", [128, 8, 4, 2, 128], BF16)
        vctx = P.sb("vctx", [128, 4, 4, 2, 128], BF16)
        kctxT = P.sb("kctxT", [128, 4, 512], BF16)
        ckd = P.sb("ckd", [128, 4, 512], BF16)
        amask = P.sb("amask", [128, 8, 2, 128], BF16)
        rmat = P.sb("rmat", [128, 128], BF16)
        ident_bf = P.sb("ident_bf", [128, 128], BF16)
        qb = [P.sb("qb%d" % i, [128, 512], BF16) for i in range(2)]
        esink = P.sb("esink", [128, 16])
        ones_f = P.sb("ones_f", [128, 128])
        m32 = P.sb("m32", [128, NT])
        hlb = P.sb("hlb", [128, 4, 2, 8])
        lbp = P.sb("lbp", [128, 2, 2, 8])
        hgn = P.sb("hgn", [128, 1])
        hmask = P.sb("hmask", [128, 2, 128], BF16)
        Sf = [P.sb("Sf%d" % i, [128, 128]) for i in range(2)]
        Sb = [P.sb("Sb%d" % i, [128, 128], BF16) for i in range(2)]
        adec = [P.sb("adec%d" % i, [128, 32]) for i in range(2)]
        rowm = P.sb("rowm", [128, 4])
        ctmp = P.sb("ctmp", [128, 32])
        vtok = P.sb("vtok", [128, 8, 128], BF16)
        qhat = [P.sb("qhat%d" % i, [128, NT], BF16) for i in range(2)]
        ktil = [P.sb("ktil%d" % i, [128, NT], BF16) for i in range(2)]
        kdT = [P.sb("kdT%d" % i, [128, NT], BF16) for i in range(2)]
        attm = [P.sb("attm%d" % i, [128, 128], BF16) for i in range(2)]
        smask = P.sb("smask", [128, 12])
        sD = P.sb("sD", [128, 8])
        sH = P.sb("sH", [128, 32, 2])
        sHent = P.sb("sHent", [128, 4, 2])
        sHx = P.sb("sHx", [128, 4, 2])
        pq = [P.ps("pq%d" % i, [128, 1024]) for i in range(4)]

        def bank(i):
            return pq[i // 2][:, (i % 2) * 512:(i % 2) * 512 + 512], ("pq", i)

        P.dma(ident[:], I["ident"], w=["ident"])
        P.dma(keep[:], I["keep"], w=["keep"])
        P.dma(cond[:], I["cond"], w=["cond"])
        P.dma(ng[:], I["ng"], w=["ng"])
        P.dma(fg[:], I["final_g"], w=["fg"])
        P.op("dve", lambda e: e.memset(ones_bf[:], 1.0), w=["ones_bf"])
        P.op("dve", lambda e: e.memset(one_f[:], 1.0), w=["one_f"])
        P.op("dve", lambda e: e.memset(ones_f[:], 1.0), w=["ones_f"])
        P.op("dve", lambda e: e.tensor_copy(out=ident_bf[:], in_=ident[:]), r=["ident"], w=["ident_bf"])
        if not STAGE.get("kv_only"):
            P.dma(rmat[:], I["rmat"], w=["rmat"], q="pool")
            P.dma(amask[:], I["amask"], w=["amask"], q="pool")
        P.op("pool", lambda e: e.memset(m32[:], 1.0), w=["m32"])
        P.op("pool", lambda e: e.memset(m32[:, 0:NT:32], 0.0), w=["m32"])
        P.op("pool", lambda e: e.memset(rowm[:], 0.0), w=["rowm"])
        for c4 in range(3):
            P.op("pool", lambda e, c4=c4: e.memset(rowm[c4 * 32:(c4 + 1) * 32, c4:c4 + 1], 1.0), w=["rowm"])
        P.op("pool", lambda e: e.memset(rowm[96:128, 3:4], 1.0), w=["rowm"])
        P.dma(hlb[:], I["hlb"], w=["hlb"])
        P.dma(hgn[:], I["hgn"], w=["hgn"])
        P.dma(hmask[:], I["hmask"], w=["hmask"], q="pool")
        P.op("dve", lambda e: e.memset(vlat[:], 0.0), w=["vlat"])
        P.op("dve", lambda e: e.memset(vlat[:, :, :, 0, 64:65], 1.0), w=["vlat"])
        P.op("dve", lambda e: e.memset(vlat[:, :, :, 1, 0:1], 1.0), w=["vlat"])
        P.op("act", lambda e: e.activation(out=s_bf[:], in_=cond[:], func=AF.Silu), r=["cond"], w=["s_bf"])

        P.dma(xT[:], I["x"].rearrange("(k p) t -> p k t", p=128), w=[("xT", k) for k in range(10)], sem="xin")

        def ada_layer(l):
            P.dma(adab[:], I["ada_b"][l], w=["adab"])
            ps, pk = bank(4)
            for n in range(12):
                wb = wada[n % 2]
                P.dma(wb[:], I["ada_w"][l, n],
                      w=[("wada", n % 2)], q="pool")
                for c4 in range(4):
                    c = n * 4 + c4
                    for k in range(8):
                        P.op("pe", lambda e, ps=ps, k=k, wb=wb, c=c, c4=c4: e.matmul(ps[:, c:c + 1], lhsT=wb[:, k, c4 * 128:(c4 + 1) * 128], rhs=s_bf[:, k:k + 1], start=(k == 0), stop=(k == 7)),
                             r=[("wada", n % 2), "s_bf"], w=[pk])
            P.op("dve", lambda e, ps=ps: e.tensor_tensor(out=mod[:], in0=ps[:, 0:48], in1=adab[:], op=ALU.add), r=[pk, "adab"], w=["mod"])
            for j in range(2):
                P.op("dve", lambda e, j=j: e.scalar_tensor_tensor(out=AB[:, 2 * j, :], in0=mod[:, (3 * j + 1) * 8:(3 * j + 2) * 8], scalar=1.0,
                                                                 in1=ng[:, j, l, :], op0=ALU.add, op1=ALU.mult),
                     r=["mod", "ng"], w=["AB"])
                P.op("dve", lambda e, j=j: e.tensor_copy(out=AB[:, 2 * j + 1, :], in_=mod[:, (3 * j) * 8:(3 * j + 1) * 8]), r=["mod"], w=["AB"])

        def rms_stats():
            b0, k0 = bank(6)
            b1, k1 = bank(7)
            for k in range(8):
                s = sq[k % 2]
                P.op("act", lambda e, s=s, k=k: e.activation(out=s[:], in_=xT[:, k, :], func=AF.Square), r=[("xT", k)], w=[("sq", k % 2)])
                for th, (b, bk) in enumerate(((b0, k0), (b1, k1))):
                    P.op("pe", lambda e, b=b, s=s, th=th, k=k: e.matmul(b, lhsT=ones_bf[:], rhs=s[:, th * 512:(th + 1) * 512], start=(k == 0), stop=(k == 7)),
                         r=[("sq", k % 2), "ones_bf"], w=[bk], inc=True)
            for th, (b, bk) in enumerate(((b0, k0), (b1, k1))):
                P.op("act", lambda e, b=b, th=th: e.activation(out=tmpA[:, th * 512:(th + 1) * 512], in_=b, func=AF.Ln, scale=1.0 / D, bias=eps_t[:, 0:1]),
                     r=[bk, "eps"], w=["tmpA"])
            P.op("act", lambda e: e.activation(out=rstd[:], in_=tmpA[:], func=AF.Exp, scale=-0.5), r=["tmpA"], w=["rstd"])

        def norm_mod(j):
            rms_stats()
            for k in range(8):
                t = tmpA if k % 2 == 0 else tmpB
                tk = "tmpA" if k % 2 == 0 else "tmpB"
                P.op("dve", lambda e, t=t, k=k: e.scalar_tensor_tensor(out=t[:], in0=xT[:, k, :], scalar=AB[:, 2 * j, k:k + 1], in1=rstd[:], op0=ALU.mult, op1=ALU.mult),
                     r=[("xT", k), "AB", "rstd"], w=[tk])
                P.op("act", lambda e, t=t, k=k: e.activation(out=hT[:, k, :], in_=t[:], func=AF.Identity, bias=AB[:, 2 * j + 1, k:k + 1], scale=1.0),
                     r=[tk, "AB"], w=[("hT", k)])

        def ffn(l):
            P.dma(cw[:], I["ffn_conv_w"][l], w=["cw"])
            P.dma(cb[:], I["ffn_conv_b"][l], w=["cb"])
            for jj, j in enumerate((0, 2)):
                P.op("dve", lambda e, jj=jj, j=j: e.tensor_scalar(out=cwk[:, jj, :], in0=cw[:, j, :], scalar1=keep[:, 1:2], scalar2=None, op0=ALU.mult),
                     r=["cw", "keep"], w=["cwk"])
            for grp in range(2):
                for fc in range(grp * 11, grp * 11 + 11):
                    wb = wup[fc % 2]
                    P.dma(wb[:], I["ffn_w_up"][l, fc], w=[("wup", fc % 2, 0), ("wup", fc % 2, 1)], q="pool")
                    outs = []
                    for gv in range(2):
                        pt = pq[(fc % 2) * 2 + gv]
                        pks = [("pq", ((fc % 2) * 2 + gv) * 2 + th) for th in range(2)]
                        for th in range(2):
                            for k in range(8):
                                P.op("pe", lambda e, pt=pt, th=th, k=k, gv=gv, wb=wb: e.matmul(pt[:, th * 512:(th + 1) * 512], lhsT=wb[:, gv, k, :], rhs=hT[:, k, th * 512:(th + 1) * 512],
                                                                                    start=(k == 0), stop=(k == 7)),
                                     r=[("wup", fc % 2, gv), ("hT", k)], w=[pks[th]], inc=(k == 7))
                        c = (cg if gv == 0 else cv)[fc % 2]
                        ck = ("cg" if gv == 0 else "cv", fc % 2)
                        col = gv * NFC + fc
                        P.op("act", lambda e, c=c, pt=pt, col=col: e.activation(out=c[:], in_=pt[:], func=AF.Identity, scale=cw[:, 1, col:col + 1], bias=cb[:, col:col + 1]),
                             r=pks + ["cw", "cb"], w=[ck])
                        P.op("dve", lambda e, c=c, pt=pt, col=col: e.scalar_tensor_tensor(out=c[:, 1:NT], in0=pt[:, 0:NT - 1], scalar=cw[:, 0, col:col + 1], in1=c[:, 1:NT], op0=ALU.mult, op1=ALU.add),
                             r=pks + ["cw", ck], w=[ck])
                        P.op("dve", lambda e, c=c, pt=pt, col=col: e.scalar_tensor_tensor(out=c[:, 0:NT - 1], in0=pt[:, 1:NT], scalar=cw[:, 2, col:col + 1], in1=c[:, 0:NT - 1], op0=ALU.mult, op1=ALU.add),
                             r=pks + ["cw", ck], w=[ck])
                        P.op("dve", lambda e, c=c, pt=pt, col=col: e.scalar_tensor_tensor(out=c[:, 256:NT:256], in0=pt[:, 255:NT - 1:256], scalar=cwk[:, 0, col:col + 1], in1=c[:, 256:NT:256], op0=ALU.mult, op1=ALU.add),
                             r=pks + ["cwk", ck], w=[ck])
                        P.op("dve", lambda e, c=c, pt=pt, col=col: e.scalar_tensor_tensor(out=c[:, 255:NT - 1:256], in0=pt[:, 256:NT:256], scalar=cwk[:, 1, col:col + 1], in1=c[:, 255:NT - 1:256], op0=ALU.mult, op1=ALU.add),
                             r=pks + ["cwk", ck], w=[ck])
                        outs.append((c, ck))
                    (cgt, cgk), (cvt, cvk) = outs
                    P.op("act", lambda e, cgt=cgt: e.activation(out=cgt[:], in_=cgt[:], func=AF.Silu), r=[cgk], w=[cgk])
                    P.op("dve", lambda e, cgt=cgt, cvt=cvt, fc=fc: e.tensor_tensor(out=aT[:, fc % 11, :], in0=cgt[:], in1=cvt[:], op=ALU.mult), r=[cgk, cvk], w=[("aT", fc % 11)])
                for dc in range(8):
                    wb = wdn[dc % 2]
                    P.dma(wb[:], I["ffn_w_down"][l, dc][:, grp * 11:grp * 11 + 11, :],
                          w=[("wdn", dc % 2)], q="pool")
                    for th in range(2):
                        ps, pk = bank((dc * 2 + th) % 8)
                        for fc in range(11):
                            P.op("pe", lambda e, ps=ps, fc=fc, th=th, wb=wb: e.matmul(ps, lhsT=wb[:, fc, :], rhs=aT[:, fc, th * 512:(th + 1) * 512], start=(fc == 0), stop=(fc == 10)),
                                 r=[("wdn", dc % 2), ("aT", fc)], w=[pk])
                        P.op("dve", lambda e, ps=ps, dc=dc, th=th: e.scalar_tensor_tensor(out=xT[:, dc, th * 512:(th + 1) * 512], in0=ps, scalar=mod[:, 40 + dc:41 + dc],
                                                                                         in1=xT[:, dc, th * 512:(th + 1) * 512], op0=ALU.mult, op1=ALU.add),
                             r=[pk, "mod", ("xT", dc)], w=[("xT", dc)])

        def attention(l):
            j = l // 3
            cosT, sinT = xT[:, 8, :], xT[:, 9, :]
            t1, t2 = cv[0], cv[1]
            if STAGE.get("kv_only"):
                wkv = wada[0]
                P.dma(wkv[:], I["wkv"][j], w=[("wada", 0)], q="pool")
                for tb in range(8):
                    ps, pk = bank(tb % 4)
                    for k in range(8):
                        P.op("pe", lambda e, ps=ps, k=k, tb=tb: e.matmul(ps, lhsT=hT[:, k, tb * 128:(tb + 1) * 128], rhs=wkv[:, k, :], start=(k == 0), stop=(k == 7)),
                             r=[("wada", 0), ("hT", k)], w=[pk])
                    kvt = tmpA if tb % 2 == 0 else tmpB
                    kvk = "tmpA" if tb % 2 == 0 else "tmpB"
                    P.op("act", lambda e, ps=ps, kvt=kvt: e.copy(out=kvt[:, 0:512], in_=ps), r=[pk], w=[kvk])
                    P.dma(O["nk"][j, tb * 128:(tb + 1) * 128, :], kvt[:, 0:256], r=[kvk], sem="kvout%d" % (tb % 2))
                    P.dma(O["nv"][j, tb * 128:(tb + 1) * 128, :], kvt[:, 256:512], r=[kvk], sem="kvout%d" % (tb % 2))
                return
            P.dma(esink[:], I["sink"][j], w=["esink"])
            P.op("act", lambda e: e.activation(out=esink[:], in_=esink[:], func=AF.Exp), r=["esink"], w=["esink"])
            if STAGE.get("attn_upto", 9) < 1:
                return
            P.dma(ckd[:], I["ckd"][j].rearrange("(kb p) n -> p kb n", p=128), w=["ckd"], q="pool")
            P.dma(vctx[:].rearrange("p kb g v n -> p kb (g v n)"), I["cvp"][j].rearrange("(kb p) n -> p kb n", p=128), w=["vctx"], q="pool")
            for g in range(4):
                ps, pk = bank(g)
                for kb in range(4):
                    P.op("pe", lambda e, ps=ps, kb=kb, g=g: e.matmul(ps[:, kb * 128:(kb + 1) * 128], lhsT=ckd[:, kb, g * 128:(g + 1) * 128], rhs=ident_bf[:], start=True, stop=True),
                         r=["ckd", "ident_bf"], w=[pk])
                P.op("act", lambda e, ps=ps, g=g: e.copy(out=kctxT[:, g, :], in_=ps), r=[pk], w=["kctxT"])
            if STAGE.get("attn_upto", 9) < 2:
                return
            def proj_rope(wsrc, dst, dkey, idx):
                wb = wup[idx % 2]
                P.dma(wb[:, 0], wsrc, w=[("wup", idx % 2, 0)], q="pool")
                for th in range(2):
                    ps, pk = bank((idx * 2 + th) % 4)
                    rps, rpk = bank(4 + (idx * 2 + th) % 2)
                    for k in range(8):
                        P.op("pe", lambda e, ps=ps, k=k, th=th, wb=wb: e.matmul(ps, lhsT=wb[:, 0, k, :], rhs=hT[:, k, th * 512:(th + 1) * 512], start=(k == 0), stop=(k == 7)),
                             r=[("wup", idx % 2, 0), ("hT", k)], w=[pk])
                    q_ = qb[th]
                    if STAGE.get("pr", 9) < 1:
                        continue
                    P.op("act", lambda e, ps=ps, q_=q_: e.copy(out=q_[:], in_=ps), r=[pk], w=[("qb", th)])
                    if STAGE.get("pr", 9) < 2:
                        continue
                    P.op("pe", lambda e, rps=rps, q_=q_: e.matmul(rps, lhsT=rmat[:], rhs=q_[:], start=True, stop=True), r=[("qb", th), "rmat"], w=[rpk])
                    sl = slice(th * 512, (th + 1) * 512)
                    if STAGE.get("pr", 9) < 3 or idx >= STAGE.get("pridx", 99):
                        continue
                    P.op("dve", lambda e, ps=ps, sl=sl: e.scalar_tensor_tensor(out=t1[:, sl], in0=ps, scalar=1.0, in1=cosT[:, sl], op0=ALU.mult, op1=ALU.mult), r=[pk, ("xT", 8), ("qb", th)], w=[("cv", 0)])
                    if STAGE.get("pr", 9) < 4:
                        continue
                    P.op("dve", lambda e, rps=rps, sl=sl: e.scalar_tensor_tensor(out=t2[:, sl], in0=rps, scalar=1.0, in1=sinT[:, sl], op0=ALU.mult, op1=ALU.mult), r=[rpk, ("xT", 9)], w=[("cv", 1)])
                    if STAGE.get("pr", 9) < 5:
                        continue
                    P.op("dve", lambda e, sl=sl, dst=dst: e.tensor_tensor(out=dst[:, sl], in0=t1[:, sl], in1=t2[:, sl], op=ALU.add), r=[("cv", 0), ("cv", 1)], w=[dkey])
            for qc in range(8):
                proj_rope(I["wq"][j, qc], aT[:, qc, :], ("aT", qc), qc)
            for g in range(4):
                proj_rope(I["wk"][j, g], aT[:, 8 + g, :], ("aT", 8 + g), 8 + g)
            if STAGE.get("attn_upto", 9) < 3:
                return
            wkv = wada[0]
            P.dma(wkv[:], I["wkv"][j], w=[("wada", 0)], q="pool")
            for tb in range(8):
                ps, pk = bank(tb % 4)
                for k in range(8):
                    P.op("pe", lambda e, ps=ps, k=k, tb=tb: e.matmul(ps, lhsT=hT[:, k, tb * 128:(tb + 1) * 128], rhs=wkv[:, k, :], start=(k == 0), stop=(k == 7)),
                         r=[("wada", 0), ("hT", k)], w=[pk])
                kvt = tmpA if tb % 2 == 0 else tmpB
                kvk = "tmpA" if tb % 2 == 0 else "tmpB"
                P.op("act", lambda e, ps=ps, kvt=kvt: e.copy(out=kvt[:, 0:512], in_=ps), r=[pk], w=[kvk])
                P.dma(O["nk"][j, tb * 128:(tb + 1) * 128, :], kvt[:, 0:256], r=[kvk], sem="kvout%d" % (tb % 2))
                P.dma(O["nv"][j, tb * 128:(tb + 1) * 128, :], kvt[:, 256:512], r=[kvk], sem="kvout%d" % (tb % 2))
                P.op("dve", lambda e, ps=ps, tb=tb: e.tensor_copy(out=vlat[:, tb, :, 0, 0:64], in_=ps[:, 256:512].rearrange("p (g d) -> p g d", g=4)), r=[pk], w=["vlat"])
                P.op("dve", lambda e, ps=ps, tb=tb: e.tensor_copy(out=vlat[:, tb, :, 1, 64:128], in_=ps[:, 256:512].rearrange("p (g d) -> p g d", g=4)), r=[pk], w=["vlat"])
            if STAGE.get("attn_upto", 9) < 4:
                return
            dsb = tmpA[:].rearrange("p (a n) -> p a n", a=2)
            osb = [cv[0][:, 0:512], cv[1][:, 0:512]]
            tb_bf = tmpB[:].bitcast(BF16)
            ebuf = [tb_bf[:, i * 512:(i + 1) * 512] for i in range(4)]
            P.alias(["dsb"], ["tmpA"])
            P.alias([("osb", 0)], [("cv", 0)])
            P.alias([("osb", 1)], [("cv", 1)])
            P.alias([("ebuf", i) for i in range(4)], ["tmpB"])
            sc = 0
            tail_pending = []
            for h in range(STAGE.get("nheads", 16)):
                g, qc, pb, var = h // 4, h // 2, (h % 2) * 64, h % 2
                dr = 64 if var == 0 else 0
                qh = aT[pb:pb + 64, qc, :]
                kh = aT[pb:pb + 64, 8 + g, :]
                kch = kctxT[pb:pb + 64, g, :]
                for th in range(2):
                    it = h * 2 + th
                    OP, opk = bank(4 + it % 2)
                    jbs = [jb for jb in range(8) if max(jb - 1, 4 * th) <= min(jb + 1, 4 * th + 3)]
                    blocks = [("c", kb) for kb in range(4)] + [("l", jb) for jb in jbs]
                    LA = 2
                    pend = []

                    def emit_S(kind, ix):
                        nonlocal sc
                        ps, pk = bank(sc % 4); eb = ebuf[sc % 4]; ek = ("ebuf", sc % 4); sc += 1
                        if kind == "c":
                            kb = ix
                            P.op("pe", lambda e, ps=ps, kb=kb, th=th, kch=kch, qh=qh: e.matmul(ps, lhsT=kch[:, kb * 128:(kb + 1) * 128], rhs=qh[:, th * 512:(th + 1) * 512], start=True, stop=True),
                                 r=["kctxT", ("aT", qc)], w=[pk])
                            P.op("act", lambda e, ps=ps, eb=eb: e.activation(out=eb[:], in_=ps, func=AF.Exp, scale=0.125), r=[pk], w=[ek])
                            return (kind, ix, eb, ek, 0, 512)
                        jb = ix
                        i0_ = max(jb - 1, 4 * th); i1_ = min(jb + 1, 4 * th + 3)
                        n = (i1_ - i0_ + 1) * 128
                        P.op("pe", lambda e, ps=ps, jb=jb, i0_=i0_, n=n, kh=kh, qh=qh: e.matmul(ps[:, 0:n], lhsT=kh[:, jb * 128:(jb + 1) * 128], rhs=qh[:, i0_ * 128:i0_ * 128 + n], start=True, stop=True),
                             r=[("aT", 8 + g), ("aT", qc)], w=[pk])
                        P.op("act", lambda e, ps=ps, eb=eb, n=n: e.activation(out=eb[:, 0:n], in_=ps[:, 0:n], func=AF.Exp, scale=0.125), r=[pk], w=[ek])
                        for i in range(i0_, i1_ + 1):
                            if i == jb:
                                continue
                            off = 0 if i == jb + 1 else 1
                            c0 = (i - i0_) * 128
                            P.op("dve", lambda e, eb=eb, c0=c0, i=i, off=off: e.tensor_tensor(out=eb[:, c0:c0 + 128], in0=eb[:, c0:c0 + 128], in1=amask[:, i, off, :], op=ALU.mult),
                                 r=[ek, "amask"], w=[ek])
                        return (kind, ix, eb, ek, (i0_ - 4 * th) * 128, n)

                    def emit_PV(st, is_first, is_last):
                        kind, ix, eb, ek, o0, n = st
                        if kind == "c":
                            P.op("pe", lambda e, OP=OP, eb=eb, ix=ix, g=g, var=var, is_first=is_first: e.matmul(OP, lhsT=vctx[:, ix, g, var, :], rhs=eb[:], start=is_first, stop=False),
                                 r=["vctx", ek], w=[opk])
                        else:
                            P.op("pe", lambda e, OP=OP, eb=eb, ix=ix, g=g, var=var, o0=o0, n=n, is_last=is_last: e.matmul(OP[:, o0:o0 + n], lhsT=vlat[:, ix, g, var, :], rhs=eb[:, 0:n], start=False, stop=is_last),
                                 r=["vlat", ek], w=[opk])

                    nb = len(blocks)
                    for i in range(nb + LA):
                        if i < nb:
                            pend.append(emit_S(*blocks[i]))
                        if i == nb - 1 and tail_pending:
                            for t_ in tail_pending:
                                t_()
                            tail_pending.clear()
                        if i - LA >= 0:
                            emit_PV(pend[i - LA], i - LA == 0, i - LA == nb - 1)
                    P.op("dve", lambda e, OP=OP, dr=dr, h=h: e.tensor_scalar(out=dsb[dr:dr + 1, 0, :], in0=OP[dr:dr + 1, :], scalar1=esink[dr:dr + 1, h:h + 1], scalar2=None, op0=ALU.add),
                         r=[opk, "esink"], w=["dsb"])
                    P.op("act", lambda e, dr=dr: e.activation(out=dsb[dr:dr + 1, 0, :], in_=dsb[dr:dr + 1, 0, :], func=AF.Ln), r=["dsb"], w=["dsb"])
                    P.op("act", lambda e, dr=dr: e.activation(out=dsb[dr:dr + 1, 1, :], in_=dsb[dr:dr + 1, 0, :], func=AF.Exp, scale=-1.0), r=["dsb"], w=["dsb"])
                    def _tail(it=it, OP=OP, opk=opk, dr=dr, pb=pb, qc=qc, th=th):
                        BC, bck = bank(6 + it % 2)
                        P.op("pe", lambda e, BC=BC, dr=dr: e.matmul(BC, lhsT=ones_f[dr:dr + 1, :], rhs=dsb[dr:dr + 1, 1, :], start=True, stop=True), r=["dsb", "ones_f"], w=[bck])
                        ob = osb[it % 2]; obk = ("osb", it % 2)
                        P.op("act", lambda e, OP=OP, ob=ob, pb=pb: e.copy(out=ob[pb:pb + 64, :], in_=OP[pb:pb + 64, :]), r=[opk], w=[obk])
                        P.op("dve", lambda e, BC=BC, ob=ob, pb=pb, qc=qc, th=th: e.tensor_tensor(out=aT[pb:pb + 64, qc, th * 512:(th + 1) * 512], in0=ob[pb:pb + 64, :], in1=BC[pb:pb + 64, :], op=ALU.mult),
                             r=[obk, bck], w=[("aT", qc)])
                    tail_pending.append(_tail)
            for t_ in tail_pending:
                t_()
            tail_pending.clear()
            P.alias(["tmpA"], ["dsb"])
            P.alias([("cv", 0)], [("osb", 0)])
            P.alias([("cv", 1)], [("osb", 1)])
            P.alias(["tmpB"], [("ebuf", i) for i in range(4)])
            for dc in range(8):
                wb = wup[dc % 2]
                P.dma(wb[:, 0], I["wo"][j, dc], w=[("wup", dc % 2, 0)], q="pool")
                for th in range(2):
                    ps, pk = bank((dc * 2 + th) % 4)
                    for k in range(8):
                        P.op("pe", lambda e, ps=ps, k=k, th=th, wb=wb: e.matmul(ps, lhsT=wb[:, 0, k, :], rhs=aT[:, k, th * 512:(th + 1) * 512], start=(k == 0), stop=(k == 7)),
                             r=[("wup", dc % 2, 0), ("aT", k)], w=[pk])
                    P.op("dve", lambda e, ps=ps, dc=dc, th=th: e.scalar_tensor_tensor(out=xT[:, dc, th * 512:(th + 1) * 512], in0=ps, scalar=mod[:, 16 + dc:17 + dc],
                                                                                     in1=xT[:, dc, th * 512:(th + 1) * 512], op0=ALU.mult, op1=ALU.add),
                         r=[pk, "mod", ("xT", dc)], w=[("xT", dc)])

        def hgrn(l):
            CH = 32
            NCH = NT // CH
            P.op("act", lambda e: e.activation(out=hlb[:], in_=hlb[:], func=AF.Exp), r=["hlb"], w=["hlb"])
            P.op("dve", lambda e: e.tensor_tensor(out=lbp[:, 1], in0=hlb[:, 0], in1=hlb[:, 1], op=ALU.add), r=["hlb"], w=["lbp"])
            P.op("dve", lambda e: e.tensor_tensor(out=lbp[:, 1], in0=lbp[:, 1], in1=hlb[:, 2], op=ALU.add), r=["hlb", "lbp"], w=["lbp"])
            P.op("dve", lambda e: e.tensor_tensor(out=lbp[:, 1], in0=lbp[:, 1], in1=hlb[:, 3], op=ALU.add), r=["hlb", "lbp"], w=["lbp"])
            P.op("dve", lambda e: e.reciprocal(out=lbp[:, 1], in_=lbp[:, 1]), r=["lbp"], w=["lbp"])
            P.op("dve", lambda e: e.tensor_tensor(out=lbp[:, 0], in0=lbp[:, 1], in1=hlb[:, 1], op=ALU.mult), r=["hlb", "lbp"], w=["lbp"])
            P.op("dve", lambda e: e.tensor_scalar(out=lbp[:, 1], in0=lbp[:, 0], scalar1=-1.0, scalar2=1.0, op0=ALU.mult, op1=ALU.add), r=["lbp"], w=["lbp"])
            bufQ, bufF, bufK, bufC = cg[0], cg[1], cv[0], cv[1]
            kQ, kF, kK, kC = ("cg", 0), ("cg", 1), ("cv", 0), ("cv", 1)
            oacc = rstd
            kdtok = [vlat[:].rearrange("p a g v n -> p (a g v n)")[:, d * 4096:(d + 1) * 4096].rearrange("p (b c n) -> p b c n", b=8, c=4) for d in range(2)]
            P.alias([("kdtok", 0), ("kdtok", 1)], ["vlat"])
            for h in range(8):
                P.dma(wup[0][:], I["hw_in"][h, 0:2].rearrange("a p k n -> p a k n"), w=[("wup", 0, 0), ("wup", 0, 1)], q="pool")
                P.dma(wup[1][:], I["hw_in"][h, 2:4].rearrange("a p k n -> p a k n"), w=[("wup", 1, 0), ("wup", 1, 1)], q="pool")
                P.dma(wdn[0][:, 0:8, :], I["hw_in"][h, 4], w=[("wdn", 0)], q="pool")

                def proj(wap, wkeys, bi):
                    outs = []
                    for th in range(2):
                        ps, pk = bank(bi * 2 + th)
                        for kk in range(8):
                            P.op("pe", lambda e, ps=ps, kk=kk, th=th, wap=wap: e.matmul(ps, lhsT=wap[:, kk, :], rhs=hT[:, kk, th * 512:(th + 1) * 512], start=(kk == 0), stop=(kk == 7)),
                                 r=list(wkeys) + [("hT", kk)], w=[pk])
                        outs.append((ps, pk))
                    return outs
                for th, (ps, pk) in enumerate(proj(wup[0][:, 0], [("wup", 0, 0)], 0)):
                    P.op("act", lambda e, ps=ps, th=th: e.copy(out=bufQ[:, th * 512:(th + 1) * 512], in_=ps), r=[pk], w=[kQ])
                for blk in range(8):
                    ps, pk = bank(2 + blk % 2)
                    for kk in range(8):
                        P.op("pe", lambda e, ps=ps, kk=kk, blk=blk: e.matmul(ps[:, 0:128], lhsT=hT[:, kk, blk * 128:(blk + 1) * 128], rhs=wup[0][:, 1, kk, :], start=(kk == 0), stop=(kk == 7)),
                             r=[("wup", 0, 1), ("hT", kk)], w=[pk])
                    P.op("act", lambda e, ps=ps, blk=blk: e.copy(out=vtok[:, blk, :], in_=ps[:, 0:128]), r=[pk], w=["vtok"])
                for d in range(2):
                    for th, (ps, pk) in enumerate(proj(wup[1][:, d], [("wup", 1, d)], 2 + d)):
                        P.op("act", lambda e, ps=ps, th=th: e.activation(out=bufF[:, th * 512:(th + 1) * 512], in_=ps, func=AF.Sigmoid), r=[pk], w=[kF])
                    P.op("dve", lambda e, d=d, h=h: e.tensor_scalar(out=bufF[:], in0=bufF[:], scalar1=lbp[:, 1, d, h:h + 1], scalar2=lbp[:, 0, d, h:h + 1], op0=ALU.mult, op1=ALU.add),
                         r=[kF, "lbp"], w=[kF])
                    P.op("dve", lambda e: e.tensor_scalar(out=bufK[:], in0=bufF[:], scalar1=-1.0, scalar2=1.0, op0=ALU.mult, op1=ALU.add), r=[kF], w=[kK])
                    P.op("act", lambda e: e.activation(out=bufF[:], in_=bufF[:], func=AF.Ln), r=[kF], w=[kF])
                    P.op("dve", lambda e: e.tensor_tensor_scan(out=bufC[:], data0=m32[:], data1=bufF[:], initial=0.0, op0=ALU.mult, op1=ALU.add), r=["m32", kF], w=[kC])
                    if d == 1:
                        P.op("dve", lambda e: e.scalar_tensor_tensor(out=tmpA[:], in0=bufC[:], scalar=-1.0, in1=bufF[:], op0=ALU.mult, op1=ALU.add), r=[kC, kF], w=["tmpA"])
                        P.op("act", lambda e: e.copy(out=ctmp[:], in_=bufC[:, CH - 1:NT:CH]), r=[kC], w=["ctmp"])
                        P.op("dve", lambda e: e.tensor_tensor(out=bufC[:].rearrange("p (c t) -> p c t", t=CH), in0=tmpA[:].rearrange("p (c t) -> p c t", t=CH),
                                                              in1=ctmp[:].unsqueeze(2).to_broadcast([128, NCH, CH]), op=ALU.add),
                             r=["tmpA", "ctmp"], w=[kC])
                        ctot = bufC[:, 0:NT:CH]
                    else:
                        ctot = bufC[:, CH - 1:NT:CH]
                    P.op("act", lambda e, d=d, ctot=ctot: e.activation(out=adec[d][:], in_=ctot, func=AF.Exp), r=[kC], w=[("adec", d)])
                    P.op("act", lambda e: e.activation(out=tmpA[:], in_=bufC[:], func=AF.Exp), r=[kC], w=["tmpA"])
                    P.op("dve", lambda e, d=d: e.tensor_tensor(out=qhat[d][:], in0=bufQ[:], in1=tmpA[:], op=ALU.mult), r=[kQ, "tmpA"], w=[("qhat", d)])
                    P.op("dve", lambda e: e.tensor_scalar(out=tmpB[:], in0=bufC[:], scalar1=-1.0, scalar2=85.0, op0=ALU.mult, op1=ALU.min), r=[kC], w=["tmpB"])
                    P.op("act", lambda e: e.activation(out=tmpB[:], in_=tmpB[:], func=AF.Exp), r=["tmpB"], w=["tmpB"])
                    P.op("dve", lambda e: e.tensor_tensor(out=tmpB[:], in0=tmpB[:], in1=bufK[:], op=ALU.mult), r=["tmpB", kK], w=["tmpB"])
                    P.op("act", lambda e, d=d: e.copy(out=ktil[d][:], in_=tmpB[:]), r=["tmpB"], w=[("ktil", d)])
                    P.op("dve", lambda e, d=d: e.tensor_tensor(out=kdT[d][:].rearrange("p (c t) -> p c t", t=CH), in0=tmpB[:].rearrange("p (c t) -> p c t", t=CH),
                                                                in1=adec[d][:].unsqueeze(2).to_broadcast([128, NCH, CH]), op=ALU.mult),
                         r=["tmpB", ("adec", d)], w=[("kdT", d)])
                    for blk in range(8):
                        ps, pk = bank(6 + blk % 2)
                        pst = ps.bitcast(BF16)
                        P.op("pe", lambda e, pst=pst, blk=blk, d=d: e.transpose(pst[:, 0:128], kdT[d][:, blk * 128:(blk + 1) * 128], ident_bf[:]), r=[("kdT", d), "ident_bf"], w=[pk])
                        for c4 in range(4):
                            P.op("act", lambda e, pst=pst, blk=blk, c4=c4, d=d: e.activation(out=kdtok[d][:, blk, c4, :], in_=pst[:, 0:128], func=AF.Identity, scale=rowm[:, c4:c4 + 1]),
                                 r=[pk, "rowm"], w=[("kdtok", d)])
                    P.dma(Sf[d][:], I["hs0"][d, h], w=[("Sf", d)])
                    P.op("act", lambda e, d=d: e.copy(out=Sb[d][:], in_=Sf[d][:]), r=[("Sf", d)], w=[("Sb", d)])
                P.op("pool", lambda e: e.memset(oacc[:], 0.0), r=["rstd"], w=["rstd"])
                for bi_ in range(8):
                    ctxs = []
                    for d in range(2):
                        blk = bi_ if d == 0 else 7 - bi_
                        bsl = slice(blk * 128, (blk + 1) * 128)
                        aps, apk = bank(0 + d * 2)
                        ops_, opk = bank(1 + d * 2)
                        dps, dpk = bank(4 + d)
                        P.op("pe", lambda e, aps=aps, d=d, bsl=bsl: e.matmul(aps[:, 0:128], lhsT=ktil[d][:, bsl], rhs=qhat[d][:, bsl], start=True, stop=True),
                             r=[("ktil", d), ("qhat", d)], w=[apk])
                        am = attm[d]
                        P.op("dve", lambda e, aps=aps, am=am, d=d: e.tensor_tensor(out=am[:], in0=aps[:, 0:128], in1=hmask[:, d, :], op=ALU.mult), r=[apk, "hmask"], w=[("attm", d)])
                        for c4 in range(4):
                            P.op("pe", lambda e, dps=dps, c4=c4, blk=blk, d=d: e.matmul(dps[:, c4 * 128:(c4 + 1) * 128], lhsT=kdtok[d][:, blk, c4, :], rhs=vtok[:, blk, :], start=True, stop=True),
                                 r=[("kdtok", d), "vtok"], w=[dpk])
                        P.op("pe", lambda e, ops_=ops_, am=am, blk=blk: e.matmul(ops_[:, 0:128], lhsT=vtok[:, blk, :], rhs=am[:], start=True, stop=False),
                             r=["vtok", ("attm", d)], w=[opk])
                        ctxs.append((blk, bsl, ops_, opk, dps, dpk))
                    for step in range(4):
                        for d in range(2):
                            blk, bsl, ops_, opk, dps, dpk = ctxs[d]
                            c4 = step if d == 0 else 3 - step
                            last = (step == 3)
                            ch = blk * 4 + c4
                            csl = slice(ch * CH, (ch + 1) * CH)
                            P.op("pe", lambda e, ops_=ops_, c4=c4, csl=csl, d=d, last=last: e.matmul(ops_[:, c4 * CH:(c4 + 1) * CH], lhsT=Sb[d][:], rhs=qhat[d][:, csl], start=False, stop=last),
                                 r=[("Sb", d), ("qhat", d)], w=[opk])
                            P.op("dve", lambda e, dps=dps, c4=c4, ch=ch, d=d: e.scalar_tensor_tensor(out=Sf[d][:], in0=Sf[d][:], scalar=adec[d][:, ch:ch + 1], in1=dps[:, c4 * 128:(c4 + 1) * 128], op0=ALU.mult, op1=ALU.add),
                                 r=[("Sf", d), ("adec", d), dpk], w=[("Sf", d)])
                            seg_end = (ch % 8 == 7) if d == 0 else (ch % 8 == 0)
                            if seg_end:
                                seg = ch // 8
                                P.dma(O["hg"][(seg * 2 + d) * 8 + h], Sf[d][:], r=[("Sf", d)], sem="hgout%d" % d)
                                P.op("dve", lambda e, d=d: e.tensor_scalar(out=Sf[d][:], in0=Sf[d][:], scalar1=keep[:, 0:1], scalar2=None, op0=ALU.mult), r=[("Sf", d), "keep"], w=[("Sf", d)])
                            P.op("act", lambda e, d=d: e.copy(out=Sb[d][:], in_=Sf[d][:]), r=[("Sf", d)], w=[("Sb", d)])
                    for d in range(2):
                        blk, bsl, ops_, opk, dps, dpk = ctxs[d]
                        P.op("dve", lambda e, ops_=ops_, bsl=bsl: e.tensor_tensor(out=oacc[:, bsl], in0=ops_[:, 0:128], in1=oacc[:, bsl], op=ALU.add), r=[opk, "rstd"], w=["rstd"])
                P.op("act", lambda e: e.activation(out=sq[0][:], in_=oacc[:], func=AF.Square), r=["rstd"], w=[("sq", 0)])
                for th in range(2):
                    ps, pk = bank(6 + th)
                    P.op("pe", lambda e, ps=ps, th=th: e.matmul(ps, lhsT=ones_bf[:], rhs=sq[0][:, th * 512:(th + 1) * 512], start=True, stop=True), r=[("sq", 0), "ones_bf"], w=[pk])
                    P.op("act", lambda e, ps=ps, th=th: e.activation(out=tmpA[:, th * 512:(th + 1) * 512], in_=ps, func=AF.Ln, scale=1.0 / 128, bias=eps_t[:, 0:1]), r=[pk, "eps"], w=["tmpA"])
                P.op("act", lambda e: e.activation(out=tmpA[:], in_=tmpA[:], func=AF.Exp, scale=-0.5), r=["tmpA"], w=["tmpA"])
                P.op("dve", lambda e: e.scalar_tensor_tensor(out=tmpA[:], in0=oacc[:], scalar=hgn[:, 0:1], in1=tmpA[:], op0=ALU.mult, op1=ALU.mult), r=["rstd", "hgn", "tmpA"], w=["tmpA"])
                for th, (ps, pk) in enumerate(proj(wdn[0][:, 0:8, :], [("wdn", 0)], 2)):
                    P.op("act", lambda e, ps=ps, th=th: e.activation(out=tmpB[:, th * 512:(th + 1) * 512], in_=ps, func=AF.Silu), r=[pk], w=["tmpB"])
                P.op("dve", lambda e, h=h: e.tensor_tensor(out=aT[:, h, :], in0=tmpA[:], in1=tmpB[:], op=ALU.mult), r=["tmpA", "tmpB"], w=[("aT", h)])
            P.alias(["vlat"], [("kdtok", 0), ("kdtok", 1)])
            P.op("dve", lambda e: e.memset(vlat[:], 0.0), w=["vlat"])
            P.op("dve", lambda e: e.memset(vlat[:, :, :, 0, 64:65], 1.0), w=["vlat"])
            P.op("dve", lambda e: e.memset(vlat[:, :, :, 1, 0:1], 1.0), w=["vlat"])
            for dc in range(8):
                wb = wup[dc % 2]
                P.dma(wb[:, 0], I["hwo"][dc], w=[("wup", dc % 2, 0)], q="pool")
                for th in range(2):
                    ps, pk = bank((dc * 2 + th) % 4)
                    for kk in range(8):
                        P.op("pe", lambda e, ps=ps, kk=kk, th=th, wb=wb: e.matmul(ps, lhsT=wb[:, 0, kk, :], rhs=aT[:, kk, th * 512:(th + 1) * 512], start=(kk == 0), stop=(kk == 7)),
                             r=[("wup", dc % 2, 0), ("aT", kk)], w=[pk])
                    P.op("dve", lambda e, ps=ps, dc=dc, th=th: e.scalar_tensor_tensor(out=xT[:, dc, th * 512:(th + 1) * 512], in0=ps, scalar=mod[:, 16 + dc:17 + dc],
                                                                                     in1=xT[:, dc, th * 512:(th + 1) * 512], op0=ALU.mult, op1=ALU.add),
                         r=[pk, "mod", ("xT", dc)], w=[("xT", dc)])

        def ssm(l):
            SEG = 256
            rs32 = rstd[:]
            sr_ = [rs32[:, i * 64:(i + 1) * 64] for i in range(16)]
            sq32 = sq[0][:].bitcast(F32)
            sq_ = [sq32[:, i * 32:(i + 1) * 32] for i in range(14)]
            sCq = [sq[1][:, i * 512:(i + 1) * 512].rearrange("p (g i) -> p g i", g=32) for i in range(2)]
            P.alias(["srr"], ["rstd"])
            P.alias(["sqq"], [("sq", 0)])
            P.alias(["sCq"], [("sq", 1)])
            m256 = m32
            P.op("pool", lambda e: e.memset(m256[:], 1.0), r=["m32"], w=["m32"])
            P.op("pool", lambda e: e.memset(m256[:, 0:NT:256], 0.0), r=["m32"], w=["m32"])
            P.dma(smask[:], I["smask"], w=["smask"])
            P.dma(sD[:], I["sD"], w=["sD"])
            TT = lambda e, o, a, b, op: e.tensor_tensor(out=o, in0=a, in1=b, op=op)

            def vop(eng, o, a, b, op, r, w):
                P.op(eng, lambda e, o=o, a=a, b=b, op=op: e.tensor_tensor(out=o, in0=a, in1=b, op=op), r=r, w=w)

            def cmul(eng, ore, oim, are_, aim_, bre, bim, t1, t2, r, w, tk):
                vop(eng, t1, are_, bre, ALU.mult, r, [tk[0]])
                vop(eng, t2, aim_, bim, ALU.mult, r, [tk[1]])
                vop(eng, ore, t1, t2, ALU.subtract, [tk[0], tk[1]], w)
                vop(eng, t1, are_, bim, ALU.mult, r, [tk[0]])
                vop(eng, t2, aim_, bre, ALU.mult, r, [tk[1]])
                vop(eng, oim, t1, t2, ALU.add, [tk[0], tk[1]], w)

            def lam_params(are_ap, aim_ap, dt_scalar_or_ap, S, key, n, per_part_dt, need_inv=True, eng="dve"):
                arec, th, mag, c, s_, t1, t2, imag = S[0], S[1], S[2], S[3], S[4], S[5], S[6], S[7]
                K = [key]
                P.op(eng, lambda e: e.tensor_scalar(out=arec[:], in0=are_ap, scalar1=-1e-4, scalar2=None, op0=ALU.min), r=K, w=K)
                if per_part_dt:
                    dtx = S[8]
                    P.op("act", lambda e: e.activation(out=dtx[:, 0:1], in_=dt_scalar_or_ap, func=AF.Exp), r=K, w=K)
                    P.op(eng, lambda e: e.tensor_scalar(out=th[:], in0=aim_ap, scalar1=dtx[:, 0:1], scalar2=None, op0=ALU.mult), r=K, w=K)
                    P.op(eng, lambda e: e.tensor_scalar(out=mag[:], in0=arec[:], scalar1=dtx[:, 0:1], scalar2=None, op0=ALU.mult), r=K, w=K)
                else:
                    dtx = S[8]
                    P.op("act", lambda e: e.activation(out=dtx[:], in_=dt_scalar_or_ap, func=AF.Exp), r=K, w=K)
                    vop(eng, th[:], aim_ap, dtx[:], ALU.mult, K, K)
                    vop(eng, mag[:], arec[:], dtx[:], ALU.mult, K, K)
                P.op("act", lambda e: e.activation(out=imag[:], in_=mag[:], func=AF.Exp, scale=-1.0), r=K, w=K)
                P.op("act", lambda e: e.activation(out=mag[:], in_=mag[:], func=AF.Exp), r=K, w=K)
                P.op("act", lambda e: e.activation(out=s_[:], in_=th[:], func=AF.Sin, scale=1.0 / 64), r=K, w=K)
                P.op("act", lambda e: e.activation(out=c[:], in_=th[:], func=AF.Sin, scale=1.0 / 64, bias=halfpi[:, 0:1]), r=K + ["halfpi"], w=K)
                for _ in range(6):
                    vop(eng, t1[:], c[:], s_[:], ALU.mult, K, K)
                    vop(eng, c[:], c[:], c[:], ALU.mult, K, K)
                    vop(eng, s_[:], s_[:], s_[:], ALU.mult, K, K)
                    vop(eng, c[:], c[:], s_[:], ALU.subtract, K, K)
                    P.op(eng, lambda e: e.tensor_scalar(out=s_[:], in0=t1[:], scalar1=2.0, scalar2=None, op0=ALU.mult), r=K, w=K)
                L1re, L1im, Lm1re, Lm1im = S[9], S[10], S[11], S[12]
                vop(eng, L1re[:], mag[:], c[:], ALU.mult, K, K)
                vop(eng, L1im[:], mag[:], s_[:], ALU.mult, K, K)
                if need_inv:
                    vop(eng, Lm1re[:], imag[:], c[:], ALU.mult, K, K)
                    vop(eng, Lm1im[:], imag[:], s_[:], ALU.mult, K, K)
                    P.op(eng, lambda e: e.tensor_scalar(out=Lm1im[:], in0=Lm1im[:], scalar1=-1.0, scalar2=None, op0=ALU.mult), r=K, w=K)
                return dict(L1re=L1re, L1im=L1im, Lm1re=Lm1re, Lm1im=Lm1im, are=arec)

            A_, B_, C_, D_, Gr, Gi = cg[0], cg[1], cv[0], cv[1], tmpA, tmpB
            kA, kB, kC2, kD, kGr, kGi = ("cg", 0), ("cg", 1), ("cv", 0), ("cv", 1), "tmpA", "tmpB"
            vl = vlat[:].rearrange("p a g v n -> p (a g v n)").bitcast(F32)
            Tre_all = vl[:, 0:2048].rearrange("p (g t) -> p g t", g=8)
            Tim_all = vl[:, 2048:4096].rearrange("p (g t) -> p g t", g=8)
            Tp_re, Tm_re, Tp_im, Tm_im = Tre_all[:, 0:4], Tre_all[:, 4:8], Tim_all[:, 0:4], Tim_all[:, 4:8]
            P.alias(["stab"], ["vlat"])
            vc = vctx[:].rearrange("p a g v n -> p (a g v n)")
            W1pad = vc[:, 0:4096].rearrange("p (q r n) -> p q r n", q=16, r=2)
            kcf = kctxT[:].rearrange("p a n -> p (a n)")
            Ewpad = kcf[:, 0:2048].rearrange("p (q r n) -> p q r n", q=8, r=2)
            P.alias(["W1pad"], ["vctx"])
            P.alias(["Ewpad"], ["kctxT"])
            Hre_b = qhat[0][:].rearrange("p (g t) -> p g t", g=4)
            Him_b = qhat[1][:].rearrange("p (g t) -> p g t", g=4)
            ysb = aT

            def _body():
              for d in range(2):
                  P.dma(sq_[0], I["sAreQ"][d], w=["sqq"])
                  P.dma(sq_[1], I["sAimQ"][d], w=["sqq"])
                  P.dma(sq_[13], I["sDtQ"][d], w=["sqq"])
                  P.dma(sCq[0], I["sCreQ"][d], w=["sCq"], q="pool")
                  P.dma(sCq[1], I["sCimQ"][d], w=["sCq"], q="pool")
                  P.dma(sH[:], I["sH0"][d], w=["sH"])
                  Q = lam_params(sq_[0], sq_[1], sq_[13], sq_[2:13] + [sq_[0], sq_[1]], "sqq", 32, False)
                  for s in range(8):
                      k0, k1 = s // 2, 4 + s // 2
                      g8b = (4 * s) % 8
                      P.op("pool", lambda e: e.memset(kcf[:, 0:2048], 0.0), r=["Ewpad"], w=["Ewpad"])
                      if s % 2 == 0:
                          P.op("pool", lambda e: e.memset(vc[:], 0.0), r=["W1pad"], w=["W1pad"])
                      if s % 2 == 0:
                          for half, kk_ in enumerate((k0, k1)):
                              Rk = ["srr"]
                              P.dma(sr_[0], I["sAreR"][d, kk_], w=Rk)
                              P.dma(sr_[1], I["sAimR"][d, kk_], w=Rk)
                              P.dma(sr_[13][:, 0:1], I["sDtR"][d, kk_], w=Rk)
                              P.dma(sr_[14], I["sBreR"][d, kk_], w=Rk)
                              P.dma(sr_[15], I["sBimR"][d, kk_], w=Rk)
                              R_ = lam_params(sr_[0], sr_[1], sr_[13][:, 0:1], sr_[2:13] + [sr_[0], sr_[0]], "srr", 64, True, need_inv=False)
                              nre, den, cre, cim, t1, t2 = sr_[3], sr_[4], sr_[5], sr_[6], sr_[7], sr_[8]
                              aim_ = sr_[1]
                              P.op("dve", lambda e, R_=R_: e.tensor_scalar(out=nre[:], in0=R_["L1re"][:], scalar1=-1.0, scalar2=None, op0=ALU.add), r=Rk, w=Rk)
                              vop("dve", den[:], R_["are"][:], R_["are"][:], ALU.mult, Rk, Rk)
                              vop("dve", t1[:], aim_[:], aim_[:], ALU.mult, Rk, Rk)
                              vop("dve", den[:], den[:], t1[:], ALU.add, Rk, Rk)
                              P.op("dve", lambda e: e.reciprocal(out=den[:], in_=den[:]), r=Rk, w=Rk)
                              vop("dve", t1[:], nre[:], R_["are"][:], ALU.mult, Rk, Rk)
                              vop("dve", t2[:], R_["L1im"][:], aim_[:], ALU.mult, Rk, Rk)
                              vop("dve", cre[:], t1[:], t2[:], ALU.add, Rk, Rk)
                              vop("dve", cre[:], cre[:], den[:], ALU.mult, Rk, Rk)
                              vop("dve", t1[:], R_["L1im"][:], R_["are"][:], ALU.mult, Rk, Rk)
                              vop("dve", t2[:], nre[:], aim_[:], ALU.mult, Rk, Rk)
                              vop("dve", cim[:], t1[:], t2[:], ALU.subtract, Rk, Rk)
                              vop("dve", cim[:], cim[:], den[:], ALU.mult, Rk, Rk)
                              wre, wim = sr_[9], sr_[10]
                              cmul("dve", wre[:], wim[:], cre[:], cim[:], sr_[14][:], sr_[15][:], t1[:], t2[:], Rk, Rk, ["srr", "srr"])
                              for g8 in range(8):
                                  for ri, wsrc in enumerate((wre, wim)):
                                      P.op("act", lambda e, half=half, ri=ri, wsrc=wsrc, g8=g8: e.activation(out=W1pad[:, half * 8 + g8, ri, half * 64:(half + 1) * 64], in_=wsrc[:], func=AF.Identity, scale=smask[:, g8:g8 + 1]),
                                           r=Rk + ["smask"], w=["W1pad"])
                      for half, kk_ in enumerate((k0, k1)):
                          for q in range(4):
                              g8 = g8b + q
                              gq = 4 * s + q
                              P.op("act", lambda e, q=q, half=half, g8=g8, gq=gq: e.activation(out=Ewpad[:, half * 4 + q, 0, g8 * 16:(g8 + 1) * 16], in_=sCq[0][:, gq, :], func=AF.Identity, scale=smask[:, 8 + half:9 + half]),
                                   r=["sCq", "smask"], w=["Ewpad"])
                              P.op("act", lambda e, q=q, half=half, g8=g8, gq=gq: e.activation(out=Ewpad[:, half * 4 + q, 1, g8 * 16:(g8 + 1) * 16], in_=sCq[1][:, gq, :], func=AF.Identity, scale=smask[:, 10 + half:11 + half]),
                                   r=["sCq", "smask"], w=["Ewpad"])
                      gsl = slice(4 * s, 4 * s + 4)
                      i0 = 0 if d == 0 else SEG - 1
                      for (Tre, Tim, bre_, bim_) in ((Tp_re, Tp_im, Q["L1re"], Q["L1im"]), (Tm_re, Tm_im, Q["Lm1re"], Q["Lm1im"])):
                          P.op("dve", lambda e, Tre=Tre, bre_=bre_, i0=i0, gsl=gsl: e.tensor_copy(out=Tre[:, :, i0:i0 + 1], in_=bre_[:, gsl].unsqueeze(2)), r=["sqq"], w=["stab"])
                          P.op("dve", lambda e, Tim=Tim, bim_=bim_, i0=i0, gsl=gsl: e.tensor_copy(out=Tim[:, :, i0:i0 + 1], in_=bim_[:, gsl].unsqueeze(2)), r=["sqq"], w=["stab"])
                      L = 1
                      while L < SEG:
                          if d == 0:
                              src = slice(0, L); dst = slice(L, 2 * L); piv = L - 1
                          else:
                              src = slice(SEG - L, SEG); dst = slice(SEG - 2 * L, SEG - L); piv = SEG - L
                          zr = Tre_all[:, :, piv:piv + 1].to_broadcast([128, 8, L])
                          zi = Tim_all[:, :, piv:piv + 1].to_broadcast([128, 8, L])
                          cmul("dve", Tre_all[:, :, dst], Tim_all[:, :, dst], Tre_all[:, :, src], Tim_all[:, :, src], zr, zi,
                               A_[:, 0:8 * L].rearrange("p (g t) -> p g t", g=8), B_[:, 0:8 * L].rearrange("p (g t) -> p g t", g=8), ["stab"], ["stab"], [kA, kB])
                          L *= 2
                      P.op("dve", lambda e, gsl=gsl: e.tensor_copy(out=sHent[:], in_=sH[:, gsl, :]), r=["sH"], w=["sHent"])
                      segs = range(4) if d == 0 else range(3, -1, -1)
                      Sre, Sim = pq[0], pq[1]
                      skr = [("pq", 0), ("pq", 1)]; ski = [("pq", 2), ("pq", 3)]

                      def emit_S(seg_):
                          tsl_ = slice(seg_ * SEG, (seg_ + 1) * SEG)
                          for ri, (St, sk) in enumerate(((Sre, skr), (Sim, ski))):
                              for q in range(4):
                                  for half, kk_ in enumerate((k0, k1)):
                                      P.op("pe", lambda e, St=St, q=q, half=half, kk_=kk_, ri=ri, tsl_=tsl_, g8b=g8b: e.matmul(St[:, q * SEG:(q + 1) * SEG], lhsT=W1pad[:, half * 8 + g8b + q, ri, :], rhs=hT[:, kk_, tsl_], start=(half == 0), stop=(half == 1)),
                                           r=["W1pad", ("hT", kk_)], w=[sk[q // 2]])

                      def emit_yevac(pend_):
                          for (yp_, ypk, kk_, tsl_, first_) in pend_:
                              if first_:
                                  P.op("dve", lambda e, yp_=yp_, kk_=kk_, tsl_=tsl_: e.scalar_tensor_tensor(out=ysb[:, kk_, tsl_], in0=hT[:, kk_, tsl_], scalar=sD[:, kk_:kk_ + 1], in1=yp_[:, 0:SEG], op0=ALU.mult, op1=ALU.add),
                                       r=[ypk, ("hT", kk_), "sD"], w=[("aT", kk_)])
                              else:
                                  P.op("dve", lambda e, yp_=yp_, kk_=kk_, tsl_=tsl_: e.tensor_tensor(out=ysb[:, kk_, tsl_], in0=yp_[:, 0:SEG], in1=ysb[:, kk_, tsl_], op=ALU.add),
                                       r=[ypk, ("aT", kk_)], w=[("aT", kk_)])
                      segl = list(segs)
                      ypend = []
                      emit_S(segl[0])
                      for si_, seg in enumerate(segl):
                          tsl = slice(seg * SEG, (seg + 1) * SEG)
                          S3r = Sre[:].rearrange("p (g t) -> p g t", g=4); S3i = Sim[:].rearrange("p (g t) -> p g t", g=4)
                          A3, B3, C3, D3 = [x[:].rearrange("p (g t) -> p g t", g=4) for x in (A_, B_, C_, D_)]
                          G3r, G3i = Gr[:].rearrange("p (g t) -> p g t", g=4), Gi[:].rearrange("p (g t) -> p g t", g=4)
                          vop("dve", A3, S3r, Tm_re, ALU.mult, skr + ["stab"], [kA])
                          vop("dve", B3, S3i, Tm_im, ALU.mult, ski + ["stab"], [kB])
                          vop("dve", A3, A3, B3, ALU.subtract, [kA, kB], [kA])
                          vop("dve", C3, S3i, Tm_re, ALU.mult, ski + ["stab"], [kC2])
                          vop("dve", D3, S3r, Tm_im, ALU.mult, skr + ["stab"], [kD])
                          vop("dve", C3, C3, D3, ALU.add, [kC2, kD], [kC2])
                          if si_ + 1 < len(segl):
                              emit_S(segl[si_ + 1])
                          emit_yevac(ypend); ypend = []
                          for (src_, dstt, ks, kd_) in ((A_, Gr, kA, kGr), (C_, Gi, kC2, kGi)):
                              P.op("dve", lambda e, src_=src_, dstt=dstt: e.tensor_tensor_scan(out=dstt[:], data0=m256[:], data1=src_[:], initial=0.0, op0=ALU.mult, op1=ALU.add), r=["m32", ks], w=[kd_])
                              if d == 1:
                                  s3 = src_[:].rearrange("p (g t) -> p g t", g=4); d3 = dstt[:].rearrange("p (g t) -> p g t", g=4)
                                  P.op("act", lambda e, dstt=dstt: e.copy(out=ctmp[:, 0:4], in_=dstt[:, SEG - 1:NT:SEG]), r=[kd_], w=["ctmp"])
                                  P.op("dve", lambda e, s3=s3, d3=d3: e.tensor_tensor(out=d3, in0=s3, in1=d3, op=ALU.subtract), r=[ks, kd_], w=[kd_])
                                  P.op("dve", lambda e, d3=d3: e.tensor_tensor(out=d3, in0=d3, in1=ctmp[:, 0:4].unsqueeze(2).to_broadcast([128, 4, SEG]), op=ALU.add), r=[kd_, "ctmp"], w=[kd_])
                          vop("dve", G3r, G3r, sHent[:, :, 0:1].to_broadcast([128, 4, SEG]), ALU.add, [kGr, "sHent"], [kGr])
                          vop("dve", G3i, G3i, sHent[:, :, 1:2].to_broadcast([128, 4, SEG]), ALU.add, [kGi, "sHent"], [kGi])
                          vop("dve", A3, G3r, Tp_re, ALU.mult, [kGr, "stab"], [kA])
                          vop("dve", B3, G3i, Tp_im, ALU.mult, [kGi, "stab"], [kB])
                          vop("dve", Hre_b, A3, B3, ALU.subtract, [kA, kB], [("qhat", 0)])
                          vop("dve", C3, G3r, Tp_im, ALU.mult, [kGr, "stab"], [kC2])
                          vop("dve", D3, G3i, Tp_re, ALU.mult, [kGi, "stab"], [kD])
                          vop("dve", Him_b, C3, D3, ALU.add, [kC2, kD], [("qhat", 1)])
                          xi = SEG - 1 if d == 0 else 0
                          vop("dve", sHx[:, :, 0:1], A3[:, :, xi:xi + 1], B3[:, :, xi:xi + 1], ALU.subtract, [kA, kB], ["sHx"])
                          vop("dve", sHx[:, :, 1:2], C3[:, :, xi:xi + 1], D3[:, :, xi:xi + 1], ALU.add, [kC2, kD], ["sHx"])
                          for half in range(2):
                              P.dma(O["ssm"][seg * 2 + d, half * 32 + 4 * s: half * 32 + 4 * s + 4].rearrange("g p r -> p g r"), sHx[half * 64:(half + 1) * 64, :, :], r=["sHx"], sem="ssmout")
                          P.op("dve", lambda e: e.tensor_scalar(out=sHent[:], in0=sHx[:], scalar1=keep[:, 0:1], scalar2=None, op0=ALU.mult), r=["sHx", "keep"], w=["sHent"])
                          if STAGE.get("ssm_dbg"):
                              P.dma(O["dbg2"][:, 0:1024], A_[:], r=[kA], sem="dbg")
                              P.dma(O["dbg2"][:, 1024:2048], B_[:], r=[kB], sem="dbg")
                              P.dma(O["dbg2"][:, 2048:3072], C_[:], r=[kC2], sem="dbg")
                              P.dma(O["dbg2"][:, 3072:4096], D_[:], r=[kD], sem="dbg")
                              P.dma(O["dbg2"][:, 4096:5120], Gr[:], r=[kGr], sem="dbg")
                              P.dma(O["dbg2"][:, 5120:6144], Gi[:], r=[kGi], sem="dbg")
                              P.dma(O["dbg3"], hT[:].rearrange("p k t -> p (k t)"), r=[("hT", kk) for kk in range(8)], sem="dbg")
                              P.dma(O["dbg"][:, 0:4096], vl, r=["stab"], sem="dbg")
                              P.dma(O["dbg"][:, 4096:5120], rs32, r=["srr"], sem="dbg")
                              P.dma(O["dbg"][:, 5120:5632], sq32, r=["sqq"], sem="dbg")
                              raise StopIteration
                          for half, kk_ in enumerate((k0, k1)):
                              yp_, ypk = bank(4 + 2 * (si_ % 2) + half)
                              n = 0
                              for q in range(4):
                                  for ri, Hb in enumerate((Hre_b, Him_b)):
                                      P.op("pe", lambda e, yp_=yp_, q=q, half=half, ri=ri, Hb=Hb, n=n: e.matmul(yp_[:, 0:SEG], lhsT=Ewpad[:, half * 4 + q, ri, :], rhs=Hb[:, q, :], start=(n == 0), stop=(n == 7)),
                                           r=["Ewpad", ("qhat", ri)], w=[ypk])
                                      n += 1
                              ypend.append((yp_, ypk, kk_, tsl, (d == 0 and s % 2 == 0)))
                      emit_yevac(ypend); ypend = []

            try:
                _body()
            except StopIteration:
                pass
            P.alias(["vlat"], ["stab"])
            P.alias(["vctx"], ["W1pad"])
            P.alias(["kctxT"], ["Ewpad"])
            P.alias(["rstd"], ["srr"])
            P.alias([("sq", 0)], ["sqq"])
            P.alias([("sq", 1)], ["sCq"])
            P.op("dve", lambda e: e.memset(vlat[:], 0.0), w=["vlat"])
            P.op("dve", lambda e: e.memset(vlat[:, :, :, 0, 64:65], 1.0), w=["vlat"])
            P.op("dve", lambda e: e.memset(vlat[:, :, :, 1, 0:1], 1.0), w=["vlat"])
            for kk_ in range(8):
                yk = ysb[:, kk_, :]
                P.op("dve", lambda e, yk=yk: e.tensor_tensor(out=A_[:], in0=yk, in1=yk, op=ALU.mult), r=[("aT", kk_)], w=[kA])
                P.op("dve", lambda e: e.tensor_scalar(out=A_[:], in0=A_[:], scalar1=0.044715, scalar2=1.0, op0=ALU.mult, op1=ALU.add), r=[kA], w=[kA])
                P.op("pool", lambda e, yk=yk: e.tensor_tensor(out=A_[:], in0=A_[:], in1=yk, op=ALU.mult), r=[kA, ("aT", kk_)], w=[kA])
                P.op("act", lambda e: e.activation(out=A_[:], in_=A_[:], func=AF.Sigmoid, scale=1.5957691216057308), r=[kA], w=[kA])
                P.op("pool", lambda e, yk=yk: e.tensor_tensor(out=yk, in0=A_[:], in1=yk, op=ALU.mult), r=[kA, ("aT", kk_)], w=[("aT", kk_)])
            for dc in range(8):
                wb = wup[dc % 2]
                P.dma(wb[:, 0], I["wglu"][dc], w=[("wup", dc % 2, 0)], q="pool")
                P.dma(wb[:, 1], I["wglu"][8 + dc], w=[("wup", dc % 2, 1)], q="pool")
                for th in range(2):
                    vps, vpk = bank((dc * 2 + th) % 4)
                    gps, gpk = bank(4 + (dc * 2 + th) % 4)
                    for gv, (ps, pk) in enumerate(((vps, vpk), (gps, gpk))):
                        for kk_ in range(8):
                            P.op("pe", lambda e, ps=ps, kk_=kk_, th=th, wb=wb, gv=gv: e.matmul(ps, lhsT=wb[:, gv, kk_, :], rhs=ysb[:, kk_, th * 512:(th + 1) * 512], start=(kk_ == 0), stop=(kk_ == 7)),
                                 r=[("wup", dc % 2, gv), ("aT", kk_)], w=[pk])
                    sl = slice(th * 512, (th + 1) * 512)
                    P.op("act", lambda e, gps=gps, sl=sl: e.activation(out=B_[:, sl], in_=gps, func=AF.Sigmoid), r=[gpk], w=[kB])
                    P.op("dve", lambda e, vps=vps, sl=sl: e.tensor_tensor(out=B_[:, sl], in0=vps, in1=B_[:, sl], op=ALU.mult), r=[vpk, kB], w=[kB])
                    P.op("dve", lambda e, dc=dc, sl=sl: e.scalar_tensor_tensor(out=xT[:, dc, sl], in0=B_[:, sl], scalar=mod[:, 16 + dc:17 + dc], in1=xT[:, dc, sl], op0=ALU.mult, op1=ALU.add),
                         r=[kB, "mod", ("xT", dc)], w=[("xT", dc)])

        eps_t = P.sb("eps_t", [128, 1])
        P.op("dve", lambda e: e.memset(eps_t[:], EPS), w=["eps"])
        halfpi = P.sb("halfpi", [128, 1])
        P.op("dve", lambda e: e.memset(halfpi[:], 1.5707963267948966), w=["halfpi"])


        for l in range(STAGE["layers"]):
            ada_layer(l)
            norm_mod(0)
            if STAGE["mixers"]:
                if l % 3 == 0:
                    attention(l)
                elif l % 3 == 1 and STAGE.get("hgrn", True):
                    hgrn(l)
                elif l % 3 == 2 and STAGE.get("ssm", True):
                    ssm(l)
            norm_mod(1)
            ffn(l)

        rms_stats()
        for k in range(8):
            P.op("dve", lambda e, k=k: e.scalar_tensor_tensor(out=xT[:, k, :], in0=xT[:, k, :], scalar=fg[:, k:k + 1], in1=rstd[:], op0=ALU.mult, op1=ALU.mult),
                 r=[("xT", k), "fg", "rstd"], w=[("xT", k)])
        for k in range(8):
            P.dma(O["y"][k * 128:(k + 1) * 128, :], xT[:, k, :], r=[("xT", k)], sem="yout")
        P.op("pool", lambda e: e.memset(rstd[:], 0.0), r=["rstd"], w=["rstd"])
        if not (STAGE["mixers"] and STAGE.get("hgrn", True) and STAGE["layers"] > 1):
            for i in range(8):
                P.dma(O["hg"][i * 8:(i + 1) * 8].rearrange("a p n -> p a n"), rstd[:].rearrange("p (a n) -> p a n", a=8), r=["rstd"], sem="sout")
        if not (STAGE["mixers"] and STAGE.get("ssm", True) and STAGE["layers"] > 2):
            for a_ in range(8):
                P.dma(O["ssm"][a_].rearrange("g p r -> g (p r)"), rstd[0:64, 0:128], r=["rstd"], sem="sout")
        P.wait_all_dma()
        P.emit()
    return nc

def _c(a):
    return np.ascontiguousarray(a, dtype=np.float32)


def prep_inputs(inp):
    g = {k: np.asarray(v) for k, v in inp.items()}
    sh = {}
    sh["ident"] = np.eye(128, dtype=np.float32)
    sh["ada_w"] = _c(g["ada_w"].reshape(DEPTH, 8, 128, 12, 512).transpose(0, 3, 2, 1, 4))
    sh["ada_b"] = _c(g["ada_b"].reshape(DEPTH, 48, 128).transpose(0, 2, 1))
    sh["ng"] = _c(np.stack([g["norm1_g"], g["norm2_g"]]).reshape(2, DEPTH, 8, 128).transpose(3, 0, 1, 2))
    sh["final_g"] = _c(g["final_g"].reshape(8, 128).T)
    sh["ffn_w_up"] = _c(g["ffn_w_up"].reshape(DEPTH, 8, 128, 2, NFC, 128).transpose(0, 4, 2, 3, 1, 5))
    sh["ffn_w_down"] = _c(g["ffn_w_down"].reshape(DEPTH, NFC, 128, 8, 128).transpose(0, 3, 2, 1, 4))
    sh["ffn_conv_w"] = _c(g["ffn_conv_w"].reshape(DEPTH, 3, 2 * NFC, 128).transpose(0, 3, 1, 2))
    sh["ffn_conv_b"] = _c(g["ffn_conv_b"].reshape(DEPTH, 2 * NFC, 128).transpose(0, 2, 1))
    wqkv = g["attn_wqkv"]
    sh["wq"] = _c(wqkv[:, :, 0:1024].reshape(2, 8, 128, 8, 128).transpose(0, 3, 2, 1, 4))
    wk = wqkv[:, :, 1024:1280].reshape(2, 8, 128, 4, 1, 64)
    sh["wk"] = _c(np.broadcast_to(wk, (2, 8, 128, 4, 2, 64)).reshape(2, 8, 128, 4, 128).transpose(0, 3, 2, 1, 4))
    sh["wkv"] = _c(wqkv[:, :, 1024:1536].reshape(2, 8, 128, 512).transpose(0, 2, 1, 3))
    sh["wo"] = _c(g["attn_wo"].reshape(2, 8, 128, 8, 128).transpose(0, 3, 2, 1, 4))
    sh["sink"] = _c(np.broadcast_to(g["attn_sink"][:, None, :], (2, 128, 16)))
    hw = g["hgrn_w_in"][0]
    sh["hw_in"] = _c(hw.reshape(8, 128, 5, 8, 128).transpose(3, 2, 1, 0, 4))
    sh["hwo"] = _c(g["hgrn_wo"][0].reshape(8, 128, 8, 128).transpose(2, 1, 0, 3))
    sh["hlb"] = _c(g["hgrn_lb"].reshape(4, 2, 8, 128).transpose(3, 0, 1, 2))
    sh["hgn"] = _c(g["hgrn_g_norm"][0].reshape(128, 1))
    ii = np.arange(128)
    same = (ii[:, None] // 32) == (ii[None, :] // 32)
    sh["hmask"] = _c(np.stack([same & (ii[:, None] <= ii[None, :]), same & (ii[:, None] >= ii[None, :])], axis=1))
    are, aim, ldt = g["ssm_a_re"][0], g["ssm_a_im"][0], g["ssm_log_dt"][0]
    bre, bim, cre, cim = g["ssm_b_re"][0], g["ssm_b_im"][0], g["ssm_c_re"][0], g["ssm_c_im"][0]
    sh["sBreR"] = _c(bre.reshape(2, 8, 8, 64, 16).transpose(0, 1, 2, 4, 3).reshape(2, 8, 128, 64))
    sh["sBimR"] = _c(bim.reshape(2, 8, 8, 64, 16).transpose(0, 1, 2, 4, 3).reshape(2, 8, 128, 64))
    sh["sAreR"] = _c(np.broadcast_to(are.reshape(2, 8, 8, 1, 64), (2, 8, 8, 16, 64)).reshape(2, 8, 128, 64))
    sh["sAimR"] = _c(np.broadcast_to(aim.reshape(2, 8, 8, 1, 64), (2, 8, 8, 16, 64)).reshape(2, 8, 128, 64))
    sh["sDtR"] = _c(np.broadcast_to(ldt.reshape(2, 8, 8, 1, 1), (2, 8, 8, 16, 1)).reshape(2, 8, 128, 1))
    sh["sAreQ"] = _c(are.reshape(2, 2, 32, 64).transpose(0, 1, 3, 2).reshape(2, 128, 32))
    sh["sAimQ"] = _c(aim.reshape(2, 2, 32, 64).transpose(0, 1, 3, 2).reshape(2, 128, 32))
    sh["sDtQ"] = _c(np.broadcast_to(ldt.reshape(2, 2, 1, 32), (2, 2, 64, 32)).reshape(2, 128, 32))
    sh["sCreQ"] = _c(cre.reshape(2, 2, 32, 16, 64).transpose(0, 1, 4, 2, 3).reshape(2, 128, 32, 16))
    sh["sCimQ"] = _c(cim.reshape(2, 2, 32, 16, 64).transpose(0, 1, 4, 2, 3).reshape(2, 128, 32, 16))
    sh["sD"] = _c(g["ssm_d"][0].reshape(8, 128).T)
    sm = np.zeros((128, 12), np.float32)
    for q in range(8):
        sm[q * 16:(q + 1) * 16, q] = 1.0
    sm[0:64, 8] = 1.0; sm[64:128, 9] = 1.0; sm[0:64, 10] = -1.0; sm[64:128, 11] = -1.0
    sh["smask"] = sm
    sh["wglu"] = _c(g["ssm_w_glu"][0].reshape(8, 128, 16, 128).transpose(2, 1, 0, 3))
    tt = np.arange(NT)
    inv = 1.0 / (10000.0 ** (np.arange(0, 32, 2, dtype=np.float32) / np.float32(32)))
    ar = (tt // 64).astype(np.float32)[:, None] * inv.astype(np.float32)
    ac = (tt % 64).astype(np.float32)[:, None] * inv.astype(np.float32)
    ang = np.concatenate([ar, ar, ac, ac], axis=-1).astype(np.float32)
    cosS = _c(np.concatenate([np.cos(ang).T] * 2, axis=0)); sinS = _c(np.concatenate([np.sin(ang).T] * 2, axis=0))
    rm = np.zeros((128, 128), np.float32)
    for m in range(128):
        d = m % 64
        if (d % 32) < 16:
            rm[m + 16, m] = -1.0
        else:
            rm[m - 16, m] = 1.0
    sh["rmat"] = rm
    kk = np.arange(128)[:, None]; qq = np.arange(128)[None, :]
    mS = np.zeros((128, 8, 2, 128), np.float32); mP = np.zeros((128, 8, 2, 128), np.float32)
    for i in range(8):
        if i >= 1:
            mS[:, i, 0, :] = (kk >= qq)
        if i <= 6:
            mS[:, i, 1, :] = (kk <= qq)
        mP[:, i, 0, :] = 1.0 if i % 2 == 1 else 0.0
        mP[:, i, 1, :] = 1.0 if i % 2 == 0 else 0.0
    if STAGE.get("kv_only"):
        for kk_ in ("wq", "wk", "wo", "sink", "rmat"):
            sh.pop(kk_, None)
    maps = []
    for c in range(8):
        m = dict(sh)
        if c < 4:
            xc = g["x_sample"][c]
            cond = g["c"][c]
            kp = 1.0
        else:
            xc = g["x_prompt"][4 * (c - 4):4 * (c - 4) + 4].reshape(NT, D)
            cond = g["c_ctx"]
            kp = 0.0
        if c < 4:
            ropec = cosS; ropes = sinS; m["amask"] = mS
            ck = g["cache_k"][c]
            m["ckd"] = _c(np.broadcast_to(ck[:, :, :, None, :], (2, 512, 4, 2, 64)).reshape(2, 512, 512))
            cv = g["cache_v"][c]
            vp = np.zeros((2, 512, 4, 2, 128), np.float32)
            vp[:, :, :, 0, 0:64] = cv; vp[:, :, :, 0, 64] = 1.0
            vp[:, :, :, 1, 64:128] = cv; vp[:, :, :, 1, 0] = 1.0
            m["cvp"] = vp.reshape(2, 512, 1024)
        else:
            ropec = np.ones((128, NT), np.float32); ropes = np.zeros((128, NT), np.float32); m["amask"] = mP
            m["ckd"] = np.zeros((2, 512, 512), np.float32); m["cvp"] = np.zeros((2, 512, 1024), np.float32)
        if STAGE.get("kv_only"):
            for kk_ in ("amask", "ckd", "cvp"):
                m.pop(kk_, None)
        if c < 4:
            st = g["state_ssm"][c, 0]
            m["sH0"] = _c(st.reshape(2, 2, 32, 64, 2).transpose(0, 1, 3, 2, 4).reshape(2, 128, 32, 2))
        else:
            m["sH0"] = np.zeros((2, 128, 32, 2), np.float32)
        m["hs0"] = _c(g["state_hgrn"][c, 0]) if c < 4 else np.zeros((2, 8, 128, 128), np.float32)
        m["x"] = _c(np.concatenate([xc.T, ropec, ropes], axis=0))
        m["cond"] = _c(cond.reshape(8, 128).T)
        m["keep"] = _c(np.stack([np.full(128, kp), np.full(128, kp - 1.0)], axis=1))
        maps.append(m)
    return maps


def assemble(results):
    f = lambda a: np.asarray(a, dtype=np.float32)
    ys = np.stack([f(results[c]["y"]).T for c in range(4)])
    yp = np.concatenate([f(results[c]["y"]).T.reshape(4, 256, D) for c in range(4, 8)])
    nk = np.concatenate([f(results[c]["nk"]).reshape(2, 4, 256, 4, 64).transpose(1, 0, 2, 3, 4) for c in range(4, 8)])
    nv = np.concatenate([f(results[c]["nv"]).reshape(2, 4, 256, 4, 64).transpose(1, 0, 2, 3, 4) for c in range(4, 8)])
    hg = np.concatenate([f(results[c]["hg"]).reshape(4, 1, 2, 8, 128, 128) for c in range(4, 8)])
    ssm = np.concatenate([f(results[c]["ssm"]).reshape(4, 1, 2, 64, 64, 2) for c in range(4, 8)])
    return (np.ascontiguousarray(yp), np.ascontiguousarray(ys), np.ascontiguousarray(nk), np.ascontiguousarray(nv),
            np.ascontiguousarray(hg), np.ascontiguousarray(ssm))


def kernel(**inputs):
    nc = build_program()
    maps = prep_inputs(inputs)
    res = run_bass_kernel_spmd(nc, maps, core_ids=list(range(8)))
    return assemble(res.results)
```
